# Optimizing a Trainium2 kernel written in Bass

```python
import math
import jax, jax.numpy as jnp
from jax import lax
import numpy as np

D_MODEL = 1024
BATCH = 4
SEQ = 4096
DEPTH = 2

GRID_W = 64
CTX_LEN = 256
N_BRANCH = 3
D_CONV = D_MODEL
CONV_WIDTH = 31
D_SSM = D_MODEL
SSM_GROUP = 16
SSM_GROUPS = D_SSM // SSM_GROUP
SSM_STATE = 64
DT_MIN = 1e-3
DT_MAX = 1e-1
ATTN_HEADS = 8
ATTN_DH = 64
ATTN_DV = 2 * ATTN_DH
QK_W = ATTN_HEADS * 2 * ATTN_DH
D_ATTN = ATTN_HEADS * ATTN_DV
Q_BLOCK = 128
ROPE_BASE = 10000.0
N_EXPERTS = 16
D_EXPERT = 2816
CAPACITY_FACTOR = 2
DEEPNORM_ALPHA = (2.0 * DEPTH) ** 0.25
DEEPNORM_BETA = (8.0 * DEPTH) ** -0.25
LN_EPS = 1e-6
RMS_EPS = 1e-5
CONV_OFF = 0
GATE_OFF = CONV_OFF + 2 * D_CONV
Q_OFF = GATE_OFF + N_BRANCH * D_MODEL
SSM_OFF = Q_OFF + QK_W
K_OFF = SSM_OFF + D_SSM
V_OFF = K_OFF + QK_W
D_IN = V_OFF + D_ATTN

kernel_name = 'hybrid_conv_s5_diffattn_ec_moe_block'


def layer_norm(x, gain=None, bias=None):
    xf = x.astype(jnp.float32)
    mu = jnp.mean(xf, -1, keepdims=True)
    var = jnp.mean(jnp.square(xf - mu), -1, keepdims=True)
    y = (xf - mu) * lax.rsqrt(var + LN_EPS)
    if gain is not None:
        y = y * gain.astype(jnp.float32) + bias.astype(jnp.float32)
    return y.astype(x.dtype)


def rms_norm(x, gain):
    xf = x.astype(jnp.float32)
    y = xf * lax.rsqrt(jnp.mean(jnp.square(xf), -1, keepdims=True) + RMS_EPS) * gain.astype(jnp.float32)
    return y.astype(x.dtype)


def axial_rope_tables(length):
    rows = length // GRID_W
    row = jnp.repeat(jnp.arange(rows), GRID_W)
    col = jnp.tile(jnp.arange(GRID_W), rows)
    n_freq = ATTN_DH // 4
    inv_freq = ROPE_BASE ** (-jnp.arange(n_freq, dtype=jnp.float32) / n_freq)
    ang = jnp.stack([row, col], -1).astype(jnp.float32)[:, :, None, None] * inv_freq
    ang = jnp.broadcast_to(ang, (length, 2, 2, n_freq)).reshape(length, ATTN_DH)
    return jnp.cos(ang), jnp.sin(ang)


def apply_rope(x, cos, sin):
    xa = x.reshape(x.shape[:-1] + (2, 2, ATTN_DH // 4))
    rot = jnp.concatenate([-xa[..., 1:, :], xa[..., :1, :]], axis=-2).reshape(x.shape)
    return (x * cos[:, None, None, :] + rot * sin[:, None, None, :]).astype(x.dtype)


def diff_attention(q, k, v, lam):
    s = jnp.einsum('bqhsd,bkhsd->bhsqk', q, k).astype(jnp.float32) * (ATTN_DH ** -0.5)
    p = jax.nn.softmax(s, axis=-1)
    w = (p[:, :, 0] - lam * p[:, :, 1]).astype(v.dtype)
    return jnp.einsum('bhqk,bkhe->bqhe', w, v)


def blocked_diff_attention(q, k, v, lam):
    b, l = q.shape[:2]
    nb = l // Q_BLOCK
    qb = q.reshape((b, nb, Q_BLOCK) + q.shape[2:]).swapaxes(0, 1)
    o = lax.map(lambda qi: diff_attention(qi, k, v, lam), qb)
    return o.swapaxes(0, 1).reshape((b, l) + o.shape[3:])


def attn_output(o, p, lam_init):
    o = rms_norm(o, p['attn_subln_g']) * (1.0 - lam_init)
    return o.reshape(o.shape[:2] + (D_ATTN,)) @ p['w_attn_out']


def conv_branch(zconv, p):
    g = zconv[..., :D_CONV] * jax.nn.sigmoid(zconv[..., D_CONV:])
    half = CONV_WIDTH // 2
    y = lax.conv_general_dilated(g, p['conv_w'][:, None, :], (1,), [(half, half)],
                                 dimension_numbers=('NWC', 'WIO', 'NWC'),
                                 feature_group_count=D_CONV) + p['conv_b']
    y = jax.nn.silu(layer_norm(y, p['conv_ln_g'], p['conv_ln_b']))
    return y @ p['w_conv_out']


def s5_discretise(a_re, a_im, log_dt, b_re, b_im):
    a = lax.complex(a_re.astype(jnp.float32), a_im.astype(jnp.float32))
    dt_a = jnp.exp(log_dt.astype(jnp.float32))[:, None] * a
    a_bar = jnp.exp(dt_a)
    b = lax.complex(b_re.astype(jnp.float32), b_im.astype(jnp.float32))
    b_bar = ((a_bar - 1.0) / a)[..., None] * b
    return dt_a, a_bar, b_bar


def _ssm_combine(left, right):
    a_l, b_l = left
    a_r, b_r = right
    return a_l * a_r, a_r * b_l + b_r


def s5_states(u, dt_a, a_bar, b_bar, reverse, h0=None):
    length = u.shape[1]
    bu = jnp.einsum('blgc,gpc->blgp', u.astype(jnp.complex64), b_bar)
    a = jnp.broadcast_to(a_bar, (1, length) + a_bar.shape)
    _, s = lax.associative_scan(_ssm_combine, (a, bu), reverse=reverse, axis=1)
    if h0 is not None:
        steps = jnp.arange(length, 0, -1) if reverse else jnp.arange(1, length + 1)
        s = s + jnp.exp(steps.astype(jnp.float32)[:, None, None] * dt_a)[None] * h0[:, None]
    return s


def s5_readout(s, c_mat):
    return jnp.real(jnp.einsum('blgp,gcp->blgc', s, c_mat))


def ssm_scans(u, u_c, p, ctx_out):
    b, l, _ = u.shape
    ug = u.reshape(b, l, SSM_GROUPS, SSM_GROUP)
    ucg = u_c.reshape(b, u_c.shape[1], SSM_GROUPS, SSM_GROUP)
    y_lat, y_ctx = [], []
    for d, rev in enumerate((False, True)):
        dt_a, a_bar, b_bar = s5_discretise(p['ssm_a_re'][d], p['ssm_a_im'][d], p['ssm_log_dt'][d],
                                           p['ssm_b_re'][d], p['ssm_b_im'][d])
        c_mat = lax.complex(p['ssm_c_re'][d].astype(jnp.float32), p['ssm_c_im'][d].astype(jnp.float32))
        s_ctx = s5_states(ucg, dt_a, a_bar, b_bar, rev)
        h0 = s_ctx[:, 0] if rev else s_ctx[:, -1]
        y_lat.append(s5_readout(s5_states(ug, dt_a, a_bar, b_bar, rev, h0), c_mat))
        if ctx_out:
            y_ctx.append(s5_readout(s_ctx, c_mat))
    return y_lat[0] + y_lat[1], (y_ctx[0] + y_ctx[1] if ctx_out else None)


def s5_output(y, u, p):
    y = y.reshape(u.shape).astype(u.dtype) + p['ssm_d'] * u
    y = jax.nn.gelu(y)
    y = y * jax.nn.sigmoid(y @ p['w_ssm_glu'])
    return y @ p['w_ssm_out']


def split_side(zs):
    b, n = zs.shape[:2]
    u = zs[..., :D_SSM]
    k = zs[..., K_OFF - SSM_OFF:V_OFF - SSM_OFF].reshape(b, n, ATTN_HEADS, 2, ATTN_DH)
    v = zs[..., V_OFF - SSM_OFF:].reshape(b, n, ATTN_HEADS, ATTN_DV)
    return u, k, v


def merge(zg, y_conv, y_ssm, y_attn, w_o):
    g = jax.nn.sigmoid(zg).reshape(zg.shape[:-1] + (N_BRANCH, D_MODEL))
    m = g[..., 0, :] * y_conv + g[..., 1, :] * y_ssm + g[..., 2, :] * y_attn
    return m @ w_o


def token_mixer(h, hc, p, layer_idx, cos, sin, ctx_out):
    b, l, _ = h.shape
    lc = hc.shape[1]
    z = h @ p['w_in']
    zc = hc @ (p['w_in'] if ctx_out else p['w_in'][:, SSM_OFF:])
    u, k, v = split_side(z[..., SSM_OFF:])
    u_c, k_c, v_c = split_side(zc[..., SSM_OFF:] if ctx_out else zc)
    y_conv = conv_branch(z[..., CONV_OFF:GATE_OFF], p)
    y_s_lat, y_s_ctx = ssm_scans(u, u_c, p, ctx_out)
    y_ssm = s5_output(y_s_lat, u, p)
    lam_init = 0.8 - 0.6 * math.exp(-0.3 * layer_idx)
    lq1, lk1, lq2, lk2 = [p['attn_lambda'][i].astype(jnp.float32) for i in range(4)]
    lam = jnp.exp(jnp.sum(lq1 * lk1)) - jnp.exp(jnp.sum(lq2 * lk2)) + lam_init
    q = apply_rope(z[..., Q_OFF:SSM_OFF].reshape(b, l, ATTN_HEADS, 2, ATTN_DH), cos, sin)
    k = apply_rope(k, cos, sin)
    o = blocked_diff_attention(q, jnp.concatenate([k_c, k], 1), jnp.concatenate([v_c, v], 1), lam)
    y_attn = attn_output(o, p, lam_init)
    y = merge(z[..., GATE_OFF:Q_OFF], y_conv, y_ssm, y_attn, p['w_o'])
    if not ctx_out:
        return y, None
    q_c = zc[..., Q_OFF:SSM_OFF].reshape(b, lc, ATTN_HEADS, 2, ATTN_DH)
    o_c = diff_attention(q_c, k_c, v_c, lam)
    y_c = merge(zc[..., GATE_OFF:Q_OFF], conv_branch(zc[..., CONV_OFF:GATE_OFF], p),
                s5_output(y_s_ctx, u_c, p), attn_output(o_c, p, lam_init), p['w_o'])
    return y, y_c


def expert_choice_moe(h, w_router, w_gate_up, w_down):
    b, n, d = h.shape
    cap = CAPACITY_FACTOR * n // N_EXPERTS
    aff = jax.nn.softmax((h @ w_router).astype(jnp.float32), axis=-1)
    gate, idx = lax.top_k(aff.swapaxes(1, 2), cap)
    xs = jax.vmap(lambda hb, ib: hb[ib])(h, idx)
    gu = jnp.einsum('becd,edf->becf', xs, w_gate_up)
    a, up = jnp.split(gu, 2, axis=-1)
    ye = jnp.einsum('becf,efd->becd', jax.nn.silu(a) * up, w_down) * gate[..., None].astype(h.dtype)
    return jax.vmap(lambda yb, ib: jnp.zeros((n, d), h.dtype).at[ib.reshape(-1)].add(yb.reshape(-1, d)))(ye, idx)


def trunk_layer(x, xc, c, c_ctx, p, layer_idx, cos, sin, ctx_out):
    mod = (jax.nn.silu(c) @ p['w_ada'] + p['b_ada'])[:, None, :]
    mod_c = (jax.nn.silu(c_ctx) @ p['w_ada'] + p['b_ada'])[None, None, :]
    sh1, sc1, g1, sh2, sc2, g2 = jnp.split(mod, 6, axis=-1)
    csh1, csc1, cg1, csh2, csc2, cg2 = jnp.split(mod_c, 6, axis=-1)
    h = layer_norm(x) * (1.0 + sc1) + sh1
    hc = layer_norm(xc) * (1.0 + csc1) + csh1
    y, y_c = token_mixer(h, hc, p, layer_idx, cos, sin, ctx_out)
    x = layer_norm(DEEPNORM_ALPHA * x + g1 * y, p['ln1_g'], p['ln1_b'])
    h2 = layer_norm(x) * (1.0 + sc2) + sh2
    x = layer_norm(DEEPNORM_ALPHA * x + g2 * expert_choice_moe(h2, p['w_router'], p['w_gate_up'], p['w_down']),
                   p['ln2_g'], p['ln2_b'])
    if ctx_out:
        xc = layer_norm(DEEPNORM_ALPHA * xc + cg1 * y_c, p['ln1_g'], p['ln1_b'])
        hc2 = layer_norm(xc) * (1.0 + csc2) + csh2
        xc = layer_norm(DEEPNORM_ALPHA * xc + cg2 * expert_choice_moe(hc2, p['w_router'], p['w_gate_up'], p['w_down']),
                        p['ln2_g'], p['ln2_b'])
    return x, xc


def setup_inputs(seed: int = 0) -> dict:
    key = jax.random.key(seed)
    ks = iter(jax.random.split(key, 40))
    f32 = jnp.float32

    def nrm(shape, std):
        return std * jax.random.normal(next(ks), shape, f32)

    return {
        'x': nrm((BATCH, SEQ, D_MODEL), 1.0),
        'c': nrm((BATCH, D_MODEL), 1.0),
        'ctx': nrm((BATCH, CTX_LEN, D_MODEL), 1.0),
        'c_ctx': nrm((D_MODEL,), 1.0),
        'w_ada': nrm((DEPTH, D_MODEL, 6 * D_MODEL), 0.3 * D_MODEL ** -0.5),
        'b_ada': nrm((DEPTH, 6 * D_MODEL), 0.02),
        'w_in': nrm((DEPTH, D_MODEL, D_IN), D_MODEL ** -0.5),
        'conv_w': nrm((DEPTH, CONV_WIDTH, D_CONV), CONV_WIDTH ** -0.5),
        'conv_b': nrm((DEPTH, D_CONV), 0.02),
        'conv_ln_g': 1.0 + nrm((DEPTH, D_CONV), 0.02),
        'conv_ln_b': nrm((DEPTH, D_CONV), 0.02),
        'w_conv_out': nrm((DEPTH, D_CONV, D_MODEL), D_CONV ** -0.5),
        'ssm_a_re': -0.5 + nrm((DEPTH, 2, SSM_GROUPS, SSM_STATE), 0.01),
        'ssm_a_im': jnp.pi * jnp.arange(SSM_STATE, dtype=f32) + nrm((DEPTH, 2, SSM_GROUPS, SSM_STATE), 0.01),
        'ssm_log_dt': jax.random.uniform(next(ks), (DEPTH, 2, SSM_GROUPS), f32,
                                         minval=math.log(DT_MIN), maxval=math.log(DT_MAX)),
        'ssm_b_re': nrm((DEPTH, 2, SSM_GROUPS, SSM_STATE, SSM_GROUP), (2.0 * SSM_GROUP) ** -0.5),
        'ssm_b_im': nrm((DEPTH, 2, SSM_GROUPS, SSM_STATE, SSM_GROUP), (2.0 * SSM_GROUP) ** -0.5),
        'ssm_c_re': nrm((DEPTH, 2, SSM_GROUPS, SSM_GROUP, SSM_STATE), 0.5),
        'ssm_c_im': nrm((DEPTH, 2, SSM_GROUPS, SSM_GROUP, SSM_STATE), 0.5),
        'ssm_d': nrm((DEPTH, D_SSM), 0.5),
        'w_ssm_glu': nrm((DEPTH, D_SSM, D_SSM), D_SSM ** -0.5),
        'w_ssm_out': nrm((DEPTH, D_SSM, D_MODEL), D_SSM ** -0.5),
        'attn_lambda': nrm((DEPTH, 4, ATTN_DH), 0.1),
        'attn_subln_g': 1.0 + nrm((DEPTH, ATTN_DV), 0.02),
        'w_attn_out': nrm((DEPTH, D_ATTN, D_MODEL), D_ATTN ** -0.5),
        'w_o': nrm((DEPTH, D_MODEL, D_MODEL), DEEPNORM_BETA * D_MODEL ** -0.5),
        'ln1_g': 1.0 + nrm((DEPTH, D_MODEL), 0.02),
        'ln1_b': nrm((DEPTH, D_MODEL), 0.02),
        'w_router': nrm((DEPTH, D_MODEL, N_EXPERTS), D_MODEL ** -0.5),
        'w_gate_up': nrm((DEPTH, N_EXPERTS, D_MODEL, 2 * D_EXPERT), D_MODEL ** -0.5),
        'w_down': nrm((DEPTH, N_EXPERTS, D_EXPERT, D_MODEL), DEEPNORM_BETA * D_EXPERT ** -0.5),
        'ln2_g': 1.0 + nrm((DEPTH, D_MODEL), 0.02),
        'ln2_b': nrm((DEPTH, D_MODEL), 0.02),
    }


def reference(x, c, ctx, c_ctx, w_ada, b_ada, w_in, conv_w, conv_b, conv_ln_g, conv_ln_b, w_conv_out,
              ssm_a_re, ssm_a_im, ssm_log_dt, ssm_b_re, ssm_b_im, ssm_c_re, ssm_c_im, ssm_d, w_ssm_glu,
              w_ssm_out, attn_lambda, attn_subln_g, w_attn_out, w_o, ln1_g, ln1_b, w_router, w_gate_up,
              w_down, ln2_g, ln2_b):
    cos, sin = axial_rope_tables(x.shape[1])
    xc = ctx
    for l in range(DEPTH):
        p = {
            'w_ada': w_ada[l], 'b_ada': b_ada[l], 'w_in': w_in[l],
            'conv_w': conv_w[l], 'conv_b': conv_b[l], 'conv_ln_g': conv_ln_g[l], 'conv_ln_b': conv_ln_b[l],
            'w_conv_out': w_conv_out[l],
            'ssm_a_re': ssm_a_re[l], 'ssm_a_im': ssm_a_im[l], 'ssm_log_dt': ssm_log_dt[l],
            'ssm_b_re': ssm_b_re[l], 'ssm_b_im': ssm_b_im[l], 'ssm_c_re': ssm_c_re[l], 'ssm_c_im': ssm_c_im[l],
            'ssm_d': ssm_d[l], 'w_ssm_glu': w_ssm_glu[l], 'w_ssm_out': w_ssm_out[l],
            'attn_lambda': attn_lambda[l], 'attn_subln_g': attn_subln_g[l], 'w_attn_out': w_attn_out[l],
            'w_o': w_o[l], 'ln1_g': ln1_g[l], 'ln1_b': ln1_b[l],
            'w_router': w_router[l], 'w_gate_up': w_gate_up[l], 'w_down': w_down[l],
            'ln2_g': ln2_g[l], 'ln2_b': ln2_b[l],
        }
        x, xc = trunk_layer(x, xc, c, c_ctx, p, l, cos, sin, ctx_out=(l < DEPTH - 1))
    return x
```

```python
import math
from contextlib import ExitStack, contextmanager
import numpy as np
import concourse.bass as bass
import concourse.mybir as mybir
from concourse.bass_utils import run_bass_kernel_spmd

F32 = mybir.dt.float32
BF16 = mybir.dt.bfloat16
I32 = mybir.dt.int32
AF = mybir.ActivationFunctionType
ALU = mybir.AluOpType
AX = mybir.AxisListType

ENGS = ['tensor', 'vector', 'scalar', 'gpsimd', 'sync']
EPOCH = 30000
DEPOCH = 28800

D = 1024
NCTX = 256
NLAT = 4096
NT = NCTX + NLAT
NTILE = NT // 128
DIN = 9216
DEPTH = 2
NEXP = 16
DEXP = 2816
LN_EPS = 1e-6
RMS_EPS = 1e-5
ALPHA = (2.0 * DEPTH) ** 0.25
TWO_PI = 2.0 * math.pi
CW1 = 6.28125
CW2 = float(np.float32(TWO_PI - CW1))
MAGIC = 12582912.0
PI_LO = 3.1415925


class DSem:
    def __init__(self, sched, name, persistent=False):
        self.sched = sched
        self.name = name
        self.sems = []
        self.persistent = persistent
        sched.dsems.append(self)
        if not persistent and sched.phase_dsems:
            sched.phase_dsems[-1].append(self)

    def next_inc(self):
        if not self.sems or self.sems[-1][1] + 16 > DEPOCH:
            self.sems.append(self.sched.take_sem())
        self.sems[-1][1] += 16
        return self.sems[-1][0]


class Buf:
    def __init__(self, t, name='', grp=None, persistent=False):
        self.persistent = persistent
        self.t = t
        self.name = name
        self.w = None
        self.r = {}
        self.grp = grp

    def __getitem__(self, idx):
        return self.t[idx]


class Sched:
    def __init__(self, nc, es):
        self.nc = nc
        self.es = es
        self.q = {e: [] for e in ENGS}
        self.esems = {e: [] for e in ENGS}
        self.cnt = {e: 0 for e in ENGS}
        self.waited = {e: {} for e in ENGS}
        self.nsem = 0
        self.nins = 0
        self.dsems = []
        self.alloc = es
        self.uid = 0
        self.sem_pool = []
        self.selfwait = True
        self.phase_dsems = []

    def take_sem(self):
        while self.sem_pool:
            ent = self.sem_pool.pop()
            if ent[1] + 16 * 64 <= DEPOCH:
                return ent
        return [self.new_sem(), 0]

    def new_sem(self):
        self.nsem += 1
        return self.es.enter_context(self.nc.semaphore('sm%d' % self.nsem))

    def sbuf(self, shape, dt, name, grp=None):
        self.uid += 1
        nm = '%s_%d' % (name, self.uid)
        t = self.alloc.enter_context(self.nc.sbuf_tensor(nm, list(shape), dt))
        return Buf(t, nm, grp)

    def psum(self, shape, dt, name):
        self.uid += 1
        nm = '%s_%d' % (name, self.uid)
        t = self.alloc.enter_context(self.nc.psum_tensor(nm, list(shape), dt))
        return Buf(t, nm)

    def dram(self, shape, dt, name, grp=None, kind="Internal"):
        t = self.nc.dram_tensor(name, list(shape), dt, kind=kind)
        return Buf(t, name, grp, persistent=True)

    def ring(self, n, shape, dt, name, psum=False):
        bufs = [(self.psum if psum else self.sbuf)(shape, dt, '%s%d' % (name, i)) for i in range(n)]
        st = {'i': 0}

        def nxt():
            b = bufs[st['i'] % n]
            st['i'] += 1
            return b
        return nxt

    def _esem(self, eng, epoch):
        lst = self.esems[eng]
        while len(lst) <= epoch:
            lst.append(self.new_sem())
        return lst[epoch]

    def _collect(self, reads, writes):
        toks = []
        for b in reads:
            if b.w is not None:
                toks.append(b.w)
        for b in writes:
            if b.w is not None:
                toks.append(b.w)
            toks.extend(b.r.values())
        return toks

    def _waits(self, eng, toks):
        need = {}
        for tk in toks:
            if tk[0] == 'e':
                _, pe, ep, seq = tk
                if pe == eng and (eng == 'tensor' or not self.selfwait):
                    continue
                key = ('e', pe, ep)
                if need.get(key, (0,))[0] < seq:
                    need[key] = (seq, self._esem(pe, ep))
            else:
                ds = tk[1]
                for i, (sem, c) in enumerate(ds.sems):
                    key = ('d', id(ds), i)
                    if need.get(key, (0,))[0] < c:
                        need[key] = (c, sem)
        wd = self.waited[eng]
        for key, (val, sem) in need.items():
            if wd.get(key, 0) >= val:
                continue
            wd[key] = val
            self.q[eng].append(lambda e, sem=sem, val=val: e.wait_ge(sem, val))

    def _tok(self, eng):
        c = self.cnt[eng]
        if c == 0:
            return None
        ep, seq = divmod(c - 1, EPOCH)
        return ('e', eng, ep, seq + 1)

    def op(self, eng, fn, reads=(), writes=()):
        self._waits(eng, self._collect(reads, writes))
        self.cnt[eng] += 1
        tok = self._tok(eng)
        sem = self._esem(eng, tok[2])
        self.q[eng].append(lambda e, fn=fn, sem=sem: fn(e).then_inc(sem, 1))
        for b in writes:
            b.w = tok
            b.r = {}
        for b in reads:
            if b.w is not tok:
                b.r[eng] = tok
        self.nins += 1
        return tok

    def dma(self, eng, out_ap, in_ap, dst, src, part=False, **kw):
        if dst.grp is None:
            dst.grp = DSem(self, dst.name, persistent=dst.persistent)
        ds = dst.grp
        if part and dst.w is not None and dst.w[0] == 'd' and dst.w[1] is ds:
            toks = self._collect([src], [])
            toks.extend(dst.r.values())
            self._waits(eng, toks)
        else:
            self._waits(eng, self._collect([src], [dst]))
        sem = ds.next_inc()
        self.q[eng].append(lambda e, sem=sem: e.dma_start(out=out_ap, in_=in_ap, **kw).then_inc(sem, 16))
        tok = ('d', ds)
        dst.w = tok
        dst.r = {}
        src.r[('d', id(ds))] = tok
        self.nins += 1
        return tok

    def wait_buf(self, eng, buf):
        self._waits(eng, self._collect([buf], []))

    def barrier(self):
        toks = [t for t in (self._tok(e) for e in ENGS) if t is not None]
        toks += [('d', ds) for ds in self.dsems]
        for e in ENGS:
            self._waits(e, toks)

    def flush(self):
        if not any(self.q.values()):
            return
        with self.nc.Block() as block:
            for en in ENGS:
                lst = self.q[en]
                if not lst:
                    continue

                def f(e, lst=lst):
                    for fn in lst:
                        fn(e)
                getattr(block, en)(f)
        self.q = {e: [] for e in ENGS}

    @contextmanager
    def phase(self):
        prev = self.alloc
        self.barrier()
        self.phase_dsems.append([])
        with ExitStack() as ph:
            self.alloc = ph
            yield
            self.barrier()
            self.flush()
        for ds in self.phase_dsems.pop():
            if ds.sems:
                self.sem_pool.append(ds.sems[-1])
        self.alloc = prev


def interleave(gens):
    gens = list(gens)
    while gens:
        for g in list(gens):
            try:
                next(g)
            except StopIteration:
                gens.remove(g)


W_NAMES = [('w_ada', [DEPTH, D, 6 * D]), ('b_ada', [DEPTH, 6 * D]), ('w_in', [DEPTH, D, DIN]),
           ('conv_w', [DEPTH, 31, D]), ('conv_b', [DEPTH, D]), ('conv_ln_g', [DEPTH, D]), ('conv_ln_b', [DEPTH, D]),
           ('w_conv_out', [DEPTH, D, D]), ('ssm_a_re', [DEPTH, 2, 64, 64]), ('ssm_a_im', [DEPTH, 2, 64, 64]),
           ('ssm_log_dt', [DEPTH, 2, 64]), ('ssm_b_re', [DEPTH, 2, 64, 64, 16]), ('ssm_b_im', [DEPTH, 2, 64, 64, 16]),
           ('ssm_c_re', [DEPTH, 2, 64, 16, 64]), ('ssm_c_im', [DEPTH, 2, 64, 16, 64]), ('ssm_d', [DEPTH, D]),
           ('w_ssm_glu', [DEPTH, D, D]), ('w_ssm_out', [DEPTH, D, D]), ('attn_lambda', [DEPTH, 4, 64]),
           ('attn_subln_g', [DEPTH, 128]), ('w_attn_out', [DEPTH, D, D]), ('w_o', [DEPTH, D, D]),
           ('ln1_g', [DEPTH, D]), ('ln1_b', [DEPTH, D]), ('w_router', [DEPTH, D, NEXP]),
           ('w_gate_up', [DEPTH, NEXP, D, 2 * DEXP]), ('w_down', [DEPTH, NEXP, DEXP, D]),
           ('ln2_g', [DEPTH, D]), ('ln2_b', [DEPTH, D])]


class Prog:
    def __init__(self, debug=(), stop=None, nlayers=DEPTH, skip_inputs=(), phases=None):
        self.debug = set(debug)
        self.phase_names = phases
        self.stop = stop
        self.nlayers = nlayers
        self.nc = bass.Bass("TRN2", target_bir_lowering=False)
        nc = self.nc
        self.I = {}
        for nm, shp in [('x', [NLAT, D]), ('c', [D]), ('ctx', [NCTX, D]), ('c_ctx', [D])] + W_NAMES:
            if nm in skip_inputs:
                continue
            self.I[nm] = Buf(nc.dram_tensor(nm, shp, F32, kind="ExternalInput"), nm)
        self.out = Buf(nc.dram_tensor("out", [NLAT, D], F32, kind="ExternalOutput"), 'out', persistent=True)
        self.scr = {}

    def scratch(self, name, shape, dt):
        if name not in self.scr:
            kind = "ExternalOutput" if name in self.debug else "Internal"
            self.scr[name] = self.S.dram(shape, dt, name, kind=kind)
        return self.scr[name]

    def build(self):
        nc = self.nc
        with ExitStack() as es:
            self.S = S = Sched(nc, es)
            self.consts()
            xres = self.scratch('xres', [NT, D], F32)
            S.dma('sync', xres[0:NCTX, :], self.I['ctx'][:, :], xres, self.I['ctx'])
            S.dma('sync', xres[NCTX:NT, :], self.I['x'][:, :], xres, self.I['x'])
            done = False
            for l in range(self.nlayers):
                plist = [self.ph_mod, self.ph_inproj, self.ph_convout, self.ph_attn, self.ph_ssm, self.ph_merge, self.ph_moe, self.ph_ln2]
                if self.phase_names is not None:
                    plist = [p for p in plist if p.__name__ in self.phase_names]
                for ph in plist:
                    ph(l)
                    if self.stop == (ph.__name__, l):
                        done = True
                        break
                if done:
                    break
            S.barrier()
            S.flush()
        return nc

    def consts(self):
        S = self.S
        self.ident_f = S.sbuf([128, 128], F32, 'ident_f')
        self.ident_b = S.sbuf([128, 128], BF16, 'ident_b')
        ones = S.sbuf([128, 128], F32, 'ones_f')
        self.ones_f = ones
        idf, idb = self.ident_f, self.ident_b
        S.op('gpsimd', lambda e: e.memset(ones[:, :], 1.0), writes=[ones])
        S.op('gpsimd', lambda e: e.affine_select(out=idf[:, :], in_=ones[:, :], pattern=[[1, 128]], compare_op=ALU.is_equal,
                                                 fill=0.0, base=0, channel_multiplier=-1), reads=[ones], writes=[idf])
        S.op('vector', lambda e: e.tensor_copy(out=idb[:, :], in_=idf[:, :]), reads=[idf], writes=[idb])

    def ph_mod(self, l):
        S = self.S
        I = self.I
        modscr = self.scratch('modscr%d' % l, [2, 128, 6 * D], F32)
        with S.phase():
            cT = S.sbuf([128, 2, 8], F32, 'cT')
            sil = S.sbuf([128, 2, 8], F32, 'sil')
            bc = S.sbuf([128, 2, 8, 128], F32, 'bc')
            bada = S.sbuf([128, 6 * D], F32, 'bada')
            S.dma('sync', cT[:, 0, :], I['c'].t.ap().rearrange("(k p) -> p k", p=128), cT, I['c'], allow_slow_non_contiguous=True)
            S.dma('sync', cT[:, 1, :], I['c_ctx'].t.ap().rearrange("(k p) -> p k", p=128), cT, I['c_ctx'], allow_slow_non_contiguous=True)
            S.dma('sync', bada[:, :], I['b_ada'][l:l + 1, :].partition_broadcast(128), bada, I['b_ada'])
            S.op('scalar', lambda e: e.activation(out=sil[:, :, :], in_=cT[:, :, :], func=AF.Silu), [cT], [sil])
            for w in range(2):
                S.op('vector', lambda e, w=w: e.tensor_copy(out=bc[:, w, :, :], in_=sil[:, w, :].unsqueeze(2).to_broadcast([128, 8, 128])), [sil], [bc])
            wring = S.ring(2, [128, 8, 512], F32, 'wada')
            pring = S.ring(4, [128, 512], F32, 'pmod', psum=True)
            mring = S.ring(4, [128, 512], F32, 'modt')
            wv = I['w_ada'].t.ap()[l].rearrange("(k p) n -> p k n", p=128)
            for nb in range(12):
                n0 = nb * 512
                wa = wring()
                S.dma('sync', wa[:, :, :], wv[:, :, n0:n0 + 512], wa, I['w_ada'])
                for w in range(2):
                    ps = pring()
                    for k in range(8):
                        S.op('tensor', lambda e, ps=ps, wa=wa, w=w, k=k: e.matmul(ps[:, :], lhsT=bc[:, w, k, :], rhs=wa[:, k, :], start=(k == 0), stop=(k == 7)), [bc, wa], [ps])
                    mt = mring()
                    S.op('vector', lambda e, ps=ps, mt=mt, n0=n0: e.tensor_tensor(out=mt[:, :], in0=ps[:, :], in1=bada[:, n0:n0 + 512], op=ALU.add), [ps, bada], [mt])
                    if nb in (2, 3, 8, 9):
                        S.op('vector', lambda e, mt=mt: e.tensor_scalar(out=mt[:, :], in0=mt[:, :], scalar1=1.0, scalar2=None, op0=ALU.add), [mt], [mt])
                    S.dma('gpsimd', modscr[w, :, n0:n0 + 512], mt[:, :], modscr, mt, part=True)

    def ph_inproj(self, l):
        S = self.S
        I = self.I
        modscr = self.scr['modscr%d' % l]
        xres = self.scr['xres']
        gates = self.scratch('gates', [NT, 3 * D], F32)
        qT = self.scratch('qT', [D, NT], BF16)
        kT = self.scratch('kT', [D, NT], BF16)
        uT = self.scratch('uT', [D, NT], F32)
        vtm = self.scratch('vtm', [NT, D], BF16)
        ycT = self.scratch('ycT', [D, NT], F32)
        with S.phase():
            hT = S.sbuf([128, 8, NT], BF16, 'hT')
            with S.phase():
                modv = S.sbuf([128, 2, 2, D], F32, 'modv')
                for w in range(2):
                    S.dma('sync', modv[:, w, 0, :], modscr[w, :, 0:D], modv, modscr)
                    S.dma('sync', modv[:, w, 1, :], modscr[w, :, D:2 * D], modv, modscr)
                xring = S.ring(3, [128, D], F32, 'xt')
                tring = S.ring(2, [128, D], F32, 'tt')
                hring = S.ring(2, [128, D], BF16, 'hb')
                sring = S.ring(2, [128, 16], F32, 'st')
                pring = S.ring(2, [128, 8, 128], BF16, 'pT', psum=True)
                for i in range(NTILE):
                    w = 1 if i < 2 else 0
                    xt = xring()
                    S.dma('sync', xt[:, :], xres[i * 128:(i + 1) * 128, :], xt, xres)
                    st = sring()
                    S.op('vector', lambda e, xt=xt, st=st: e.bn_stats(out=st[:, 0:6], in_=xt[:, 0:512]), [xt], [st])
                    S.op('vector', lambda e, xt=xt, st=st: e.bn_stats(out=st[:, 6:12], in_=xt[:, 512:1024]), [xt], [st])
                    S.op('vector', lambda e, st=st: e.bn_aggr(out=st[:, 12:14], in_=st[:, 0:12]), [st], [st])
                    self.rstd(st, 13, 14, LN_EPS)
                    tt = tring()
                    S.op('vector', lambda e, xt=xt, st=st, tt=tt: e.tensor_scalar(out=tt[:, :], in0=xt[:, :], scalar1=st[:, 12:13], scalar2=st[:, 14:15], op0=ALU.subtract, op1=ALU.mult), [xt, st], [tt])
                    S.op('vector', lambda e, tt=tt, w=w: e.tensor_tensor(out=tt[:, :], in0=tt[:, :], in1=modv[:, w, 1, :], op=ALU.mult), [tt, modv], [tt])
                    hb = hring()
                    S.op('vector', lambda e, tt=tt, hb=hb, w=w: e.tensor_tensor(out=hb[:, :], in0=tt[:, :], in1=modv[:, w, 0, :], op=ALU.add), [tt, modv], [hb])
                    pT = pring()
                    for k in range(8):
                        S.op('tensor', lambda e, pT=pT, hb=hb, k=k: e.transpose(out=pT[:, k, :], in_=hb[:, k * 128:(k + 1) * 128], identity=self.ident_b[:, :]), [hb, self.ident_b], [pT])
                    S.op('scalar', lambda e, pT=pT, i=i: e.copy(out=hT[:, :, i * 128:(i + 1) * 128], in_=pT[:, :, :]), [pT], [hT])
            if 'hT' in self.debug:
                hdbg = self.scratch('hT', [128, 8, NT], BF16)
                S.dma('sync', hdbg[:, :, :], hT[:, :, :], hdbg, hT)
            self.inproj_blocks(l, hT, gates, qT, kT, uT, vtm, ycT)

    def rstd(self, st, ci, co, eps):
        S = self.S
        S.op('vector', lambda e: e.tensor_scalar(out=st[:, co:co + 1], in0=st[:, ci:ci + 1], scalar1=eps, scalar2=None, op0=ALU.add), [st], [st])
        S.op('scalar', lambda e: e.activation(out=st[:, co:co + 1], in_=st[:, co:co + 1], func=AF.Sqrt), [st], [st])
        S.op('vector', lambda e: e.reciprocal(out=st[:, co:co + 1], in_=st[:, co:co + 1]), [st], [st])

    def load_w_bf16(self, dst, wap, src):
        self.S.dma('gpsimd', dst[:, :, :], wap, dst, src)

    def inproj_blocks(self, l, hT, gates, qT, kT, uT, vtm, ycT):
        S = self.S
        I = self.I
        wv = I['w_in'].t.ap()[l].rearrange("(k p) n -> p k n", p=128)
        TG = [(t0, min(t0 + 512, NT)) for t0 in range(0, NT, 512)]

        def mm_fm(ps, wb, cc, t0, t1):
            for k in range(8):
                S.op('tensor', lambda e, k=k: e.matmul(ps[:, 0:t1 - t0], lhsT=wb[:, k, cc * 128:(cc + 1) * 128], rhs=hT[:, k, t0:t1], start=(k == 0), stop=(k == 7)), [wb, hT], [ps])

        def mm_tm(ps, wb, i):
            for k in range(8):
                S.op('tensor', lambda e, k=k: e.matmul(ps[:, :], lhsT=hT[:, k, i * 128:(i + 1) * 128], rhs=wb[:, k, :], start=(k == 0), stop=(k == 7)), [wb, hT], [ps])

        with S.phase():
            wring = S.ring(2, [128, 8, 512], BF16, 'wb')
            pring = S.ring(4, [128, 512], F32, 'pz', psum=True)
            gring = S.ring(3, [128, 512], F32, 'gst')
            vring = S.ring(3, [128, 512], BF16, 'vst')
            for nb in list(range(4, 10)) + [16, 17]:
                wb = wring()
                self.load_w_bf16(wb, wv[:, :, nb * 512:(nb + 1) * 512], I['w_in'])
                for i in range(NTILE):
                    ps = pring()
                    mm_tm(ps, wb, i)
                    if nb < 10:
                        g = gring()
                        S.op('scalar', lambda e, ps=ps, g=g: e.activation(out=g[:, :], in_=ps[:, :], func=AF.Sigmoid), [ps], [g])
                        c0 = (nb - 4) * 512
                        S.dma('sync', gates[i * 128:(i + 1) * 128, c0:c0 + 512], g[:, :], gates, g, part=True)
                    else:
                        v = vring()
                        S.op('vector', lambda e, ps=ps, v=v: e.tensor_copy(out=v[:, :], in_=ps[:, :]), [ps], [v])
                        c0 = (nb - 16) * 512
                        S.dma('sync', vtm[i * 128:(i + 1) * 128, c0:c0 + 512], v[:, :], vtm, v, part=True)
        with S.phase():
            wring = S.ring(2, [128, 8, 512], BF16, 'wb')
            pring = S.ring(4, [128, 512], F32, 'pz', psum=True)
            uring = S.ring(3, [128, 512], F32, 'ust')
            for nb in (12, 13):
                wb = wring()
                self.load_w_bf16(wb, wv[:, :, nb * 512:(nb + 1) * 512], I['w_in'])
                for cc in range(4):
                    r0 = (nb - 12) * 512 + cc * 128
                    for (t0, t1) in TG:
                        ps = pring()
                        mm_fm(ps, wb, cc, t0, t1)
                        u = uring()
                        S.op('scalar', lambda e, ps=ps, u=u, n=t1 - t0: e.copy(out=u[:, 0:n], in_=ps[:, 0:n]), [ps], [u])
                        S.dma('sync', uT[r0:r0 + 128, t0:t1], u[:, 0:t1 - t0], uT, u, part=True)
        with S.phase():
            cosT, sinT = self.rope_tables()
            wring = S.ring(2, [128, 8, 512], BF16, 'wb')
            rring = S.ring(2, [128, 8, 512], BF16, 'wr')
            pring = S.ring(6, [128, 512], F32, 'pz', psum=True)
            t1ring = S.ring(2, [128, 512], F32, 'rt1')
            t2ring = S.ring(2, [128, 512], F32, 'rt2')
            oring = S.ring(3, [128, 512], BF16, 'rout')
            for nb in (10, 11, 14, 15):
                dst = qT if nb < 12 else kT
                base = (nb - 10) * 512 if nb < 12 else (nb - 14) * 512
                wb = wring()
                self.load_w_bf16(wb, wv[:, :, nb * 512:(nb + 1) * 512], I['w_in'])
                wr = rring()
                wb4 = wb.t.ap().rearrange("p k (g h f) -> p (k g) h f", h=2, f=16)
                wr4 = wr.t.ap().rearrange("p k (g h f) -> p (k g) h f", h=2, f=16)
                S.op('vector', lambda e, wb4=wb4, wr4=wr4: e.tensor_scalar(out=wr4[:, :, 0, :], in0=wb4[:, :, 1, :], scalar1=-1.0, scalar2=None, op0=ALU.mult), [wb], [wr])
                S.op('vector', lambda e, wb4=wb4, wr4=wr4: e.tensor_copy(out=wr4[:, :, 1, :], in_=wb4[:, :, 0, :]), [wb], [wr])
                for cc in range(4):
                    r0 = base + cc * 128
                    for (t0, t1) in TG:
                        n = t1 - t0
                        ps = pring()
                        mm_fm(ps, wb, cc, t0, t1)
                        pr = pring()
                        mm_fm(pr, wr, cc, t0, t1)
                        a = t1ring()
                        b = t2ring()
                        S.op('vector', lambda e, ps=ps, a=a, t0=t0, t1=t1, n=n: e.tensor_tensor(out=a[:, 0:n], in0=ps[:, 0:n], in1=cosT[:, t0:t1], op=ALU.mult), [ps, cosT], [a])
                        S.op('vector', lambda e, pr=pr, b=b, t0=t0, t1=t1, n=n: e.tensor_tensor(out=b[:, 0:n], in0=pr[:, 0:n], in1=sinT[:, t0:t1], op=ALU.mult), [pr, sinT], [b])
                        o = oring()
                        S.op('vector', lambda e, a=a, b=b, o=o, n=n: e.tensor_tensor(out=o[:, 0:n], in0=a[:, 0:n], in1=b[:, 0:n], op=ALU.add), [a, b], [o])
                        S.dma('sync', dst[r0:r0 + 128, t0:t1], o[:, 0:n], dst, o, part=True)
        with S.phase():
            wa_r = S.ring(2, [128, 8, 512], BF16, 'wa')
            wb_r = S.ring(2, [128, 8, 512], BF16, 'wbb')
            pring = S.ring(6, [128, 512], F32, 'pz', psum=True)
            sgring = S.ring(2, [128, 512], F32, 'sg')
            GW = 15 + NCTX + 15 + NLAT + 15
            OW = NCTX + 15 + NLAT
            gpad_r = S.ring(2, [128, GW], F32, 'gpad')
            acc_r = S.ring(2, [128, OW], F32, 'cacc')
            cw = S.sbuf([128, 8, 31], F32, 'cw')
            cb = S.sbuf([128, 8], F32, 'cb')
            for j in range(8):
                S.dma('sync', cw[:, j, :], I['conv_w'].t.ap()[l].rearrange("k (j p) -> j p k", p=128)[j], cw, I['conv_w'], allow_slow_non_contiguous=True)
            S.dma('sync', cb[:, :], I['conv_b'].t.ap()[l].rearrange("(j p) -> p j", p=128), cb, I['conv_b'], allow_slow_non_contiguous=True)
            for sb in range(2):
                wa = wa_r()
                wb = wb_r()
                self.load_w_bf16(wa, wv[:, :, sb * 512:(sb + 1) * 512], I['w_in'])
                self.load_w_bf16(wb, wv[:, :, (2 + sb) * 512:(3 + sb) * 512], I['w_in'])
                for cc in range(4):
                    j = sb * 4 + cc
                    gp = gpad_r()
                    S.op('gpsimd', lambda e, gp=gp: e.memset(gp[:, 0:15], 0.0), [], [gp])
                    S.op('gpsimd', lambda e, gp=gp: e.memset(gp[:, 15 + NCTX:30 + NCTX], 0.0), [], [gp])
                    S.op('gpsimd', lambda e, gp=gp: e.memset(gp[:, GW - 15:GW], 0.0), [], [gp])
                    for (t0, t1) in TG:
                        n = t1 - t0
                        pa = pring()
                        mm_fm(pa, wa, cc, t0, t1)
                        pb = pring()
                        mm_fm(pb, wb, cc, t0, t1)
                        sg = sgring()
                        S.op('scalar', lambda e, pb=pb, sg=sg, n=n: e.activation(out=sg[:, 0:n], in_=pb[:, 0:n], func=AF.Sigmoid), [pb], [sg])
                        segs = []
                        if t0 < NCTX:
                            segs.append((t0, min(t1, NCTX), 15))
                        if t1 > NCTX:
                            segs.append((max(t0, NCTX), t1, 30))
                        for (a0, a1, off) in segs:
                            S.op('vector', lambda e, pa=pa, sg=sg, gp=gp, a0=a0, a1=a1, off=off, t0=t0: e.tensor_tensor(out=gp[:, off + a0:off + a1], in0=pa[:, a0 - t0:a1 - t0], in1=sg[:, a0 - t0:a1 - t0], op=ALU.mult), [pa, sg], [gp])
                    acc = acc_r()
                    S.op('vector', lambda e, gp=gp, acc=acc, j=j: e.tensor_scalar(out=acc[:, :], in0=gp[:, 0:OW], scalar1=cw[:, j, 0:1], scalar2=cb[:, j:j + 1], op0=ALU.mult, op1=ALU.add), [gp, cw, cb], [acc])
                    for kk in range(1, 31):
                        S.op('vector', lambda e, gp=gp, acc=acc, j=j, kk=kk: e.scalar_tensor_tensor(out=acc[:, :], in0=gp[:, kk:kk + OW], scalar=cw[:, j, kk:kk + 1], in1=acc[:, :], op0=ALU.mult, op1=ALU.add), [gp, cw, acc], [acc])
                    S.dma('sync', ycT[j * 128:(j + 1) * 128, 0:NCTX], acc[:, 0:NCTX], ycT, acc, part=True)
                    S.dma('sync', ycT[j * 128:(j + 1) * 128, NCTX:NT], acc[:, NCTX + 15:OW], ycT, acc, part=True)


    def ph_convout(self, l):
        S = self.S
        I = self.I
        ycT = self.scratch('ycT', [D, NT], F32)
        gates = self.scratch('gates', [NT, 3 * D], F32)
        y0 = self.scratch('y0', [NT, D], F32)
        ycv = ycT.t.ap().rearrange("(j p) t -> p j t", p=128)
        TG = [(t0, min(t0 + 512, NT)) for t0 in range(0, NT, 512)]
        with S.phase():
            wc = S.sbuf([128, 8, D], BF16, 'wc')
            wcv = I['w_conv_out'].t.ap()[l].rearrange("(k p) n -> p k n", p=128)
            for hh in range(2):
                S.dma('gpsimd', wc[:, :, hh * 512:(hh + 1) * 512], wcv[:, :, hh * 512:(hh + 1) * 512], wc, I['w_conv_out'], part=True)
            gb = S.sbuf([128, 2, 8], F32, 'gb')
            S.dma('sync', gb[:, 0, :], I['conv_ln_g'].t.ap()[l].rearrange("(j p) -> p j", p=128), gb, I['conv_ln_g'], allow_slow_non_contiguous=True)
            S.dma('sync', gb[:, 1, :], I['conv_ln_b'].t.ap()[l].rearrange("(j p) -> p j", p=128), gb, I['conv_ln_b'], allow_slow_non_contiguous=True, part=True)
            ycr = S.ring(2, [128, 8, 512], F32, 'yc')
            sqr = S.ring(1, [128, 8, 512], F32, 'sq')
            ps1r = S.ring(2, [128, 512], F32, 'ps1', psum=True)
            ps2r = S.ring(2, [128, 512], F32, 'ps2', psum=True)
            pyr = S.ring(4, [128, 512], F32, 'py', psum=True)
            mr = S.ring(2, [128, 512], F32, 'mean')
            rr = S.ring(2, [128, 512], F32, 'rstd')
            tr = S.ring(2, [128, 512], F32, 'tn')
            ar = S.ring(2, [128, 8, 512], BF16, 'aT')
            gr = S.ring(2, [128, D], F32, 'g0')
            yr = S.ring(2, [128, D], F32, 'y0t')
            for (t0, t1) in TG:
                n = t1 - t0
                yc = ycr()
                S.dma('sync', yc[:, :, 0:n], ycv[:, :, t0:t1], yc, ycT)
                sq = sqr()
                S.op('scalar', lambda e, yc=yc, sq=sq, n=n: e.activation(out=sq[:, :, 0:n], in_=yc[:, :, 0:n], func=AF.Square), [yc], [sq])
                p1 = ps1r()
                p2 = ps2r()
                for j in range(8):
                    S.op('tensor', lambda e, p1=p1, yc=yc, j=j, n=n: e.matmul(p1[:, 0:n], lhsT=self.ones_f[:, :], rhs=yc[:, j, 0:n], start=(j == 0), stop=(j == 7)), [yc, self.ones_f], [p1])
                for j in range(8):
                    S.op('tensor', lambda e, p2=p2, sq=sq, j=j, n=n: e.matmul(p2[:, 0:n], lhsT=self.ones_f[:, :], rhs=sq[:, j, 0:n], start=(j == 0), stop=(j == 7)), [sq, self.ones_f], [p2])
                mean = mr()
                rstd = rr()
                S.op('scalar', lambda e, p1=p1, mean=mean, n=n: e.mul(out=mean[:, 0:n], in_=p1[:, 0:n], mul=1.0 / D), [p1], [mean])
                S.op('vector', lambda e, mean=mean, rstd=rstd, n=n: e.tensor_tensor(out=rstd[:, 0:n], in0=mean[:, 0:n], in1=mean[:, 0:n], op=ALU.mult), [mean], [rstd])
                S.op('vector', lambda e, p2=p2, rstd=rstd, n=n: e.scalar_tensor_tensor(out=rstd[:, 0:n], in0=p2[:, 0:n], scalar=1.0 / D, in1=rstd[:, 0:n], op0=ALU.mult, op1=ALU.subtract), [p2, rstd], [rstd])
                S.op('vector', lambda e, rstd=rstd, n=n: e.tensor_scalar(out=rstd[:, 0:n], in0=rstd[:, 0:n], scalar1=LN_EPS, scalar2=None, op0=ALU.add), [rstd], [rstd])
                S.op('scalar', lambda e, rstd=rstd, n=n: e.activation(out=rstd[:, 0:n], in_=rstd[:, 0:n], func=AF.Sqrt), [rstd], [rstd])
                S.op('vector', lambda e, rstd=rstd, n=n: e.reciprocal(out=rstd[:, 0:n], in_=rstd[:, 0:n]), [rstd], [rstd])
                aT = ar()
                for j in range(8):
                    tn = tr()
                    S.op('vector', lambda e, yc=yc, mean=mean, tn=tn, j=j, n=n: e.tensor_tensor(out=tn[:, 0:n], in0=yc[:, j, 0:n], in1=mean[:, 0:n], op=ALU.subtract), [yc, mean], [tn])
                    S.op('vector', lambda e, rstd=rstd, tn=tn, n=n: e.tensor_tensor(out=tn[:, 0:n], in0=tn[:, 0:n], in1=rstd[:, 0:n], op=ALU.mult), [tn, rstd], [tn])
                    S.op('scalar', lambda e, tn=tn, aT=aT, j=j, n=n: e.activation(out=aT[:, j, 0:n], in_=tn[:, 0:n], func=AF.Silu, scale=gb[:, 0, j:j + 1], bias=gb[:, 1, j:j + 1]), [tn, gb], [aT])
                for ii in range(n // 128):
                    i = t0 // 128 + ii
                    g0 = gr()
                    S.dma('sync', g0[:, :], gates[i * 128:(i + 1) * 128, 0:D], g0, gates)
                    yt = yr()
                    for nb in range(2):
                        py = pyr()
                        for j in range(8):
                            S.op('tensor', lambda e, py=py, aT=aT, j=j, ii=ii, nb=nb: e.matmul(py[:, :], lhsT=aT[:, j, ii * 128:(ii + 1) * 128], rhs=wc[:, j, nb * 512:(nb + 1) * 512], start=(j == 0), stop=(j == 7)), [aT, wc], [py])
                        S.op('vector', lambda e, py=py, yt=yt, g0=g0, nb=nb: e.tensor_tensor(out=yt[:, nb * 512:(nb + 1) * 512], in0=py[:, :], in1=g0[:, nb * 512:(nb + 1) * 512], op=ALU.mult), [py, g0], [yt])
                    S.dma('sync', y0[i * 128:(i + 1) * 128, :], yt[:, :], y0, yt, part=True)

    def ph_attn(self, l):
        S = self.S
        I = self.I
        qT = self.scratch('qT', [D, NT], BF16)
        kT = self.scratch('kT', [D, NT], BF16)
        vtm = self.scratch('vtm', [NT, D], BF16)
        oT = self.scratch('oT', [D, NT], BF16)
        gates = self.scratch('gates', [NT, 3 * D], F32)
        y2 = self.scratch('y2', [NT, D], F32)
        lam_init = 0.8 - 0.6 * math.exp(-0.3 * l)
        ctx_out = l < DEPTH - 1
        vv = vtm.t.ap().rearrange("(i p) c -> p i c", p=128)
        with S.phase():
            lamt = S.sbuf([128, 4, 64], F32, 'lamt')
            lw = S.sbuf([128, 8], F32, 'lw')
            gsub = S.sbuf([128, 128], F32, 'gsub')
            S.dma('sync', lamt[:, :, :], I['attn_lambda'][l:l + 1, :, :].partition_broadcast(128), lamt, I['attn_lambda'])
            S.dma('sync', gsub[:, :], I['attn_subln_g'][l:l + 1, :].partition_broadcast(128), gsub, I['attn_subln_g'])
            S.op('vector', lambda e: e.tensor_scalar(out=gsub[:, :], in0=gsub[:, :], scalar1=1.0 - lam_init, scalar2=None, op0=ALU.mult), [gsub], [gsub])
            prod = S.sbuf([128, 2, 64], F32, 'lprod')
            S.op('vector', lambda e: e.tensor_tensor(out=prod[:, 0, :], in0=lamt[:, 0, :], in1=lamt[:, 1, :], op=ALU.mult), [lamt], [prod])
            S.op('vector', lambda e: e.tensor_tensor(out=prod[:, 1, :], in0=lamt[:, 2, :], in1=lamt[:, 3, :], op=ALU.mult), [lamt], [prod])
            S.op('vector', lambda e: e.reduce_sum(out=lw[:, 0:2], in_=prod[:, :, :], axis=AX.X), [prod], [lw])
            S.op('scalar', lambda e: e.activation(out=lw[:, 2:4], in_=lw[:, 0:2], func=AF.Exp), [lw], [lw])
            S.op('vector', lambda e: e.tensor_tensor(out=lw[:, 4:5], in0=lw[:, 3:4], in1=lw[:, 2:3], op=ALU.subtract), [lw], [lw])
            S.op('vector', lambda e: e.tensor_scalar(out=lw[:, 5:6], in0=lw[:, 4:5], scalar1=-lam_init, scalar2=None, op0=ALU.add), [lw], [lw])
            neglam = lw
            qr = S.ring(2, [128, NT], BF16, 'qh')
            kr = S.ring(2, [128, NT], BF16, 'kh')
            vr = S.ring(2, [128, NTILE, 129], BF16, 'vh')
            psr = S.ring(3, [128, 512], F32, 'psc', psum=True)
            por = S.ring(4, [128, 512], F32, 'pov', psum=True)
            ptr_ = S.ring(1, [128, 128], BF16, 'ptr', psum=True)
            pr = S.ring(3, [128, 512], BF16, 'pT')
            accr = S.ring(8, [128, 128], F32, 'oacc')
            smr = S.ring(8, [128, 8], F32, 'osm')
            onr = S.ring(3, [128, 128], BF16, 'onb')
            otr = S.ring(3, [128, 512], BF16, 'oTst')
            qgroups = []
            if ctx_out:
                qgroups.append((0, NCTX, 2))
            for g in range(8):
                qgroups.append((NCTX + g * 512, NCTX + (g + 1) * 512, NTILE))
            for h in range(8):
                qh = qr()
                kh = kr()
                vh = vr()
                S.dma('sync', qh[:, :], qT[h * 128:(h + 1) * 128, :], qh, qT)
                S.dma('sync', kh[:, :], kT[h * 128:(h + 1) * 128, :], kh, kT)
                S.dma('sync', vh[:, :, 0:128], vv[:, :, h * 128:(h + 1) * 128], vh, vtm)
                S.op('gpsimd', lambda e, vh=vh: e.memset(vh[:, :, 128:129], 1.0), [], [vh])
                for (q0, q1, nkt) in qgroups:
                    nq = q1 - q0
                    nqt = nq // 128
                    accs = [accr() for _ in range(nqt)]
                    for s_ in range(2):
                        pos = [por() for _ in range(nqt)]
                        def score(kt, s_=s_, q0=q0, q1=q1, nq=nq, kh=kh, qh=qh):
                            ps = psr()
                            S.op('tensor', lambda e, ps=ps: e.matmul(ps[:, 0:nq], lhsT=kh[s_ * 64:(s_ + 1) * 64, kt * 128:(kt + 1) * 128], rhs=qh[s_ * 64:(s_ + 1) * 64, q0:q1], start=True, stop=True), [kh, qh], [ps])
                            return ps
                        ps_next = score(0)
                        for kt in range(nkt):
                            ps = ps_next
                            if kt + 1 < nkt:
                                ps_next = score(kt + 1)
                            pT = pr()
                            S.op('scalar', lambda e, ps=ps, pT=pT, nq=nq: e.activation(out=pT[:, 0:nq], in_=ps[:, 0:nq], func=AF.Exp, scale=0.125), [ps], [pT])
                            for qi in range(nqt):
                                S.op('tensor', lambda e, po=pos[qi], pT=pT, vh=vh, kt=kt, qi=qi, nkt=nkt: e.matmul(po[:, 0:129], lhsT=pT[:, qi * 128:(qi + 1) * 128], rhs=vh[:, kt, :], start=(kt == 0), stop=(kt == nkt - 1)), [pT, vh], [pos[qi]])
                        for qi in range(nqt):
                            po = pos[qi]
                            acc = accs[qi]
                            sm = smr()
                            S.op('vector', lambda e, po=po, sm=sm: e.reciprocal(out=sm[:, 0:1], in_=po[:, 128:129]), [po], [sm])
                            if s_ == 0:
                                S.op('vector', lambda e, po=po, sm=sm, acc=acc: e.tensor_scalar(out=acc[:, :], in0=po[:, 0:128], scalar1=sm[:, 0:1], scalar2=None, op0=ALU.mult), [po, sm], [acc])
                            else:
                                S.op('vector', lambda e, sm=sm: e.tensor_tensor(out=sm[:, 1:2], in0=sm[:, 0:1], in1=neglam[:, 5:6], op=ALU.mult), [sm, neglam], [sm])
                                S.op('vector', lambda e, po=po, sm=sm, acc=acc: e.scalar_tensor_tensor(out=acc[:, :], in0=po[:, 0:128], scalar=sm[:, 1:2], in1=acc[:, :], op0=ALU.mult, op1=ALU.add), [po, sm, acc], [acc])
                    ost = otr()
                    for qi in range(nqt):
                        acc = accs[qi]
                        sm = smr()
                        sqs = onr()
                        S.op('scalar', lambda e, acc=acc, sqs=sqs, sm=sm: e.activation(out=sqs[:, :], in_=acc[:, :], func=AF.Square, accum_out=sm[:, 0:1]), [acc], [sqs, sm])
                        S.op('vector', lambda e, sm=sm: e.tensor_scalar(out=sm[:, 1:2], in0=sm[:, 0:1], scalar1=1.0 / 128.0, scalar2=None, op0=ALU.mult), [sm], [sm])
                        self.rstd(sm, 1, 2, RMS_EPS)
                        S.op('vector', lambda e, acc=acc, sm=sm: e.tensor_scalar(out=acc[:, :], in0=acc[:, :], scalar1=sm[:, 2:3], scalar2=None, op0=ALU.mult), [acc, sm], [acc])
                        on = onr()
                        S.op('vector', lambda e, acc=acc, on=on: e.tensor_tensor(out=on[:, :], in0=acc[:, :], in1=gsub[:, :], op=ALU.mult), [acc, gsub], [on])
                        pt = ptr_()
                        S.op('tensor', lambda e, pt=pt, on=on: e.transpose(out=pt[:, :], in_=on[:, :], identity=self.ident_b[:, :]), [on, self.ident_b], [pt])
                        S.op('scalar', lambda e, pt=pt, ost=ost, qi=qi: e.copy(out=ost[:, qi * 128:(qi + 1) * 128], in_=pt[:, :]), [pt], [ost])
                    S.dma('sync', oT[h * 128:(h + 1) * 128, q0:q1], ost[:, 0:nq], oT, ost, part=True)
        with S.phase():
            wo = S.sbuf([128, 8, D], BF16, 'wao')
            wov = I['w_attn_out'].t.ap()[l].rearrange("(k p) n -> p k n", p=128)
            for hh in range(2):
                S.dma('gpsimd', wo[:, :, hh * 512:(hh + 1) * 512], wov[:, :, hh * 512:(hh + 1) * 512], wo, I['w_attn_out'], part=True)
            otr2 = S.ring(2, [128, 8, 512], BF16, 'oTl')
            gr = S.ring(2, [128, D], F32, 'g2')
            yr = S.ring(2, [128, D], F32, 'y2t')
            pyr = S.ring(4, [128, 512], F32, 'py', psum=True)
            otv = oT.t.ap().rearrange("(j p) t -> p j t", p=128)
            tstart = 0 if ctx_out else NCTX
            for t0 in range(tstart, NT, 512):
                t1 = min(t0 + 512, NT)
                n = t1 - t0
                ol = otr2()
                S.dma('sync', ol[:, :, 0:n], otv[:, :, t0:t1], ol, oT)
                for ii in range(n // 128):
                    i = t0 // 128 + ii
                    g2 = gr()
                    S.dma('sync', g2[:, :], gates[i * 128:(i + 1) * 128, 2 * D:3 * D], g2, gates)
                    yt = yr()
                    for nb in range(2):
                        py = pyr()
                        for j in range(8):
                            S.op('tensor', lambda e, py=py, ol=ol, j=j, ii=ii, nb=nb: e.matmul(py[:, :], lhsT=ol[:, j, ii * 128:(ii + 1) * 128], rhs=wo[:, j, nb * 512:(nb + 1) * 512], start=(j == 0), stop=(j == 7)), [ol, wo], [py])
                        S.op('vector', lambda e, py=py, yt=yt, g2=g2, nb=nb: e.tensor_tensor(out=yt[:, nb * 512:(nb + 1) * 512], in0=py[:, :], in1=g2[:, nb * 512:(nb + 1) * 512], op=ALU.mult), [py, g2], [yt])
                    S.dma('sync', y2[i * 128:(i + 1) * 128, :], yt[:, :], y2, yt, part=True)


    def rr_mixed(self, x, k):
        S = self.S
        S.op('gpsimd', lambda e: e.tensor_scalar(out=k[:, :], in0=x[:, :], scalar1=float(1.0 / TWO_PI), scalar2=MAGIC, op0=ALU.mult, op1=ALU.add), [x], [k])
        S.op('gpsimd', lambda e: e.tensor_scalar(out=k[:, :], in0=k[:, :], scalar1=-MAGIC, scalar2=None, op0=ALU.add), [k], [k])
        S.op('vector', lambda e: e.scalar_tensor_tensor(out=x[:, :], in0=k[:, :], scalar=-CW1, in1=x[:, :], op0=ALU.mult, op1=ALU.add), [x, k], [x])
        S.op('vector', lambda e: e.scalar_tensor_tensor(out=x[:, :], in0=k[:, :], scalar=-CW2, in1=x[:, :], op0=ALU.mult, op1=ALU.add), [x, k], [x])
        S.op('gpsimd', lambda e: e.tensor_scalar(out=x[:, :], in0=x[:, :], scalar1=PI_LO, scalar2=-PI_LO, op0=ALU.min, op1=ALU.max), [x], [x])

    def ph_ssm_old(self, l):
        S = self.S
        I = self.I
        uT = self.scratch('uT', [D, NT], F32)
        ygT = self.scratch('ygT', [D, NT], BF16)
        gates = self.scratch('gates', [NT, 3 * D], F32)
        y1 = self.scratch('y1', [NT, D], F32)
        TG = [(t0, min(t0 + 512, NT)) for t0 in range(0, NT, 512)]
        V = lambda fn, r, w: S.op('vector', fn, r, w)
        G = lambda fn, r, w: S.op('gpsimd', fn, r, w)
        A = lambda fn, r, w: S.op('scalar', fn, r, w)
        with S.phase():
            LT = {(v, par): S.sbuf([128, 16, 128], BF16, 'LT%d%d' % (v, par)) for v in (1, 2) for par in range(4)}
            Rf = {v: S.sbuf([128, 16, 128], BF16, 'Rf%d' % v) for v in (1, 2)}
            RHO = S.sbuf([128, 128], F32, 'RHO')
            THR = S.sbuf([128, 128], F32, 'THR')
            dvec = S.sbuf([128, 8], F32, 'dvec')
            S.dma('sync', dvec[:, :], I['ssm_d'].t.ap()[l].rearrange("(j p) -> p j", p=128), dvec, I['ssm_d'], allow_slow_non_contiguous=True)
            with S.phase():
                def T(name):
                    return S.sbuf([128, 128], F32, name)
                AR, AI, DT, TH, SN, CS, ABR, ABI, DEN, CR, CI, SCR, SCI, TMP, TMP2 = [T(n) for n in ['AR', 'AI', 'DT', 'TH', 'SN', 'CS', 'ABR', 'ABI', 'DEN', 'CR', 'CI', 'SCR', 'SCI', 'TMP', 'TMP2']]
                are = I['ssm_a_re'].t.ap()[l].rearrange("d g n -> n (d g)")
                aim = I['ssm_a_im'].t.ap()[l].rearrange("d g n -> n (d g)")
                for hf in range(2):
                    S.dma('sync', AR[hf * 64:(hf + 1) * 64, :], are, AR, I['ssm_a_re'], part=(hf == 1), allow_slow_non_contiguous=True)
                    S.dma('sync', AI[hf * 64:(hf + 1) * 64, :], aim, AI, I['ssm_a_im'], part=(hf == 1), allow_slow_non_contiguous=True)
                S.dma('sync', DT[:, :], I['ssm_log_dt'][l:l + 1, :, :].rearrange("o d g -> o (d g)").partition_broadcast(128), DT, I['ssm_log_dt'])
                A(lambda e: e.activation(out=DT[:, :], in_=DT[:, :], func=AF.Exp), [DT], [DT])
                V(lambda e: e.tensor_tensor(out=TMP[:, :], in0=DT[:, :], in1=AR[:, :], op=ALU.mult), [DT, AR], [TMP])
                V(lambda e: e.tensor_tensor(out=TH[:, :], in0=DT[:, :], in1=AI[:, :], op=ALU.mult), [DT, AI], [TH])
                A(lambda e: e.activation(out=RHO[:, :], in_=TMP[:, :], func=AF.Exp), [TMP], [RHO])
                V(lambda e: e.tensor_copy(out=THR[:, :], in_=TH[:, :]), [TH], [THR])
                self.rr_mixed(THR, TMP2)
                A(lambda e: e.activation(out=SN[:, :], in_=THR[:, :], func=AF.Sin), [THR], [SN])
                A(lambda e: e.activation(out=CS[:, :], in_=THR[:, :], func=AF.Sin, scale=0.5), [THR], [CS])
                V(lambda e: e.tensor_tensor(out=CS[:, :], in0=CS[:, :], in1=CS[:, :], op=ALU.mult), [CS], [CS])
                V(lambda e: e.tensor_scalar(out=CS[:, :], in0=CS[:, :], scalar1=-2.0, scalar2=1.0, op0=ALU.mult, op1=ALU.add), [CS], [CS])
                V(lambda e: e.tensor_tensor(out=ABR[:, :], in0=RHO[:, :], in1=CS[:, :], op=ALU.mult), [RHO, CS], [ABR])
                V(lambda e: e.tensor_scalar(out=ABR[:, :], in0=ABR[:, :], scalar1=-1.0, scalar2=None, op0=ALU.add), [ABR], [ABR])
                V(lambda e: e.tensor_tensor(out=ABI[:, :], in0=RHO[:, :], in1=SN[:, :], op=ALU.mult), [RHO, SN], [ABI])
                V(lambda e: e.tensor_tensor(out=DEN[:, :], in0=AR[:, :], in1=AR[:, :], op=ALU.mult), [AR], [DEN])
                V(lambda e: e.tensor_tensor(out=TMP[:, :], in0=AI[:, :], in1=AI[:, :], op=ALU.mult), [AI], [TMP])
                V(lambda e: e.tensor_tensor(out=DEN[:, :], in0=DEN[:, :], in1=TMP[:, :], op=ALU.add), [DEN, TMP], [DEN])
                V(lambda e: e.reciprocal(out=DEN[:, :], in_=DEN[:, :]), [DEN], [DEN])
                V(lambda e: e.tensor_tensor(out=CR[:, :], in0=ABR[:, :], in1=AR[:, :], op=ALU.mult), [ABR, AR], [CR])
                V(lambda e: e.tensor_tensor(out=TMP[:, :], in0=ABI[:, :], in1=AI[:, :], op=ALU.mult), [ABI, AI], [TMP])
                V(lambda e: e.tensor_tensor(out=CR[:, :], in0=CR[:, :], in1=TMP[:, :], op=ALU.add), [CR, TMP], [CR])
                V(lambda e: e.tensor_tensor(out=CR[:, :], in0=CR[:, :], in1=DEN[:, :], op=ALU.mult), [CR, DEN], [CR])
                V(lambda e: e.tensor_tensor(out=CI[:, :], in0=ABI[:, :], in1=AR[:, :], op=ALU.mult), [ABI, AR], [CI])
                V(lambda e: e.tensor_tensor(out=TMP[:, :], in0=ABR[:, :], in1=AI[:, :], op=ALU.mult), [ABR, AI], [TMP])
                V(lambda e: e.tensor_tensor(out=CI[:, :], in0=CI[:, :], in1=TMP[:, :], op=ALU.subtract), [CI, TMP], [CI])
                V(lambda e: e.tensor_tensor(out=CI[:, :], in0=CI[:, :], in1=DEN[:, :], op=ALU.mult), [CI, DEN], [CI])
                sg = S.sbuf([128, 8], F32, 'sg')
                pidx = S.sbuf([128, 2], I32, 'pidx')
                G(lambda e: e.memset(sg[:, 0:1], 1.0), [], [sg])
                G(lambda e: e.memset(sg[0:64, 0:1], -1.0), [sg], [sg])
                G(lambda e: e.iota(pidx[:, 0:1], pattern=[[0, 1]], base=0, channel_multiplier=1), [], [pidx])
                V(lambda e: e.tensor_scalar(out=pidx[:, 1:2], in0=pidx[:, 0:1], scalar1=4, scalar2=3, op0=ALU.logical_shift_right, op1=ALU.bitwise_and), [pidx], [pidx])
                V(lambda e: e.tensor_copy(out=sg[:, 1:2], in_=pidx[:, 1:2]), [pidx], [sg])
                for m_ in range(4):
                    V(lambda e, m_=m_: e.tensor_scalar(out=sg[:, 4 + m_:5 + m_], in0=sg[:, 1:2], scalar1=float(m_), scalar2=None, op0=ALU.is_equal), [sg], [sg])
                V(lambda e: e.tensor_scalar(out=SCR[:, :], in0=CR[:, :], scalar1=sg[:, 0:1], scalar2=None, op0=ALU.mult), [CR, sg], [SCR])
                V(lambda e: e.tensor_scalar(out=SCI[:, :], in0=CI[:, :], scalar1=sg[:, 0:1], scalar2=None, op0=ALU.mult), [CI, sg], [SCI])
                BA = S.sbuf([128, 128, 16], F32, 'BA')
                BB = S.sbuf([128, 128, 16], F32, 'BB')
                X1 = S.sbuf([128, 128, 16], F32, 'X1')
                X2 = S.sbuf([128, 128, 16], F32, 'X2')
                XT = S.sbuf([128, 128, 16], F32, 'XT')
                bre = I['ssm_b_re'].t.ap()[l].rearrange("d g n c -> n (d g) c")
                bim = I['ssm_b_im'].t.ap()[l].rearrange("d g n c -> n (d g) c")
                S.dma('sync', BA[0:64, :, :], bre, BA, I['ssm_b_re'])
                S.dma('sync', BA[64:128, :, :], bim, BA, I['ssm_b_im'], part=True)
                S.dma('sync', BB[0:64, :, :], bim, BB, I['ssm_b_im'])
                S.dma('sync', BB[64:128, :, :], bre, BB, I['ssm_b_re'], part=True)

                def bc(t):
                    return t[:, :].unsqueeze(2).to_broadcast([128, 128, 16])
                V(lambda e: e.tensor_tensor(out=X1[:, :, :], in0=BA[:, :, :], in1=bc(CR), op=ALU.mult), [BA, CR], [X1])
                V(lambda e: e.tensor_tensor(out=XT[:, :, :], in0=BB[:, :, :], in1=bc(SCI), op=ALU.mult), [BB, SCI], [XT])
                V(lambda e: e.tensor_tensor(out=X1[:, :, :], in0=X1[:, :, :], in1=XT[:, :, :], op=ALU.add), [X1, XT], [X1])
                V(lambda e: e.tensor_tensor(out=X2[:, :, :], in0=BA[:, :, :], in1=bc(CI), op=ALU.mult), [BA, CI], [X2])
                V(lambda e: e.tensor_tensor(out=XT[:, :, :], in0=BB[:, :, :], in1=bc(SCR), op=ALU.mult), [BB, SCR], [XT])
                V(lambda e: e.tensor_tensor(out=X2[:, :, :], in0=X2[:, :, :], in1=XT[:, :, :], op=ALU.subtract), [X2, XT], [X2])
                ptr_ = S.ring(3, [128, 128], F32, 'ptp', psum=True)
                for v, X in ((1, X1), (2, X2)):
                    for col in range(16):
                        pt = ptr_()
                        S.op('tensor', lambda e, pt=pt, X=X, col=col: e.transpose(out=pt[:, :], in_=X[:, col * 8:(col + 1) * 8, :], identity=self.ident_f[:, :]), [X, self.ident_f], [pt])
                        for m_ in range(4):
                            V(lambda e, pt=pt, v=v, col=col, m_=m_: e.tensor_scalar(out=LT[(v, m_)][:, col, :], in0=pt[:, :], scalar1=sg[:, 4 + m_:5 + m_], scalar2=None, op0=ALU.mult), [pt, sg], [LT[(v, m_)]])
                M1 = S.sbuf([128, 16, 128], F32, 'M1')
                M2 = S.sbuf([128, 16, 128], F32, 'M2')
                cre = I['ssm_c_re'].t.ap()[l].rearrange("d (b g) c n -> (g c) (d b) n", g=8)
                cim = I['ssm_c_im'].t.ap()[l].rearrange("d (b g) c n -> (g c) (d b) n", g=8)
                S.dma('sync', M1[:, :, 0:64], cre, M1, I['ssm_c_re'])
                S.dma('sync', M1[:, :, 64:128], cim, M1, I['ssm_c_im'], part=True)
                V(lambda e: e.tensor_scalar(out=M2[:, :, 64:128], in0=M1[:, :, 0:64], scalar1=-1.0, scalar2=None, op0=ALU.mult), [M1], [M2])
                V(lambda e: e.tensor_scalar(out=M2[:, :, 0:64], in0=M1[:, :, 64:128], scalar1=-1.0, scalar2=None, op0=ALU.mult), [M1], [M2])
                V(lambda e: e.tensor_scalar(out=M1[:, :, 64:128], in0=M1[:, :, 64:128], scalar1=-1.0, scalar2=None, op0=ALU.mult), [M1, M2], [M1])
                for v, M in ((1, M1), (2, M2)):
                    for col in range(16):
                        pt = ptr_()
                        S.op('tensor', lambda e, pt=pt, M=M, col=col: e.transpose(out=pt[:, :], in_=M[:, col, :], identity=self.ident_f[:, :]), [M, self.ident_f], [pt])
                        A(lambda e, pt=pt, v=v, col=col: e.copy(out=Rf[v][:, col, :], in_=pt[:, :]), [pt], [Rf[v]])
            with S.phase():
                pidxs = [S.sbuf([128, NT], F32, 'pidx%d' % d_) for d_ in range(2)]
                with S.phase():
                    tmpi = S.sbuf([128, NT], I32, 'tmpi')
                    for d_ in range(2):
                        pf = pidxs[d_]
                        if d_ == 0:
                            G(lambda e: e.iota(tmpi[:, :], pattern=[[1, NT]], base=0, channel_multiplier=0), [], [tmpi])
                        else:
                            G(lambda e: e.iota(tmpi[:, 0:NCTX], pattern=[[-1, NCTX]], base=NCTX - 1, channel_multiplier=0), [], [tmpi])
                            G(lambda e: e.iota(tmpi[:, NCTX:NT], pattern=[[-1, NLAT]], base=NT - 1, channel_multiplier=0), [tmpi], [tmpi])
                        V(lambda e, pf=pf: e.tensor_copy(out=pf[:, :], in_=tmpi[:, :]), [tmpi], [pf])
                bufA = tmpi_f = S.sbuf([128, NT], F32, 'bufA')
                bufB = S.sbuf([128, NT], F32, 'bufB')
                cosT = S.sbuf([128, NT], F32, 'cosS')
                sinT = S.sbuf([128, NT], F32, 'sinS')
                D1 = S.sbuf([128, NT], BF16, 'D1')
                D2 = S.sbuf([128, NT], BF16, 'D2')
                Yacc = S.sbuf([128, NT], F32, 'Yacc')
                ub = S.sbuf([128, NT], BF16, 'ub')
                t1r = S.ring(2, [128, 512], F32, 'st1')
                t2r = S.ring(2, [128, 512], F32, 'st2')
                rz1r = S.ring(2, [128, 128], BF16, 'rz1')
                rz2r = S.ring(2, [128, 128], BF16, 'rz2')
                p1r = S.ring(2, [128, 512], F32, 'pp1', psum=True)
                p2r = S.ring(2, [128, 512], F32, 'pp2', psum=True)
                pyr = S.ring(2, [128, 512], F32, 'ppy', psum=True)
                for blk in range(8):
                    S.dma('gpsimd', ub[:, :], uT[blk * 128:(blk + 1) * 128, :], ub, uT)
                    S.dma('sync', bufA[:, :], uT[blk * 128:(blk + 1) * 128, :], bufA, uT)
                    V(lambda e, blk=blk: e.tensor_scalar(out=Yacc[:, :], in0=bufA[:, :], scalar1=dvec[:, blk:blk + 1], scalar2=None, op0=ALU.mult), [bufA, dvec], [Yacc])
                    for d_ in range(2):
                        for g8 in range(8):
                            dg = d_ * 64 + blk * 8 + g8
                            col = d_ * 8 + blk
                            pr, par = g8 // 4, g8 % 4
                            pf = pidxs[d_]
                            G(lambda e, pf=pf, dg=dg: e.tensor_scalar(out=bufA[:, :], in0=pf[:, :], scalar1=THR[:, dg:dg + 1], scalar2=None, op0=ALU.mult), [pf, THR], [bufA])
                            self.rr_mixed(bufA, bufB)
                            A(lambda e: e.activation(out=sinT[:, :], in_=bufA[:, :], func=AF.Sin), [bufA], [sinT])
                            A(lambda e: e.activation(out=cosT[:, :], in_=bufA[:, :], func=AF.Sin, scale=0.5), [bufA], [cosT])
                            G(lambda e: e.tensor_tensor(out=cosT[:, :], in0=cosT[:, :], in1=cosT[:, :], op=ALU.mult), [cosT], [cosT])
                            G(lambda e: e.tensor_scalar(out=cosT[:, :], in0=cosT[:, :], scalar1=-2.0, scalar2=1.0, op0=ALU.mult, op1=ALU.add), [cosT], [cosT])
                            rz1, rz2 = rz1r(), rz2r()
                            for rz, v in ((rz1, 1), (rz2, 2)):
                                G(lambda e, rz=rz: e.memset(rz[:, :], 0.0), [], [rz])
                                G(lambda e, rz=rz, v=v, col=col, g8=g8: e.tensor_copy(out=rz[:, g8 * 16:(g8 + 1) * 16], in_=Rf[v][:, col, g8 * 16:(g8 + 1) * 16]), [Rf[v]], [rz])
                            l1, l2 = LT[(1, par)], LT[(2, par)]
                            for (t0, t1) in TG:
                                n = t1 - t0
                                p1, p2 = p1r(), p2r()
                                S.op('tensor', lambda e, p1=p1, l1=l1, pr=pr, col=col, t0=t0, t1=t1, n=n: e.matmul(p1[:, 0:n], lhsT=l1[64 * pr:64 * pr + 64, col, :], rhs=ub[64 * pr:64 * pr + 64, t0:t1], start=True, stop=True), [l1, ub], [p1])
                                S.op('tensor', lambda e, p2=p2, l2=l2, pr=pr, col=col, t0=t0, t1=t1, n=n: e.matmul(p2[:, 0:n], lhsT=l2[64 * pr:64 * pr + 64, col, :], rhs=ub[64 * pr:64 * pr + 64, t0:t1], start=True, stop=True), [l2, ub], [p2])
                                a, b = t1r(), t2r()
                                V(lambda e, p1=p1, a=a, t0=t0, t1=t1, n=n: e.tensor_tensor(out=a[:, 0:n], in0=p1[:, 0:n], in1=cosT[:, t0:t1], op=ALU.mult), [p1, cosT], [a])
                                V(lambda e, p2=p2, b=b, t0=t0, t1=t1, n=n: e.tensor_tensor(out=b[:, 0:n], in0=p2[:, 0:n], in1=sinT[:, t0:t1], op=ALU.mult), [p2, sinT], [b])
                                G(lambda e, a=a, b=b, t0=t0, t1=t1, n=n: e.tensor_tensor(out=bufB[:, t0:t1], in0=a[:, 0:n], in1=b[:, 0:n], op=ALU.add), [a, b], [bufB])
                            rho_b = RHO[:, dg:dg + 1]
                            if d_ == 0:
                                V(lambda e, rho_b=rho_b: e.tensor_tensor_scan(out=bufA[:, :], data0=rho_b.to_broadcast([128, NT]), data1=bufB[:, :], initial=0.0, op0=ALU.mult, op1=ALU.add), [bufB, RHO], [bufA])
                            else:
                                V(lambda e, rho_b=rho_b: e.tensor_tensor_scan(out=bufA[:, 0:NCTX][:, ::-1], data0=rho_b.to_broadcast([128, NCTX]), data1=bufB[:, 0:NCTX][:, ::-1], initial=0.0, op0=ALU.mult, op1=ALU.add), [bufB, RHO], [bufA])
                                V(lambda e, rho_b=rho_b: e.tensor_tensor_scan(out=bufA[:, NCTX:NT][:, ::-1], data0=rho_b.to_broadcast([128, NLAT]), data1=bufB[:, NCTX:NT][:, ::-1], initial=bufA[:, 0:1], op0=ALU.mult, op1=ALU.add), [bufB, RHO, bufA], [bufA])
                            V(lambda e: e.tensor_tensor(out=D1[:, :], in0=cosT[:, :], in1=bufA[:, :], op=ALU.mult), [cosT, bufA], [D1])
                            G(lambda e: e.tensor_tensor(out=D2[:, :], in0=sinT[:, :], in1=bufA[:, :], op=ALU.mult), [sinT, bufA], [D2])
                            for (t0, t1) in TG:
                                n = t1 - t0
                                py = pyr()
                                S.op('tensor', lambda e, py=py, rz1=rz1, t0=t0, t1=t1, n=n: e.matmul(py[:, 0:n], lhsT=rz1[:, :], rhs=D1[:, t0:t1], start=True, stop=False), [rz1, D1], [py])
                                S.op('tensor', lambda e, py=py, rz2=rz2, t0=t0, t1=t1, n=n: e.matmul(py[:, 0:n], lhsT=rz2[:, :], rhs=D2[:, t0:t1], start=False, stop=True), [rz2, D2], [py])
                                V(lambda e, py=py, t0=t0, t1=t1, n=n: e.tensor_tensor(out=Yacc[:, t0:t1], in0=py[:, 0:n], in1=Yacc[:, t0:t1], op=ALU.add), [py, Yacc], [Yacc])
                    if 'yscanT' in self.debug:
                        ysd = self.scratch('yscanT', [D, NT], F32)
                        S.dma('sync', bufA[:, :], uT[blk * 128:(blk + 1) * 128, :], bufA, uT)
                        V(lambda e, blk=blk: e.tensor_scalar(out=bufA[:, :], in0=bufA[:, :], scalar1=dvec[:, blk:blk + 1], scalar2=None, op0=ALU.mult), [bufA, dvec], [bufA])
                        V(lambda e: e.tensor_tensor(out=bufA[:, :], in0=Yacc[:, :], in1=bufA[:, :], op=ALU.subtract), [Yacc, bufA], [bufA])
                        S.dma('sync', ysd[blk * 128:(blk + 1) * 128, :], bufA[:, :], ysd, bufA, part=True)
                    G(lambda e: e.tensor_tensor(out=bufB[:, :], in0=Yacc[:, :], in1=Yacc[:, :], op=ALU.mult), [Yacc], [bufB])
                    G(lambda e: e.tensor_scalar(out=bufB[:, :], in0=bufB[:, :], scalar1=0.044715, scalar2=1.0, op0=ALU.mult, op1=ALU.add), [bufB], [bufB])
                    V(lambda e: e.tensor_tensor(out=bufB[:, :], in0=bufB[:, :], in1=Yacc[:, :], op=ALU.mult), [bufB, Yacc], [bufB])
                    A(lambda e: e.activation(out=bufB[:, :], in_=bufB[:, :], func=AF.Sigmoid, scale=2.0 * math.sqrt(2.0 / math.pi)), [bufB], [bufB])
                    V(lambda e: e.tensor_tensor(out=D1[:, :], in0=bufB[:, :], in1=Yacc[:, :], op=ALU.mult), [bufB, Yacc], [D1])
                    S.dma('sync', ygT[blk * 128:(blk + 1) * 128, :], D1[:, :], ygT, D1, part=True)
        self.ssm_glu_out(l)


    def rr_ap(self, x, k, xa, ka):
        S = self.S
        S.op('gpsimd', lambda e: e.tensor_scalar(out=ka, in0=xa, scalar1=float(1.0 / TWO_PI), scalar2=MAGIC, op0=ALU.mult, op1=ALU.add), [x], [k])
        S.op('gpsimd', lambda e: e.tensor_scalar(out=ka, in0=ka, scalar1=-MAGIC, scalar2=None, op0=ALU.add), [k], [k])
        S.op('vector', lambda e: e.scalar_tensor_tensor(out=xa, in0=ka, scalar=-CW1, in1=xa, op0=ALU.mult, op1=ALU.add), [x, k], [x])
        S.op('vector', lambda e: e.scalar_tensor_tensor(out=xa, in0=ka, scalar=-CW2, in1=xa, op0=ALU.mult, op1=ALU.add), [x, k], [x])
        S.op('gpsimd', lambda e: e.tensor_scalar(out=xa, in0=xa, scalar1=PI_LO, scalar2=-PI_LO, op0=ALU.min, op1=ALU.max), [x], [x])

    def rr_gen(self, x, k, xa, ka, eng='gpsimd'):
        S = self.S
        S.op(eng, lambda e: e.tensor_scalar(out=ka, in0=xa, scalar1=float(1.0 / TWO_PI), scalar2=MAGIC, op0=ALU.mult, op1=ALU.add), [x], [k])
        yield
        S.op(eng, lambda e: e.tensor_scalar(out=ka, in0=ka, scalar1=-MAGIC, scalar2=None, op0=ALU.add), [k], [k])
        yield
        S.op('vector', lambda e: e.scalar_tensor_tensor(out=xa, in0=ka, scalar=-CW1, in1=xa, op0=ALU.mult, op1=ALU.add), [x, k], [x])
        yield
        S.op('vector', lambda e: e.scalar_tensor_tensor(out=xa, in0=ka, scalar=-CW2, in1=xa, op0=ALU.mult, op1=ALU.add), [x, k], [x])
        yield
        S.op(eng, lambda e: e.tensor_scalar(out=xa, in0=xa, scalar1=PI_LO, scalar2=-PI_LO, op0=ALU.min, op1=ALU.max), [x], [x])
        yield

    def ph_ssm(self, l):
        S = self.S
        I = self.I
        uT = self.scratch('uT', [D, NT], F32)
        ygT = self.scratch('ygT', [D, NT], BF16)
        V = lambda fn, r, w: S.op('vector', fn, r, w)
        G = lambda fn, r, w: S.op('gpsimd', fn, r, w)
        A = lambda fn, r, w: S.op('scalar', fn, r, w)
        P = lambda fn, r, w: S.op('tensor', fn, r, w)
        NC = NT // 8
        NCC = NCTX // 8
        HC = [(0, NC // 2), (NC // 2, NC)]
        HW = NC // 2
        with S.phase():
            dvec = S.sbuf([128, 8], F32, 'dvec')
            S.dma('sync', dvec[:, :], I['ssm_d'].t.ap()[l].rearrange("(j p) -> p j", p=128), dvec, I['ssm_d'], allow_slow_non_contiguous=True)
            PWr = S.sbuf([128, 9, 128], F32, 'PWr')
            PWi = S.sbuf([128, 9, 128], F32, 'PWi')
            CRe, CIe = [S.sbuf([128, 8, 128], F32, n) for n in ('CRe', 'CIe')]
            T1 = S.sbuf([128, 16, 8, 16], F32, 'T1')
            T2 = S.sbuf([128, 16, 8, 16], F32, 'T2')
            sg = S.sbuf([128, 8], F32, 'sg')
            blkmask = S.sbuf([128, 128], F32, 'blkmask')
            TH8 = S.sbuf([128, 128], F32, 'TH8')
            RHO8 = S.sbuf([128, 128], F32, 'RHO8')
            with S.phase():
                def T(name, w=128):
                    return S.sbuf([128, w], F32, name)
                AR, AI, DT, ARD, TH, THR, SN, CS, ABR, ABI, DEN, CR, CI, TMP, TMP2, RHO = [T(n) for n in ['AR', 'AI', 'DT', 'ARD', 'TH', 'THR', 'SN', 'CS', 'ABR', 'ABI', 'DEN', 'CR', 'CI', 'TMP', 'TMP2', 'RHO']]
                are = I['ssm_a_re'].t.ap()[l].rearrange("d g n -> n (d g)")
                aim = I['ssm_a_im'].t.ap()[l].rearrange("d g n -> n (d g)")
                for hf in range(2):
                    S.dma('sync', AR[hf * 64:(hf + 1) * 64, :], are, AR, I['ssm_a_re'], part=(hf == 1), allow_slow_non_contiguous=True)
                    S.dma('sync', AI[hf * 64:(hf + 1) * 64, :], aim, AI, I['ssm_a_im'], part=(hf == 1), allow_slow_non_contiguous=True)
                S.dma('sync', DT[:, :], I['ssm_log_dt'][l:l + 1, :, :].rearrange("o d g -> o (d g)").partition_broadcast(128), DT, I['ssm_log_dt'])
                A(lambda e: e.activation(out=DT[:, :], in_=DT[:, :], func=AF.Exp), [DT], [DT])
                V(lambda e: e.tensor_tensor(out=ARD[:, :], in0=DT[:, :], in1=AR[:, :], op=ALU.mult), [DT, AR], [ARD])
                V(lambda e: e.tensor_tensor(out=TH[:, :], in0=DT[:, :], in1=AI[:, :], op=ALU.mult), [DT, AI], [TH])
                A(lambda e: e.activation(out=RHO[:, :], in_=ARD[:, :], func=AF.Exp), [ARD], [RHO])
                V(lambda e: e.tensor_copy(out=THR[:, :], in_=TH[:, :]), [TH], [THR])
                self.rr_ap(THR, TMP2, THR[:, :], TMP2[:, :])
                A(lambda e: e.activation(out=SN[:, :], in_=THR[:, :], func=AF.Sin), [THR], [SN])
                A(lambda e: e.activation(out=CS[:, :], in_=THR[:, :], func=AF.Sin, scale=0.5), [THR], [CS])
                V(lambda e: e.tensor_tensor(out=CS[:, :], in0=CS[:, :], in1=CS[:, :], op=ALU.mult), [CS], [CS])
                V(lambda e: e.tensor_scalar(out=CS[:, :], in0=CS[:, :], scalar1=-2.0, scalar2=1.0, op0=ALU.mult, op1=ALU.add), [CS], [CS])
                V(lambda e: e.tensor_tensor(out=ABR[:, :], in0=RHO[:, :], in1=CS[:, :], op=ALU.mult), [RHO, CS], [ABR])
                V(lambda e: e.tensor_scalar(out=ABR[:, :], in0=ABR[:, :], scalar1=-1.0, scalar2=None, op0=ALU.add), [ABR], [ABR])
                V(lambda e: e.tensor_tensor(out=ABI[:, :], in0=RHO[:, :], in1=SN[:, :], op=ALU.mult), [RHO, SN], [ABI])
                V(lambda e: e.tensor_tensor(out=DEN[:, :], in0=AR[:, :], in1=AR[:, :], op=ALU.mult), [AR], [DEN])
                V(lambda e: e.tensor_tensor(out=TMP[:, :], in0=AI[:, :], in1=AI[:, :], op=ALU.mult), [AI], [TMP])
                V(lambda e: e.tensor_tensor(out=DEN[:, :], in0=DEN[:, :], in1=TMP[:, :], op=ALU.add), [DEN, TMP], [DEN])
                V(lambda e: e.reciprocal(out=DEN[:, :], in_=DEN[:, :]), [DEN], [DEN])
                V(lambda e: e.tensor_tensor(out=CR[:, :], in0=ABR[:, :], in1=AR[:, :], op=ALU.mult), [ABR, AR], [CR])
                V(lambda e: e.tensor_tensor(out=TMP[:, :], in0=ABI[:, :], in1=AI[:, :], op=ALU.mult), [ABI, AI], [TMP])
                V(lambda e: e.tensor_tensor(out=CR[:, :], in0=CR[:, :], in1=TMP[:, :], op=ALU.add), [CR, TMP], [CR])
                V(lambda e: e.tensor_tensor(out=CR[:, :], in0=CR[:, :], in1=DEN[:, :], op=ALU.mult), [CR, DEN], [CR])
                V(lambda e: e.tensor_tensor(out=CI[:, :], in0=ABI[:, :], in1=AR[:, :], op=ALU.mult), [ABI, AR], [CI])
                V(lambda e: e.tensor_tensor(out=TMP[:, :], in0=ABR[:, :], in1=AI[:, :], op=ALU.mult), [ABR, AI], [TMP])
                V(lambda e: e.tensor_tensor(out=CI[:, :], in0=CI[:, :], in1=TMP[:, :], op=ALU.subtract), [CI, TMP], [CI])
                V(lambda e: e.tensor_tensor(out=CI[:, :], in0=CI[:, :], in1=DEN[:, :], op=ALU.mult), [CI, DEN], [CI])
                pidx = S.sbuf([128, 4], I32, 'pidx')
                G(lambda e: e.memset(sg[:, 0:1], 1.0), [], [sg])
                G(lambda e: e.memset(sg[0:64, 0:1], -1.0), [sg], [sg])
                V(lambda e: e.tensor_scalar(out=sg[:, 2:3], in0=sg[:, 0:1], scalar1=-1.0, scalar2=None, op0=ALU.mult), [sg], [sg])
                G(lambda e: e.iota(pidx[:, 0:1], pattern=[[0, 1]], base=0, channel_multiplier=1), [], [pidx])
                V(lambda e: e.tensor_scalar(out=pidx[:, 1:2], in0=pidx[:, 0:1], scalar1=4, scalar2=3, op0=ALU.logical_shift_right, op1=ALU.bitwise_and), [pidx], [pidx])
                V(lambda e: e.tensor_scalar(out=pidx[:, 2:3], in0=pidx[:, 0:1], scalar1=4, scalar2=None, op0=ALU.logical_shift_right), [pidx], [pidx])
                V(lambda e: e.tensor_copy(out=sg[:, 1:2], in_=pidx[:, 1:2]), [pidx], [sg])
                V(lambda e: e.tensor_copy(out=sg[:, 3:4], in_=pidx[:, 2:3]), [pidx], [sg])
                for m_ in range(4):
                    V(lambda e, m_=m_: e.tensor_scalar(out=sg[:, 4 + m_:5 + m_], in0=sg[:, 1:2], scalar1=float(m_), scalar2=None, op0=ALU.is_equal), [sg], [sg])
                iq = S.sbuf([128, 128], I32, 'iq')
                G(lambda e: e.iota(iq[:, :], pattern=[[1, 128]], base=0, channel_multiplier=0), [], [iq])
                V(lambda e: e.tensor_scalar(out=iq[:, :], in0=iq[:, :], scalar1=4, scalar2=None, op0=ALU.logical_shift_right), [iq], [iq])
                V(lambda e: e.tensor_copy(out=blkmask[:, :], in_=iq[:, :]), [iq], [blkmask])
                V(lambda e: e.tensor_scalar(out=blkmask[:, :], in0=blkmask[:, :], scalar1=sg[:, 3:4], scalar2=None, op0=ALU.is_equal), [blkmask, sg], [blkmask])
                ANG, KK, SNm, CSm, RHm = [T(n, 9 * 128) for n in ('ANG', 'KKp', 'SNm', 'CSm', 'RHm')]
                def pw_chain(m_):
                    V(lambda e, m_=m_: e.tensor_scalar(out=ANG[:, m_ * 128:(m_ + 1) * 128], in0=THR[:, :], scalar1=float(m_), scalar2=None, op0=ALU.mult), [THR], [ANG])
                    yield
                    V(lambda e, m_=m_: e.tensor_scalar(out=RHm[:, m_ * 128:(m_ + 1) * 128], in0=ARD[:, :], scalar1=float(m_), scalar2=None, op0=ALU.mult), [ARD], [RHm])
                    yield
                interleave([pw_chain(m_) for m_ in range(9)])
                self.rr_ap(ANG, KK, ANG[:, :], KK[:, :])
                A(lambda e: e.activation(out=RHm[:, :], in_=RHm[:, :], func=AF.Exp), [RHm], [RHm])
                A(lambda e: e.activation(out=SNm[:, :], in_=ANG[:, :], func=AF.Sin), [ANG], [SNm])
                A(lambda e: e.activation(out=CSm[:, :], in_=ANG[:, :], func=AF.Sin, scale=0.5), [ANG], [CSm])
                V(lambda e: e.tensor_tensor(out=CSm[:, :], in0=CSm[:, :], in1=CSm[:, :], op=ALU.mult), [CSm], [CSm])
                V(lambda e: e.tensor_scalar(out=CSm[:, :], in0=CSm[:, :], scalar1=-2.0, scalar2=1.0, op0=ALU.mult, op1=ALU.add), [CSm], [CSm])
                pwr2 = PWr.t.ap().rearrange("p m g -> p (m g)")
                pwi2 = PWi.t.ap().rearrange("p m g -> p (m g)")
                V(lambda e: e.tensor_tensor(out=pwr2, in0=RHm[:, :], in1=CSm[:, :], op=ALU.mult), [RHm, CSm], [PWr])
                V(lambda e: e.tensor_tensor(out=pwi2, in0=RHm[:, :], in1=SNm[:, :], op=ALU.mult), [RHm, SNm], [PWi])
                V(lambda e: e.tensor_copy(out=TH8[:, :], in_=ANG[:, 8 * 128:9 * 128]), [ANG], [TH8])
                V(lambda e: e.tensor_copy(out=RHO8[:, :], in_=RHm[:, 8 * 128:9 * 128]), [RHm], [RHO8])
                TM8 = S.sbuf([128, 8, 128], F32, 'TM8')

                def b8(t):
                    return t[:, :].unsqueeze(1).to_broadcast([128, 8, 128])
                V(lambda e: e.tensor_tensor(out=CRe[:, :, :], in0=PWr[:, 0:8, :], in1=b8(CR), op=ALU.mult), [PWr, CR], [CRe])
                V(lambda e: e.tensor_tensor(out=TM8[:, :, :], in0=PWi[:, 0:8, :], in1=b8(CI), op=ALU.mult), [PWi, CI], [TM8])
                V(lambda e: e.tensor_tensor(out=CRe[:, :, :], in0=CRe[:, :, :], in1=TM8[:, :, :], op=ALU.subtract), [CRe, TM8], [CRe])
                V(lambda e: e.tensor_tensor(out=CIe[:, :, :], in0=PWi[:, 0:8, :], in1=b8(CR), op=ALU.mult), [PWi, CR], [CIe])
                V(lambda e: e.tensor_tensor(out=TM8[:, :, :], in0=PWr[:, 0:8, :], in1=b8(CI), op=ALU.mult), [PWr, CI], [TM8])
                V(lambda e: e.tensor_tensor(out=CIe[:, :, :], in0=CIe[:, :, :], in1=TM8[:, :, :], op=ALU.add), [CIe, TM8], [CIe])
                M1 = S.sbuf([128, 16, 128], F32, 'M1')
                M2 = S.sbuf([128, 16, 128], F32, 'M2')
                cre = I['ssm_c_re'].t.ap()[l].rearrange("d (b g) c n -> (g c) (d b) n", g=8)
                cim = I['ssm_c_im'].t.ap()[l].rearrange("d (b g) c n -> (g c) (d b) n", g=8)
                S.dma('sync', M1[:, :, 0:64], cre, M1, I['ssm_c_re'])
                S.dma('sync', M1[:, :, 64:128], cim, M1, I['ssm_c_im'], part=True)
                S.dma('sync', M2[:, :, 0:64], cim, M2, I['ssm_c_im'])
                S.dma('sync', M2[:, :, 64:128], cre, M2, I['ssm_c_re'], part=True)
                ptr_ = S.ring(3, [128, 128], F32, 'ptp', psum=True)
                for M, Tt in ((M1, T1), (M2, T2)):
                    for col in range(16):
                        pt = ptr_()
                        P(lambda e, pt=pt, M=M, col=col: e.transpose(out=pt[:, :], in_=M[:, col, :], identity=self.ident_f[:, :]), [M, self.ident_f], [pt])
                        A(lambda e, pt=pt, Tt=Tt, col=col: e.copy(out=Tt[:, col, :, :], in_=pt[:, :].rearrange("p (g o) -> p g o", o=16)), [pt], [Tt])
            with S.phase():
                pidxc = [S.sbuf([128, NC], F32, 'pidxc%d' % d_) for d_ in range(2)]
                with S.phase():
                    tmpi = S.sbuf([128, NC], I32, 'tmpi')
                    G(lambda e: e.iota(tmpi[:, :], pattern=[[1, NC]], base=0, channel_multiplier=0), [], [tmpi])
                    V(lambda e: e.tensor_copy(out=pidxc[0][:, :], in_=tmpi[:, :]), [tmpi], [pidxc[0]])
                    G(lambda e: e.iota(tmpi[:, 0:NCC], pattern=[[-1, NCC]], base=NCC - 1, channel_multiplier=0), [pidxc[0]], [tmpi])
                    G(lambda e: e.iota(tmpi[:, NCC:NC], pattern=[[-1, NC - NCC]], base=NC - 1, channel_multiplier=0), [tmpi], [tmpi])
                    V(lambda e: e.tensor_copy(out=pidxc[1][:, :], in_=tmpi[:, :]), [tmpi], [pidxc[1]])
                bre = I['ssm_b_re'].t.ap()[l].rearrange("d g n c -> n (d g) c")
                bim = I['ssm_b_im'].t.ap()[l].rearrange("d g n c -> n (d g) c")
                bar = S.ring(1, [128, 8, 16], F32, 'BAb')
                bbr = S.ring(1, [128, 8, 16], F32, 'BBb')
                uf = S.sbuf([128, NT], F32, 'uf')
                Yacc = S.sbuf([128, NT], F32, 'Yacc')
                Ytp = [Buf(Yacc.t, 'Ytp%d' % tp_) for tp_ in range(8)]
                ud = S.sbuf([128, 8, NC], BF16, 'ud')
                udm = [S.sbuf([128, 8, NC], BF16, 'udm%d' % m_) for m_ in range(4)]
                LT = S.sbuf([128, 2, 8, 2, 128], BF16, 'LT')
                x1fr = S.ring(2, [128, 128], F32, 'X1f')
                K0 = S.sbuf([128, 128], F32, 'K0')
                R1f = S.sbuf([128, 9, 128], F32, 'R1f')
                Rb = S.sbuf([128, 2, 9, 2, 128], BF16, 'Rb')
                KT = S.sbuf([128, 15, 128], BF16, 'KT')
                xbig = [S.sbuf([128, 1152], F32, 'xbig%d' % k_) for k_ in range(3)]
                angr = S.ring(2, [128, NC], F32, 'angc')
                kkr = S.ring(2, [128, NC], F32, 'kkc')
                snr = S.ring(2, [128, NC], F32, 'snc')
                csr = S.ring(2, [128, NC], F32, 'csc')
                wmr = S.ring(2, [128, NC], F32, 'wmc')
                wsr = S.ring(2, [128, NC], F32, 'wsc')
                e1r = S.ring(4, [128, NC], BF16, 'e1c')
                e2r = S.ring(4, [128, NC], BF16, 'e2c')
                rzr = S.ring(2, [128, 16, 128], BF16, 'rzc')
                t1r = S.ring(2, [128, HW], F32, 'st1')
                t2r = S.ring(2, [128, HW], F32, 'st2')
                ptr_ = S.ring(2, [128, 128], F32, 'ptp', psum=True)
                p1r = S.ring(2, [128, HW], F32, 'pp1', psum=True)
                p2r = S.ring(2, [128, HW], F32, 'pp2', psum=True)
                pyr = S.ring(2, [128, HW], F32, 'ppy', psum=True)
                ud2 = ud.t.ap().rearrange("p j k -> p (j k)")
                ufv = uf.t.ap().rearrange("p (k j) -> p j k", j=8)
                Yv = Yacc.t.ap().rearrange("p (k j) -> p k j", j=8)
                for blk in range(8):
                    S.dma('sync', uf[:, :], uT[blk * 128:(blk + 1) * 128, :], uf, uT)
                    V(lambda e, blk=blk: e.tensor_scalar(out=Yacc[:, :], in0=uf[:, :], scalar1=dvec[:, blk:blk + 1], scalar2=None, op0=ALU.mult), [uf, dvec], Ytp)
                    V(lambda e: e.tensor_copy(out=ud[:, :, :], in_=ufv), [uf], [ud])
                    for m_ in range(4):
                        V(lambda e, m_=m_: e.tensor_scalar(out=udm[m_][:, :, :], in0=ud[:, :, :], scalar1=sg[:, 4 + m_:5 + m_], scalar2=None, op0=ALU.mult), [ud, sg], [udm[m_]])
                    for d_ in range(2):
                        g0 = d_ * 64 + blk * 8
                        col = d_ * 8 + blk
                        X1f = x1fr()
                        BA, BB = bar(), bbr()
                        S.dma('sync', BA[0:64, :, :], bre[:, g0:g0 + 8, :], BA, I['ssm_b_re'])
                        S.dma('sync', BA[64:128, :, :], bim[:, g0:g0 + 8, :], BA, I['ssm_b_im'], part=True)
                        S.dma('sync', BB[0:64, :, :], bim[:, g0:g0 + 8, :], BB, I['ssm_b_im'])
                        S.dma('sync', BB[64:128, :, :], bre[:, g0:g0 + 8, :], BB, I['ssm_b_re'], part=True)
                        BAb = BA[:, :, :]
                        BBb = BB[:, :, :]

                        XA, XB, XC = xbig[0], xbig[1], xbig[2]
                        xa4 = XA[:, 0:1024].rearrange("p (e g c) -> p e g c", e=8, g=8)
                        xb4 = XB[:, 0:1024].rearrange("p (e g c) -> p e g c", e=8, g=8)
                        xc4 = XC[:, 0:1024].rearrange("p (e g c) -> p e g c", e=8, g=8)
                        BAe = BAb.unsqueeze(1).to_broadcast([128, 8, 8, 16])
                        BBe = BBb.unsqueeze(1).to_broadcast([128, 8, 8, 16])
                        cre = CRe[:, :, g0:g0 + 8].unsqueeze(3).to_broadcast([128, 8, 8, 16])
                        cie = CIe[:, :, g0:g0 + 8].unsqueeze(3).to_broadcast([128, 8, 8, 16])
                        V(lambda e, xa4=xa4, BAe=BAe, cre=cre: e.tensor_tensor(out=xa4, in0=BAe, in1=cre, op=ALU.mult), [BA, CRe], [XA])
                        V(lambda e, xb4=xb4, BBe=BBe, cie=cie: e.tensor_tensor(out=xb4, in0=BBe, in1=cie, op=ALU.mult), [BB, CIe], [XB])
                        V(lambda e: e.scalar_tensor_tensor(out=XA[:, 0:1024], in0=XB[:, 0:1024], scalar=sg[:, 0:1], in1=XA[:, 0:1024], op0=ALU.mult, op1=ALU.add), [XA, XB, sg], [XA])
                        A(lambda e, X1f=X1f: e.copy(out=X1f[:, :], in_=XA[:, 0:128]), [XA], [X1f])
                        for e_ in range(8):
                            pt = ptr_()
                            P(lambda e, pt=pt, e_=e_: e.transpose(out=pt[:, :], in_=XA[:, e_ * 128:(e_ + 1) * 128], identity=self.ident_f[:, :]), [XA, self.ident_f], [pt])
                            A(lambda e, pt=pt, d_=d_, e_=e_: e.copy(out=LT[:, d_, e_, 0, :], in_=pt[:, :]), [pt], [LT])
                        V(lambda e, xb4=xb4, BAe=BAe, cie=cie: e.tensor_tensor(out=xb4, in0=BAe, in1=cie, op=ALU.mult), [BA, CIe], [XB])
                        V(lambda e, xc4=xc4, BBe=BBe, cre=cre: e.tensor_tensor(out=xc4, in0=BBe, in1=cre, op=ALU.mult), [BB, CRe], [XC])
                        V(lambda e: e.scalar_tensor_tensor(out=XB[:, 0:1024], in0=XC[:, 0:1024], scalar=sg[:, 2:3], in1=XB[:, 0:1024], op0=ALU.mult, op1=ALU.add), [XB, XC, sg], [XB])
                        for e_ in range(8):
                            pt = ptr_()
                            P(lambda e, pt=pt, e_=e_: e.transpose(out=pt[:, :], in_=XB[:, e_ * 128:(e_ + 1) * 128], identity=self.ident_f[:, :]), [XB, self.ident_f], [pt])
                            A(lambda e, pt=pt, d_=d_, e_=e_: e.copy(out=LT[:, d_, e_, 1, :], in_=pt[:, :]), [pt], [LT])
                        T1e = T1[:, col, :, :].unsqueeze(1).to_broadcast([128, 9, 8, 16])
                        T2e = T2[:, col, :, :].unsqueeze(1).to_broadcast([128, 9, 8, 16])
                        pwr = PWr[:, :, g0:g0 + 8].unsqueeze(3).to_broadcast([128, 9, 8, 16])
                        pwi = PWi[:, :, g0:g0 + 8].unsqueeze(3).to_broadcast([128, 9, 8, 16])
                        xa9 = XA[:, :].rearrange("p (m g o) -> p m g o", m=9, g=8)
                        xc9 = XC[:, :].rearrange("p (m g o) -> p m g o", m=9, g=8)
                        r1v = R1f.t.ap().rearrange("p m q -> p (m q)")
                        V(lambda e, xa9=xa9, T1e=T1e, pwr=pwr: e.tensor_tensor(out=xa9, in0=T1e, in1=pwr, op=ALU.mult), [T1, PWr], [XA])
                        V(lambda e, xc9=xc9, T2e=T2e, pwi=pwi: e.tensor_tensor(out=xc9, in0=T2e, in1=pwi, op=ALU.mult), [T2, PWi], [XC])
                        V(lambda e, r1v=r1v: e.scalar_tensor_tensor(out=r1v, in0=XA[:, :], scalar=sg[:, 2:3], in1=XC[:, :], op0=ALU.mult, op1=ALU.subtract), [XA, XC, sg], [R1f])
                        A(lambda e, d_=d_: e.copy(out=Rb[:, d_, :, 0, :], in_=R1f[:, :, :]), [R1f], [Rb])
                        V(lambda e, xa9=xa9, T1e=T1e, pwi=pwi: e.tensor_tensor(out=xa9, in0=T1e, in1=pwi, op=ALU.mult), [T1, PWi], [XA])
                        V(lambda e, xc9=xc9, T2e=T2e, pwr=pwr: e.tensor_tensor(out=xc9, in0=T2e, in1=pwr, op=ALU.mult), [T2, PWr], [XC])
                        V(lambda e, d_=d_: e.scalar_tensor_tensor(out=Rb[:, d_, :, 1, :], in0=XA[:, :].rearrange("p (m q) -> p m q", m=9), scalar=sg[:, 0:1], in1=XC[:, :].rearrange("p (m q) -> p m q", m=9), op0=ALU.mult, op1=ALU.subtract), [XA, XC, sg], [Rb])
                        for tau in range(8):
                            pt = ptr_()
                            P(lambda e, pt=pt, tau=tau, X1f=X1f: e.matmul(pt[:, :], lhsT=X1f[:, :], rhs=R1f[:, tau, :], start=True, stop=True), [X1f, R1f], [pt])
                            if tau == 0 and d_ == 0:
                                V(lambda e, pt=pt: e.tensor_tensor(out=K0[:, :], in0=pt[:, :], in1=blkmask[:, :], op=ALU.mult), [pt, blkmask], [K0])
                            elif tau == 0:
                                V(lambda e, pt=pt: e.tensor_tensor(out=K0[:, :], in0=pt[:, :], in1=K0[:, :], op=ALU.add), [pt, K0], [K0])
                                V(lambda e: e.tensor_tensor(out=KT[:, 7, :], in0=K0[:, :], in1=blkmask[:, :], op=ALU.mult), [K0, blkmask], [KT])
                            else:
                                ki = 7 + tau if d_ == 0 else 7 - tau
                                V(lambda e, pt=pt, ki=ki: e.tensor_tensor(out=KT[:, ki, :], in0=pt[:, :], in1=blkmask[:, :], op=ALU.mult), [pt, blkmask], [KT])
                    if 'dbgKT' in self.debug and blk == 0:
                        for nm, tt, shp in (('dbgKT', KT, [128, 15, 128]), ('dbgLT', LT, [128, 2, 8, 2, 128]), ('dbgRb', Rb, [128, 2, 9, 2, 128]), ('dbgud', ud, [128, 8, NC]), ('dbgudm1', udm[1], [128, 8, NC])):
                            dd = self.scratch(nm, shp, BF16)
                            S.dma('sync', dd.t.ap(), tt.t.ap(), dd, tt)
                        for nm, tt, shp in (('dbgPWr', PWr, [128, 9, 128]), ('dbgPWi', PWi, [128, 9, 128]), ('dbgCRe', CRe, [128, 8, 128]), ('dbgT1', T1, [128, 16, 8, 16]), ('dbgmask', blkmask, [128, 128]), ('dbgsg', sg, [128, 8])):
                            dd = self.scratch(nm, shp, F32)
                            S.dma('sync', dd.t.ap(), tt.t.ap(), dd, tt)
                    for tp in range(8):
                        for (h0, h1) in HC:
                            py = pyr()
                            for j in range(8):
                                P(lambda e, py=py, tp=tp, j=j, h0=h0, h1=h1: e.matmul(py[:, :], lhsT=KT[:, 7 + tp - j, :], rhs=ud[:, j, h0:h1], start=(j == 0), stop=(j == 7)), [KT, ud], [py])
                            V(lambda e, py=py, tp=tp, h0=h0, h1=h1: e.tensor_tensor(out=Yv[:, h0:h1, tp], in0=py[:, :], in1=Yv[:, h0:h1, tp], op=ALU.add), [py, Ytp[tp]], [Ytp[tp]])
                    for g8 in range(8):
                        pr, m4 = g8 // 4, g8 % 4
                        um = udm[m4]
                        per_dir = [None, None]

                        def dir_chain(d_, g8=g8, pr=pr, m4=m4, um=um):
                            dg = d_ * 64 + blk * 8 + g8
                            ang, kk = angr(), kkr()
                            V(lambda e, ang=ang, d_=d_, dg=dg: e.tensor_scalar(out=ang[:, :], in0=pidxc[d_][:, :], scalar1=TH8[:, dg:dg + 1], scalar2=None, op0=ALU.mult), [pidxc[d_], TH8], [ang])
                            yield
                            yield from self.rr_gen(ang, kk, ang[:, :], kk[:, :], eng='vector')
                            sn, cs = snr(), csr()
                            A(lambda e, ang=ang, sn=sn: e.activation(out=sn[:, :], in_=ang[:, :], func=AF.Sin), [ang], [sn])
                            yield
                            A(lambda e, ang=ang, cs=cs: e.activation(out=cs[:, :], in_=ang[:, :], func=AF.Sin, scale=0.5), [ang], [cs])
                            yield
                            A(lambda e, cs=cs: e.activation(out=cs[:, :], in_=cs[:, :], func=AF.Square), [cs], [cs])
                            yield
                            V(lambda e, cs=cs: e.tensor_scalar(out=cs[:, :], in0=cs[:, :], scalar1=-2.0, scalar2=1.0, op0=ALU.mult, op1=ALU.add), [cs], [cs])
                            yield
                            wm, ws = wmr(), wsr()
                            for (h0, h1) in HC:
                                p1, p2 = p1r(), p2r()
                                for v_, pp in ((0, p1), (1, p2)):
                                    for j in range(8):
                                        e_ = 7 - j if d_ == 0 else j
                                        P(lambda e, pp=pp, d_=d_, e_=e_, v_=v_, j=j, h0=h0, h1=h1, pr=pr, um=um: e.matmul(pp[:, :], lhsT=LT[64 * pr:64 * pr + 64, d_, e_, v_, :], rhs=um[64 * pr:64 * pr + 64, j, h0:h1], start=(j == 0), stop=(j == 7)), [LT, um], [pp])
                                a, b = t1r(), t2r()
                                V(lambda e, p1=p1, a=a, cs=cs, h0=h0, h1=h1: e.tensor_tensor(out=a[:, :], in0=p1[:, :], in1=cs[:, h0:h1], op=ALU.mult), [p1, cs], [a])
                                yield
                                V(lambda e, p2=p2, b=b, sn=sn, h0=h0, h1=h1: e.tensor_tensor(out=b[:, :], in0=p2[:, :], in1=sn[:, h0:h1], op=ALU.mult), [p2, sn], [b])
                                yield
                                V(lambda e, a=a, b=b, wm=wm, h0=h0, h1=h1: e.tensor_tensor(out=wm[:, h0:h1], in0=a[:, :], in1=b[:, :], op=ALU.add), [a, b], [wm])
                                yield
                            rho_b = RHO8[:, dg:dg + 1]
                            e1, e2 = e1r(), e2r()
                            if d_ == 0:
                                V(lambda e, rho_b=rho_b, wm=wm, ws=ws: e.tensor_tensor_scan(out=ws[:, :], data0=rho_b.to_broadcast([128, NC]), data1=wm[:, :], initial=0.0, op0=ALU.mult, op1=ALU.add), [wm, RHO8], [ws])
                                yield
                                V(lambda e, cs=cs, ws=ws, e1=e1: e.tensor_tensor(out=e1[:, 1:NC], in0=cs[:, 0:NC - 1], in1=ws[:, 0:NC - 1], op=ALU.mult), [cs, ws], [e1])
                                yield
                                G(lambda e, e1=e1: e.memset(e1[:, 0:1], 0.0), [e1], [e1])
                                yield
                                V(lambda e, sn=sn, ws=ws, e2=e2: e.tensor_tensor(out=e2[:, 1:NC], in0=sn[:, 0:NC - 1], in1=ws[:, 0:NC - 1], op=ALU.mult), [sn, ws], [e2])
                                yield
                                G(lambda e, e2=e2: e.memset(e2[:, 0:1], 0.0), [e2], [e2])
                                yield
                            else:
                                V(lambda e, rho_b=rho_b, wm=wm, ws=ws: e.tensor_tensor_scan(out=ws[:, 0:NCC][:, ::-1], data0=rho_b.to_broadcast([128, NCC]), data1=wm[:, 0:NCC][:, ::-1], initial=0.0, op0=ALU.mult, op1=ALU.add), [wm, RHO8], [ws])
                                yield
                                V(lambda e, rho_b=rho_b, wm=wm, ws=ws: e.tensor_tensor_scan(out=ws[:, NCC:NC][:, ::-1], data0=rho_b.to_broadcast([128, NC - NCC]), data1=wm[:, NCC:NC][:, ::-1], initial=ws[:, 0:1], op0=ALU.mult, op1=ALU.add), [wm, RHO8, ws], [ws])
                                yield
                                for (tb, ee, eng) in ((cs, e1, V), (sn, e2, V)):
                                    eng(lambda e, tb=tb, ee=ee, ws=ws: e.tensor_tensor(out=ee[:, 0:NC - 1], in0=tb[:, 1:NC], in1=ws[:, 1:NC], op=ALU.mult), [tb, ws], [ee])
                                    yield
                                    eng(lambda e, tb=tb, ee=ee, ws=ws: e.tensor_tensor(out=ee[:, NC - 1:NC], in0=tb[:, 0:1], in1=ws[:, 0:1], op=ALU.mult), [tb, ws, ee], [ee])
                                    yield
                                    G(lambda e, ee=ee: e.memset(ee[:, NCC - 1:NCC], 0.0), [ee], [ee])
                                    yield
                            rz = rzr()
                            G(lambda e, rz=rz: e.memset(rz[:, :, :], 0.0), [], [rz])
                            yield
                            G(lambda e, rz=rz, d_=d_, g8=g8: e.tensor_copy(out=rz[:, :, g8 * 16:(g8 + 1) * 16], in_=Rb[:, d_, 1:9, :, g8 * 16:(g8 + 1) * 16].rearrange("p m v o -> p (m v) o")), [Rb], [rz])
                            yield
                            per_dir[d_] = (e1, e2, rz)
                        interleave([dir_chain(0), dir_chain(1)])
                        (e1f, e2f, rzf), (e1b, e2b, rzb) = per_dir
                        for tp in range(8):
                            mf, mb = tp + 1, 8 - tp
                            for (h0, h1) in HC:
                                py = pyr()
                                P(lambda e, py=py, mf=mf, h0=h0, h1=h1, rzf=rzf, e1f=e1f: e.matmul(py[:, :], lhsT=rzf[:, (mf - 1) * 2, :], rhs=e1f[:, h0:h1], start=True, stop=False), [rzf, e1f], [py])
                                P(lambda e, py=py, mf=mf, h0=h0, h1=h1, rzf=rzf, e2f=e2f: e.matmul(py[:, :], lhsT=rzf[:, (mf - 1) * 2 + 1, :], rhs=e2f[:, h0:h1], start=False, stop=False), [rzf, e2f], [py])
                                P(lambda e, py=py, mb=mb, h0=h0, h1=h1, rzb=rzb, e1b=e1b: e.matmul(py[:, :], lhsT=rzb[:, (mb - 1) * 2, :], rhs=e1b[:, h0:h1], start=False, stop=False), [rzb, e1b], [py])
                                P(lambda e, py=py, mb=mb, h0=h0, h1=h1, rzb=rzb, e2b=e2b: e.matmul(py[:, :], lhsT=rzb[:, (mb - 1) * 2 + 1, :], rhs=e2b[:, h0:h1], start=False, stop=True), [rzb, e2b], [py])
                                V(lambda e, py=py, tp=tp, h0=h0, h1=h1: e.tensor_tensor(out=Yv[:, h0:h1, tp], in0=py[:, :], in1=Yv[:, h0:h1, tp], op=ALU.add), [py, Ytp[tp]], [Ytp[tp]])
                    if 'yscanT' in self.debug:
                        ysd = self.scratch('yscanT', [D, NT], F32)
                        V(lambda e, blk=blk: e.tensor_scalar(out=uf[:, :], in0=uf[:, :], scalar1=dvec[:, blk:blk + 1], scalar2=None, op0=ALU.mult), [uf, dvec], [uf])
                        V(lambda e: e.tensor_tensor(out=uf[:, :], in0=Yacc[:, :], in1=uf[:, :], op=ALU.subtract), Ytp + [uf], [uf])
                        S.dma('sync', ysd[blk * 128:(blk + 1) * 128, :], uf[:, :], ysd, uf, part=True)
                    V(lambda e: e.tensor_tensor(out=uf[:, :], in0=Yacc[:, :], in1=Yacc[:, :], op=ALU.mult), Ytp, [uf])
                    V(lambda e: e.tensor_scalar(out=uf[:, :], in0=uf[:, :], scalar1=0.044715, scalar2=1.0, op0=ALU.mult, op1=ALU.add), [uf], [uf])
                    V(lambda e: e.tensor_tensor(out=uf[:, :], in0=uf[:, :], in1=Yacc[:, :], op=ALU.mult), Ytp + [uf], [uf])
                    A(lambda e: e.activation(out=uf[:, :], in_=uf[:, :], func=AF.Sigmoid, scale=2.0 * math.sqrt(2.0 / math.pi)), [uf], [uf])
                    V(lambda e: e.tensor_tensor(out=ud2, in0=uf[:, :], in1=Yacc[:, :], op=ALU.mult), Ytp + [uf], [ud])
                    S.dma('sync', ygT[blk * 128:(blk + 1) * 128, :], ud2, ygT, ud, part=True)
        self.ssm_glu_out(l)

    def ssm_glu_out(self, l):
        S = self.S
        I = self.I
        ygT = self.scratch('ygT', [D, NT], BF16)
        gates = self.scratch('gates', [NT, 3 * D], F32)
        y1 = self.scratch('y1', [NT, D], F32)
        TG = [(t0, min(t0 + 512, NT)) for t0 in range(0, NT, 512)]
        V = lambda fn, r, w: S.op('vector', fn, r, w)
        A = lambda fn, r, w: S.op('scalar', fn, r, w)
        with S.phase():
            wg = S.sbuf([128, 8, D], BF16, 'wglu')
            wo = S.sbuf([128, 8, D], BF16, 'wsout')
            for (wt, nm) in ((wg, 'w_ssm_glu'), (wo, 'w_ssm_out')):
                wvv = I[nm].t.ap()[l].rearrange("(k p) n -> p k n", p=128)
                for hh in range(2):
                    S.dma('gpsimd', wt[:, :, hh * 512:(hh + 1) * 512], wvv[:, :, hh * 512:(hh + 1) * 512], wt, I[nm], part=(hh == 1))
            ygr = S.ring(2, [128, 8, 512], BF16, 'ygl')
            y2r = S.ring(2, [128, 8, 512], BF16, 'y2T')
            sgr = S.ring(2, [128, 512], F32, 'sgl')
            gr = S.ring(2, [128, D], F32, 'g1')
            yr = S.ring(2, [128, D], F32, 'y1t')
            pgr = S.ring(3, [128, 512], F32, 'pgl', psum=True)
            pyr = S.ring(4, [128, 512], F32, 'py', psum=True)
            ygv = ygT.t.ap().rearrange("(j p) t -> p j t", p=128)
            for (t0, t1) in TG:
                n = t1 - t0
                yg = ygr()
                S.dma('sync', yg[:, :, 0:n], ygv[:, :, t0:t1], yg, ygT)
                y2 = y2r()
                for m in range(8):
                    pg = pgr()
                    for k in range(8):
                        S.op('tensor', lambda e, pg=pg, yg=yg, m=m, k=k, n=n: e.matmul(pg[:, 0:n], lhsT=wg[:, k, m * 128:(m + 1) * 128], rhs=yg[:, k, 0:n], start=(k == 0), stop=(k == 7)), [wg, yg], [pg])
                    sgt = sgr()
                    A(lambda e, pg=pg, sgt=sgt, n=n: e.activation(out=sgt[:, 0:n], in_=pg[:, 0:n], func=AF.Sigmoid), [pg], [sgt])
                    V(lambda e, yg=yg, y2=y2, sgt=sgt, m=m, n=n: e.tensor_tensor(out=y2[:, m, 0:n], in0=yg[:, m, 0:n], in1=sgt[:, 0:n], op=ALU.mult), [yg, sgt], [y2])
                for ii in range(n // 128):
                    i = t0 // 128 + ii
                    g1 = gr()
                    S.dma('sync', g1[:, :], gates[i * 128:(i + 1) * 128, D:2 * D], g1, gates)
                    yt = yr()
                    for nb in range(2):
                        py = pyr()
                        for m in range(8):
                            S.op('tensor', lambda e, py=py, y2=y2, m=m, ii=ii, nb=nb: e.matmul(py[:, :], lhsT=y2[:, m, ii * 128:(ii + 1) * 128], rhs=wo[:, m, nb * 512:(nb + 1) * 512], start=(m == 0), stop=(m == 7)), [y2, wo], [py])
                        V(lambda e, py=py, yt=yt, g1=g1, nb=nb: e.tensor_tensor(out=yt[:, nb * 512:(nb + 1) * 512], in0=py[:, :], in1=g1[:, nb * 512:(nb + 1) * 512], op=ALU.mult), [py, g1], [yt])
                    S.dma('sync', y1[i * 128:(i + 1) * 128, :], yt[:, :], y1, yt, part=True)


    def ln_stats_gen(self, x, st):
        S = self.S
        S.op('vector', lambda e: e.bn_stats(out=st[:, 0:6], in_=x[:, 0:512]), [x], [st])
        yield
        S.op('vector', lambda e: e.bn_stats(out=st[:, 6:12], in_=x[:, 512:1024]), [x], [st])
        yield
        S.op('vector', lambda e: e.bn_aggr(out=st[:, 12:14], in_=st[:, 0:12]), [st], [st])
        yield
        S.op('vector', lambda e: e.tensor_scalar(out=st[:, 14:15], in0=st[:, 13:14], scalar1=LN_EPS, scalar2=None, op0=ALU.add), [st], [st])
        yield
        S.op('scalar', lambda e: e.activation(out=st[:, 14:15], in_=st[:, 14:15], func=AF.Sqrt), [st], [st])
        yield
        S.op('vector', lambda e: e.reciprocal(out=st[:, 14:15], in_=st[:, 14:15]), [st], [st])
        yield

    def ln_stats(self, x, st):
        S = self.S
        S.op('vector', lambda e: e.bn_stats(out=st[:, 0:6], in_=x[:, 0:512]), [x], [st])
        S.op('vector', lambda e: e.bn_stats(out=st[:, 6:12], in_=x[:, 512:1024]), [x], [st])
        S.op('vector', lambda e: e.bn_aggr(out=st[:, 12:14], in_=st[:, 0:12]), [st], [st])
        self.rstd(st, 13, 14, LN_EPS)

    def ph_merge(self, l):
        S = self.S
        I = self.I
        V = lambda fn, r, w: S.op('vector', fn, r, w)
        G = lambda fn, r, w: S.op('gpsimd', fn, r, w)
        A = lambda fn, r, w: S.op('scalar', fn, r, w)
        ctx_out = l < DEPTH - 1
        ys = [self.scratch('y%d' % b, [NT, D], F32) for b in range(3)]
        xres = self.scr['xres']
        modscr = self.scr['modscr%d' % l]
        h2tm = self.scratch('h2tm', [NT, D], BF16)
        aff = self.scratch('aff', [NT, NEXP], F32)
        affT = self.scratch('affT', [NEXP, NT], F32)
        with S.phase():
            wo = S.sbuf([128, 8, D], BF16, 'wo')
            wov = I['w_o'].t.ap()[l].rearrange("(k p) n -> p k n", p=128)
            for hh in range(2):
                S.dma('gpsimd', wo[:, :, hh * 512:(hh + 1) * 512], wov[:, :, hh * 512:(hh + 1) * 512], wo, I['w_o'], part=(hh == 1))
            wr = S.sbuf([128, 8, NEXP], F32, 'wr')
            S.dma('sync', wr[:, :, :], I['w_router'].t.ap()[l].rearrange("(k p) e -> p k e", p=128), wr, I['w_router'])
            modv = S.sbuf([128, 2, 3, D], F32, 'modv2')
            for w in range(2):
                for jj, c0 in enumerate((2 * D, 3 * D, 4 * D)):
                    S.dma('sync', modv[:, w, jj, :], modscr[w, :, c0:c0 + D], modv, modscr, part=not (w == 0 and jj == 0))
            lng = S.sbuf([128, 2, D], F32, 'ln1gb')
            S.dma('sync', lng[:, 0, :], I['ln1_g'][l:l + 1, :].partition_broadcast(128), lng, I['ln1_g'])
            S.dma('sync', lng[:, 1, :], I['ln1_b'][l:l + 1, :].partition_broadcast(128), lng, I['ln1_b'], part=True)
            yar, ybr, ycr = (S.ring(2, [128, D], F32, nm) for nm in ('ya', 'yb', 'yc'))
            mbr = S.ring(2, [128, D], BF16, 'mb')
            mtr = S.ring(2, [128, 8, 128], BF16, 'mT')
            xr = S.ring(2, [128, D], F32, 'xt')
            tr = S.ring(2, [128, D], F32, 'tt')
            hr = S.ring(2, [128, D], F32, 'h2f')
            hbr = S.ring(2, [128, D], BF16, 'h2b')
            htr = S.ring(2, [128, 8, 128], F32, 'h2T')
            str_ = S.ring(6, [128, 16], F32, 'st')
            afr = S.ring(2, [128, 2, NEXP], F32, 'afft')
            atr = S.ring(2, [NEXP, 128], F32, 'affTt')
            ptr_ = S.ring(1, [128, 8, 128], BF16, 'pTm', psum=True)
            pyr = S.ring(2, [128, 512], F32, 'pym', psum=True)
            phr = S.ring(1, [128, 8, 128], F32, 'pTh', psum=True)
            plr = S.ring(2, [128, NEXP], F32, 'plog', psum=True)
            par_ = S.ring(1, [NEXP, 128], F32, 'paT', psum=True)
            def tile_chain(i):
                w = 1 if i < 2 else 0
                rows = slice(i * 128, (i + 1) * 128)
                ya, yb, yc = yar(), ybr(), ycr()
                for yt_, ysrc in ((ya, ys[0]), (yb, ys[1]), (yc, ys[2])):
                    S.dma('sync', yt_[:, :], ysrc[rows, :], yt_, ysrc)
                V(lambda e, ya=ya, yb=yb: e.tensor_tensor(out=ya[:, :], in0=ya[:, :], in1=yb[:, :], op=ALU.add), [ya, yb], [ya])
                yield
                mb = mbr()
                V(lambda e, ya=ya, yc=yc, mb=mb: e.tensor_tensor(out=mb[:, :], in0=ya[:, :], in1=yc[:, :], op=ALU.add), [ya, yc], [mb])
                yield
                pT = ptr_()
                for k in range(8):
                    S.op('tensor', lambda e, pT=pT, mb=mb, k=k: e.transpose(out=pT[:, k, :], in_=mb[:, k * 128:(k + 1) * 128], identity=self.ident_b[:, :]), [mb, self.ident_b], [pT])
                mT = mtr()
                A(lambda e, pT=pT, mT=mT: e.copy(out=mT[:, :, :], in_=pT[:, :, :]), [pT], [mT])
                yield
                xt = xr()
                S.dma('sync', xt[:, :], xres[rows, :], xt, xres)
                tt = tr()
                for nb in range(2):
                    py = pyr()
                    for k in range(8):
                        S.op('tensor', lambda e, py=py, mT=mT, k=k, nb=nb: e.matmul(py[:, :], lhsT=mT[:, k, :], rhs=wo[:, k, nb * 512:(nb + 1) * 512], start=(k == 0), stop=(k == 7)), [mT, wo], [py])
                    V(lambda e, py=py, tt=tt, nb=nb, w=w: e.tensor_tensor(out=tt[:, nb * 512:(nb + 1) * 512], in0=py[:, :], in1=modv[:, w, 0, nb * 512:(nb + 1) * 512], op=ALU.mult), [py, modv], [tt])
                    yield
                V(lambda e, xt=xt, tt=tt: e.scalar_tensor_tensor(out=tt[:, :], in0=xt[:, :], scalar=ALPHA, in1=tt[:, :], op0=ALU.mult, op1=ALU.add), [xt, tt], [tt])
                yield
                st = str_()
                yield from self.ln_stats_gen(tt, st)
                V(lambda e, tt=tt, st=st: e.tensor_scalar(out=tt[:, :], in0=tt[:, :], scalar1=st[:, 12:13], scalar2=st[:, 14:15], op0=ALU.subtract, op1=ALU.mult), [tt, st], [tt])
                yield
                V(lambda e, tt=tt: e.tensor_tensor(out=tt[:, :], in0=tt[:, :], in1=lng[:, 0, :], op=ALU.mult), [tt, lng], [tt])
                yield
                V(lambda e, tt=tt, xt=xt: e.tensor_tensor(out=xt[:, :], in0=tt[:, :], in1=lng[:, 1, :], op=ALU.add), [tt, lng], [xt])
                yield
                S.dma('sync', xres[rows, :], xt[:, :], xres, xt, part=True)
                st2 = str_()
                yield from self.ln_stats_gen(xt, st2)
                hf = hr()
                V(lambda e, xt=xt, st2=st2, hf=hf: e.tensor_scalar(out=hf[:, :], in0=xt[:, :], scalar1=st2[:, 12:13], scalar2=st2[:, 14:15], op0=ALU.subtract, op1=ALU.mult), [xt, st2], [hf])
                yield
                V(lambda e, hf=hf, w=w: e.tensor_tensor(out=hf[:, :], in0=hf[:, :], in1=modv[:, w, 2, :], op=ALU.mult), [hf, modv], [hf])
                yield
                V(lambda e, hf=hf, w=w: e.tensor_tensor(out=hf[:, :], in0=hf[:, :], in1=modv[:, w, 1, :], op=ALU.add), [hf, modv], [hf])
                yield
                hb = hbr()
                A(lambda e, hf=hf, hb=hb: e.copy(out=hb[:, :], in_=hf[:, :]), [hf], [hb])
                yield
                S.dma('sync', h2tm[rows, :], hb[:, :], h2tm, hb, part=True)
                ph = phr()
                for k in range(8):
                    S.op('tensor', lambda e, ph=ph, hf=hf, k=k: e.transpose(out=ph[:, k, :], in_=hf[:, k * 128:(k + 1) * 128], identity=self.ident_f[:, :]), [hf, self.ident_f], [ph])
                hT = htr()
                A(lambda e, ph=ph, hT=hT: e.copy(out=hT[:, :, :], in_=ph[:, :, :]), [ph], [hT])
                yield
                pl = plr()
                for k in range(8):
                    S.op('tensor', lambda e, pl=pl, hT=hT, k=k: e.matmul(pl[:, :], lhsT=hT[:, k, :], rhs=wr[:, k, :], start=(k == 0), stop=(k == 7)), [hT, wr], [pl])
                st3 = str_()
                af = afr()
                V(lambda e, pl=pl, st3=st3: e.reduce_max(out=st3[:, 0:1], in_=pl[:, :], axis=AX.X), [pl], [st3])
                yield
                V(lambda e, st3=st3: e.tensor_scalar(out=st3[:, 1:2], in0=st3[:, 0:1], scalar1=-1.0, scalar2=None, op0=ALU.mult), [st3], [st3])
                yield
                A(lambda e, pl=pl, st3=st3, af=af: e.activation(out=af[:, 0, :], in_=pl[:, :], func=AF.Exp, bias=st3[:, 1:2], accum_out=st3[:, 2:3]), [pl, st3], [af, st3])
                yield
                V(lambda e, st3=st3: e.reciprocal(out=st3[:, 3:4], in_=st3[:, 2:3]), [st3], [st3])
                yield
                V(lambda e, st3=st3, af=af: e.tensor_scalar(out=af[:, 1, :], in0=af[:, 0, :], scalar1=st3[:, 3:4], scalar2=None, op0=ALU.mult), [af, st3], [af])
                yield
                S.dma('sync', aff[rows, :], af[:, 1, :], aff, af, part=True)
                pa = par_()
                S.op('tensor', lambda e, pa=pa, af=af: e.transpose(out=pa[:, :], in_=af[:, 1, :], identity=self.ident_f[:, :]), [af, self.ident_f], [pa])
                at = atr()
                A(lambda e, pa=pa, at=at: e.copy(out=at[:, :], in_=pa[:, :]), [pa], [at])
                yield
                S.dma('sync', affT[:, i * 128:(i + 1) * 128], at[:, :], affT, at, part=True)


            tiles = list(range(0 if ctx_out else 2, NTILE))
            for t0_ in range(0, len(tiles), 2):
                interleave([tile_chain(t_) for t_ in tiles[t0_:t0_ + 2]])
    def ph_moe(self, l):
        S = self.S
        I = self.I
        V = lambda fn, r, w: S.op('vector', fn, r, w)
        G = lambda fn, r, w: S.op('gpsimd', fn, r, w)
        A = lambda fn, r, w: S.op('scalar', fn, r, w)
        P = lambda fn, r, w: S.op('tensor', fn, r, w)
        ctx_out = l < DEPTH - 1
        h2tm = self.scr['h2tm']
        aff = self.scr['aff']
        affT = self.scr['affT']
        ymoe = self.scratch('ymoe', [NT, D], F32)
        sets = []
        if ctx_out:
            sets.append((0, 2, 2 * NCTX // NEXP, 1))
        sets.append((2, 32, 2 * NLAT // NEXP, 4))
        with S.phase():
            zt = S.sbuf([128, D], F32, 'zt')
            G(lambda e: e.memset(zt[:, :], 0.0), [], [zt])
            for i in range(NTILE):
                S.dma('sync', ymoe[i * 128:(i + 1) * 128, :], zt[:, :], ymoe, zt, part=(i > 0))
            ymoe_t = [Buf(ymoe.t, 'ymoe_t%d' % i) for i in range(NTILE)]
            for b in ymoe_t:
                b.w = ymoe.w
            acc_grp = [DSem(S, 'ymacc0'), DSem(S, 'ymacc1')]
            aff_all = S.sbuf([128, NTILE, NEXP], F32, 'aff_all')
            mask_all = S.sbuf([128, NTILE, NEXP], F32, 'mask_all')
            slot_all = S.sbuf([128, NTILE, NEXP], F32, 'slot_all')
            gate_all = S.sbuf([128, NTILE, NEXP], F32, 'gate_all')
            slotT = S.sbuf([NEXP, NT], F32, 'slotT')
            S.dma('sync', aff_all[:, :, :], aff.t.ap().rearrange("(i p) e -> p i e", p=128), aff_all, aff)
            esel = S.sbuf([NEXP, NEXP, 128], F32, 'esel')
            V(lambda e: e.tensor_copy(out=esel[:, :, :], in_=self.ident_f[0:NEXP, 0:NEXP].unsqueeze(2).to_broadcast([NEXP, NEXP, 128])), [self.ident_f], [esel])
            ustr = S.sbuf([128, 128], F32, 'ustr')
            G(lambda e: e.affine_select(out=ustr[:, :], in_=self.ones_f[:, :], pattern=[[1, 128]], compare_op=ALU.is_gt, fill=0.0, base=0, channel_multiplier=-1), [self.ones_f], [ustr])
            irow = S.sbuf([128, 512], F32, 'irow')
            pcol = S.sbuf([128, 4], F32, 'pcol')
            with S.phase():
                ti = S.sbuf([128, 512], I32, 'ti')
                G(lambda e: e.iota(ti[:, :], pattern=[[1, 512]], base=0, channel_multiplier=0), [], [ti])
                V(lambda e: e.tensor_copy(out=irow[:, :], in_=ti[:, :]), [ti], [irow])
                G(lambda e: e.iota(ti[:, 0:4], pattern=[[128, 4]], base=0, channel_multiplier=1), [irow], [ti])
                V(lambda e: e.tensor_copy(out=pcol[:, :], in_=ti[:, 0:4]), [ti], [pcol])
                affTs = S.sbuf([NEXP, NT], F32, 'affTs')
                junk = S.sbuf([NEXP, NLAT], F32, 'junk')
                S.dma('sync', affTs[:, :], affT[:, :], affTs, affT)
                pb = S.ring(2, [128, NEXP], F32, 'pthr', psum=True)
                pc = S.ring(2, [128, NEXP], F32, 'pcum', psum=True)
                pst = S.ring(2, [NEXP, 128], F32, 'pslT', psum=True)
                for (ti0, ntl, cap, nst) in sets:
                    c0, c1 = ti0 * 128, (ti0 + ntl) * 128
                    bs = S.sbuf([NEXP, 8], F32, 'bs')
                    V(lambda e, bs=bs: e.memset(bs[:, 0:1], 0.0), [], [bs])
                    V(lambda e, bs=bs: e.memset(bs[:, 1:2], 1.0), [bs], [bs])
                    for it in range(30):
                        V(lambda e, bs=bs: e.tensor_tensor(out=bs[:, 2:3], in0=bs[:, 0:1], in1=bs[:, 1:2], op=ALU.add), [bs], [bs])
                        V(lambda e, bs=bs: e.tensor_scalar(out=bs[:, 2:3], in0=bs[:, 2:3], scalar1=0.5, scalar2=None, op0=ALU.mult), [bs], [bs])
                        V(lambda e, bs=bs, c0=c0, c1=c1: e.tensor_scalar(out=junk[:, 0:c1 - c0], in0=affTs[:, c0:c1], scalar1=bs[:, 2:3], scalar2=0.0, op0=ALU.is_ge, op1=ALU.add, accum_out=bs[:, 3:4]), [affTs, bs], [junk, bs])
                        V(lambda e, bs=bs, cap=cap: e.tensor_scalar(out=bs[:, 4:5], in0=bs[:, 3:4], scalar1=float(cap), scalar2=None, op0=ALU.is_ge), [bs], [bs])
                        V(lambda e, bs=bs: e.tensor_scalar(out=bs[:, 5:6], in0=bs[:, 4:5], scalar1=-1.0, scalar2=1.0, op0=ALU.mult, op1=ALU.add), [bs], [bs])
                        V(lambda e, bs=bs: e.tensor_tensor(out=bs[:, 6:7], in0=bs[:, 2:3], in1=bs[:, 0:1], op=ALU.subtract), [bs], [bs])
                        V(lambda e, bs=bs: e.tensor_tensor(out=bs[:, 7:8], in0=bs[:, 2:3], in1=bs[:, 1:2], op=ALU.subtract), [bs], [bs])
                        V(lambda e, bs=bs: e.scalar_tensor_tensor(out=bs[:, 0:1], in0=bs[:, 6:7], scalar=bs[:, 4:5], in1=bs[:, 0:1], op0=ALU.mult, op1=ALU.add), [bs], [bs])
                        V(lambda e, bs=bs: e.scalar_tensor_tensor(out=bs[:, 1:2], in0=bs[:, 7:8], scalar=bs[:, 5:6], in1=bs[:, 1:2], op0=ALU.mult, op1=ALU.add), [bs], [bs])
                    dg = S.sbuf([NEXP, NEXP], F32, 'dg')
                    V(lambda e, bs=bs, dg=dg: e.tensor_scalar(out=dg[:, :], in0=self.ident_f[0:NEXP, 0:NEXP], scalar1=bs[:, 0:1], scalar2=None, op0=ALU.mult), [bs, self.ident_f], [dg])
                    pthr = pb()
                    P(lambda e, pthr=pthr, dg=dg: e.matmul(pthr[:, :], lhsT=self.ones_f[0:NEXP, :], rhs=dg[:, :], start=True, stop=True), [dg, self.ones_f], [pthr])
                    thr = S.sbuf([128, NEXP], F32, 'thr')
                    V(lambda e, pthr=pthr, thr=thr: e.tensor_copy(out=thr[:, :], in_=pthr[:, :]), [pthr], [thr])
                    for ii in range(ntl):
                        i = ti0 + ii
                        V(lambda e, i=i, thr=thr: e.tensor_tensor(out=mask_all[:, i, :], in0=aff_all[:, i, :], in1=thr[:, :], op=ALU.is_ge), [aff_all, thr], [mask_all])
                        V(lambda e, i=i: e.tensor_tensor(out=gate_all[:, i, :], in0=aff_all[:, i, :], in1=mask_all[:, i, :], op=ALU.mult), [aff_all, mask_all], [gate_all])
                        pcm = pc()
                        for jj in range(ii):
                            j = ti0 + jj
                            P(lambda e, pcm=pcm, j=j, jj=jj: e.matmul(pcm[:, :], lhsT=self.ones_f[:, :], rhs=mask_all[:, j, :], start=(jj == 0), stop=False), [mask_all, self.ones_f], [pcm])
                        P(lambda e, pcm=pcm, i=i, ii=ii: e.matmul(pcm[:, :], lhsT=ustr[:, :], rhs=mask_all[:, i, :], start=(ii == 0), stop=True), [mask_all, ustr], [pcm])
                        V(lambda e, pcm=pcm, i=i: e.tensor_tensor(out=slot_all[:, i, :], in0=pcm[:, :], in1=mask_all[:, i, :], op=ALU.mult), [pcm, mask_all], [slot_all])
                        V(lambda e, i=i: e.tensor_tensor(out=slot_all[:, i, :], in0=slot_all[:, i, :], in1=mask_all[:, i, :], op=ALU.add), [slot_all, mask_all], [slot_all])
                        V(lambda e, i=i: e.tensor_scalar(out=slot_all[:, i, :], in0=slot_all[:, i, :], scalar1=-1.0, scalar2=None, op0=ALU.add), [slot_all], [slot_all])
                        ps_ = pst()
                        P(lambda e, ps_=ps_, i=i: e.transpose(out=ps_[:, :], in_=slot_all[:, i, :], identity=self.ident_f[:, :]), [slot_all, self.ident_f], [ps_])
                        A(lambda e, ps_=ps_, i=i: e.copy(out=slotT[:, i * 128:(i + 1) * 128], in_=ps_[:, :]), [ps_], [slotT])
            if 'slotdbg' in self.debug:
                sd = self.scratch('slotdbg', [128, NTILE, NEXP], F32)
                S.dma('sync', sd[:, :, :], slot_all[:, :, :], sd, slot_all)
            with S.phase():
                selT = S.sbuf([128, 32, 512], BF16, 'selT')
                sel = S.sbuf([128, 4, NLAT], BF16, 'sel')
                xsT = S.sbuf([128, 8, 512], BF16, 'xsT')
                actT = S.sbuf([128, 22, 512], BF16, 'actT')
                ye = S.sbuf([128, 4, D], BF16, 'ye')
                h2r = S.ring(4, [128, D], BF16, 'h2l')
                wgr = S.ring(3, [128, 8, 256], BF16, 'wga')
                wur = S.ring(3, [128, 8, 256], BF16, 'wgu')
                wdr = S.ring(4, [128, 2, 512], BF16, 'wdn')
                sar = S.ring(2, [128, 512], F32, 'sact')
                ytr = S.ring(3, [128, D], F32, 'ysc')
                acc4 = S.ring(4, [128, 512], F32, 'pacc', psum=True)
                gen4 = S.ring(4, [128, 512], F32, 'pgen', psum=True)
                items = []

                def make_item(ex, ti0, ntl, cap, nst):
                    if True:
                        wgv = I['w_gate_up'].t.ap()[l, ex].rearrange("(k p) n -> p k n", p=128)
                        wdv = I['w_down'].t.ap()[l, ex].rearrange("(f p) n -> p f n", p=128)
                        ns = nst * 128
                        ntok = ntl * 128
                        tok0 = ti0 * 128
                        def selT_build():
                            for ii in range(ntl):
                                i = ti0 + ii
                                V(lambda e, ii=ii, i=i, ex=ex, ns=ns: e.tensor_scalar(out=selT[:, ii, 0:ns], in0=irow[:, 0:ns], scalar1=slot_all[:, i, ex:ex + 1], scalar2=None, op0=ALU.is_equal), [irow, slot_all], [selT])
                        def sel_build():
                            for c0 in range(0, ntok, 512):
                                cn = min(512, ntok - c0)
                                pbc = gen4()
                                P(lambda e, pbc=pbc, ex=ex, c0=c0, cn=cn, tok0=tok0: e.matmul(pbc[:, 0:cn], lhsT=esel[:, ex, :], rhs=slotT[:, tok0 + c0:tok0 + c0 + cn], start=True, stop=True), [esel, slotT], [pbc])
                                for st_ in range(nst):
                                    V(lambda e, pbc=pbc, st_=st_, c0=c0, cn=cn: e.tensor_scalar(out=sel[:, st_, c0:c0 + cn], in0=pbc[:, 0:cn], scalar1=pcol[:, st_:st_ + 1], scalar2=None, op0=ALU.is_equal), [pbc, pcol], [sel])
                        def gather():
                            for kh in range(2):
                                pgs = [acc4() for _ in range(4)]
                                for ii in range(ntl):
                                    i = ti0 + ii
                                    ht = h2r()
                                    S.dma('sync', ht[:, :], h2tm[i * 128:(i + 1) * 128, :], ht, h2tm)
                                    for kk in range(4):
                                        k = kh * 4 + kk
                                        P(lambda e, pg=pgs[kk], ht=ht, k=k, ii=ii, ns=ns, ntl=ntl: e.matmul(pg[:, 0:ns], lhsT=ht[:, k * 128:(k + 1) * 128], rhs=selT[:, ii, 0:ns], start=(ii == 0), stop=(ii == ntl - 1)), [ht, selT], [pgs[kk]])
                                for kk in range(4):
                                    k = kh * 4 + kk
                                    A(lambda e, pg=pgs[kk], k=k, ns=ns: e.copy(out=xsT[:, k, 0:ns], in_=pg[:, 0:ns]), [pgs[kk]], [xsT])
                        def mlp():
                            for b in range(11):
                                wa, wu = wgr(), wur()
                                S.dma('gpsimd', wa[:, :, :], wgv[:, :, b * 256:(b + 1) * 256], wa, I['w_gate_up'])
                                S.dma('gpsimd', wu[:, :, :], wgv[:, :, DEXP + b * 256:DEXP + (b + 1) * 256], wu, I['w_gate_up'])
                                for ff in range(2):
                                    f = 2 * b + ff
                                    pa_, pu_ = gen4(), gen4()
                                    for k in range(8):
                                        P(lambda e, pa_=pa_, wa=wa, k=k, ff=ff, ns=ns: e.matmul(pa_[:, 0:ns], lhsT=wa[:, k, ff * 128:(ff + 1) * 128], rhs=xsT[:, k, 0:ns], start=(k == 0), stop=(k == 7)), [wa, xsT], [pa_])
                                    for k in range(8):
                                        P(lambda e, pu_=pu_, wu=wu, k=k, ff=ff, ns=ns: e.matmul(pu_[:, 0:ns], lhsT=wu[:, k, ff * 128:(ff + 1) * 128], rhs=xsT[:, k, 0:ns], start=(k == 0), stop=(k == 7)), [wu, xsT], [pu_])
                                    sa = sar()
                                    A(lambda e, pa_=pa_, sa=sa, ns=ns: e.activation(out=sa[:, 0:ns], in_=pa_[:, 0:ns], func=AF.Silu), [pa_], [sa])
                                    V(lambda e, pu_=pu_, sa=sa, f=f, ns=ns: e.tensor_tensor(out=actT[:, f, 0:ns], in0=pu_[:, 0:ns], in1=sa[:, 0:ns], op=ALU.mult), [pu_, sa], [actT])
                            for nb in range(2):
                                pds = [acc4() for _ in range(nst)]
                                for f2 in range(11):
                                    wd = wdr()
                                    S.dma('gpsimd', wd[:, :, :], wdv[:, 2 * f2:2 * f2 + 2, nb * 512:(nb + 1) * 512], wd, I['w_down'])
                                    for ff in range(2):
                                        f = 2 * f2 + ff
                                        for st_ in range(nst):
                                            P(lambda e, pd=pds[st_], wd=wd, f=f, ff=ff, st_=st_: e.matmul(pd[:, :], lhsT=actT[:, f, st_ * 128:(st_ + 1) * 128], rhs=wd[:, ff, :], start=(f == 0), stop=(f == 21)), [actT, wd], [pds[st_]])
                                for st_ in range(nst):
                                    A(lambda e, pd=pds[st_], st_=st_, nb=nb: e.copy(out=ye[:, st_, nb * 512:(nb + 1) * 512], in_=pd[:, :]), [pds[st_]], [ye])
                        def scatter():
                            for ii in range(ntl):
                                i = ti0 + ii
                                yt = ytr()
                                for nb in range(2):
                                    psc = gen4()
                                    for st_ in range(nst):
                                        P(lambda e, psc=psc, st_=st_, ii=ii, nb=nb, nst=nst: e.matmul(psc[:, :], lhsT=sel[:, st_, ii * 128:(ii + 1) * 128], rhs=ye[:, st_, nb * 512:(nb + 1) * 512], start=(st_ == 0), stop=(st_ == nst - 1)), [sel, ye], [psc])
                                    V(lambda e, psc=psc, yt=yt, nb=nb, i=i, ex=ex: e.tensor_scalar(out=yt[:, nb * 512:(nb + 1) * 512], in0=psc[:, :], scalar1=gate_all[:, i, ex:ex + 1], scalar2=None, op0=ALU.mult), [psc, gate_all], [yt])
                                yb = ymoe_t[i]
                                yb.grp = acc_grp[ex % 2]
                                S.dma('gpsimd', ymoe[i * 128:(i + 1) * 128, :], yt[:, :], yb, yt, accum_op=ALU.add)
                        return (selT_build, sel_build, gather, mlp, scatter)
                for ex in range(NEXP):
                    for (ti0, ntl, cap, nst) in sets:
                        items.append(make_item(ex, ti0, ntl, cap, nst))
                items[0][0]()
                for idx_, (sb_, s2b_, ga_, ml_, sc_) in enumerate(items):
                    ga_()
                    if idx_ + 1 < len(items):
                        items[idx_ + 1][0]()
                    s2b_()
                    ml_()
                    sc_()
            ymoe.grp = acc_grp[0]
            ymoe.w = ('d', acc_grp[0])
            ymoe.r = {('d', id(acc_grp[1])): ('d', acc_grp[1])}

    def ph_ln2(self, l):
        S = self.S
        I = self.I
        V = lambda fn, r, w: S.op('vector', fn, r, w)
        G = lambda fn, r, w: S.op('gpsimd', fn, r, w)
        ctx_out = l < DEPTH - 1
        last = (l == DEPTH - 1)
        xres = self.scr['xres']
        modscr = self.scr['modscr%d' % l]
        ymoe = self.scratch('ymoe', [NT, D], F32)
        with S.phase():
            g2 = S.sbuf([128, 2, D], F32, 'g2v')
            for w in range(2):
                S.dma('sync', g2[:, w, :], modscr[w, :, 5 * D:6 * D], g2, modscr, part=(w == 1))
            lng = S.sbuf([128, 2, D], F32, 'ln2gb')
            S.dma('sync', lng[:, 0, :], I['ln2_g'][l:l + 1, :].partition_broadcast(128), lng, I['ln2_g'])
            S.dma('sync', lng[:, 1, :], I['ln2_b'][l:l + 1, :].partition_broadcast(128), lng, I['ln2_b'], part=True)
            xr = S.ring(3, [128, D], F32, 'xt')
            yr = S.ring(3, [128, D], F32, 'ym')
            str_ = S.ring(3, [128, 16], F32, 'st')
            def tile_chain(i):
                w = 1 if i < 2 else 0
                rows = slice(i * 128, (i + 1) * 128)
                xt, ym = xr(), yr()
                S.dma('sync', xt[:, :], xres[rows, :], xt, xres)
                S.dma('sync', ym[:, :], ymoe[rows, :], ym, ymoe)
                V(lambda e, ym=ym, w=w: e.tensor_tensor(out=ym[:, :], in0=ym[:, :], in1=g2[:, w, :], op=ALU.mult), [ym, g2], [ym])
                yield
                V(lambda e, xt=xt, ym=ym: e.scalar_tensor_tensor(out=ym[:, :], in0=xt[:, :], scalar=ALPHA, in1=ym[:, :], op0=ALU.mult, op1=ALU.add), [xt, ym], [ym])
                yield
                st = str_()
                yield from self.ln_stats_gen(ym, st)
                V(lambda e, ym=ym, st=st: e.tensor_scalar(out=ym[:, :], in0=ym[:, :], scalar1=st[:, 12:13], scalar2=st[:, 14:15], op0=ALU.subtract, op1=ALU.mult), [ym, st], [ym])
                yield
                V(lambda e, ym=ym: e.tensor_tensor(out=ym[:, :], in0=ym[:, :], in1=lng[:, 0, :], op=ALU.mult), [ym, lng], [ym])
                yield
                V(lambda e, ym=ym, xt=xt: e.tensor_tensor(out=xt[:, :], in0=ym[:, :], in1=lng[:, 1, :], op=ALU.add), [ym, lng], [xt])
                yield
                if last:
                    S.dma('sync', self.out[(i - 2) * 128:(i - 1) * 128, :], xt[:, :], self.out, xt, part=True)
                else:
                    S.dma('sync', xres[rows, :], xt[:, :], xres, xt, part=True)

            tiles = list(range(0 if ctx_out else 2, NTILE))
            for t0_ in range(0, len(tiles), 2):
                interleave([tile_chain(t_) for t_ in tiles[t0_:t0_ + 2]])
    def range_reduce(self, eng, x, k):
        S = self.S
        S.op(eng, lambda e: e.tensor_scalar(out=k[:, :], in0=x[:, :], scalar1=float(1.0 / TWO_PI), scalar2=MAGIC, op0=ALU.mult, op1=ALU.add), [x], [k])
        S.op(eng, lambda e: e.tensor_scalar(out=k[:, :], in0=k[:, :], scalar1=-MAGIC, scalar2=None, op0=ALU.add), [k], [k])
        S.op('vector', lambda e: e.scalar_tensor_tensor(out=x[:, :], in0=k[:, :], scalar=-CW1, in1=x[:, :], op0=ALU.mult, op1=ALU.add), [x, k], [x])
        S.op('vector', lambda e: e.scalar_tensor_tensor(out=x[:, :], in0=k[:, :], scalar=-CW2, in1=x[:, :], op0=ALU.mult, op1=ALU.add), [x, k], [x])
        S.op(eng, lambda e: e.tensor_scalar(out=x[:, :], in0=x[:, :], scalar1=PI_LO, scalar2=-PI_LO, op0=ALU.min, op1=ALU.max), [x], [x])

    def rope_tables(self):
        S = self.S
        cosT = S.sbuf([128, NT], F32, 'cosT')
        sinT = S.sbuf([128, NT], F32, 'sinT')
        ang = S.sbuf([128, NLAT], F32, 'ang')
        kk = S.sbuf([128, NLAT], F32, 'kk')
        pidx = S.sbuf([128, 1], I32, 'pidx')
        pf = S.sbuf([128, 4], F32, 'pf')
        ti = S.sbuf([128, 2], I32, 'ti')
        rowi = S.sbuf([128, NLAT], I32, 'rowi')
        S.op('gpsimd', lambda e: e.iota(pidx[:, :], pattern=[[0, 1]], base=0, channel_multiplier=1), [], [pidx])
        S.op('vector', lambda e: e.tensor_scalar(out=ti[:, 0:1], in0=pidx[:, :], scalar1=15, scalar2=None, op0=ALU.bitwise_and), [pidx], [ti])
        S.op('vector', lambda e: e.tensor_scalar(out=ti[:, 1:2], in0=pidx[:, :], scalar1=5, scalar2=1, op0=ALU.logical_shift_right, op1=ALU.bitwise_and), [pidx], [ti])
        S.op('vector', lambda e: e.tensor_copy(out=pf[:, 0:2], in_=ti[:, 0:2]), [ti], [pf])
        S.op('scalar', lambda e: e.activation(out=pf[:, 2:3], in_=pf[:, 0:1], func=AF.Exp, scale=-math.log(10000.0) / 16.0), [pf], [pf])
        S.op('vector', lambda e: e.tensor_tensor(out=pf[:, 3:4], in0=pf[:, 1:2], in1=pf[:, 2:3], op=ALU.mult), [pf], [pf])
        S.op('vector', lambda e: e.tensor_tensor(out=pf[:, 0:1], in0=pf[:, 2:3], in1=pf[:, 3:4], op=ALU.subtract), [pf], [pf])
        S.op('gpsimd', lambda e: e.iota(rowi[:, :], pattern=[[1, 64], [0, 64]], base=0, channel_multiplier=0), [], [rowi])
        S.op('vector', lambda e: e.tensor_copy(out=ang[:, :], in_=rowi[:, :]), [rowi], [ang])
        S.op('vector', lambda e: e.tensor_scalar(out=ang[:, :], in0=ang[:, :], scalar1=pf[:, 0:1], scalar2=None, op0=ALU.mult), [ang, pf], [ang])
        S.op('gpsimd', lambda e: e.iota(rowi[:, :], pattern=[[0, 64], [1, 64]], base=0, channel_multiplier=0), [ang], [rowi])
        S.op('vector', lambda e: e.tensor_copy(out=kk[:, :], in_=rowi[:, :]), [rowi], [kk])
        S.op('vector', lambda e: e.scalar_tensor_tensor(out=ang[:, :], in0=kk[:, :], scalar=pf[:, 3:4], in1=ang[:, :], op0=ALU.mult, op1=ALU.add), [kk, pf, ang], [ang])
        self.range_reduce('vector', ang, kk)
        S.op('scalar', lambda e: e.activation(out=sinT[:, NCTX:NT], in_=ang[:, :], func=AF.Sin), [ang], [sinT])
        S.op('scalar', lambda e: e.activation(out=kk[:, :], in_=ang[:, :], func=AF.Sin, scale=0.5), [ang], [kk])
        S.op('vector', lambda e: e.tensor_tensor(out=kk[:, :], in0=kk[:, :], in1=kk[:, :], op=ALU.mult), [kk], [kk])
        S.op('vector', lambda e: e.tensor_scalar(out=cosT[:, NCTX:NT], in0=kk[:, :], scalar1=-2.0, scalar2=1.0, op0=ALU.mult, op1=ALU.add), [kk], [cosT])
        S.op('gpsimd', lambda e: e.memset(cosT[:, 0:NCTX], 1.0), [], [cosT])
        S.op('gpsimd', lambda e: e.memset(sinT[:, 0:NCTX], 0.0), [], [sinT])
        return cosT, sinT


def make_in_maps(inputs):
    maps = []
    for core in range(8):
        s = core % 4
        m = {'x': np.ascontiguousarray(inputs['x'][s]), 'c': np.ascontiguousarray(inputs['c'][s]),
             'ctx': np.ascontiguousarray(inputs['ctx'][s]), 'c_ctx': np.ascontiguousarray(inputs['c_ctx'])}
        for nm, _ in W_NAMES:
            m[nm] = np.ascontiguousarray(inputs[nm])
        maps.append(m)
    return maps


def kernel(**inputs):
    inputs = {k: np.asarray(v, dtype=np.float32) for k, v in inputs.items()}
    prog = Prog()
    nc = prog.build()
    res = run_bass_kernel_spmd(nc, make_in_maps(inputs), core_ids=list(range(8)))
    out = np.stack([np.asarray(res.results[s]['out'], dtype=np.float32) for s in range(4)], axis=0)
    return out
```

```python
import math
from contextlib import ExitStack, contextmanager
import numpy as np
import concourse.bass as bass
import concourse.mybir as mybir
from concourse.bass_utils import run_bass_kernel_spmd

F32 = mybir.dt.float32
BF16 = mybir.dt.bfloat16
I32 = mybir.dt.int32
AF = mybir.ActivationFunctionType
ALU = mybir.AluOpType
AX = mybir.AxisListType

ENGS = ['tensor', 'vector', 'scalar', 'gpsimd', 'sync']
EPOCH = 30000
DEPOCH = 28800

D = 1024
NCTX = 256
NLAT = 4096
NT = NCTX + NLAT
NTILE = NT // 128
DIN = 9216
DEPTH = 2
NEXP = 16
DEXP = 2816
LN_EPS = 1e-6
RMS_EPS = 1e-5
ALPHA = (2.0 * DEPTH) ** 0.25
TWO_PI = 2.0 * math.pi
CW1 = 6.28125
CW2 = float(np.float32(TWO_PI - CW1))
MAGIC = 12582912.0
PI_LO = 3.1415925


class DSem:
    def __init__(self, sched, name, persistent=False):
        self.sched = sched
        self.name = name
        self.sems = []
        self.persistent = persistent
        sched.dsems.append(self)
        if not persistent and sched.phase_dsems:
            sched.phase_dsems[-1].append(self)

    def next_inc(self):
        if not self.sems or self.sems[-1][1] + 16 > DEPOCH:
            self.sems.append(self.sched.take_sem())
        self.sems[-1][1] += 16
        return self.sems[-1][0]


class Buf:
    def __init__(self, t, name='', grp=None, persistent=False):
        self.persistent = persistent
        self.t = t
        self.name = name
        self.w = None
        self.r = {}
        self.grp = grp

    def __getitem__(self, idx):
        return self.t[idx]


class Sched:
    def __init__(self, nc, es):
        self.nc = nc
        self.es = es
        self.q = {e: [] for e in ENGS}
        self.esems = {e: [] for e in ENGS}
        self.cnt = {e: 0 for e in ENGS}
        self.waited = {e: {} for e in ENGS}
        self.nsem = 0
        self.nins = 0
        self.dsems = []
        self.alloc = es
        self.uid = 0
        self.sem_pool = []
        self.selfwait = True
        self.phase_dsems = []

    def take_sem(self):
        while self.sem_pool:
            ent = self.sem_pool.pop()
            if ent[1] + 16 * 64 <= DEPOCH:
                return ent
        return [self.new_sem(), 0]

    def new_sem(self):
        self.nsem += 1
        return self.es.enter_context(self.nc.semaphore('sm%d' % self.nsem))

    def sbuf(self, shape, dt, name, grp=None):
        self.uid += 1
        nm = '%s_%d' % (name, self.uid)
        t = self.alloc.enter_context(self.nc.sbuf_tensor(nm, list(shape), dt))
        return Buf(t, nm, grp)

    def psum(self, shape, dt, name):
        self.uid += 1
        nm = '%s_%d' % (name, self.uid)
        t = self.alloc.enter_context(self.nc.psum_tensor(nm, list(shape), dt))
        return Buf(t, nm)

    def dram(self, shape, dt, name, grp=None, kind="Internal"):
        t = self.nc.dram_tensor(name, list(shape), dt, kind=kind)
        return Buf(t, name, grp, persistent=True)

    def ring(self, n, shape, dt, name, psum=False):
        bufs = [(self.psum if psum else self.sbuf)(shape, dt, '%s%d' % (name, i)) for i in range(n)]
        st = {'i': 0}

        def nxt():
            b = bufs[st['i'] % n]
            st['i'] += 1
            return b
        return nxt

    def _esem(self, eng, epoch):
        lst = self.esems[eng]
        while len(lst) <= epoch:
            lst.append(self.new_sem())
        return lst[epoch]

    def _collect(self, reads, writes):
        toks = []
        for b in reads:
            if b.w is not None:
                toks.append(b.w)
        for b in writes:
            if b.w is not None:
                toks.append(b.w)
            toks.extend(b.r.values())
        return toks

    def _waits(self, eng, toks):
        need = {}
        for tk in toks:
            if tk[0] == 'e':
                _, pe, ep, seq = tk
                if pe == eng and (eng == 'tensor' or not self.selfwait):
                    continue
                key = ('e', pe, ep)
                if need.get(key, (0,))[0] < seq:
                    need[key] = (seq, self._esem(pe, ep))
            else:
                ds = tk[1]
                for i, (sem, c) in enumerate(ds.sems):
                    key = ('d', id(ds), i)
                    if need.get(key, (0,))[0] < c:
                        need[key] = (c, sem)
        wd = self.waited[eng]
        for key, (val, sem) in need.items():
            if wd.get(key, 0) >= val:
                continue
            wd[key] = val
            self.q[eng].append(lambda e, sem=sem, val=val: e.wait_ge(sem, val))

    def _tok(self, eng):
        c = self.cnt[eng]
        if c == 0:
            return None
        ep, seq = divmod(c - 1, EPOCH)
        return ('e', eng, ep, seq + 1)

    def op(self, eng, fn, reads=(), writes=()):
        self._waits(eng, self._collect(reads, writes))
        self.cnt[eng] += 1
        tok = self._tok(eng)
        sem = self._esem(eng, tok[2])
        self.q[eng].append(lambda e, fn=fn, sem=sem: fn(e).then_inc(sem, 1))
        for b in writes:
            b.w = tok
            b.r = {}
        for b in reads:
            if b.w is not tok:
                b.r[eng] = tok
        self.nins += 1
        return tok

    def dma(self, eng, out_ap, in_ap, dst, src, part=False, **kw):
        if dst.grp is None:
            dst.grp = DSem(self, dst.name, persistent=dst.persistent)
        ds = dst.grp
        if part and dst.w is not None and dst.w[0] == 'd' and dst.w[1] is ds:
            toks = self._collect([src], [])
            toks.extend(dst.r.values())
            self._waits(eng, toks)
        else:
            self._waits(eng, self._collect([src], [dst]))
        sem = ds.next_inc()
        self.q[eng].append(lambda e, sem=sem: e.dma_start(out=out_ap, in_=in_ap, **kw).then_inc(sem, 16))
        tok = ('d', ds)
        dst.w = tok
        dst.r = {}
        src.r[('d', id(ds))] = tok
        self.nins += 1
        return tok

    def wait_buf(self, eng, buf):
        self._waits(eng, self._collect([buf], []))

    def barrier(self):
        toks = [t for t in (self._tok(e) for e in ENGS) if t is not None]
        toks += [('d', ds) for ds in self.dsems]
        for e in ENGS:
            self._waits(e, toks)

    def flush(self):
        if not any(self.q.values()):
            return
        with self.nc.Block() as block:
            for en in ENGS:
                lst = self.q[en]
                if not lst:
                    continue

                def f(e, lst=lst):
                    for fn in lst:
                        fn(e)
                getattr(block, en)(f)
        self.q = {e: [] for e in ENGS}

    @contextmanager
    def phase(self):
        prev = self.alloc
        self.barrier()
        self.phase_dsems.append([])
        with ExitStack() as ph:
            self.alloc = ph
            yield
            self.barrier()
            self.flush()
        for ds in self.phase_dsems.pop():
            if ds.sems:
                self.sem_pool.append(ds.sems[-1])
        self.alloc = prev


def interleave(gens):
    gens = list(gens)
    while gens:
        for g in list(gens):
            try:
                next(g)
            except StopIteration:
                gens.remove(g)


W_NAMES = [('w_ada', [DEPTH, D, 6 * D]), ('b_ada', [DEPTH, 6 * D]), ('w_in', [DEPTH, D, DIN]),
           ('conv_w', [DEPTH, 31, D]), ('conv_b', [DEPTH, D]), ('conv_ln_g', [DEPTH, D]), ('conv_ln_b', [DEPTH, D]),
           ('w_conv_out', [DEPTH, D, D]), ('ssm_a_re', [DEPTH, 2, 64, 64]), ('ssm_a_im', [DEPTH, 2, 64, 64]),
           ('ssm_log_dt', [DEPTH, 2, 64]), ('ssm_b_re', [DEPTH, 2, 64, 64, 16]), ('ssm_b_im', [DEPTH, 2, 64, 64, 16]),
           ('ssm_c_re', [DEPTH, 2, 64, 16, 64]), ('ssm_c_im', [DEPTH, 2, 64, 16, 64]), ('ssm_d', [DEPTH, D]),
           ('w_ssm_glu', [DEPTH, D, D]), ('w_ssm_out', [DEPTH, D, D]), ('attn_lambda', [DEPTH, 4, 64]),
           ('attn_subln_g', [DEPTH, 128]), ('w_attn_out', [DEPTH, D, D]), ('w_o', [DEPTH, D, D]),
           ('ln1_g', [DEPTH, D]), ('ln1_b', [DEPTH, D]), ('w_router', [DEPTH, D, NEXP]),
           ('w_gate_up', [DEPTH, NEXP, D, 2 * DEXP]), ('w_down', [DEPTH, NEXP, DEXP, D]),
           ('ln2_g', [DEPTH, D]), ('ln2_b', [DEPTH, D])]


class Prog:
    def __init__(self, debug=(), stop=None, nlayers=DEPTH, skip_inputs=(), phases=None):
        self.debug = set(debug)
        self.phase_names = phases
        self.stop = stop
        self.nlayers = nlayers
        self.nc = bass.Bass("TRN2", target_bir_lowering=False)
        nc = self.nc
        self.I = {}
        for nm, shp in [('x', [NLAT, D]), ('c', [D]), ('ctx', [NCTX, D]), ('c_ctx', [D])] + W_NAMES:
            if nm in skip_inputs:
                continue
            self.I[nm] = Buf(nc.dram_tensor(nm, shp, F32, kind="ExternalInput"), nm)
        self.out = Buf(nc.dram_tensor("out", [NLAT, D], F32, kind="ExternalOutput"), 'out', persistent=True)
        self.scr = {}

    def scratch(self, name, shape, dt):
        if name not in self.scr:
            kind = "ExternalOutput" if name in self.debug else "Internal"
            self.scr[name] = self.S.dram(shape, dt, name, kind=kind)
        return self.scr[name]

    def build(self):
        nc = self.nc
        with ExitStack() as es:
            self.S = S = Sched(nc, es)
            self.consts()
            xres = self.scratch('xres', [NT, D], F32)
            S.dma('sync', xres[0:NCTX, :], self.I['ctx'][:, :], xres, self.I['ctx'])
            S.dma('sync', xres[NCTX:NT, :], self.I['x'][:, :], xres, self.I['x'])
            done = False
            for l in range(self.nlayers):
                plist = [self.ph_mod, self.ph_inproj, self.ph_convout, self.ph_attn, self.ph_ssm, self.ph_merge, self.ph_moe, self.ph_ln2]
                if self.phase_names is not None:
                    plist = [p for p in plist if p.__name__ in self.phase_names]
                for ph in plist:
                    ph(l)
                    if self.stop == (ph.__name__, l):
                        done = True
                        break
                if done:
                    break
            S.barrier()
            S.flush()
        return nc

    def consts(self):
        S = self.S
        self.ident_f = S.sbuf([128, 128], F32, 'ident_f')
        self.ident_b = S.sbuf([128, 128], BF16, 'ident_b')
        ones = S.sbuf([128, 128], F32, 'ones_f')
        self.ones_f = ones
        idf, idb = self.ident_f, self.ident_b
        S.op('gpsimd', lambda e: e.memset(ones[:, :], 1.0), writes=[ones])
        S.op('gpsimd', lambda e: e.affine_select(out=idf[:, :], in_=ones[:, :], pattern=[[1, 128]], compare_op=ALU.is_equal,
                                                 fill=0.0, base=0, channel_multiplier=-1), reads=[ones], writes=[idf])
        S.op('vector', lambda e: e.tensor_copy(out=idb[:, :], in_=idf[:, :]), reads=[idf], writes=[idb])

    def ph_mod(self, l):
        S = self.S
        I = self.I
        modscr = self.scratch('modscr%d' % l, [2, 128, 6 * D], F32)
        with S.phase():
            cT = S.sbuf([128, 2, 8], F32, 'cT')
            sil = S.sbuf([128, 2, 8], F32, 'sil')
            bc = S.sbuf([128, 2, 8, 128], F32, 'bc')
            bada = S.sbuf([128, 6 * D], F32, 'bada')
            S.dma('sync', cT[:, 0, :], I['c'].t.ap().rearrange("(k p) -> p k", p=128), cT, I['c'], allow_slow_non_contiguous=True)
            S.dma('sync', cT[:, 1, :], I['c_ctx'].t.ap().rearrange("(k p) -> p k", p=128), cT, I['c_ctx'], allow_slow_non_contiguous=True)
            S.dma('sync', bada[:, :], I['b_ada'][l:l + 1, :].partition_broadcast(128), bada, I['b_ada'])
            S.op('scalar', lambda e: e.activation(out=sil[:, :, :], in_=cT[:, :, :], func=AF.Silu), [cT], [sil])
            for w in range(2):
                S.op('vector', lambda e, w=w: e.tensor_copy(out=bc[:, w, :, :], in_=sil[:, w, :].unsqueeze(2).to_broadcast([128, 8, 128])), [sil], [bc])
            wring = S.ring(2, [128, 8, 512], F32, 'wada')
            pring = S.ring(4, [128, 512], F32, 'pmod', psum=True)
            mring = S.ring(4, [128, 512], F32, 'modt')
            wv = I['w_ada'].t.ap()[l].rearrange("(k p) n -> p k n", p=128)
            for nb in range(12):
                n0 = nb * 512
                wa = wring()
                S.dma('sync', wa[:, :, :], wv[:, :, n0:n0 + 512], wa, I['w_ada'])
                for w in range(2):
                    ps = pring()
                    for k in range(8):
                        S.op('tensor', lambda e, ps=ps, wa=wa, w=w, k=k: e.matmul(ps[:, :], lhsT=bc[:, w, k, :], rhs=wa[:, k, :], start=(k == 0), stop=(k == 7)), [bc, wa], [ps])
                    mt = mring()
                    S.op('vector', lambda e, ps=ps, mt=mt, n0=n0: e.tensor_tensor(out=mt[:, :], in0=ps[:, :], in1=bada[:, n0:n0 + 512], op=ALU.add), [ps, bada], [mt])
                    if nb in (2, 3, 8, 9):
                        S.op('vector', lambda e, mt=mt: e.tensor_scalar(out=mt[:, :], in0=mt[:, :], scalar1=1.0, scalar2=None, op0=ALU.add), [mt], [mt])
                    S.dma('gpsimd', modscr[w, :, n0:n0 + 512], mt[:, :], modscr, mt, part=True)

    def ph_inproj(self, l):
        S = self.S
        I = self.I
        modscr = self.scr['modscr%d' % l]
        xres = self.scr['xres']
        gates = self.scratch('gates', [NT, 3 * D], F32)
        qT = self.scratch('qT', [D, NT], BF16)
        kT = self.scratch('kT', [D, NT], BF16)
        uT = self.scratch('uT', [D, NT], F32)
        vtm = self.scratch('vtm', [NT, D], BF16)
        ycT = self.scratch('ycT', [D, NT], F32)
        with S.phase():
            hT = S.sbuf([128, 8, NT], BF16, 'hT')
            with S.phase():
                modv = S.sbuf([128, 2, 2, D], F32, 'modv')
                for w in range(2):
                    S.dma('sync', modv[:, w, 0, :], modscr[w, :, 0:D], modv, modscr)
                    S.dma('sync', modv[:, w, 1, :], modscr[w, :, D:2 * D], modv, modscr)
                xring = S.ring(3, [128, D], F32, 'xt')
                tring = S.ring(2, [128, D], F32, 'tt')
                hring = S.ring(2, [128, D], BF16, 'hb')
                sring = S.ring(2, [128, 16], F32, 'st')
                pring = S.ring(2, [128, 8, 128], BF16, 'pT', psum=True)
                for i in range(NTILE):
                    w = 1 if i < 2 else 0
                    xt = xring()
                    S.dma('sync', xt[:, :], xres[i * 128:(i + 1) * 128, :], xt, xres)
                    st = sring()
                    S.op('vector', lambda e, xt=xt, st=st: e.bn_stats(out=st[:, 0:6], in_=xt[:, 0:512]), [xt], [st])
                    S.op('vector', lambda e, xt=xt, st=st: e.bn_stats(out=st[:, 6:12], in_=xt[:, 512:1024]), [xt], [st])
                    S.op('vector', lambda e, st=st: e.bn_aggr(out=st[:, 12:14], in_=st[:, 0:12]), [st], [st])
                    self.rstd(st, 13, 14, LN_EPS)
                    tt = tring()
                    S.op('vector', lambda e, xt=xt, st=st, tt=tt: e.tensor_scalar(out=tt[:, :], in0=xt[:, :], scalar1=st[:, 12:13], scalar2=st[:, 14:15], op0=ALU.subtract, op1=ALU.mult), [xt, st], [tt])
                    S.op('vector', lambda e, tt=tt, w=w: e.tensor_tensor(out=tt[:, :], in0=tt[:, :], in1=modv[:, w, 1, :], op=ALU.mult), [tt, modv], [tt])
                    hb = hring()
                    S.op('vector', lambda e, tt=tt, hb=hb, w=w: e.tensor_tensor(out=hb[:, :], in0=tt[:, :], in1=modv[:, w, 0, :], op=ALU.add), [tt, modv], [hb])
                    pT = pring()
                    for k in range(8):
                        S.op('tensor', lambda e, pT=pT, hb=hb, k=k: e.transpose(out=pT[:, k, :], in_=hb[:, k * 128:(k + 1) * 128], identity=self.ident_b[:, :]), [hb, self.ident_b], [pT])
                    S.op('scalar', lambda e, pT=pT, i=i: e.copy(out=hT[:, :, i * 128:(i + 1) * 128], in_=pT[:, :, :]), [pT], [hT])
            if 'hT' in self.debug:
                hdbg = self.scratch('hT', [128, 8, NT], BF16)
                S.dma('sync', hdbg[:, :, :], hT[:, :, :], hdbg, hT)
            self.inproj_blocks(l, hT, gates, qT, kT, uT, vtm, ycT)

    def rstd(self, st, ci, co, eps):
        S = self.S
        S.op('vector', lambda e: e.tensor_scalar(out=st[:, co:co + 1], in0=st[:, ci:ci + 1], scalar1=eps, scalar2=None, op0=ALU.add), [st], [st])
        S.op('scalar', lambda e: e.activation(out=st[:, co:co + 1], in_=st[:, co:co + 1], func=AF.Sqrt), [st], [st])
        S.op('vector', lambda e: e.reciprocal(out=st[:, co:co + 1], in_=st[:, co:co + 1]), [st], [st])

    def load_w_bf16(self, dst, wap, src):
        self.S.dma('gpsimd', dst[:, :, :], wap, dst, src)

    def inproj_blocks(self, l, hT, gates, qT, kT, uT, vtm, ycT):
        S = self.S
        I = self.I
        wv = I['w_in'].t.ap()[l].rearrange("(k p) n -> p k n", p=128)
        TG = [(t0, min(t0 + 512, NT)) for t0 in range(0, NT, 512)]

        def mm_fm(ps, wb, cc, t0, t1):
            for k in range(8):
                S.op('tensor', lambda e, k=k: e.matmul(ps[:, 0:t1 - t0], lhsT=wb[:, k, cc * 128:(cc + 1) * 128], rhs=hT[:, k, t0:t1], start=(k == 0), stop=(k == 7)), [wb, hT], [ps])

        def mm_tm(ps, wb, i):
            for k in range(8):
                S.op('tensor', lambda e, k=k: e.matmul(ps[:, :], lhsT=hT[:, k, i * 128:(i + 1) * 128], rhs=wb[:, k, :], start=(k == 0), stop=(k == 7)), [wb, hT], [ps])

        with S.phase():
            wring = S.ring(2, [128, 8, 512], BF16, 'wb')
            pring = S.ring(4, [128, 512], F32, 'pz', psum=True)
            gring = S.ring(3, [128, 512], F32, 'gst')
            vring = S.ring(3, [128, 512], BF16, 'vst')
            for nb in list(range(4, 10)) + [16, 17]:
                wb = wring()
                self.load_w_bf16(wb, wv[:, :, nb * 512:(nb + 1) * 512], I['w_in'])
                for i in range(NTILE):
                    ps = pring()
                    mm_tm(ps, wb, i)
                    if nb < 10:
                        g = gring()
                        S.op('scalar', lambda e, ps=ps, g=g: e.activation(out=g[:, :], in_=ps[:, :], func=AF.Sigmoid), [ps], [g])
                        c0 = (nb - 4) * 512
                        S.dma('sync', gates[i * 128:(i + 1) * 128, c0:c0 + 512], g[:, :], gates, g, part=True)
                    else:
                        v = vring()
                        S.op('vector', lambda e, ps=ps, v=v: e.tensor_copy(out=v[:, :], in_=ps[:, :]), [ps], [v])
                        c0 = (nb - 16) * 512
                        S.dma('sync', vtm[i * 128:(i + 1) * 128, c0:c0 + 512], v[:, :], vtm, v, part=True)
        with S.phase():
            wring = S.ring(2, [128, 8, 512], BF16, 'wb')
            pring = S.ring(4, [128, 512], F32, 'pz', psum=True)
            uring = S.ring(3, [128, 512], F32, 'ust')
            for nb in (12, 13):
                wb = wring()
                self.load_w_bf16(wb, wv[:, :, nb * 512:(nb + 1) * 512], I['w_in'])
                for cc in range(4):
                    r0 = (nb - 12) * 512 + cc * 128
                    for (t0, t1) in TG:
                        ps = pring()
                        mm_fm(ps, wb, cc, t0, t1)
                        u = uring()
                        S.op('scalar', lambda e, ps=ps, u=u, n=t1 - t0: e.copy(out=u[:, 0:n], in_=ps[:, 0:n]), [ps], [u])
                        S.dma('sync', uT[r0:r0 + 128, t0:t1], u[:, 0:t1 - t0], uT, u, part=True)
        with S.phase():
            cosT, sinT = self.rope_tables()
            wring = S.ring(2, [128, 8, 512], BF16, 'wb')
            rring = S.ring(2, [128, 8, 512], BF16, 'wr')
            pring = S.ring(6, [128, 512], F32, 'pz', psum=True)
            t1ring = S.ring(2, [128, 512], F32, 'rt1')
            t2ring = S.ring(2, [128, 512], F32, 'rt2')
            oring = S.ring(3, [128, 512], BF16, 'rout')
            for nb in (10, 11, 14, 15):
                dst = qT if nb < 12 else kT
                base = (nb - 10) * 512 if nb < 12 else (nb - 14) * 512
                wb = wring()
                self.load_w_bf16(wb, wv[:, :, nb * 512:(nb + 1) * 512], I['w_in'])
                wr = rring()
                wb4 = wb.t.ap().rearrange("p k (g h f) -> p (k g) h f", h=2, f=16)
                wr4 = wr.t.ap().rearrange("p k (g h f) -> p (k g) h f", h=2, f=16)
                S.op('vector', lambda e, wb4=wb4, wr4=wr4: e.tensor_scalar(out=wr4[:, :, 0, :], in0=wb4[:, :, 1, :], scalar1=-1.0, scalar2=None, op0=ALU.mult), [wb], [wr])
                S.op('vector', lambda e, wb4=wb4, wr4=wr4: e.tensor_copy(out=wr4[:, :, 1, :], in_=wb4[:, :, 0, :]), [wb], [wr])
                for cc in range(4):
                    r0 = base + cc * 128
                    for (t0, t1) in TG:
                        n = t1 - t0
                        ps = pring()
                        mm_fm(ps, wb, cc, t0, t1)
                        pr = pring()
                        mm_fm(pr, wr, cc, t0, t1)
                        a = t1ring()
                        b = t2ring()
                        S.op('vector', lambda e, ps=ps, a=a, t0=t0, t1=t1, n=n: e.tensor_tensor(out=a[:, 0:n], in0=ps[:, 0:n], in1=cosT[:, t0:t1], op=ALU.mult), [ps, cosT], [a])
                        S.op('vector', lambda e, pr=pr, b=b, t0=t0, t1=t1, n=n: e.tensor_tensor(out=b[:, 0:n], in0=pr[:, 0:n], in1=sinT[:, t0:t1], op=ALU.mult), [pr, sinT], [b])
                        o = oring()
                        S.op('vector', lambda e, a=a, b=b, o=o, n=n: e.tensor_tensor(out=o[:, 0:n], in0=a[:, 0:n], in1=b[:, 0:n], op=ALU.add), [a, b], [o])
                        S.dma('sync', dst[r0:r0 + 128, t0:t1], o[:, 0:n], dst, o, part=True)
        with S.phase():
            wa_r = S.ring(2, [128, 8, 512], BF16, 'wa')
            wb_r = S.ring(2, [128, 8, 512], BF16, 'wbb')
            pring = S.ring(6, [128, 512], F32, 'pz', psum=True)
            sgring = S.ring(2, [128, 512], F32, 'sg')
            GW = 15 + NCTX + 15 + NLAT + 15
            OW = NCTX + 15 + NLAT
            gpad_r = S.ring(2, [128, GW], F32, 'gpad')
            acc_r = S.ring(2, [128, OW], F32, 'cacc')
            cw = S.sbuf([128, 8, 31], F32, 'cw')
            cb = S.sbuf([128, 8], F32, 'cb')
            for j in range(8):
                S.dma('sync', cw[:, j, :], I['conv_w'].t.ap()[l].rearrange("k (j p) -> j p k", p=128)[j], cw, I['conv_w'], allow_slow_non_contiguous=True)
            S.dma('sync', cb[:, :], I['conv_b'].t.ap()[l].rearrange("(j p) -> p j", p=128), cb, I['conv_b'], allow_slow_non_contiguous=True)
            for sb in range(2):
                wa = wa_r()
                wb = wb_r()
                self.load_w_bf16(wa, wv[:, :, sb * 512:(sb + 1) * 512], I['w_in'])
                self.load_w_bf16(wb, wv[:, :, (2 + sb) * 512:(3 + sb) * 512], I['w_in'])
                for cc in range(4):
                    j = sb * 4 + cc
                    gp = gpad_r()
                    S.op('gpsimd', lambda e, gp=gp: e.memset(gp[:, 0:15], 0.0), [], [gp])
                    S.op('gpsimd', lambda e, gp=gp: e.memset(gp[:, 15 + NCTX:30 + NCTX], 0.0), [], [gp])
                    S.op('gpsimd', lambda e, gp=gp: e.memset(gp[:, GW - 15:GW], 0.0), [], [gp])
                    for (t0, t1) in TG:
                        n = t1 - t0
                        pa = pring()
                        mm_fm(pa, wa, cc, t0, t1)
                        pb = pring()
                        mm_fm(pb, wb, cc, t0, t1)
                        sg = sgring()
                        S.op('scalar', lambda e, pb=pb, sg=sg, n=n: e.activation(out=sg[:, 0:n], in_=pb[:, 0:n], func=AF.Sigmoid), [pb], [sg])
                        segs = []
                        if t0 < NCTX:
                            segs.append((t0, min(t1, NCTX), 15))
                        if t1 > NCTX:
                            segs.append((max(t0, NCTX), t1, 30))
                        for (a0, a1, off) in segs:
                            S.op('vector', lambda e, pa=pa, sg=sg, gp=gp, a0=a0, a1=a1, off=off, t0=t0: e.tensor_tensor(out=gp[:, off + a0:off + a1], in0=pa[:, a0 - t0:a1 - t0], in1=sg[:, a0 - t0:a1 - t0], op=ALU.mult), [pa, sg], [gp])
                    acc = acc_r()
                    S.op('vector', lambda e, gp=gp, acc=acc, j=j: e.tensor_scalar(out=acc[:, :], in0=gp[:, 0:OW], scalar1=cw[:, j, 0:1], scalar2=cb[:, j:j + 1], op0=ALU.mult, op1=ALU.add), [gp, cw, cb], [acc])
                    for kk in range(1, 31):
                        S.op('vector', lambda e, gp=gp, acc=acc, j=j, kk=kk: e.scalar_tensor_tensor(out=acc[:, :], in0=gp[:, kk:kk + OW], scalar=cw[:, j, kk:kk + 1], in1=acc[:, :], op0=ALU.mult, op1=ALU.add), [gp, cw, acc], [acc])
                    S.dma('sync', ycT[j * 128:(j + 1) * 128, 0:NCTX], acc[:, 0:NCTX], ycT, acc, part=True)
                    S.dma('sync', ycT[j * 128:(j + 1) * 128, NCTX:NT], acc[:, NCTX + 15:OW], ycT, acc, part=True)


    def ph_convout(self, l):
        S = self.S
        I = self.I
        ycT = self.scratch('ycT', [D, NT], F32)
        gates = self.scratch('gates', [NT, 3 * D], F32)
        y0 = self.scratch('y0', [NT, D], F32)
        ycv = ycT.t.ap().rearrange("(j p) t -> p j t", p=128)
        TG = [(t0, min(t0 + 512, NT)) for t0 in range(0, NT, 512)]
        with S.phase():
            wc = S.sbuf([128, 8, D], BF16, 'wc')
            wcv = I['w_conv_out'].t.ap()[l].rearrange("(k p) n -> p k n", p=128)
            for hh in range(2):
                S.dma('gpsimd', wc[:, :, hh * 512:(hh + 1) * 512], wcv[:, :, hh * 512:(hh + 1) * 512], wc, I['w_conv_out'], part=True)
            gb = S.sbuf([128, 2, 8], F32, 'gb')
            S.dma('sync', gb[:, 0, :], I['conv_ln_g'].t.ap()[l].rearrange("(j p) -> p j", p=128), gb, I['conv_ln_g'], allow_slow_non_contiguous=True)
            S.dma('sync', gb[:, 1, :], I['conv_ln_b'].t.ap()[l].rearrange("(j p) -> p j", p=128), gb, I['conv_ln_b'], allow_slow_non_contiguous=True, part=True)
            ycr = S.ring(2, [128, 8, 512], F32, 'yc')
            sqr = S.ring(1, [128, 8, 512], F32, 'sq')
            ps1r = S.ring(2, [128, 512], F32, 'ps1', psum=True)
            ps2r = S.ring(2, [128, 512], F32, 'ps2', psum=True)
            pyr = S.ring(4, [128, 512], F32, 'py', psum=True)
            mr = S.ring(2, [128, 512], F32, 'mean')
            rr = S.ring(2, [128, 512], F32, 'rstd')
            tr = S.ring(2, [128, 512], F32, 'tn')
            ar = S.ring(2, [128, 8, 512], BF16, 'aT')
            gr = S.ring(2, [128, D], F32, 'g0')
            yr = S.ring(2, [128, D], F32, 'y0t')
            for (t0, t1) in TG:
                n = t1 - t0
                yc = ycr()
                S.dma('sync', yc[:, :, 0:n], ycv[:, :, t0:t1], yc, ycT)
                sq = sqr()
                S.op('scalar', lambda e, yc=yc, sq=sq, n=n: e.activation(out=sq[:, :, 0:n], in_=yc[:, :, 0:n], func=AF.Square), [yc], [sq])
                p1 = ps1r()
                p2 = ps2r()
                for j in range(8):
                    S.op('tensor', lambda e, p1=p1, yc=yc, j=j, n=n: e.matmul(p1[:, 0:n], lhsT=self.ones_f[:, :], rhs=yc[:, j, 0:n], start=(j == 0), stop=(j == 7)), [yc, self.ones_f], [p1])
                for j in range(8):
                    S.op('tensor', lambda e, p2=p2, sq=sq, j=j, n=n: e.matmul(p2[:, 0:n], lhsT=self.ones_f[:, :], rhs=sq[:, j, 0:n], start=(j == 0), stop=(j == 7)), [sq, self.ones_f], [p2])
                mean = mr()
                rstd = rr()
                S.op('scalar', lambda e, p1=p1, mean=mean, n=n: e.mul(out=mean[:, 0:n], in_=p1[:, 0:n], mul=1.0 / D), [p1], [mean])
                S.op('vector', lambda e, mean=mean, rstd=rstd, n=n: e.tensor_tensor(out=rstd[:, 0:n], in0=mean[:, 0:n], in1=mean[:, 0:n], op=ALU.mult), [mean], [rstd])
                S.op('vector', lambda e, p2=p2, rstd=rstd, n=n: e.scalar_tensor_tensor(out=rstd[:, 0:n], in0=p2[:, 0:n], scalar=1.0 / D, in1=rstd[:, 0:n], op0=ALU.mult, op1=ALU.subtract), [p2, rstd], [rstd])
                S.op('vector', lambda e, rstd=rstd, n=n: e.tensor_scalar(out=rstd[:, 0:n], in0=rstd[:, 0:n], scalar1=LN_EPS, scalar2=None, op0=ALU.add), [rstd], [rstd])
                S.op('scalar', lambda e, rstd=rstd, n=n: e.activation(out=rstd[:, 0:n], in_=rstd[:, 0:n], func=AF.Sqrt), [rstd], [rstd])
                S.op('vector', lambda e, rstd=rstd, n=n: e.reciprocal(out=rstd[:, 0:n], in_=rstd[:, 0:n]), [rstd], [rstd])
                aT = ar()
                for j in range(8):
                    tn = tr()
                    S.op('vector', lambda e, yc=yc, mean=mean, tn=tn, j=j, n=n: e.tensor_tensor(out=tn[:, 0:n], in0=yc[:, j, 0:n], in1=mean[:, 0:n], op=ALU.subtract), [yc, mean], [tn])
                    S.op('vector', lambda e, rstd=rstd, tn=tn, n=n: e.tensor_tensor(out=tn[:, 0:n], in0=tn[:, 0:n], in1=rstd[:, 0:n], op=ALU.mult), [tn, rstd], [tn])
                    S.op('scalar', lambda e, tn=tn, aT=aT, j=j, n=n: e.activation(out=aT[:, j, 0:n], in_=tn[:, 0:n], func=AF.Silu, scale=gb[:, 0, j:j + 1], bias=gb[:, 1, j:j + 1]), [tn, gb], [aT])
                for ii in range(n // 128):
                    i = t0 // 128 + ii
                    g0 = gr()
                    S.dma('sync', g0[:, :], gates[i * 128:(i + 1) * 128, 0:D], g0, gates)
                    yt = yr()
                    for nb in range(2):
                        py = pyr()
                        for j in range(8):
                            S.op('tensor', lambda e, py=py, aT=aT, j=j, ii=ii, nb=nb: e.matmul(py[:, :], lhsT=aT[:, j, ii * 128:(ii + 1) * 128], rhs=wc[:, j, nb * 512:(nb + 1) * 512], start=(j == 0), stop=(j == 7)), [aT, wc], [py])
                        S.op('vector', lambda e, py=py, yt=yt, g0=g0, nb=nb: e.tensor_tensor(out=yt[:, nb * 512:(nb + 1) * 512], in0=py[:, :], in1=g0[:, nb * 512:(nb + 1) * 512], op=ALU.mult), [py, g0], [yt])
                    S.dma('sync', y0[i * 128:(i + 1) * 128, :], yt[:, :], y0, yt, part=True)

    def ph_attn(self, l):
        S = self.S
        I = self.I
        qT = self.scratch('qT', [D, NT], BF16)
        kT = self.scratch('kT', [D, NT], BF16)
        vtm = self.scratch('vtm', [NT, D], BF16)
        oT = self.scratch('oT', [D, NT], BF16)
        gates = self.scratch('gates', [NT, 3 * D], F32)
        y2 = self.scratch('y2', [NT, D], F32)
        lam_init = 0.8 - 0.6 * math.exp(-0.3 * l)
        ctx_out = l < DEPTH - 1
        vv = vtm.t.ap().rearrange("(i p) c -> p i c", p=128)
        with S.phase():
            lamt = S.sbuf([128, 4, 64], F32, 'lamt')
            lw = S.sbuf([128, 8], F32, 'lw')
            gsub = S.sbuf([128, 128], F32, 'gsub')
            S.dma('sync', lamt[:, :, :], I['attn_lambda'][l:l + 1, :, :].partition_broadcast(128), lamt, I['attn_lambda'])
            S.dma('sync', gsub[:, :], I['attn_subln_g'][l:l + 1, :].partition_broadcast(128), gsub, I['attn_subln_g'])
            S.op('vector', lambda e: e.tensor_scalar(out=gsub[:, :], in0=gsub[:, :], scalar1=1.0 - lam_init, scalar2=None, op0=ALU.mult), [gsub], [gsub])
            prod = S.sbuf([128, 2, 64], F32, 'lprod')
            S.op('vector', lambda e: e.tensor_tensor(out=prod[:, 0, :], in0=lamt[:, 0, :], in1=lamt[:, 1, :], op=ALU.mult), [lamt], [prod])
            S.op('vector', lambda e: e.tensor_tensor(out=prod[:, 1, :], in0=lamt[:, 2, :], in1=lamt[:, 3, :], op=ALU.mult), [lamt], [prod])
            S.op('vector', lambda e: e.reduce_sum(out=lw[:, 0:2], in_=prod[:, :, :], axis=AX.X), [prod], [lw])
            S.op('scalar', lambda e: e.activation(out=lw[:, 2:4], in_=lw[:, 0:2], func=AF.Exp), [lw], [lw])
            S.op('vector', lambda e: e.tensor_tensor(out=lw[:, 4:5], in0=lw[:, 3:4], in1=lw[:, 2:3], op=ALU.subtract), [lw], [lw])
            S.op('vector', lambda e: e.tensor_scalar(out=lw[:, 5:6], in0=lw[:, 4:5], scalar1=-lam_init, scalar2=None, op0=ALU.add), [lw], [lw])
            neglam = lw
            qr = S.ring(2, [128, NT], BF16, 'qh')
            kzs = [[S.sbuf([128, NT], BF16, 'kz%d_%d' % (b_, s_)) for s_ in range(2)] for b_ in range(2)]
            for b_ in range(2):
                S.op('gpsimd', lambda e, t=kzs[b_][0]: e.memset(t[64:128, :], 0.0), [], [kzs[b_][0]])
                S.op('gpsimd', lambda e, t=kzs[b_][1]: e.memset(t[0:64, :], 0.0), [], [kzs[b_][1]])
            vr = S.ring(2, [128, NTILE, 129], BF16, 'vh')
            psr = S.ring(3, [128, 512], F32, 'psc', psum=True)
            por = S.ring(4, [128, 512], F32, 'pov', psum=True)
            ptr_ = S.ring(1, [128, 128], BF16, 'ptr', psum=True)
            pr = S.ring(4, [128, 512], BF16, 'pT')
            accr = S.ring(8, [128, 128], F32, 'oacc')
            smr = S.ring(8, [128, 8], F32, 'osm')
            onr = S.ring(3, [128, 128], BF16, 'onb')
            otr = S.ring(3, [128, 512], BF16, 'oTst')
            qgroups = []
            if ctx_out:
                qgroups.append((0, NCTX, 2))
            for g in range(8):
                qgroups.append((NCTX + g * 512, NCTX + (g + 1) * 512, NTILE))
            for h in range(8):
                qh = qr()
                kz = kzs[h % 2]
                vh = vr()
                S.dma('sync', qh[:, :], qT[h * 128:(h + 1) * 128, :], qh, qT)
                S.dma('sync', kz[0][0:64, :], kT[h * 128:h * 128 + 64, :], kz[0], kT)
                S.dma('sync', kz[1][64:128, :], kT[h * 128 + 64:(h + 1) * 128, :], kz[1], kT)
                S.dma('sync', vh[:, :, 0:128], vv[:, :, h * 128:(h + 1) * 128], vh, vtm)
                S.op('gpsimd', lambda e, vh=vh: e.memset(vh[:, :, 128:129], 1.0), [], [vh])
                for (q0, q1, nkt) in qgroups:
                    nq = q1 - q0
                    nqt = nq // 128
                    accs = [accr() for _ in range(nqt)]
                    for s_ in range(2):
                        pos = [por() for _ in range(nqt)]
                        def score(kt, s_=s_, q0=q0, q1=q1, nq=nq, kzt=kz[s_], qh=qh):
                            ps = psr()
                            S.op('tensor', lambda e, ps=ps: e.matmul(ps[:, 0:nq], lhsT=kzt[:, kt * 128:(kt + 1) * 128], rhs=qh[:, q0:q1], start=True, stop=True), [kzt, qh], [ps])
                            return ps
                        pend = [score(0)]
                        if nkt > 1:
                            pend.append(score(1))
                        for kt in range(nkt):
                            ps = pend.pop(0)
                            if kt + 2 < nkt:
                                pend.append(score(kt + 2))
                            pT = pr()
                            S.op('scalar', lambda e, ps=ps, pT=pT, nq=nq: e.activation(out=pT[:, 0:nq], in_=ps[:, 0:nq], func=AF.Exp, scale=0.125), [ps], [pT])
                            for qi in range(nqt):
                                S.op('tensor', lambda e, po=pos[qi], pT=pT, vh=vh, kt=kt, qi=qi, nkt=nkt: e.matmul(po[:, 0:129], lhsT=pT[:, qi * 128:(qi + 1) * 128], rhs=vh[:, kt, :], start=(kt == 0), stop=(kt == nkt - 1)), [pT, vh], [pos[qi]])
                        for qi in range(nqt):
                            po = pos[qi]
                            acc = accs[qi]
                            sm = smr()
                            S.op('vector', lambda e, po=po, sm=sm: e.reciprocal(out=sm[:, 0:1], in_=po[:, 128:129]), [po], [sm])
                            if s_ == 0:
                                S.op('vector', lambda e, po=po, sm=sm, acc=acc: e.tensor_scalar(out=acc[:, :], in0=po[:, 0:128], scalar1=sm[:, 0:1], scalar2=None, op0=ALU.mult), [po, sm], [acc])
                            else:
                                S.op('vector', lambda e, sm=sm: e.tensor_tensor(out=sm[:, 1:2], in0=sm[:, 0:1], in1=neglam[:, 5:6], op=ALU.mult), [sm, neglam], [sm])
                                S.op('vector', lambda e, po=po, sm=sm, acc=acc: e.scalar_tensor_tensor(out=acc[:, :], in0=po[:, 0:128], scalar=sm[:, 1:2], in1=acc[:, :], op0=ALU.mult, op1=ALU.add), [po, sm, acc], [acc])
                    ost = otr()
                    for qi in range(nqt):
                        acc = accs[qi]
                        sm = smr()
                        sqs = onr()
                        S.op('scalar', lambda e, acc=acc, sqs=sqs, sm=sm: e.activation(out=sqs[:, :], in_=acc[:, :], func=AF.Square, accum_out=sm[:, 0:1]), [acc], [sqs, sm])
                        S.op('vector', lambda e, sm=sm: e.tensor_scalar(out=sm[:, 1:2], in0=sm[:, 0:1], scalar1=1.0 / 128.0, scalar2=None, op0=ALU.mult), [sm], [sm])
                        self.rstd(sm, 1, 2, RMS_EPS)
                        S.op('vector', lambda e, acc=acc, sm=sm: e.tensor_scalar(out=acc[:, :], in0=acc[:, :], scalar1=sm[:, 2:3], scalar2=None, op0=ALU.mult), [acc, sm], [acc])
                        on = onr()
                        S.op('vector', lambda e, acc=acc, on=on: e.tensor_tensor(out=on[:, :], in0=acc[:, :], in1=gsub[:, :], op=ALU.mult), [acc, gsub], [on])
                        pt = ptr_()
                        S.op('tensor', lambda e, pt=pt, on=on: e.transpose(out=pt[:, :], in_=on[:, :], identity=self.ident_b[:, :]), [on, self.ident_b], [pt])
                        S.op('scalar', lambda e, pt=pt, ost=ost, qi=qi: e.copy(out=ost[:, qi * 128:(qi + 1) * 128], in_=pt[:, :]), [pt], [ost])
                    S.dma('sync', oT[h * 128:(h + 1) * 128, q0:q1], ost[:, 0:nq], oT, ost, part=True)
        with S.phase():
            wo = S.sbuf([128, 8, D], BF16, 'wao')
            wov = I['w_attn_out'].t.ap()[l].rearrange("(k p) n -> p k n", p=128)
            for hh in range(2):
                S.dma('gpsimd', wo[:, :, hh * 512:(hh + 1) * 512], wov[:, :, hh * 512:(hh + 1) * 512], wo, I['w_attn_out'], part=True)
            otr2 = S.ring(2, [128, 8, 512], BF16, 'oTl')
            gr = S.ring(2, [128, D], F32, 'g2')
            yr = S.ring(2, [128, D], F32, 'y2t')
            pyr = S.ring(4, [128, 512], F32, 'py', psum=True)
            otv = oT.t.ap().rearrange("(j p) t -> p j t", p=128)
            tstart = 0 if ctx_out else NCTX
            for t0 in range(tstart, NT, 512):
                t1 = min(t0 + 512, NT)
                n = t1 - t0
                ol = otr2()
                S.dma('sync', ol[:, :, 0:n], otv[:, :, t0:t1], ol, oT)
                for ii in range(n // 128):
                    i = t0 // 128 + ii
                    g2 = gr()
                    S.dma('sync', g2[:, :], gates[i * 128:(i + 1) * 128, 2 * D:3 * D], g2, gates)
                    yt = yr()
                    for nb in range(2):
                        py = pyr()
                        for j in range(8):
                            S.op('tensor', lambda e, py=py, ol=ol, j=j, ii=ii, nb=nb: e.matmul(py[:, :], lhsT=ol[:, j, ii * 128:(ii + 1) * 128], rhs=wo[:, j, nb * 512:(nb + 1) * 512], start=(j == 0), stop=(j == 7)), [ol, wo], [py])
                        S.op('vector', lambda e, py=py, yt=yt, g2=g2, nb=nb: e.tensor_tensor(out=yt[:, nb * 512:(nb + 1) * 512], in0=py[:, :], in1=g2[:, nb * 512:(nb + 1) * 512], op=ALU.mult), [py, g2], [yt])
                    S.dma('sync', y2[i * 128:(i + 1) * 128, :], yt[:, :], y2, yt, part=True)


    def rr_mixed(self, x, k):
        S = self.S
        S.op('gpsimd', lambda e: e.tensor_scalar(out=k[:, :], in0=x[:, :], scalar1=float(1.0 / TWO_PI), scalar2=MAGIC, op0=ALU.mult, op1=ALU.add), [x], [k])
        S.op('gpsimd', lambda e: e.tensor_scalar(out=k[:, :], in0=k[:, :], scalar1=-MAGIC, scalar2=None, op0=ALU.add), [k], [k])
        S.op('vector', lambda e: e.scalar_tensor_tensor(out=x[:, :], in0=k[:, :], scalar=-CW1, in1=x[:, :], op0=ALU.mult, op1=ALU.add), [x, k], [x])
        S.op('vector', lambda e: e.scalar_tensor_tensor(out=x[:, :], in0=k[:, :], scalar=-CW2, in1=x[:, :], op0=ALU.mult, op1=ALU.add), [x, k], [x])
        S.op('gpsimd', lambda e: e.tensor_scalar(out=x[:, :], in0=x[:, :], scalar1=PI_LO, scalar2=-PI_LO, op0=ALU.min, op1=ALU.max), [x], [x])

    def ph_ssm_old(self, l):
        S = self.S
        I = self.I
        uT = self.scratch('uT', [D, NT], F32)
        ygT = self.scratch('ygT', [D, NT], BF16)
        gates = self.scratch('gates', [NT, 3 * D], F32)
        y1 = self.scratch('y1', [NT, D], F32)
        TG = [(t0, min(t0 + 512, NT)) for t0 in range(0, NT, 512)]
        V = lambda fn, r, w: S.op('vector', fn, r, w)
        G = lambda fn, r, w: S.op('gpsimd', fn, r, w)
        A = lambda fn, r, w: S.op('scalar', fn, r, w)
        with S.phase():
            LT = {(v, par): S.sbuf([128, 16, 128], BF16, 'LT%d%d' % (v, par)) for v in (1, 2) for par in range(4)}
            Rf = {v: S.sbuf([128, 16, 128], BF16, 'Rf%d' % v) for v in (1, 2)}
            RHO = S.sbuf([128, 128], F32, 'RHO')
            THR = S.sbuf([128, 128], F32, 'THR')
            dvec = S.sbuf([128, 8], F32, 'dvec')
            S.dma('sync', dvec[:, :], I['ssm_d'].t.ap()[l].rearrange("(j p) -> p j", p=128), dvec, I['ssm_d'], allow_slow_non_contiguous=True)
            with S.phase():
                def T(name):
                    return S.sbuf([128, 128], F32, name)
                AR, AI, DT, TH, SN, CS, ABR, ABI, DEN, CR, CI, SCR, SCI, TMP, TMP2 = [T(n) for n in ['AR', 'AI', 'DT', 'TH', 'SN', 'CS', 'ABR', 'ABI', 'DEN', 'CR', 'CI', 'SCR', 'SCI', 'TMP', 'TMP2']]
                are = I['ssm_a_re'].t.ap()[l].rearrange("d g n -> n (d g)")
                aim = I['ssm_a_im'].t.ap()[l].rearrange("d g n -> n (d g)")
                for hf in range(2):
                    S.dma('sync', AR[hf * 64:(hf + 1) * 64, :], are, AR, I['ssm_a_re'], part=(hf == 1), allow_slow_non_contiguous=True)
                    S.dma('sync', AI[hf * 64:(hf + 1) * 64, :], aim, AI, I['ssm_a_im'], part=(hf == 1), allow_slow_non_contiguous=True)
                S.dma('sync', DT[:, :], I['ssm_log_dt'][l:l + 1, :, :].rearrange("o d g -> o (d g)").partition_broadcast(128), DT, I['ssm_log_dt'])
                A(lambda e: e.activation(out=DT[:, :], in_=DT[:, :], func=AF.Exp), [DT], [DT])
                V(lambda e: e.tensor_tensor(out=TMP[:, :], in0=DT[:, :], in1=AR[:, :], op=ALU.mult), [DT, AR], [TMP])
                V(lambda e: e.tensor_tensor(out=TH[:, :], in0=DT[:, :], in1=AI[:, :], op=ALU.mult), [DT, AI], [TH])
                A(lambda e: e.activation(out=RHO[:, :], in_=TMP[:, :], func=AF.Exp), [TMP], [RHO])
                V(lambda e: e.tensor_copy(out=THR[:, :], in_=TH[:, :]), [TH], [THR])
                self.rr_mixed(THR, TMP2)
                A(lambda e: e.activation(out=SN[:, :], in_=THR[:, :], func=AF.Sin), [THR], [SN])
                A(lambda e: e.activation(out=CS[:, :], in_=THR[:, :], func=AF.Sin, scale=0.5), [THR], [CS])
                V(lambda e: e.tensor_tensor(out=CS[:, :], in0=CS[:, :], in1=CS[:, :], op=ALU.mult), [CS], [CS])
                V(lambda e: e.tensor_scalar(out=CS[:, :], in0=CS[:, :], scalar1=-2.0, scalar2=1.0, op0=ALU.mult, op1=ALU.add), [CS], [CS])
                V(lambda e: e.tensor_tensor(out=ABR[:, :], in0=RHO[:, :], in1=CS[:, :], op=ALU.mult), [RHO, CS], [ABR])
                V(lambda e: e.tensor_scalar(out=ABR[:, :], in0=ABR[:, :], scalar1=-1.0, scalar2=None, op0=ALU.add), [ABR], [ABR])
                V(lambda e: e.tensor_tensor(out=ABI[:, :], in0=RHO[:, :], in1=SN[:, :], op=ALU.mult), [RHO, SN], [ABI])
                V(lambda e: e.tensor_tensor(out=DEN[:, :], in0=AR[:, :], in1=AR[:, :], op=ALU.mult), [AR], [DEN])
                V(lambda e: e.tensor_tensor(out=TMP[:, :], in0=AI[:, :], in1=AI[:, :], op=ALU.mult), [AI], [TMP])
                V(lambda e: e.tensor_tensor(out=DEN[:, :], in0=DEN[:, :], in1=TMP[:, :], op=ALU.add), [DEN, TMP], [DEN])
                V(lambda e: e.reciprocal(out=DEN[:, :], in_=DEN[:, :]), [DEN], [DEN])
                V(lambda e: e.tensor_tensor(out=CR[:, :], in0=ABR[:, :], in1=AR[:, :], op=ALU.mult), [ABR, AR], [CR])
                V(lambda e: e.tensor_tensor(out=TMP[:, :], in0=ABI[:, :], in1=AI[:, :], op=ALU.mult), [ABI, AI], [TMP])
                V(lambda e: e.tensor_tensor(out=CR[:, :], in0=CR[:, :], in1=TMP[:, :], op=ALU.add), [CR, TMP], [CR])
                V(lambda e: e.tensor_tensor(out=CR[:, :], in0=CR[:, :], in1=DEN[:, :], op=ALU.mult), [CR, DEN], [CR])
                V(lambda e: e.tensor_tensor(out=CI[:, :], in0=ABI[:, :], in1=AR[:, :], op=ALU.mult), [ABI, AR], [CI])
                V(lambda e: e.tensor_tensor(out=TMP[:, :], in0=ABR[:, :], in1=AI[:, :], op=ALU.mult), [ABR, AI], [TMP])
                V(lambda e: e.tensor_tensor(out=CI[:, :], in0=CI[:, :], in1=TMP[:, :], op=ALU.subtract), [CI, TMP], [CI])
                V(lambda e: e.tensor_tensor(out=CI[:, :], in0=CI[:, :], in1=DEN[:, :], op=ALU.mult), [CI, DEN], [CI])
                sg = S.sbuf([128, 8], F32, 'sg')
                pidx = S.sbuf([128, 2], I32, 'pidx')
                G(lambda e: e.memset(sg[:, 0:1], 1.0), [], [sg])
                G(lambda e: e.memset(sg[0:64, 0:1], -1.0), [sg], [sg])
                G(lambda e: e.iota(pidx[:, 0:1], pattern=[[0, 1]], base=0, channel_multiplier=1), [], [pidx])
                V(lambda e: e.tensor_scalar(out=pidx[:, 1:2], in0=pidx[:, 0:1], scalar1=4, scalar2=3, op0=ALU.logical_shift_right, op1=ALU.bitwise_and), [pidx], [pidx])
                V(lambda e: e.tensor_copy(out=sg[:, 1:2], in_=pidx[:, 1:2]), [pidx], [sg])
                for m_ in range(4):
                    V(lambda e, m_=m_: e.tensor_scalar(out=sg[:, 4 + m_:5 + m_], in0=sg[:, 1:2], scalar1=float(m_), scalar2=None, op0=ALU.is_equal), [sg], [sg])
                V(lambda e: e.tensor_scalar(out=SCR[:, :], in0=CR[:, :], scalar1=sg[:, 0:1], scalar2=None, op0=ALU.mult), [CR, sg], [SCR])
                V(lambda e: e.tensor_scalar(out=SCI[:, :], in0=CI[:, :], scalar1=sg[:, 0:1], scalar2=None, op0=ALU.mult), [CI, sg], [SCI])
                BA = S.sbuf([128, 128, 16], F32, 'BA')
                BB = S.sbuf([128, 128, 16], F32, 'BB')
                X1 = S.sbuf([128, 128, 16], F32, 'X1')
                X2 = S.sbuf([128, 128, 16], F32, 'X2')
                XT = S.sbuf([128, 128, 16], F32, 'XT')
                bre = I['ssm_b_re'].t.ap()[l].rearrange("d g n c -> n (d g) c")
                bim = I['ssm_b_im'].t.ap()[l].rearrange("d g n c -> n (d g) c")
                S.dma('sync', BA[0:64, :, :], bre, BA, I['ssm_b_re'])
                S.dma('sync', BA[64:128, :, :], bim, BA, I['ssm_b_im'], part=True)
                S.dma('sync', BB[0:64, :, :], bim, BB, I['ssm_b_im'])
                S.dma('sync', BB[64:128, :, :], bre, BB, I['ssm_b_re'], part=True)

                def bc(t):
                    return t[:, :].unsqueeze(2).to_broadcast([128, 128, 16])
                V(lambda e: e.tensor_tensor(out=X1[:, :, :], in0=BA[:, :, :], in1=bc(CR), op=ALU.mult), [BA, CR], [X1])
                V(lambda e: e.tensor_tensor(out=XT[:, :, :], in0=BB[:, :, :], in1=bc(SCI), op=ALU.mult), [BB, SCI], [XT])
                V(lambda e: e.tensor_tensor(out=X1[:, :, :], in0=X1[:, :, :], in1=XT[:, :, :], op=ALU.add), [X1, XT], [X1])
                V(lambda e: e.tensor_tensor(out=X2[:, :, :], in0=BA[:, :, :], in1=bc(CI), op=ALU.mult), [BA, CI], [X2])
                V(lambda e: e.tensor_tensor(out=XT[:, :, :], in0=BB[:, :, :], in1=bc(SCR), op=ALU.mult), [BB, SCR], [XT])
                V(lambda e: e.tensor_tensor(out=X2[:, :, :], in0=X2[:, :, :], in1=XT[:, :, :], op=ALU.subtract), [X2, XT], [X2])
                ptr_ = S.ring(3, [128, 128], F32, 'ptp', psum=True)
                for v, X in ((1, X1), (2, X2)):
                    for col in range(16):
                        pt = ptr_()
                        S.op('tensor', lambda e, pt=pt, X=X, col=col: e.transpose(out=pt[:, :], in_=X[:, col * 8:(col + 1) * 8, :], identity=self.ident_f[:, :]), [X, self.ident_f], [pt])
                        for m_ in range(4):
                            V(lambda e, pt=pt, v=v, col=col, m_=m_: e.tensor_scalar(out=LT[(v, m_)][:, col, :], in0=pt[:, :], scalar1=sg[:, 4 + m_:5 + m_], scalar2=None, op0=ALU.mult), [pt, sg], [LT[(v, m_)]])
                M1 = S.sbuf([128, 16, 128], F32, 'M1')
                M2 = S.sbuf([128, 16, 128], F32, 'M2')
                cre = I['ssm_c_re'].t.ap()[l].rearrange("d (b g) c n -> (g c) (d b) n", g=8)
                cim = I['ssm_c_im'].t.ap()[l].rearrange("d (b g) c n -> (g c) (d b) n", g=8)
                S.dma('sync', M1[:, :, 0:64], cre, M1, I['ssm_c_re'])
                S.dma('sync', M1[:, :, 64:128], cim, M1, I['ssm_c_im'], part=True)
                V(lambda e: e.tensor_scalar(out=M2[:, :, 64:128], in0=M1[:, :, 0:64], scalar1=-1.0, scalar2=None, op0=ALU.mult), [M1], [M2])
                V(lambda e: e.tensor_scalar(out=M2[:, :, 0:64], in0=M1[:, :, 64:128], scalar1=-1.0, scalar2=None, op0=ALU.mult), [M1], [M2])
                V(lambda e: e.tensor_scalar(out=M1[:, :, 64:128], in0=M1[:, :, 64:128], scalar1=-1.0, scalar2=None, op0=ALU.mult), [M1, M2], [M1])
                for v, M in ((1, M1), (2, M2)):
                    for col in range(16):
                        pt = ptr_()
                        S.op('tensor', lambda e, pt=pt, M=M, col=col: e.transpose(out=pt[:, :], in_=M[:, col, :], identity=self.ident_f[:, :]), [M, self.ident_f], [pt])
                        A(lambda e, pt=pt, v=v, col=col: e.copy(out=Rf[v][:, col, :], in_=pt[:, :]), [pt], [Rf[v]])
            with S.phase():
                pidxs = [S.sbuf([128, NT], F32, 'pidx%d' % d_) for d_ in range(2)]
                with S.phase():
                    tmpi = S.sbuf([128, NT], I32, 'tmpi')
                    for d_ in range(2):
                        pf = pidxs[d_]
                        if d_ == 0:
                            G(lambda e: e.iota(tmpi[:, :], pattern=[[1, NT]], base=0, channel_multiplier=0), [], [tmpi])
                        else:
                            G(lambda e: e.iota(tmpi[:, 0:NCTX], pattern=[[-1, NCTX]], base=NCTX - 1, channel_multiplier=0), [], [tmpi])
                            G(lambda e: e.iota(tmpi[:, NCTX:NT], pattern=[[-1, NLAT]], base=NT - 1, channel_multiplier=0), [tmpi], [tmpi])
                        V(lambda e, pf=pf: e.tensor_copy(out=pf[:, :], in_=tmpi[:, :]), [tmpi], [pf])
                bufA = tmpi_f = S.sbuf([128, NT], F32, 'bufA')
                bufB = S.sbuf([128, NT], F32, 'bufB')
                cosT = S.sbuf([128, NT], F32, 'cosS')
                sinT = S.sbuf([128, NT], F32, 'sinS')
                D1 = S.sbuf([128, NT], BF16, 'D1')
                D2 = S.sbuf([128, NT], BF16, 'D2')
                Yacc = S.sbuf([128, NT], F32, 'Yacc')
                ub = S.sbuf([128, NT], BF16, 'ub')
                t1r = S.ring(2, [128, 512], F32, 'st1')
                t2r = S.ring(2, [128, 512], F32, 'st2')
                rz1r = S.ring(2, [128, 128], BF16, 'rz1')
                rz2r = S.ring(2, [128, 128], BF16, 'rz2')
                p1r = S.ring(2, [128, 512], F32, 'pp1', psum=True)
                p2r = S.ring(2, [128, 512], F32, 'pp2', psum=True)
                pyr = S.ring(2, [128, 512], F32, 'ppy', psum=True)
                for blk in range(8):
                    S.dma('gpsimd', ub[:, :], uT[blk * 128:(blk + 1) * 128, :], ub, uT)
                    S.dma('sync', bufA[:, :], uT[blk * 128:(blk + 1) * 128, :], bufA, uT)
                    V(lambda e, blk=blk: e.tensor_scalar(out=Yacc[:, :], in0=bufA[:, :], scalar1=dvec[:, blk:blk + 1], scalar2=None, op0=ALU.mult), [bufA, dvec], [Yacc])
                    for d_ in range(2):
                        for g8 in range(8):
                            dg = d_ * 64 + blk * 8 + g8
                            col = d_ * 8 + blk
                            pr, par = g8 // 4, g8 % 4
                            pf = pidxs[d_]
                            G(lambda e, pf=pf, dg=dg: e.tensor_scalar(out=bufA[:, :], in0=pf[:, :], scalar1=THR[:, dg:dg + 1], scalar2=None, op0=ALU.mult), [pf, THR], [bufA])
                            self.rr_mixed(bufA, bufB)
                            A(lambda e: e.activation(out=sinT[:, :], in_=bufA[:, :], func=AF.Sin), [bufA], [sinT])
                            A(lambda e: e.activation(out=cosT[:, :], in_=bufA[:, :], func=AF.Sin, scale=0.5), [bufA], [cosT])
                            G(lambda e: e.tensor_tensor(out=cosT[:, :], in0=cosT[:, :], in1=cosT[:, :], op=ALU.mult), [cosT], [cosT])
                            G(lambda e: e.tensor_scalar(out=cosT[:, :], in0=cosT[:, :], scalar1=-2.0, scalar2=1.0, op0=ALU.mult, op1=ALU.add), [cosT], [cosT])
                            rz1, rz2 = rz1r(), rz2r()
                            for rz, v in ((rz1, 1), (rz2, 2)):
                                G(lambda e, rz=rz: e.memset(rz[:, :], 0.0), [], [rz])
                                G(lambda e, rz=rz, v=v, col=col, g8=g8: e.tensor_copy(out=rz[:, g8 * 16:(g8 + 1) * 16], in_=Rf[v][:, col, g8 * 16:(g8 + 1) * 16]), [Rf[v]], [rz])
                            l1, l2 = LT[(1, par)], LT[(2, par)]
                            for (t0, t1) in TG:
                                n = t1 - t0
                                p1, p2 = p1r(), p2r()
                                S.op('tensor', lambda e, p1=p1, l1=l1, pr=pr, col=col, t0=t0, t1=t1, n=n: e.matmul(p1[:, 0:n], lhsT=l1[64 * pr:64 * pr + 64, col, :], rhs=ub[64 * pr:64 * pr + 64, t0:t1], start=True, stop=True), [l1, ub], [p1])
                                S.op('tensor', lambda e, p2=p2, l2=l2, pr=pr, col=col, t0=t0, t1=t1, n=n: e.matmul(p2[:, 0:n], lhsT=l2[64 * pr:64 * pr + 64, col, :], rhs=ub[64 * pr:64 * pr + 64, t0:t1], start=True, stop=True), [l2, ub], [p2])
                                a, b = t1r(), t2r()
                                V(lambda e, p1=p1, a=a, t0=t0, t1=t1, n=n: e.tensor_tensor(out=a[:, 0:n], in0=p1[:, 0:n], in1=cosT[:, t0:t1], op=ALU.mult), [p1, cosT], [a])
                                V(lambda e, p2=p2, b=b, t0=t0, t1=t1, n=n: e.tensor_tensor(out=b[:, 0:n], in0=p2[:, 0:n], in1=sinT[:, t0:t1], op=ALU.mult), [p2, sinT], [b])
                                G(lambda e, a=a, b=b, t0=t0, t1=t1, n=n: e.tensor_tensor(out=bufB[:, t0:t1], in0=a[:, 0:n], in1=b[:, 0:n], op=ALU.add), [a, b], [bufB])
                            rho_b = RHO[:, dg:dg + 1]
                            if d_ == 0:
                                V(lambda e, rho_b=rho_b: e.tensor_tensor_scan(out=bufA[:, :], data0=rho_b.to_broadcast([128, NT]), data1=bufB[:, :], initial=0.0, op0=ALU.mult, op1=ALU.add), [bufB, RHO], [bufA])
                            else:
                                V(lambda e, rho_b=rho_b: e.tensor_tensor_scan(out=bufA[:, 0:NCTX][:, ::-1], data0=rho_b.to_broadcast([128, NCTX]), data1=bufB[:, 0:NCTX][:, ::-1], initial=0.0, op0=ALU.mult, op1=ALU.add), [bufB, RHO], [bufA])
                                V(lambda e, rho_b=rho_b: e.tensor_tensor_scan(out=bufA[:, NCTX:NT][:, ::-1], data0=rho_b.to_broadcast([128, NLAT]), data1=bufB[:, NCTX:NT][:, ::-1], initial=bufA[:, 0:1], op0=ALU.mult, op1=ALU.add), [bufB, RHO, bufA], [bufA])
                            V(lambda e: e.tensor_tensor(out=D1[:, :], in0=cosT[:, :], in1=bufA[:, :], op=ALU.mult), [cosT, bufA], [D1])
                            G(lambda e: e.tensor_tensor(out=D2[:, :], in0=sinT[:, :], in1=bufA[:, :], op=ALU.mult), [sinT, bufA], [D2])
                            for (t0, t1) in TG:
                                n = t1 - t0
                                py = pyr()
                                S.op('tensor', lambda e, py=py, rz1=rz1, t0=t0, t1=t1, n=n: e.matmul(py[:, 0:n], lhsT=rz1[:, :], rhs=D1[:, t0:t1], start=True, stop=False), [rz1, D1], [py])
                                S.op('tensor', lambda e, py=py, rz2=rz2, t0=t0, t1=t1, n=n: e.matmul(py[:, 0:n], lhsT=rz2[:, :], rhs=D2[:, t0:t1], start=False, stop=True), [rz2, D2], [py])
                                V(lambda e, py=py, t0=t0, t1=t1, n=n: e.tensor_tensor(out=Yacc[:, t0:t1], in0=py[:, 0:n], in1=Yacc[:, t0:t1], op=ALU.add), [py, Yacc], [Yacc])
                    if 'yscanT' in self.debug:
                        ysd = self.scratch('yscanT', [D, NT], F32)
                        S.dma('sync', bufA[:, :], uT[blk * 128:(blk + 1) * 128, :], bufA, uT)
                        V(lambda e, blk=blk: e.tensor_scalar(out=bufA[:, :], in0=bufA[:, :], scalar1=dvec[:, blk:blk + 1], scalar2=None, op0=ALU.mult), [bufA, dvec], [bufA])
                        V(lambda e: e.tensor_tensor(out=bufA[:, :], in0=Yacc[:, :], in1=bufA[:, :], op=ALU.subtract), [Yacc, bufA], [bufA])
                        S.dma('sync', ysd[blk * 128:(blk + 1) * 128, :], bufA[:, :], ysd, bufA, part=True)
                    G(lambda e: e.tensor_tensor(out=bufB[:, :], in0=Yacc[:, :], in1=Yacc[:, :], op=ALU.mult), [Yacc], [bufB])
                    G(lambda e: e.tensor_scalar(out=bufB[:, :], in0=bufB[:, :], scalar1=0.044715, scalar2=1.0, op0=ALU.mult, op1=ALU.add), [bufB], [bufB])
                    V(lambda e: e.tensor_tensor(out=bufB[:, :], in0=bufB[:, :], in1=Yacc[:, :], op=ALU.mult), [bufB, Yacc], [bufB])
                    A(lambda e: e.activation(out=bufB[:, :], in_=bufB[:, :], func=AF.Sigmoid, scale=2.0 * math.sqrt(2.0 / math.pi)), [bufB], [bufB])
                    V(lambda e: e.tensor_tensor(out=D1[:, :], in0=bufB[:, :], in1=Yacc[:, :], op=ALU.mult), [bufB, Yacc], [D1])
                    S.dma('sync', ygT[blk * 128:(blk + 1) * 128, :], D1[:, :], ygT, D1, part=True)
        self.ssm_glu_out(l)


    def rr_ap(self, x, k, xa, ka):
        S = self.S
        S.op('gpsimd', lambda e: e.tensor_scalar(out=ka, in0=xa, scalar1=float(1.0 / TWO_PI), scalar2=MAGIC, op0=ALU.mult, op1=ALU.add), [x], [k])
        S.op('gpsimd', lambda e: e.tensor_scalar(out=ka, in0=ka, scalar1=-MAGIC, scalar2=None, op0=ALU.add), [k], [k])
        S.op('vector', lambda e: e.scalar_tensor_tensor(out=xa, in0=ka, scalar=-CW1, in1=xa, op0=ALU.mult, op1=ALU.add), [x, k], [x])
        S.op('vector', lambda e: e.scalar_tensor_tensor(out=xa, in0=ka, scalar=-CW2, in1=xa, op0=ALU.mult, op1=ALU.add), [x, k], [x])
        S.op('gpsimd', lambda e: e.tensor_scalar(out=xa, in0=xa, scalar1=PI_LO, scalar2=-PI_LO, op0=ALU.min, op1=ALU.max), [x], [x])

    def rr_gen(self, x, k, xa, ka, eng='gpsimd'):
        S = self.S
        S.op(eng, lambda e: e.tensor_scalar(out=ka, in0=xa, scalar1=float(1.0 / TWO_PI), scalar2=MAGIC, op0=ALU.mult, op1=ALU.add), [x], [k])
        yield
        S.op(eng, lambda e: e.tensor_scalar(out=ka, in0=ka, scalar1=-MAGIC, scalar2=None, op0=ALU.add), [k], [k])
        yield
        S.op('vector', lambda e: e.scalar_tensor_tensor(out=xa, in0=ka, scalar=-CW1, in1=xa, op0=ALU.mult, op1=ALU.add), [x, k], [x])
        yield
        S.op('vector', lambda e: e.scalar_tensor_tensor(out=xa, in0=ka, scalar=-CW2, in1=xa, op0=ALU.mult, op1=ALU.add), [x, k], [x])
        yield
        S.op(eng, lambda e: e.tensor_scalar(out=xa, in0=xa, scalar1=PI_LO, scalar2=-PI_LO, op0=ALU.min, op1=ALU.max), [x], [x])
        yield

    def ph_ssm(self, l):
        S = self.S
        I = self.I
        uT = self.scratch('uT', [D, NT], F32)
        ygT = self.scratch('ygT', [D, NT], BF16)
        V = lambda fn, r, w: S.op('vector', fn, r, w)
        G = lambda fn, r, w: S.op('gpsimd', fn, r, w)
        A = lambda fn, r, w: S.op('scalar', fn, r, w)
        P = lambda fn, r, w: S.op('tensor', fn, r, w)
        NC = NT // 8
        NCC = NCTX // 8
        HC = [(0, NC // 2), (NC // 2, NC)]
        HW = NC // 2
        with S.phase():
            dvec = S.sbuf([128, 8], F32, 'dvec')
            S.dma('sync', dvec[:, :], I['ssm_d'].t.ap()[l].rearrange("(j p) -> p j", p=128), dvec, I['ssm_d'], allow_slow_non_contiguous=True)
            PWr = S.sbuf([128, 9, 128], F32, 'PWr')
            PWi = S.sbuf([128, 9, 128], F32, 'PWi')
            CRe, CIe = [S.sbuf([128, 8, 128], F32, n) for n in ('CRe', 'CIe')]
            T1 = S.sbuf([128, 16, 8, 16], F32, 'T1')
            T2 = S.sbuf([128, 16, 8, 16], F32, 'T2')
            sg = S.sbuf([128, 8], F32, 'sg')
            blkmask = S.sbuf([128, 128], F32, 'blkmask')
            TH8 = S.sbuf([128, 128], F32, 'TH8')
            RHO8 = S.sbuf([128, 128], F32, 'RHO8')
            with S.phase():
                def T(name, w=128):
                    return S.sbuf([128, w], F32, name)
                AR, AI, DT, ARD, TH, THR, SN, CS, ABR, ABI, DEN, CR, CI, TMP, TMP2, RHO = [T(n) for n in ['AR', 'AI', 'DT', 'ARD', 'TH', 'THR', 'SN', 'CS', 'ABR', 'ABI', 'DEN', 'CR', 'CI', 'TMP', 'TMP2', 'RHO']]
                are = I['ssm_a_re'].t.ap()[l].rearrange("d g n -> n (d g)")
                aim = I['ssm_a_im'].t.ap()[l].rearrange("d g n -> n (d g)")
                for hf in range(2):
                    S.dma('sync', AR[hf * 64:(hf + 1) * 64, :], are, AR, I['ssm_a_re'], part=(hf == 1), allow_slow_non_contiguous=True)
                    S.dma('sync', AI[hf * 64:(hf + 1) * 64, :], aim, AI, I['ssm_a_im'], part=(hf == 1), allow_slow_non_contiguous=True)
                S.dma('sync', DT[:, :], I['ssm_log_dt'][l:l + 1, :, :].rearrange("o d g -> o (d g)").partition_broadcast(128), DT, I['ssm_log_dt'])
                A(lambda e: e.activation(out=DT[:, :], in_=DT[:, :], func=AF.Exp), [DT], [DT])
                V(lambda e: e.tensor_tensor(out=ARD[:, :], in0=DT[:, :], in1=AR[:, :], op=ALU.mult), [DT, AR], [ARD])
                V(lambda e: e.tensor_tensor(out=TH[:, :], in0=DT[:, :], in1=AI[:, :], op=ALU.mult), [DT, AI], [TH])
                A(lambda e: e.activation(out=RHO[:, :], in_=ARD[:, :], func=AF.Exp), [ARD], [RHO])
                V(lambda e: e.tensor_copy(out=THR[:, :], in_=TH[:, :]), [TH], [THR])
                self.rr_ap(THR, TMP2, THR[:, :], TMP2[:, :])
                A(lambda e: e.activation(out=SN[:, :], in_=THR[:, :], func=AF.Sin), [THR], [SN])
                A(lambda e: e.activation(out=CS[:, :], in_=THR[:, :], func=AF.Sin, scale=0.5), [THR], [CS])
                V(lambda e: e.tensor_tensor(out=CS[:, :], in0=CS[:, :], in1=CS[:, :], op=ALU.mult), [CS], [CS])
                V(lambda e: e.tensor_scalar(out=CS[:, :], in0=CS[:, :], scalar1=-2.0, scalar2=1.0, op0=ALU.mult, op1=ALU.add), [CS], [CS])
                V(lambda e: e.tensor_tensor(out=ABR[:, :], in0=RHO[:, :], in1=CS[:, :], op=ALU.mult), [RHO, CS], [ABR])
                V(lambda e: e.tensor_scalar(out=ABR[:, :], in0=ABR[:, :], scalar1=-1.0, scalar2=None, op0=ALU.add), [ABR], [ABR])
                V(lambda e: e.tensor_tensor(out=ABI[:, :], in0=RHO[:, :], in1=SN[:, :], op=ALU.mult), [RHO, SN], [ABI])
                V(lambda e: e.tensor_tensor(out=DEN[:, :], in0=AR[:, :], in1=AR[:, :], op=ALU.mult), [AR], [DEN])
                V(lambda e: e.tensor_tensor(out=TMP[:, :], in0=AI[:, :], in1=AI[:, :], op=ALU.mult), [AI], [TMP])
                V(lambda e: e.tensor_tensor(out=DEN[:, :], in0=DEN[:, :], in1=TMP[:, :], op=ALU.add), [DEN, TMP], [DEN])
                V(lambda e: e.reciprocal(out=DEN[:, :], in_=DEN[:, :]), [DEN], [DEN])
                V(lambda e: e.tensor_tensor(out=CR[:, :], in0=ABR[:, :], in1=AR[:, :], op=ALU.mult), [ABR, AR], [CR])
                V(lambda e: e.tensor_tensor(out=TMP[:, :], in0=ABI[:, :], in1=AI[:, :], op=ALU.mult), [ABI, AI], [TMP])
                V(lambda e: e.tensor_tensor(out=CR[:, :], in0=CR[:, :], in1=TMP[:, :], op=ALU.add), [CR, TMP], [CR])
                V(lambda e: e.tensor_tensor(out=CR[:, :], in0=CR[:, :], in1=DEN[:, :], op=ALU.mult), [CR, DEN], [CR])
                V(lambda e: e.tensor_tensor(out=CI[:, :], in0=ABI[:, :], in1=AR[:, :], op=ALU.mult), [ABI, AR], [CI])
                V(lambda e: e.tensor_tensor(out=TMP[:, :], in0=ABR[:, :], in1=AI[:, :], op=ALU.mult), [ABR, AI], [TMP])
                V(lambda e: e.tensor_tensor(out=CI[:, :], in0=CI[:, :], in1=TMP[:, :], op=ALU.subtract), [CI, TMP], [CI])
                V(lambda e: e.tensor_tensor(out=CI[:, :], in0=CI[:, :], in1=DEN[:, :], op=ALU.mult), [CI, DEN], [CI])
                pidx = S.sbuf([128, 4], I32, 'pidx')
                G(lambda e: e.memset(sg[:, 0:1], 1.0), [], [sg])
                G(lambda e: e.memset(sg[0:64, 0:1], -1.0), [sg], [sg])
                V(lambda e: e.tensor_scalar(out=sg[:, 2:3], in0=sg[:, 0:1], scalar1=-1.0, scalar2=None, op0=ALU.mult), [sg], [sg])
                G(lambda e: e.iota(pidx[:, 0:1], pattern=[[0, 1]], base=0, channel_multiplier=1), [], [pidx])
                V(lambda e: e.tensor_scalar(out=pidx[:, 1:2], in0=pidx[:, 0:1], scalar1=4, scalar2=3, op0=ALU.logical_shift_right, op1=ALU.bitwise_and), [pidx], [pidx])
                V(lambda e: e.tensor_scalar(out=pidx[:, 2:3], in0=pidx[:, 0:1], scalar1=4, scalar2=None, op0=ALU.logical_shift_right), [pidx], [pidx])
                V(lambda e: e.tensor_copy(out=sg[:, 1:2], in_=pidx[:, 1:2]), [pidx], [sg])
                V(lambda e: e.tensor_copy(out=sg[:, 3:4], in_=pidx[:, 2:3]), [pidx], [sg])
                for m_ in range(4):
                    V(lambda e, m_=m_: e.tensor_scalar(out=sg[:, 4 + m_:5 + m_], in0=sg[:, 1:2], scalar1=float(m_), scalar2=None, op0=ALU.is_equal), [sg], [sg])
                iq = S.sbuf([128, 128], I32, 'iq')
                G(lambda e: e.iota(iq[:, :], pattern=[[1, 128]], base=0, channel_multiplier=0), [], [iq])
                V(lambda e: e.tensor_scalar(out=iq[:, :], in0=iq[:, :], scalar1=4, scalar2=None, op0=ALU.logical_shift_right), [iq], [iq])
                V(lambda e: e.tensor_copy(out=blkmask[:, :], in_=iq[:, :]), [iq], [blkmask])
                V(lambda e: e.tensor_scalar(out=blkmask[:, :], in0=blkmask[:, :], scalar1=sg[:, 3:4], scalar2=None, op0=ALU.is_equal), [blkmask, sg], [blkmask])
                ANG, KK, SNm, CSm, RHm = [T(n, 9 * 128) for n in ('ANG', 'KKp', 'SNm', 'CSm', 'RHm')]
                def pw_chain(m_):
                    V(lambda e, m_=m_: e.tensor_scalar(out=ANG[:, m_ * 128:(m_ + 1) * 128], in0=THR[:, :], scalar1=float(m_), scalar2=None, op0=ALU.mult), [THR], [ANG])
                    yield
                    V(lambda e, m_=m_: e.tensor_scalar(out=RHm[:, m_ * 128:(m_ + 1) * 128], in0=ARD[:, :], scalar1=float(m_), scalar2=None, op0=ALU.mult), [ARD], [RHm])
                    yield
                interleave([pw_chain(m_) for m_ in range(9)])
                self.rr_ap(ANG, KK, ANG[:, :], KK[:, :])
                A(lambda e: e.activation(out=RHm[:, :], in_=RHm[:, :], func=AF.Exp), [RHm], [RHm])
                A(lambda e: e.activation(out=SNm[:, :], in_=ANG[:, :], func=AF.Sin), [ANG], [SNm])
                A(lambda e: e.activation(out=CSm[:, :], in_=ANG[:, :], func=AF.Sin, scale=0.5), [ANG], [CSm])
                V(lambda e: e.tensor_tensor(out=CSm[:, :], in0=CSm[:, :], in1=CSm[:, :], op=ALU.mult), [CSm], [CSm])
                V(lambda e: e.tensor_scalar(out=CSm[:, :], in0=CSm[:, :], scalar1=-2.0, scalar2=1.0, op0=ALU.mult, op1=ALU.add), [CSm], [CSm])
                pwr2 = PWr.t.ap().rearrange("p m g -> p (m g)")
                pwi2 = PWi.t.ap().rearrange("p m g -> p (m g)")
                V(lambda e: e.tensor_tensor(out=pwr2, in0=RHm[:, :], in1=CSm[:, :], op=ALU.mult), [RHm, CSm], [PWr])
                V(lambda e: e.tensor_tensor(out=pwi2, in0=RHm[:, :], in1=SNm[:, :], op=ALU.mult), [RHm, SNm], [PWi])
                V(lambda e: e.tensor_copy(out=TH8[:, :], in_=ANG[:, 8 * 128:9 * 128]), [ANG], [TH8])
                V(lambda e: e.tensor_copy(out=RHO8[:, :], in_=RHm[:, 8 * 128:9 * 128]), [RHm], [RHO8])
                TM8 = S.sbuf([128, 8, 128], F32, 'TM8')

                def b8(t):
                    return t[:, :].unsqueeze(1).to_broadcast([128, 8, 128])
                V(lambda e: e.tensor_tensor(out=CRe[:, :, :], in0=PWr[:, 0:8, :], in1=b8(CR), op=ALU.mult), [PWr, CR], [CRe])
                V(lambda e: e.tensor_tensor(out=TM8[:, :, :], in0=PWi[:, 0:8, :], in1=b8(CI), op=ALU.mult), [PWi, CI], [TM8])
                V(lambda e: e.tensor_tensor(out=CRe[:, :, :], in0=CRe[:, :, :], in1=TM8[:, :, :], op=ALU.subtract), [CRe, TM8], [CRe])
                V(lambda e: e.tensor_tensor(out=CIe[:, :, :], in0=PWi[:, 0:8, :], in1=b8(CR), op=ALU.mult), [PWi, CR], [CIe])
                V(lambda e: e.tensor_tensor(out=TM8[:, :, :], in0=PWr[:, 0:8, :], in1=b8(CI), op=ALU.mult), [PWr, CI], [TM8])
                V(lambda e: e.tensor_tensor(out=CIe[:, :, :], in0=CIe[:, :, :], in1=TM8[:, :, :], op=ALU.add), [CIe, TM8], [CIe])
                M1 = S.sbuf([128, 16, 128], F32, 'M1')
                M2 = S.sbuf([128, 16, 128], F32, 'M2')
                cre = I['ssm_c_re'].t.ap()[l].rearrange("d (b g) c n -> (g c) (d b) n", g=8)
                cim = I['ssm_c_im'].t.ap()[l].rearrange("d (b g) c n -> (g c) (d b) n", g=8)
                S.dma('sync', M1[:, :, 0:64], cre, M1, I['ssm_c_re'])
                S.dma('sync', M1[:, :, 64:128], cim, M1, I['ssm_c_im'], part=True)
                S.dma('sync', M2[:, :, 0:64], cim, M2, I['ssm_c_im'])
                S.dma('sync', M2[:, :, 64:128], cre, M2, I['ssm_c_re'], part=True)
                ptr_ = S.ring(3, [128, 128], F32, 'ptp', psum=True)
                for M, Tt in ((M1, T1), (M2, T2)):
                    for col in range(16):
                        pt = ptr_()
                        P(lambda e, pt=pt, M=M, col=col: e.transpose(out=pt[:, :], in_=M[:, col, :], identity=self.ident_f[:, :]), [M, self.ident_f], [pt])
                        A(lambda e, pt=pt, Tt=Tt, col=col: e.copy(out=Tt[:, col, :, :], in_=pt[:, :].rearrange("p (g o) -> p g o", o=16)), [pt], [Tt])
            with S.phase():
                pidxc = [S.sbuf([128, NC], F32, 'pidxc%d' % d_) for d_ in range(2)]
                with S.phase():
                    tmpi = S.sbuf([128, NC], I32, 'tmpi')
                    G(lambda e: e.iota(tmpi[:, :], pattern=[[1, NC]], base=0, channel_multiplier=0), [], [tmpi])
                    V(lambda e: e.tensor_copy(out=pidxc[0][:, :], in_=tmpi[:, :]), [tmpi], [pidxc[0]])
                    G(lambda e: e.iota(tmpi[:, 0:NCC], pattern=[[-1, NCC]], base=NCC - 1, channel_multiplier=0), [pidxc[0]], [tmpi])
                    G(lambda e: e.iota(tmpi[:, NCC:NC], pattern=[[-1, NC - NCC]], base=NC - 1, channel_multiplier=0), [tmpi], [tmpi])
                    V(lambda e: e.tensor_copy(out=pidxc[1][:, :], in_=tmpi[:, :]), [tmpi], [pidxc[1]])
                bre = I['ssm_b_re'].t.ap()[l].rearrange("d g n c -> n (d g) c")
                bim = I['ssm_b_im'].t.ap()[l].rearrange("d g n c -> n (d g) c")
                bar = S.ring(1, [128, 8, 16], F32, 'BAb')
                bbr = S.ring(1, [128, 8, 16], F32, 'BBb')
                uf = S.sbuf([128, NT], F32, 'uf')
                Yacc = S.sbuf([128, NT], F32, 'Yacc')
                Ytp = [Buf(Yacc.t, 'Ytp%d' % tp_) for tp_ in range(8)]
                ud = S.sbuf([128, 8, NC], BF16, 'ud')
                udm = [S.sbuf([128, 8, NC], BF16, 'udm%d' % m_) for m_ in range(4)]
                LT = S.sbuf([128, 2, 8, 2, 128], BF16, 'LT')
                x1fr = S.ring(2, [128, 128], F32, 'X1f')
                K0 = S.sbuf([128, 128], F32, 'K0')
                R1f = S.sbuf([128, 9, 128], F32, 'R1f')
                Rb = S.sbuf([128, 2, 9, 2, 128], BF16, 'Rb')
                KT = S.sbuf([128, 15, 128], BF16, 'KT')
                xbig = [S.sbuf([128, 1152], F32, 'xbig%d' % k_) for k_ in range(3)]
                angr = S.ring(2, [128, NC], F32, 'angc')
                kkr = S.ring(2, [128, NC], F32, 'kkc')
                snr = S.ring(2, [128, NC], F32, 'snc')
                csr = S.ring(2, [128, NC], F32, 'csc')
                wmr = S.ring(2, [128, NC], F32, 'wmc')
                wsr = S.ring(2, [128, NC], F32, 'wsc')
                e1r = S.ring(4, [128, NC], BF16, 'e1c')
                e2r = S.ring(4, [128, NC], BF16, 'e2c')
                rzr = S.ring(2, [128, 16, 128], BF16, 'rzc')
                t1r = S.ring(2, [128, HW], F32, 'st1')
                t2r = S.ring(2, [128, HW], F32, 'st2')
                ptr_ = S.ring(2, [128, 128], F32, 'ptp', psum=True)
                p1r = S.ring(2, [128, HW], F32, 'pp1', psum=True)
                p2r = S.ring(2, [128, HW], F32, 'pp2', psum=True)
                pyr = S.ring(2, [128, HW], F32, 'ppy', psum=True)
                ud2 = ud.t.ap().rearrange("p j k -> p (j k)")
                ufv = uf.t.ap().rearrange("p (k j) -> p j k", j=8)
                Yv = Yacc.t.ap().rearrange("p (k j) -> p k j", j=8)
                for blk in range(8):
                    S.dma('sync', uf[:, :], uT[blk * 128:(blk + 1) * 128, :], uf, uT)
                    V(lambda e, blk=blk: e.tensor_scalar(out=Yacc[:, :], in0=uf[:, :], scalar1=dvec[:, blk:blk + 1], scalar2=None, op0=ALU.mult), [uf, dvec], Ytp)
                    V(lambda e: e.tensor_copy(out=ud[:, :, :], in_=ufv), [uf], [ud])
                    for m_ in range(4):
                        V(lambda e, m_=m_: e.tensor_scalar(out=udm[m_][:, :, :], in0=ud[:, :, :], scalar1=sg[:, 4 + m_:5 + m_], scalar2=None, op0=ALU.mult), [ud, sg], [udm[m_]])
                    for d_ in range(2):
                        g0 = d_ * 64 + blk * 8
                        col = d_ * 8 + blk
                        X1f = x1fr()
                        BA, BB = bar(), bbr()
                        S.dma('sync', BA[0:64, :, :], bre[:, g0:g0 + 8, :], BA, I['ssm_b_re'])
                        S.dma('sync', BA[64:128, :, :], bim[:, g0:g0 + 8, :], BA, I['ssm_b_im'], part=True)
                        S.dma('sync', BB[0:64, :, :], bim[:, g0:g0 + 8, :], BB, I['ssm_b_im'])
                        S.dma('sync', BB[64:128, :, :], bre[:, g0:g0 + 8, :], BB, I['ssm_b_re'], part=True)
                        BAb = BA[:, :, :]
                        BBb = BB[:, :, :]

                        XA, XB, XC = xbig[0], xbig[1], xbig[2]
                        xa4 = XA[:, 0:1024].rearrange("p (e g c) -> p e g c", e=8, g=8)
                        xb4 = XB[:, 0:1024].rearrange("p (e g c) -> p e g c", e=8, g=8)
                        xc4 = XC[:, 0:1024].rearrange("p (e g c) -> p e g c", e=8, g=8)
                        BAe = BAb.unsqueeze(1).to_broadcast([128, 8, 8, 16])
                        BBe = BBb.unsqueeze(1).to_broadcast([128, 8, 8, 16])
                        cre = CRe[:, :, g0:g0 + 8].unsqueeze(3).to_broadcast([128, 8, 8, 16])
                        cie = CIe[:, :, g0:g0 + 8].unsqueeze(3).to_broadcast([128, 8, 8, 16])
                        V(lambda e, xa4=xa4, BAe=BAe, cre=cre: e.tensor_tensor(out=xa4, in0=BAe, in1=cre, op=ALU.mult), [BA, CRe], [XA])
                        V(lambda e, xb4=xb4, BBe=BBe, cie=cie: e.tensor_tensor(out=xb4, in0=BBe, in1=cie, op=ALU.mult), [BB, CIe], [XB])
                        V(lambda e: e.scalar_tensor_tensor(out=XA[:, 0:1024], in0=XB[:, 0:1024], scalar=sg[:, 0:1], in1=XA[:, 0:1024], op0=ALU.mult, op1=ALU.add), [XA, XB, sg], [XA])
                        A(lambda e, X1f=X1f: e.copy(out=X1f[:, :], in_=XA[:, 0:128]), [XA], [X1f])
                        for e_ in range(8):
                            pt = ptr_()
                            P(lambda e, pt=pt, e_=e_: e.transpose(out=pt[:, :], in_=XA[:, e_ * 128:(e_ + 1) * 128], identity=self.ident_f[:, :]), [XA, self.ident_f], [pt])
                            A(lambda e, pt=pt, d_=d_, e_=e_: e.copy(out=LT[:, d_, e_, 0, :], in_=pt[:, :]), [pt], [LT])
                        V(lambda e, xb4=xb4, BAe=BAe, cie=cie: e.tensor_tensor(out=xb4, in0=BAe, in1=cie, op=ALU.mult), [BA, CIe], [XB])
                        V(lambda e, xc4=xc4, BBe=BBe, cre=cre: e.tensor_tensor(out=xc4, in0=BBe, in1=cre, op=ALU.mult), [BB, CRe], [XC])
                        V(lambda e: e.scalar_tensor_tensor(out=XB[:, 0:1024], in0=XC[:, 0:1024], scalar=sg[:, 2:3], in1=XB[:, 0:1024], op0=ALU.mult, op1=ALU.add), [XB, XC, sg], [XB])
                        for e_ in range(8):
                            pt = ptr_()
                            P(lambda e, pt=pt, e_=e_: e.transpose(out=pt[:, :], in_=XB[:, e_ * 128:(e_ + 1) * 128], identity=self.ident_f[:, :]), [XB, self.ident_f], [pt])
                            A(lambda e, pt=pt, d_=d_, e_=e_: e.copy(out=LT[:, d_, e_, 1, :], in_=pt[:, :]), [pt], [LT])
                        T1e = T1[:, col, :, :].unsqueeze(1).to_broadcast([128, 9, 8, 16])
                        T2e = T2[:, col, :, :].unsqueeze(1).to_broadcast([128, 9, 8, 16])
                        pwr = PWr[:, :, g0:g0 + 8].unsqueeze(3).to_broadcast([128, 9, 8, 16])
                        pwi = PWi[:, :, g0:g0 + 8].unsqueeze(3).to_broadcast([128, 9, 8, 16])
                        xa9 = XA[:, :].rearrange("p (m g o) -> p m g o", m=9, g=8)
                        xc9 = XC[:, :].rearrange("p (m g o) -> p m g o", m=9, g=8)
                        r1v = R1f.t.ap().rearrange("p m q -> p (m q)")
                        V(lambda e, xa9=xa9, T1e=T1e, pwr=pwr: e.tensor_tensor(out=xa9, in0=T1e, in1=pwr, op=ALU.mult), [T1, PWr], [XA])
                        V(lambda e, xc9=xc9, T2e=T2e, pwi=pwi: e.tensor_tensor(out=xc9, in0=T2e, in1=pwi, op=ALU.mult), [T2, PWi], [XC])
                        V(lambda e, r1v=r1v: e.scalar_tensor_tensor(out=r1v, in0=XA[:, :], scalar=sg[:, 2:3], in1=XC[:, :], op0=ALU.mult, op1=ALU.subtract), [XA, XC, sg], [R1f])
                        A(lambda e, d_=d_: e.copy(out=Rb[:, d_, :, 0, :], in_=R1f[:, :, :]), [R1f], [Rb])
                        V(lambda e, xa9=xa9, T1e=T1e, pwi=pwi: e.tensor_tensor(out=xa9, in0=T1e, in1=pwi, op=ALU.mult), [T1, PWi], [XA])
                        V(lambda e, xc9=xc9, T2e=T2e, pwr=pwr: e.tensor_tensor(out=xc9, in0=T2e, in1=pwr, op=ALU.mult), [T2, PWr], [XC])
                        V(lambda e, d_=d_: e.scalar_tensor_tensor(out=Rb[:, d_, :, 1, :], in0=XA[:, :].rearrange("p (m q) -> p m q", m=9), scalar=sg[:, 0:1], in1=XC[:, :].rearrange("p (m q) -> p m q", m=9), op0=ALU.mult, op1=ALU.subtract), [XA, XC, sg], [Rb])
                        for tau in range(8):
                            pt = ptr_()
                            P(lambda e, pt=pt, tau=tau, X1f=X1f: e.matmul(pt[:, :], lhsT=X1f[:, :], rhs=R1f[:, tau, :], start=True, stop=True), [X1f, R1f], [pt])
                            if tau == 0 and d_ == 0:
                                V(lambda e, pt=pt: e.tensor_tensor(out=K0[:, :], in0=pt[:, :], in1=blkmask[:, :], op=ALU.mult), [pt, blkmask], [K0])
                            elif tau == 0:
                                V(lambda e, pt=pt: e.tensor_tensor(out=K0[:, :], in0=pt[:, :], in1=K0[:, :], op=ALU.add), [pt, K0], [K0])
                                V(lambda e: e.tensor_tensor(out=KT[:, 7, :], in0=K0[:, :], in1=blkmask[:, :], op=ALU.mult), [K0, blkmask], [KT])
                            else:
                                ki = 7 + tau if d_ == 0 else 7 - tau
                                V(lambda e, pt=pt, ki=ki: e.tensor_tensor(out=KT[:, ki, :], in0=pt[:, :], in1=blkmask[:, :], op=ALU.mult), [pt, blkmask], [KT])
                    if 'dbgKT' in self.debug and blk == 0:
                        for nm, tt, shp in (('dbgKT', KT, [128, 15, 128]), ('dbgLT', LT, [128, 2, 8, 2, 128]), ('dbgRb', Rb, [128, 2, 9, 2, 128]), ('dbgud', ud, [128, 8, NC]), ('dbgudm1', udm[1], [128, 8, NC])):
                            dd = self.scratch(nm, shp, BF16)
                            S.dma('sync', dd.t.ap(), tt.t.ap(), dd, tt)
                        for nm, tt, shp in (('dbgPWr', PWr, [128, 9, 128]), ('dbgPWi', PWi, [128, 9, 128]), ('dbgCRe', CRe, [128, 8, 128]), ('dbgT1', T1, [128, 16, 8, 16]), ('dbgmask', blkmask, [128, 128]), ('dbgsg', sg, [128, 8])):
                            dd = self.scratch(nm, shp, F32)
                            S.dma('sync', dd.t.ap(), tt.t.ap(), dd, tt)
                    for tp in range(8):
                        for (h0, h1) in HC:
                            py = pyr()
                            for j in range(8):
                                P(lambda e, py=py, tp=tp, j=j, h0=h0, h1=h1: e.matmul(py[:, :], lhsT=KT[:, 7 + tp - j, :], rhs=ud[:, j, h0:h1], start=(j == 0), stop=(j == 7)), [KT, ud], [py])
                            V(lambda e, py=py, tp=tp, h0=h0, h1=h1: e.tensor_tensor(out=Yv[:, h0:h1, tp], in0=py[:, :], in1=Yv[:, h0:h1, tp], op=ALU.add), [py, Ytp[tp]], [Ytp[tp]])
                    for g8 in range(8):
                        pr, m4 = g8 // 4, g8 % 4
                        um = udm[m4]
                        per_dir = [None, None]

                        def dir_chain(d_, g8=g8, pr=pr, m4=m4, um=um):
                            dg = d_ * 64 + blk * 8 + g8
                            ang, kk = angr(), kkr()
                            V(lambda e, ang=ang, d_=d_, dg=dg: e.tensor_scalar(out=ang[:, :], in0=pidxc[d_][:, :], scalar1=TH8[:, dg:dg + 1], scalar2=None, op0=ALU.mult), [pidxc[d_], TH8], [ang])
                            yield
                            yield from self.rr_gen(ang, kk, ang[:, :], kk[:, :], eng='vector')
                            sn, cs = snr(), csr()
                            A(lambda e, ang=ang, sn=sn: e.activation(out=sn[:, :], in_=ang[:, :], func=AF.Sin), [ang], [sn])
                            yield
                            A(lambda e, ang=ang, cs=cs: e.activation(out=cs[:, :], in_=ang[:, :], func=AF.Sin, scale=0.5), [ang], [cs])
                            yield
                            A(lambda e, cs=cs: e.activation(out=cs[:, :], in_=cs[:, :], func=AF.Square), [cs], [cs])
                            yield
                            V(lambda e, cs=cs: e.tensor_scalar(out=cs[:, :], in0=cs[:, :], scalar1=-2.0, scalar2=1.0, op0=ALU.mult, op1=ALU.add), [cs], [cs])
                            yield
                            wm, ws = wmr(), wsr()
                            for (h0, h1) in HC:
                                p1, p2 = p1r(), p2r()
                                for v_, pp in ((0, p1), (1, p2)):
                                    for j in range(8):
                                        e_ = 7 - j if d_ == 0 else j
                                        P(lambda e, pp=pp, d_=d_, e_=e_, v_=v_, j=j, h0=h0, h1=h1, pr=pr, um=um: e.matmul(pp[:, :], lhsT=LT[64 * pr:64 * pr + 64, d_, e_, v_, :], rhs=um[64 * pr:64 * pr + 64, j, h0:h1], start=(j == 0), stop=(j == 7)), [LT, um], [pp])
                                a, b = t1r(), t2r()
                                V(lambda e, p1=p1, a=a, cs=cs, h0=h0, h1=h1: e.tensor_tensor(out=a[:, :], in0=p1[:, :], in1=cs[:, h0:h1], op=ALU.mult), [p1, cs], [a])
                                yield
                                V(lambda e, p2=p2, b=b, sn=sn, h0=h0, h1=h1: e.tensor_tensor(out=b[:, :], in0=p2[:, :], in1=sn[:, h0:h1], op=ALU.mult), [p2, sn], [b])
                                yield
                                V(lambda e, a=a, b=b, wm=wm, h0=h0, h1=h1: e.tensor_tensor(out=wm[:, h0:h1], in0=a[:, :], in1=b[:, :], op=ALU.add), [a, b], [wm])
                                yield
                            rho_b = RHO8[:, dg:dg + 1]
                            e1, e2 = e1r(), e2r()
                            if d_ == 0:
                                V(lambda e, rho_b=rho_b, wm=wm, ws=ws: e.tensor_tensor_scan(out=ws[:, :], data0=rho_b.to_broadcast([128, NC]), data1=wm[:, :], initial=0.0, op0=ALU.mult, op1=ALU.add), [wm, RHO8], [ws])
                                yield
                                V(lambda e, cs=cs, ws=ws, e1=e1: e.tensor_tensor(out=e1[:, 1:NC], in0=cs[:, 0:NC - 1], in1=ws[:, 0:NC - 1], op=ALU.mult), [cs, ws], [e1])
                                yield
                                G(lambda e, e1=e1: e.memset(e1[:, 0:1], 0.0), [e1], [e1])
                                yield
                                V(lambda e, sn=sn, ws=ws, e2=e2: e.tensor_tensor(out=e2[:, 1:NC], in0=sn[:, 0:NC - 1], in1=ws[:, 0:NC - 1], op=ALU.mult), [sn, ws], [e2])
                                yield
                                G(lambda e, e2=e2: e.memset(e2[:, 0:1], 0.0), [e2], [e2])
                                yield
                            else:
                                V(lambda e, rho_b=rho_b, wm=wm, ws=ws: e.tensor_tensor_scan(out=ws[:, 0:NCC][:, ::-1], data0=rho_b.to_broadcast([128, NCC]), data1=wm[:, 0:NCC][:, ::-1], initial=0.0, op0=ALU.mult, op1=ALU.add), [wm, RHO8], [ws])
                                yield
                                V(lambda e, rho_b=rho_b, wm=wm, ws=ws: e.tensor_tensor_scan(out=ws[:, NCC:NC][:, ::-1], data0=rho_b.to_broadcast([128, NC - NCC]), data1=wm[:, NCC:NC][:, ::-1], initial=ws[:, 0:1], op0=ALU.mult, op1=ALU.add), [wm, RHO8, ws], [ws])
                                yield
                                for (tb, ee, eng) in ((cs, e1, V), (sn, e2, V)):
                                    eng(lambda e, tb=tb, ee=ee, ws=ws: e.tensor_tensor(out=ee[:, 0:NC - 1], in0=tb[:, 1:NC], in1=ws[:, 1:NC], op=ALU.mult), [tb, ws], [ee])
                                    yield
                                    eng(lambda e, tb=tb, ee=ee, ws=ws: e.tensor_tensor(out=ee[:, NC - 1:NC], in0=tb[:, 0:1], in1=ws[:, 0:1], op=ALU.mult), [tb, ws, ee], [ee])
                                    yield
                                    G(lambda e, ee=ee: e.memset(ee[:, NCC - 1:NCC], 0.0), [ee], [ee])
                                    yield
                            rz = rzr()
                            G(lambda e, rz=rz: e.memset(rz[:, :, :], 0.0), [], [rz])
                            yield
                            G(lambda e, rz=rz, d_=d_, g8=g8: e.tensor_copy(out=rz[:, :, g8 * 16:(g8 + 1) * 16], in_=Rb[:, d_, 1:9, :, g8 * 16:(g8 + 1) * 16].rearrange("p m v o -> p (m v) o")), [Rb], [rz])
                            yield
                            per_dir[d_] = (e1, e2, rz)
                        interleave([dir_chain(0), dir_chain(1)])
                        (e1f, e2f, rzf), (e1b, e2b, rzb) = per_dir
                        for tp in range(8):
                            mf, mb = tp + 1, 8 - tp
                            for (h0, h1) in HC:
                                py = pyr()
                                P(lambda e, py=py, mf=mf, h0=h0, h1=h1, rzf=rzf, e1f=e1f: e.matmul(py[:, :], lhsT=rzf[:, (mf - 1) * 2, :], rhs=e1f[:, h0:h1], start=True, stop=False), [rzf, e1f], [py])
                                P(lambda e, py=py, mf=mf, h0=h0, h1=h1, rzf=rzf, e2f=e2f: e.matmul(py[:, :], lhsT=rzf[:, (mf - 1) * 2 + 1, :], rhs=e2f[:, h0:h1], start=False, stop=False), [rzf, e2f], [py])
                                P(lambda e, py=py, mb=mb, h0=h0, h1=h1, rzb=rzb, e1b=e1b: e.matmul(py[:, :], lhsT=rzb[:, (mb - 1) * 2, :], rhs=e1b[:, h0:h1], start=False, stop=False), [rzb, e1b], [py])
                                P(lambda e, py=py, mb=mb, h0=h0, h1=h1, rzb=rzb, e2b=e2b: e.matmul(py[:, :], lhsT=rzb[:, (mb - 1) * 2 + 1, :], rhs=e2b[:, h0:h1], start=False, stop=True), [rzb, e2b], [py])
                                V(lambda e, py=py, tp=tp, h0=h0, h1=h1: e.tensor_tensor(out=Yv[:, h0:h1, tp], in0=py[:, :], in1=Yv[:, h0:h1, tp], op=ALU.add), [py, Ytp[tp]], [Ytp[tp]])
                    if 'yscanT' in self.debug:
                        ysd = self.scratch('yscanT', [D, NT], F32)
                        V(lambda e, blk=blk: e.tensor_scalar(out=uf[:, :], in0=uf[:, :], scalar1=dvec[:, blk:blk + 1], scalar2=None, op0=ALU.mult), [uf, dvec], [uf])
                        V(lambda e: e.tensor_tensor(out=uf[:, :], in0=Yacc[:, :], in1=uf[:, :], op=ALU.subtract), Ytp + [uf], [uf])
                        S.dma('sync', ysd[blk * 128:(blk + 1) * 128, :], uf[:, :], ysd, uf, part=True)
                    V(lambda e: e.tensor_tensor(out=uf[:, :], in0=Yacc[:, :], in1=Yacc[:, :], op=ALU.mult), Ytp, [uf])
                    V(lambda e: e.tensor_scalar(out=uf[:, :], in0=uf[:, :], scalar1=0.044715, scalar2=1.0, op0=ALU.mult, op1=ALU.add), [uf], [uf])
                    V(lambda e: e.tensor_tensor(out=uf[:, :], in0=uf[:, :], in1=Yacc[:, :], op=ALU.mult), Ytp + [uf], [uf])
                    A(lambda e: e.activation(out=uf[:, :], in_=uf[:, :], func=AF.Sigmoid, scale=2.0 * math.sqrt(2.0 / math.pi)), [uf], [uf])
                    V(lambda e: e.tensor_tensor(out=ud2, in0=uf[:, :], in1=Yacc[:, :], op=ALU.mult), Ytp + [uf], [ud])
                    S.dma('sync', ygT[blk * 128:(blk + 1) * 128, :], ud2, ygT, ud, part=True)
        self.ssm_glu_out(l)

    def ssm_glu_out(self, l):
        S = self.S
        I = self.I
        ygT = self.scratch('ygT', [D, NT], BF16)
        gates = self.scratch('gates', [NT, 3 * D], F32)
        y1 = self.scratch('y1', [NT, D], F32)
        TG = [(t0, min(t0 + 512, NT)) for t0 in range(0, NT, 512)]
        V = lambda fn, r, w: S.op('vector', fn, r, w)
        A = lambda fn, r, w: S.op('scalar', fn, r, w)
        with S.phase():
            wg = S.sbuf([128, 8, D], BF16, 'wglu')
            wo = S.sbuf([128, 8, D], BF16, 'wsout')
            for (wt, nm) in ((wg, 'w_ssm_glu'), (wo, 'w_ssm_out')):
                wvv = I[nm].t.ap()[l].rearrange("(k p) n -> p k n", p=128)
                for hh in range(2):
                    S.dma('gpsimd', wt[:, :, hh * 512:(hh + 1) * 512], wvv[:, :, hh * 512:(hh + 1) * 512], wt, I[nm], part=(hh == 1))
            ygr = S.ring(2, [128, 8, 512], BF16, 'ygl')
            y2r = S.ring(2, [128, 8, 512], BF16, 'y2T')
            sgr = S.ring(2, [128, 512], F32, 'sgl')
            gr = S.ring(2, [128, D], F32, 'g1')
            yr = S.ring(2, [128, D], F32, 'y1t')
            pgr = S.ring(3, [128, 512], F32, 'pgl', psum=True)
            pyr = S.ring(4, [128, 512], F32, 'py', psum=True)
            ygv = ygT.t.ap().rearrange("(j p) t -> p j t", p=128)
            for (t0, t1) in TG:
                n = t1 - t0
                yg = ygr()
                S.dma('sync', yg[:, :, 0:n], ygv[:, :, t0:t1], yg, ygT)
                y2 = y2r()
                for m in range(8):
                    pg = pgr()
                    for k in range(8):
                        S.op('tensor', lambda e, pg=pg, yg=yg, m=m, k=k, n=n: e.matmul(pg[:, 0:n], lhsT=wg[:, k, m * 128:(m + 1) * 128], rhs=yg[:, k, 0:n], start=(k == 0), stop=(k == 7)), [wg, yg], [pg])
                    sgt = sgr()
                    A(lambda e, pg=pg, sgt=sgt, n=n: e.activation(out=sgt[:, 0:n], in_=pg[:, 0:n], func=AF.Sigmoid), [pg], [sgt])
                    V(lambda e, yg=yg, y2=y2, sgt=sgt, m=m, n=n: e.tensor_tensor(out=y2[:, m, 0:n], in0=yg[:, m, 0:n], in1=sgt[:, 0:n], op=ALU.mult), [yg, sgt], [y2])
                for ii in range(n // 128):
                    i = t0 // 128 + ii
                    g1 = gr()
                    S.dma('sync', g1[:, :], gates[i * 128:(i + 1) * 128, D:2 * D], g1, gates)
                    yt = yr()
                    for nb in range(2):
                        py = pyr()
                        for m in range(8):
                            S.op('tensor', lambda e, py=py, y2=y2, m=m, ii=ii, nb=nb: e.matmul(py[:, :], lhsT=y2[:, m, ii * 128:(ii + 1) * 128], rhs=wo[:, m, nb * 512:(nb + 1) * 512], start=(m == 0), stop=(m == 7)), [y2, wo], [py])
                        V(lambda e, py=py, yt=yt, g1=g1, nb=nb: e.tensor_tensor(out=yt[:, nb * 512:(nb + 1) * 512], in0=py[:, :], in1=g1[:, nb * 512:(nb + 1) * 512], op=ALU.mult), [py, g1], [yt])
                    S.dma('sync', y1[i * 128:(i + 1) * 128, :], yt[:, :], y1, yt, part=True)


    def ln_stats_gen(self, x, st):
        S = self.S
        S.op('vector', lambda e: e.bn_stats(out=st[:, 0:6], in_=x[:, 0:512]), [x], [st])
        yield
        S.op('vector', lambda e: e.bn_stats(out=st[:, 6:12], in_=x[:, 512:1024]), [x], [st])
        yield
        S.op('vector', lambda e: e.bn_aggr(out=st[:, 12:14], in_=st[:, 0:12]), [st], [st])
        yield
        S.op('vector', lambda e: e.tensor_scalar(out=st[:, 14:15], in0=st[:, 13:14], scalar1=LN_EPS, scalar2=None, op0=ALU.add), [st], [st])
        yield
        S.op('scalar', lambda e: e.activation(out=st[:, 14:15], in_=st[:, 14:15], func=AF.Sqrt), [st], [st])
        yield
        S.op('vector', lambda e: e.reciprocal(out=st[:, 14:15], in_=st[:, 14:15]), [st], [st])
        yield

    def ln_stats(self, x, st):
        S = self.S
        S.op('vector', lambda e: e.bn_stats(out=st[:, 0:6], in_=x[:, 0:512]), [x], [st])
        S.op('vector', lambda e: e.bn_stats(out=st[:, 6:12], in_=x[:, 512:1024]), [x], [st])
        S.op('vector', lambda e: e.bn_aggr(out=st[:, 12:14], in_=st[:, 0:12]), [st], [st])
        self.rstd(st, 13, 14, LN_EPS)

    def ph_merge(self, l):
        S = self.S
        I = self.I
        V = lambda fn, r, w: S.op('vector', fn, r, w)
        G = lambda fn, r, w: S.op('gpsimd', fn, r, w)
        A = lambda fn, r, w: S.op('scalar', fn, r, w)
        ctx_out = l < DEPTH - 1
        ys = [self.scratch('y%d' % b, [NT, D], F32) for b in range(3)]
        xres = self.scr['xres']
        modscr = self.scr['modscr%d' % l]
        h2tm = self.scratch('h2tm', [NT, D], BF16)
        aff = self.scratch('aff', [NT, NEXP], F32)
        affT = self.scratch('affT', [NEXP, NT], F32)
        with S.phase():
            wo = S.sbuf([128, 8, D], BF16, 'wo')
            wov = I['w_o'].t.ap()[l].rearrange("(k p) n -> p k n", p=128)
            for hh in range(2):
                S.dma('gpsimd', wo[:, :, hh * 512:(hh + 1) * 512], wov[:, :, hh * 512:(hh + 1) * 512], wo, I['w_o'], part=(hh == 1))
            wr = S.sbuf([128, 8, NEXP], F32, 'wr')
            S.dma('sync', wr[:, :, :], I['w_router'].t.ap()[l].rearrange("(k p) e -> p k e", p=128), wr, I['w_router'])
            modv = S.sbuf([128, 2, 3, D], F32, 'modv2')
            for w in range(2):
                for jj, c0 in enumerate((2 * D, 3 * D, 4 * D)):
                    S.dma('sync', modv[:, w, jj, :], modscr[w, :, c0:c0 + D], modv, modscr, part=not (w == 0 and jj == 0))
            lng = S.sbuf([128, 2, D], F32, 'ln1gb')
            S.dma('sync', lng[:, 0, :], I['ln1_g'][l:l + 1, :].partition_broadcast(128), lng, I['ln1_g'])
            S.dma('sync', lng[:, 1, :], I['ln1_b'][l:l + 1, :].partition_broadcast(128), lng, I['ln1_b'], part=True)
            yar, ybr, ycr = (S.ring(2, [128, D], F32, nm) for nm in ('ya', 'yb', 'yc'))
            mbr = S.ring(2, [128, D], BF16, 'mb')
            mtr = S.ring(2, [128, 8, 128], BF16, 'mT')
            xr = S.ring(2, [128, D], F32, 'xt')
            tr = S.ring(2, [128, D], F32, 'tt')
            hr = S.ring(2, [128, D], F32, 'h2f')
            hbr = S.ring(2, [128, D], BF16, 'h2b')
            htr = S.ring(2, [128, 8, 128], F32, 'h2T')
            str_ = S.ring(6, [128, 16], F32, 'st')
            afr = S.ring(2, [128, 2, NEXP], F32, 'afft')
            atr = S.ring(2, [NEXP, 128], F32, 'affTt')
            ptr_ = S.ring(1, [128, 8, 128], BF16, 'pTm', psum=True)
            pyr = S.ring(2, [128, 512], F32, 'pym', psum=True)
            phr = S.ring(1, [128, 8, 128], F32, 'pTh', psum=True)
            plr = S.ring(2, [128, NEXP], F32, 'plog', psum=True)
            par_ = S.ring(1, [NEXP, 128], F32, 'paT', psum=True)
            def tile_chain(i):
                w = 1 if i < 2 else 0
                rows = slice(i * 128, (i + 1) * 128)
                ya, yb, yc = yar(), ybr(), ycr()
                for yt_, ysrc in ((ya, ys[0]), (yb, ys[1]), (yc, ys[2])):
                    S.dma('sync', yt_[:, :], ysrc[rows, :], yt_, ysrc)
                V(lambda e, ya=ya, yb=yb: e.tensor_tensor(out=ya[:, :], in0=ya[:, :], in1=yb[:, :], op=ALU.add), [ya, yb], [ya])
                yield
                mb = mbr()
                V(lambda e, ya=ya, yc=yc, mb=mb: e.tensor_tensor(out=mb[:, :], in0=ya[:, :], in1=yc[:, :], op=ALU.add), [ya, yc], [mb])
                yield
                pT = ptr_()
                for k in range(8):
                    S.op('tensor', lambda e, pT=pT, mb=mb, k=k: e.transpose(out=pT[:, k, :], in_=mb[:, k * 128:(k + 1) * 128], identity=self.ident_b[:, :]), [mb, self.ident_b], [pT])
                mT = mtr()
                A(lambda e, pT=pT, mT=mT: e.copy(out=mT[:, :, :], in_=pT[:, :, :]), [pT], [mT])
                yield
                xt = xr()
                S.dma('sync', xt[:, :], xres[rows, :], xt, xres)
                tt = tr()
                for nb in range(2):
                    py = pyr()
                    for k in range(8):
                        S.op('tensor', lambda e, py=py, mT=mT, k=k, nb=nb: e.matmul(py[:, :], lhsT=mT[:, k, :], rhs=wo[:, k, nb * 512:(nb + 1) * 512], start=(k == 0), stop=(k == 7)), [mT, wo], [py])
                    V(lambda e, py=py, tt=tt, nb=nb, w=w: e.tensor_tensor(out=tt[:, nb * 512:(nb + 1) * 512], in0=py[:, :], in1=modv[:, w, 0, nb * 512:(nb + 1) * 512], op=ALU.mult), [py, modv], [tt])
                    yield
                V(lambda e, xt=xt, tt=tt: e.scalar_tensor_tensor(out=tt[:, :], in0=xt[:, :], scalar=ALPHA, in1=tt[:, :], op0=ALU.mult, op1=ALU.add), [xt, tt], [tt])
                yield
                st = str_()
                yield from self.ln_stats_gen(tt, st)
                V(lambda e, tt=tt, st=st: e.tensor_scalar(out=tt[:, :], in0=tt[:, :], scalar1=st[:, 12:13], scalar2=st[:, 14:15], op0=ALU.subtract, op1=ALU.mult), [tt, st], [tt])
                yield
                V(lambda e, tt=tt: e.tensor_tensor(out=tt[:, :], in0=tt[:, :], in1=lng[:, 0, :], op=ALU.mult), [tt, lng], [tt])
                yield
                V(lambda e, tt=tt, xt=xt: e.tensor_tensor(out=xt[:, :], in0=tt[:, :], in1=lng[:, 1, :], op=ALU.add), [tt, lng], [xt])
                yield
                S.dma('sync', xres[rows, :], xt[:, :], xres, xt, part=True)
                st2 = str_()
                yield from self.ln_stats_gen(xt, st2)
                hf = hr()
                V(lambda e, xt=xt, st2=st2, hf=hf: e.tensor_scalar(out=hf[:, :], in0=xt[:, :], scalar1=st2[:, 12:13], scalar2=st2[:, 14:15], op0=ALU.subtract, op1=ALU.mult), [xt, st2], [hf])
                yield
                V(lambda e, hf=hf, w=w: e.tensor_tensor(out=hf[:, :], in0=hf[:, :], in1=modv[:, w, 2, :], op=ALU.mult), [hf, modv], [hf])
                yield
                V(lambda e, hf=hf, w=w: e.tensor_tensor(out=hf[:, :], in0=hf[:, :], in1=modv[:, w, 1, :], op=ALU.add), [hf, modv], [hf])
                yield
                hb = hbr()
                A(lambda e, hf=hf, hb=hb: e.copy(out=hb[:, :], in_=hf[:, :]), [hf], [hb])
                yield
                S.dma('sync', h2tm[rows, :], hb[:, :], h2tm, hb, part=True)
                ph = phr()
                for k in range(8):
                    S.op('tensor', lambda e, ph=ph, hf=hf, k=k: e.transpose(out=ph[:, k, :], in_=hf[:, k * 128:(k + 1) * 128], identity=self.ident_f[:, :]), [hf, self.ident_f], [ph])
                hT = htr()
                A(lambda e, ph=ph, hT=hT: e.copy(out=hT[:, :, :], in_=ph[:, :, :]), [ph], [hT])
                yield
                pl = plr()
                for k in range(8):
                    S.op('tensor', lambda e, pl=pl, hT=hT, k=k: e.matmul(pl[:, :], lhsT=hT[:, k, :], rhs=wr[:, k, :], start=(k == 0), stop=(k == 7)), [hT, wr], [pl])
                st3 = str_()
                af = afr()
                V(lambda e, pl=pl, st3=st3: e.reduce_max(out=st3[:, 0:1], in_=pl[:, :], axis=AX.X), [pl], [st3])
                yield
                V(lambda e, st3=st3: e.tensor_scalar(out=st3[:, 1:2], in0=st3[:, 0:1], scalar1=-1.0, scalar2=None, op0=ALU.mult), [st3], [st3])
                yield
                A(lambda e, pl=pl, st3=st3, af=af: e.activation(out=af[:, 0, :], in_=pl[:, :], func=AF.Exp, bias=st3[:, 1:2], accum_out=st3[:, 2:3]), [pl, st3], [af, st3])
                yield
                V(lambda e, st3=st3: e.reciprocal(out=st3[:, 3:4], in_=st3[:, 2:3]), [st3], [st3])
                yield
                V(lambda e, st3=st3, af=af: e.tensor_scalar(out=af[:, 1, :], in0=af[:, 0, :], scalar1=st3[:, 3:4], scalar2=None, op0=ALU.mult), [af, st3], [af])
                yield
                S.dma('sync', aff[rows, :], af[:, 1, :], aff, af, part=True)
                pa = par_()
                S.op('tensor', lambda e, pa=pa, af=af: e.transpose(out=pa[:, :], in_=af[:, 1, :], identity=self.ident_f[:, :]), [af, self.ident_f], [pa])
                at = atr()
                A(lambda e, pa=pa, at=at: e.copy(out=at[:, :], in_=pa[:, :]), [pa], [at])
                yield
                S.dma('sync', affT[:, i * 128:(i + 1) * 128], at[:, :], affT, at, part=True)


            tiles = list(range(0 if ctx_out else 2, NTILE))
            for t0_ in range(0, len(tiles), 2):
                interleave([tile_chain(t_) for t_ in tiles[t0_:t0_ + 2]])
    def ph_moe(self, l):
        S = self.S
        I = self.I
        V = lambda fn, r, w: S.op('vector', fn, r, w)
        G = lambda fn, r, w: S.op('gpsimd', fn, r, w)
        A = lambda fn, r, w: S.op('scalar', fn, r, w)
        P = lambda fn, r, w: S.op('tensor', fn, r, w)
        ctx_out = l < DEPTH - 1
        h2tm = self.scr['h2tm']
        aff = self.scr['aff']
        affT = self.scr['affT']
        ymoe = self.scratch('ymoe', [NT, D], F32)
        sets = []
        if ctx_out:
            sets.append((0, 2, 2 * NCTX // NEXP, 1))
        sets.append((2, 32, 2 * NLAT // NEXP, 4))
        with S.phase():
            zt = S.sbuf([128, D], F32, 'zt')
            G(lambda e: e.memset(zt[:, :], 0.0), [], [zt])
            for i in range(NTILE):
                S.dma('sync', ymoe[i * 128:(i + 1) * 128, :], zt[:, :], ymoe, zt, part=(i > 0))
            ymoe_t = [Buf(ymoe.t, 'ymoe_t%d' % i) for i in range(NTILE)]
            for b in ymoe_t:
                b.w = ymoe.w
            acc_grp = [DSem(S, 'ymacc0'), DSem(S, 'ymacc1')]
            aff_all = S.sbuf([128, NTILE, NEXP], F32, 'aff_all')
            mask_all = S.sbuf([128, NTILE, NEXP], F32, 'mask_all')
            slot_all = S.sbuf([128, NTILE, NEXP], F32, 'slot_all')
            gate_all = S.sbuf([128, NTILE, NEXP], F32, 'gate_all')
            slotT = S.sbuf([NEXP, NT], F32, 'slotT')
            S.dma('sync', aff_all[:, :, :], aff.t.ap().rearrange("(i p) e -> p i e", p=128), aff_all, aff)
            esel = S.sbuf([NEXP, NEXP, 128], F32, 'esel')
            V(lambda e: e.tensor_copy(out=esel[:, :, :], in_=self.ident_f[0:NEXP, 0:NEXP].unsqueeze(2).to_broadcast([NEXP, NEXP, 128])), [self.ident_f], [esel])
            ustr = S.sbuf([128, 128], F32, 'ustr')
            G(lambda e: e.affine_select(out=ustr[:, :], in_=self.ones_f[:, :], pattern=[[1, 128]], compare_op=ALU.is_gt, fill=0.0, base=0, channel_multiplier=-1), [self.ones_f], [ustr])
            irow = S.sbuf([128, 512], F32, 'irow')
            pcol = S.sbuf([128, 4], F32, 'pcol')
            with S.phase():
                ti = S.sbuf([128, 512], I32, 'ti')
                G(lambda e: e.iota(ti[:, :], pattern=[[1, 512]], base=0, channel_multiplier=0), [], [ti])
                V(lambda e: e.tensor_copy(out=irow[:, :], in_=ti[:, :]), [ti], [irow])
                G(lambda e: e.iota(ti[:, 0:4], pattern=[[128, 4]], base=0, channel_multiplier=1), [irow], [ti])
                V(lambda e: e.tensor_copy(out=pcol[:, :], in_=ti[:, 0:4]), [ti], [pcol])
                affTs = S.sbuf([NEXP, NT], F32, 'affTs')
                junk = S.sbuf([NEXP, NLAT], F32, 'junk')
                S.dma('sync', affTs[:, :], affT[:, :], affTs, affT)
                pb = S.ring(2, [128, NEXP], F32, 'pthr', psum=True)
                pc = S.ring(2, [128, NEXP], F32, 'pcum', psum=True)
                pst = S.ring(2, [NEXP, 128], F32, 'pslT', psum=True)
                for (ti0, ntl, cap, nst) in sets:
                    c0, c1 = ti0 * 128, (ti0 + ntl) * 128
                    bs = S.sbuf([NEXP, 8], F32, 'bs')
                    V(lambda e, bs=bs: e.memset(bs[:, 0:1], 0.0), [], [bs])
                    V(lambda e, bs=bs: e.memset(bs[:, 1:2], 1.0), [bs], [bs])
                    for it in range(30):
                        V(lambda e, bs=bs: e.tensor_tensor(out=bs[:, 2:3], in0=bs[:, 0:1], in1=bs[:, 1:2], op=ALU.add), [bs], [bs])
                        V(lambda e, bs=bs: e.tensor_scalar(out=bs[:, 2:3], in0=bs[:, 2:3], scalar1=0.5, scalar2=None, op0=ALU.mult), [bs], [bs])
                        V(lambda e, bs=bs, c0=c0, c1=c1: e.tensor_scalar(out=junk[:, 0:c1 - c0], in0=affTs[:, c0:c1], scalar1=bs[:, 2:3], scalar2=0.0, op0=ALU.is_ge, op1=ALU.add, accum_out=bs[:, 3:4]), [affTs, bs], [junk, bs])
                        V(lambda e, bs=bs, cap=cap: e.tensor_scalar(out=bs[:, 4:5], in0=bs[:, 3:4], scalar1=float(cap), scalar2=None, op0=ALU.is_ge), [bs], [bs])
                        V(lambda e, bs=bs: e.tensor_scalar(out=bs[:, 5:6], in0=bs[:, 4:5], scalar1=-1.0, scalar2=1.0, op0=ALU.mult, op1=ALU.add), [bs], [bs])
                        V(lambda e, bs=bs: e.tensor_tensor(out=bs[:, 6:7], in0=bs[:, 2:3], in1=bs[:, 0:1], op=ALU.subtract), [bs], [bs])
                        V(lambda e, bs=bs: e.tensor_tensor(out=bs[:, 7:8], in0=bs[:, 2:3], in1=bs[:, 1:2], op=ALU.subtract), [bs], [bs])
                        V(lambda e, bs=bs: e.scalar_tensor_tensor(out=bs[:, 0:1], in0=bs[:, 6:7], scalar=bs[:, 4:5], in1=bs[:, 0:1], op0=ALU.mult, op1=ALU.add), [bs], [bs])
                        V(lambda e, bs=bs: e.scalar_tensor_tensor(out=bs[:, 1:2], in0=bs[:, 7:8], scalar=bs[:, 5:6], in1=bs[:, 1:2], op0=ALU.mult, op1=ALU.add), [bs], [bs])
                    dg = S.sbuf([NEXP, NEXP], F32, 'dg')
                    V(lambda e, bs=bs, dg=dg: e.tensor_scalar(out=dg[:, :], in0=self.ident_f[0:NEXP, 0:NEXP], scalar1=bs[:, 0:1], scalar2=None, op0=ALU.mult), [bs, self.ident_f], [dg])
                    pthr = pb()
                    P(lambda e, pthr=pthr, dg=dg: e.matmul(pthr[:, :], lhsT=self.ones_f[0:NEXP, :], rhs=dg[:, :], start=True, stop=True), [dg, self.ones_f], [pthr])
                    thr = S.sbuf([128, NEXP], F32, 'thr')
                    V(lambda e, pthr=pthr, thr=thr: e.tensor_copy(out=thr[:, :], in_=pthr[:, :]), [pthr], [thr])
                    for ii in range(ntl):
                        i = ti0 + ii
                        V(lambda e, i=i, thr=thr: e.tensor_tensor(out=mask_all[:, i, :], in0=aff_all[:, i, :], in1=thr[:, :], op=ALU.is_ge), [aff_all, thr], [mask_all])
                        V(lambda e, i=i: e.tensor_tensor(out=gate_all[:, i, :], in0=aff_all[:, i, :], in1=mask_all[:, i, :], op=ALU.mult), [aff_all, mask_all], [gate_all])
                        pcm = pc()
                        for jj in range(ii):
                            j = ti0 + jj
                            P(lambda e, pcm=pcm, j=j, jj=jj: e.matmul(pcm[:, :], lhsT=self.ones_f[:, :], rhs=mask_all[:, j, :], start=(jj == 0), stop=False), [mask_all, self.ones_f], [pcm])
                        P(lambda e, pcm=pcm, i=i, ii=ii: e.matmul(pcm[:, :], lhsT=ustr[:, :], rhs=mask_all[:, i, :], start=(ii == 0), stop=True), [mask_all, ustr], [pcm])
                        V(lambda e, pcm=pcm, i=i: e.tensor_tensor(out=slot_all[:, i, :], in0=pcm[:, :], in1=mask_all[:, i, :], op=ALU.mult), [pcm, mask_all], [slot_all])
                        V(lambda e, i=i: e.tensor_tensor(out=slot_all[:, i, :], in0=slot_all[:, i, :], in1=mask_all[:, i, :], op=ALU.add), [slot_all, mask_all], [slot_all])
                        V(lambda e, i=i: e.tensor_scalar(out=slot_all[:, i, :], in0=slot_all[:, i, :], scalar1=-1.0, scalar2=None, op0=ALU.add), [slot_all], [slot_all])
                        ps_ = pst()
                        P(lambda e, ps_=ps_, i=i: e.transpose(out=ps_[:, :], in_=slot_all[:, i, :], identity=self.ident_f[:, :]), [slot_all, self.ident_f], [ps_])
                        A(lambda e, ps_=ps_, i=i: e.copy(out=slotT[:, i * 128:(i + 1) * 128], in_=ps_[:, :]), [ps_], [slotT])
            if 'slotdbg' in self.debug:
                sd = self.scratch('slotdbg', [128, NTILE, NEXP], F32)
                S.dma('sync', sd[:, :, :], slot_all[:, :, :], sd, slot_all)
            with S.phase():
                selT = S.sbuf([128, 32, 512], BF16, 'selT')
                sel = S.sbuf([128, 4, NLAT], BF16, 'sel')
                xsT = S.sbuf([128, 8, 512], BF16, 'xsT')
                actT = S.sbuf([128, 22, 512], BF16, 'actT')
                ye = S.sbuf([128, 4, D], BF16, 'ye')
                h2r = S.ring(4, [128, D], BF16, 'h2l')
                wgr = S.ring(3, [128, 8, 256], BF16, 'wga')
                wur = S.ring(3, [128, 8, 256], BF16, 'wgu')
                wdr = S.ring(4, [128, 2, 512], BF16, 'wdn')
                sar = S.ring(2, [128, 512], F32, 'sact')
                ytr = S.ring(3, [128, D], F32, 'ysc')
                acc4 = S.ring(4, [128, 512], F32, 'pacc', psum=True)
                gen4 = S.ring(4, [128, 512], F32, 'pgen', psum=True)
                items = []

                def make_item(ex, ti0, ntl, cap, nst):
                    if True:
                        wgv = I['w_gate_up'].t.ap()[l, ex].rearrange("(k p) n -> p k n", p=128)
                        wdv = I['w_down'].t.ap()[l, ex].rearrange("(f p) n -> p f n", p=128)
                        ns = nst * 128
                        ntok = ntl * 128
                        tok0 = ti0 * 128
                        def selT_build():
                            for ii in range(ntl):
                                i = ti0 + ii
                                V(lambda e, ii=ii, i=i, ex=ex, ns=ns: e.tensor_scalar(out=selT[:, ii, 0:ns], in0=irow[:, 0:ns], scalar1=slot_all[:, i, ex:ex + 1], scalar2=None, op0=ALU.is_equal), [irow, slot_all], [selT])
                        def sel_build():
                            for c0 in range(0, ntok, 512):
                                cn = min(512, ntok - c0)
                                pbc = gen4()
                                P(lambda e, pbc=pbc, ex=ex, c0=c0, cn=cn, tok0=tok0: e.matmul(pbc[:, 0:cn], lhsT=esel[:, ex, :], rhs=slotT[:, tok0 + c0:tok0 + c0 + cn], start=True, stop=True), [esel, slotT], [pbc])
                                for st_ in range(nst):
                                    V(lambda e, pbc=pbc, st_=st_, c0=c0, cn=cn: e.tensor_scalar(out=sel[:, st_, c0:c0 + cn], in0=pbc[:, 0:cn], scalar1=pcol[:, st_:st_ + 1], scalar2=None, op0=ALU.is_equal), [pbc, pcol], [sel])
                        def gather():
                            for kh in range(2):
                                pgs = [acc4() for _ in range(4)]
                                for ii in range(ntl):
                                    i = ti0 + ii
                                    ht = h2r()
                                    S.dma('sync', ht[:, :], h2tm[i * 128:(i + 1) * 128, :], ht, h2tm)
                                    for kk in range(4):
                                        k = kh * 4 + kk
                                        P(lambda e, pg=pgs[kk], ht=ht, k=k, ii=ii, ns=ns, ntl=ntl: e.matmul(pg[:, 0:ns], lhsT=ht[:, k * 128:(k + 1) * 128], rhs=selT[:, ii, 0:ns], start=(ii == 0), stop=(ii == ntl - 1)), [ht, selT], [pgs[kk]])
                                for kk in range(4):
                                    k = kh * 4 + kk
                                    A(lambda e, pg=pgs[kk], k=k, ns=ns: e.copy(out=xsT[:, k, 0:ns], in_=pg[:, 0:ns]), [pgs[kk]], [xsT])
                        def mlp():
                            for b in range(11):
                                wa, wu = wgr(), wur()
                                S.dma('gpsimd', wa[:, :, :], wgv[:, :, b * 256:(b + 1) * 256], wa, I['w_gate_up'])
                                S.dma('gpsimd', wu[:, :, :], wgv[:, :, DEXP + b * 256:DEXP + (b + 1) * 256], wu, I['w_gate_up'])
                                for ff in range(2):
                                    f = 2 * b + ff
                                    pa_, pu_ = gen4(), gen4()
                                    for k in range(8):
                                        P(lambda e, pa_=pa_, wa=wa, k=k, ff=ff, ns=ns: e.matmul(pa_[:, 0:ns], lhsT=wa[:, k, ff * 128:(ff + 1) * 128], rhs=xsT[:, k, 0:ns], start=(k == 0), stop=(k == 7)), [wa, xsT], [pa_])
                                    for k in range(8):
                                        P(lambda e, pu_=pu_, wu=wu, k=k, ff=ff, ns=ns: e.matmul(pu_[:, 0:ns], lhsT=wu[:, k, ff * 128:(ff + 1) * 128], rhs=xsT[:, k, 0:ns], start=(k == 0), stop=(k == 7)), [wu, xsT], [pu_])
                                    sa = sar()
                                    A(lambda e, pa_=pa_, sa=sa, ns=ns: e.activation(out=sa[:, 0:ns], in_=pa_[:, 0:ns], func=AF.Silu), [pa_], [sa])
                                    V(lambda e, pu_=pu_, sa=sa, f=f, ns=ns: e.tensor_tensor(out=actT[:, f, 0:ns], in0=pu_[:, 0:ns], in1=sa[:, 0:ns], op=ALU.mult), [pu_, sa], [actT])
                            for nb in range(2):
                                pds = [acc4() for _ in range(nst)]
                                for f2 in range(11):
                                    wd = wdr()
                                    S.dma('gpsimd', wd[:, :, :], wdv[:, 2 * f2:2 * f2 + 2, nb * 512:(nb + 1) * 512], wd, I['w_down'])
                                    for ff in range(2):
                                        f = 2 * f2 + ff
                                        for st_ in range(nst):
                                            P(lambda e, pd=pds[st_], wd=wd, f=f, ff=ff, st_=st_: e.matmul(pd[:, :], lhsT=actT[:, f, st_ * 128:(st_ + 1) * 128], rhs=wd[:, ff, :], start=(f == 0), stop=(f == 21)), [actT, wd], [pds[st_]])
                                for st_ in range(nst):
                                    A(lambda e, pd=pds[st_], st_=st_, nb=nb: e.copy(out=ye[:, st_, nb * 512:(nb + 1) * 512], in_=pd[:, :]), [pds[st_]], [ye])
                        def scatter():
                            for ii in range(ntl):
                                i = ti0 + ii
                                yt = ytr()
                                for nb in range(2):
                                    psc = gen4()
                                    for st_ in range(nst):
                                        P(lambda e, psc=psc, st_=st_, ii=ii, nb=nb, nst=nst: e.matmul(psc[:, :], lhsT=sel[:, st_, ii * 128:(ii + 1) * 128], rhs=ye[:, st_, nb * 512:(nb + 1) * 512], start=(st_ == 0), stop=(st_ == nst - 1)), [sel, ye], [psc])
                                    if nb == 0:
                                        V(lambda e, psc=psc, yt=yt, nb=nb, i=i, ex=ex: e.tensor_scalar(out=yt[:, nb * 512:(nb + 1) * 512], in0=psc[:, :], scalar1=gate_all[:, i, ex:ex + 1], scalar2=None, op0=ALU.mult), [psc, gate_all], [yt])
                                    else:
                                        A(lambda e, psc=psc, yt=yt, nb=nb, i=i, ex=ex: e.activation(out=yt[:, nb * 512:(nb + 1) * 512], in_=psc[:, :], func=AF.Identity, scale=gate_all[:, i, ex:ex + 1]), [psc, gate_all], [yt])
                                yb = ymoe_t[i]
                                yb.grp = acc_grp[ex % 2]
                                S.dma('gpsimd', ymoe[i * 128:(i + 1) * 128, :], yt[:, :], yb, yt, accum_op=ALU.add)
                        return (selT_build, sel_build, gather, mlp, scatter)
                for ex in range(NEXP):
                    for (ti0, ntl, cap, nst) in sets:
                        items.append(make_item(ex, ti0, ntl, cap, nst))
                items[0][0]()
                for idx_, (sb_, s2b_, ga_, ml_, sc_) in enumerate(items):
                    ga_()
                    if idx_ + 1 < len(items):
                        items[idx_ + 1][0]()
                    s2b_()
                    ml_()
                    sc_()
            ymoe.grp = acc_grp[0]
            ymoe.w = ('d', acc_grp[0])
            ymoe.r = {('d', id(acc_grp[1])): ('d', acc_grp[1])}

    def ph_ln2(self, l):
        S = self.S
        I = self.I
        V = lambda fn, r, w: S.op('vector', fn, r, w)
        G = lambda fn, r, w: S.op('gpsimd', fn, r, w)
        ctx_out = l < DEPTH - 1
        last = (l == DEPTH - 1)
        xres = self.scr['xres']
        modscr = self.scr['modscr%d' % l]
        ymoe = self.scratch('ymoe', [NT, D], F32)
        with S.phase():
            g2 = S.sbuf([128, 2, D], F32, 'g2v')
            for w in range(2):
                S.dma('sync', g2[:, w, :], modscr[w, :, 5 * D:6 * D], g2, modscr, part=(w == 1))
            lng = S.sbuf([128, 2, D], F32, 'ln2gb')
            S.dma('sync', lng[:, 0, :], I['ln2_g'][l:l + 1, :].partition_broadcast(128), lng, I['ln2_g'])
            S.dma('sync', lng[:, 1, :], I['ln2_b'][l:l + 1, :].partition_broadcast(128), lng, I['ln2_b'], part=True)
            xr = S.ring(3, [128, D], F32, 'xt')
            yr = S.ring(3, [128, D], F32, 'ym')
            str_ = S.ring(3, [128, 16], F32, 'st')
            def tile_chain(i):
                w = 1 if i < 2 else 0
                rows = slice(i * 128, (i + 1) * 128)
                xt, ym = xr(), yr()
                S.dma('sync', xt[:, :], xres[rows, :], xt, xres)
                S.dma('sync', ym[:, :], ymoe[rows, :], ym, ymoe)
                V(lambda e, ym=ym, w=w: e.tensor_tensor(out=ym[:, :], in0=ym[:, :], in1=g2[:, w, :], op=ALU.mult), [ym, g2], [ym])
                yield
                V(lambda e, xt=xt, ym=ym: e.scalar_tensor_tensor(out=ym[:, :], in0=xt[:, :], scalar=ALPHA, in1=ym[:, :], op0=ALU.mult, op1=ALU.add), [xt, ym], [ym])
                yield
                st = str_()
                yield from self.ln_stats_gen(ym, st)
                V(lambda e, ym=ym, st=st: e.tensor_scalar(out=ym[:, :], in0=ym[:, :], scalar1=st[:, 12:13], scalar2=st[:, 14:15], op0=ALU.subtract, op1=ALU.mult), [ym, st], [ym])
                yield
                V(lambda e, ym=ym: e.tensor_tensor(out=ym[:, :], in0=ym[:, :], in1=lng[:, 0, :], op=ALU.mult), [ym, lng], [ym])
                yield
                V(lambda e, ym=ym, xt=xt: e.tensor_tensor(out=xt[:, :], in0=ym[:, :], in1=lng[:, 1, :], op=ALU.add), [ym, lng], [xt])
                yield
                if last:
                    S.dma('sync', self.out[(i - 2) * 128:(i - 1) * 128, :], xt[:, :], self.out, xt, part=True)
                else:
                    S.dma('sync', xres[rows, :], xt[:, :], xres, xt, part=True)

            tiles = list(range(0 if ctx_out else 2, NTILE))
            for t0_ in range(0, len(tiles), 2):
                interleave([tile_chain(t_) for t_ in tiles[t0_:t0_ + 2]])
    def range_reduce(self, eng, x, k):
        S = self.S
        S.op(eng, lambda e: e.tensor_scalar(out=k[:, :], in0=x[:, :], scalar1=float(1.0 / TWO_PI), scalar2=MAGIC, op0=ALU.mult, op1=ALU.add), [x], [k])
        S.op(eng, lambda e: e.tensor_scalar(out=k[:, :], in0=k[:, :], scalar1=-MAGIC, scalar2=None, op0=ALU.add), [k], [k])
        S.op('vector', lambda e: e.scalar_tensor_tensor(out=x[:, :], in0=k[:, :], scalar=-CW1, in1=x[:, :], op0=ALU.mult, op1=ALU.add), [x, k], [x])
        S.op('vector', lambda e: e.scalar_tensor_tensor(out=x[:, :], in0=k[:, :], scalar=-CW2, in1=x[:, :], op0=ALU.mult, op1=ALU.add), [x, k], [x])
        S.op(eng, lambda e: e.tensor_scalar(out=x[:, :], in0=x[:, :], scalar1=PI_LO, scalar2=-PI_LO, op0=ALU.min, op1=ALU.max), [x], [x])

    def rope_tables(self):
        S = self.S
        cosT = S.sbuf([128, NT], F32, 'cosT')
        sinT = S.sbuf([128, NT], F32, 'sinT')
        ang = S.sbuf([128, NLAT], F32, 'ang')
        kk = S.sbuf([128, NLAT], F32, 'kk')
        pidx = S.sbuf([128, 1], I32, 'pidx')
        pf = S.sbuf([128, 4], F32, 'pf')
        ti = S.sbuf([128, 2], I32, 'ti')
        rowi = S.sbuf([128, NLAT], I32, 'rowi')
        S.op('gpsimd', lambda e: e.iota(pidx[:, :], pattern=[[0, 1]], base=0, channel_multiplier=1), [], [pidx])
        S.op('vector', lambda e: e.tensor_scalar(out=ti[:, 0:1], in0=pidx[:, :], scalar1=15, scalar2=None, op0=ALU.bitwise_and), [pidx], [ti])
        S.op('vector', lambda e: e.tensor_scalar(out=ti[:, 1:2], in0=pidx[:, :], scalar1=5, scalar2=1, op0=ALU.logical_shift_right, op1=ALU.bitwise_and), [pidx], [ti])
        S.op('vector', lambda e: e.tensor_copy(out=pf[:, 0:2], in_=ti[:, 0:2]), [ti], [pf])
        S.op('scalar', lambda e: e.activation(out=pf[:, 2:3], in_=pf[:, 0:1], func=AF.Exp, scale=-math.log(10000.0) / 16.0), [pf], [pf])
        S.op('vector', lambda e: e.tensor_tensor(out=pf[:, 3:4], in0=pf[:, 1:2], in1=pf[:, 2:3], op=ALU.mult), [pf], [pf])
        S.op('vector', lambda e: e.tensor_tensor(out=pf[:, 0:1], in0=pf[:, 2:3], in1=pf[:, 3:4], op=ALU.subtract), [pf], [pf])
        S.op('gpsimd', lambda e: e.iota(rowi[:, :], pattern=[[1, 64], [0, 64]], base=0, channel_multiplier=0), [], [rowi])
        S.op('vector', lambda e: e.tensor_copy(out=ang[:, :], in_=rowi[:, :]), [rowi], [ang])
        S.op('vector', lambda e: e.tensor_scalar(out=ang[:, :], in0=ang[:, :], scalar1=pf[:, 0:1], scalar2=None, op0=ALU.mult), [ang, pf], [ang])
        S.op('gpsimd', lambda e: e.iota(rowi[:, :], pattern=[[0, 64], [1, 64]], base=0, channel_multiplier=0), [ang], [rowi])
        S.op('vector', lambda e: e.tensor_copy(out=kk[:, :], in_=rowi[:, :]), [rowi], [kk])
        S.op('vector', lambda e: e.scalar_tensor_tensor(out=ang[:, :], in0=kk[:, :], scalar=pf[:, 3:4], in1=ang[:, :], op0=ALU.mult, op1=ALU.add), [kk, pf, ang], [ang])
        self.range_reduce('vector', ang, kk)
        S.op('scalar', lambda e: e.activation(out=sinT[:, NCTX:NT], in_=ang[:, :], func=AF.Sin), [ang], [sinT])
        S.op('scalar', lambda e: e.activation(out=kk[:, :], in_=ang[:, :], func=AF.Sin, scale=0.5), [ang], [kk])
        S.op('vector', lambda e: e.tensor_tensor(out=kk[:, :], in0=kk[:, :], in1=kk[:, :], op=ALU.mult), [kk], [kk])
        S.op('vector', lambda e: e.tensor_scalar(out=cosT[:, NCTX:NT], in0=kk[:, :], scalar1=-2.0, scalar2=1.0, op0=ALU.mult, op1=ALU.add), [kk], [cosT])
        S.op('gpsimd', lambda e: e.memset(cosT[:, 0:NCTX], 1.0), [], [cosT])
        S.op('gpsimd', lambda e: e.memset(sinT[:, 0:NCTX], 0.0), [], [sinT])
        return cosT, sinT


def make_in_maps(inputs):
    maps = []
    for core in range(8):
        s = core % 4
        m = {'x': np.ascontiguousarray(inputs['x'][s]), 'c': np.ascontiguousarray(inputs['c'][s]),
             'ctx': np.ascontiguousarray(inputs['ctx'][s]), 'c_ctx': np.ascontiguousarray(inputs['c_ctx'])}
        for nm, _ in W_NAMES:
            m[nm] = np.ascontiguousarray(inputs[nm])
        maps.append(m)
    return maps


def kernel(**inputs):
    inputs = {k: np.asarray(v, dtype=np.float32) for k, v in inputs.items()}
    prog = Prog()
    nc = prog.build()
    res = run_bass_kernel_spmd(nc, make_in_maps(inputs), core_ids=list(range(8)))
    out = np.stack([np.asarray(res.results[s]['out'], dtype=np.float32) for s in range(4)], axis=0)
    return out
```

```python
import math
from contextlib import ExitStack, contextmanager
import numpy as np
import concourse.bass as bass
import concourse.mybir as mybir
from concourse.bass_utils import run_bass_kernel_spmd

F32 = mybir.dt.float32
BF16 = mybir.dt.bfloat16
I32 = mybir.dt.int32
AF = mybir.ActivationFunctionType
ALU = mybir.AluOpType
AX = mybir.AxisListType

ENGS = ['tensor', 'vector', 'scalar', 'gpsimd', 'sync']
EPOCH = 30000
DEPOCH = 28800

D = 1024
NCTX = 256
NLAT = 4096
NT = NCTX + NLAT
NTILE = NT // 128
DIN = 9216
DEPTH = 2
NEXP = 16
DEXP = 2816
LN_EPS = 1e-6
RMS_EPS = 1e-5
ALPHA = (2.0 * DEPTH) ** 0.25
TWO_PI = 2.0 * math.pi
CW1 = 6.28125
CW2 = float(np.float32(TWO_PI - CW1))
MAGIC = 12582912.0
PI_LO = 3.1415925


class DSem:
    def __init__(self, sched, name, persistent=False):
        self.sched = sched
        self.name = name
        self.sems = []
        self.persistent = persistent
        sched.dsems.append(self)
        if not persistent and sched.phase_dsems:
            sched.phase_dsems[-1].append(self)

    def next_inc(self):
        if not self.sems or self.sems[-1][1] + 16 > DEPOCH:
            self.sems.append(self.sched.take_sem())
        self.sems[-1][1] += 16
        return self.sems[-1][0]


class Buf:
    def __init__(self, t, name='', grp=None, persistent=False):
        self.persistent = persistent
        self.t = t
        self.name = name
        self.w = None
        self.r = {}
        self.grp = grp

    def __getitem__(self, idx):
        return self.t[idx]


class Sched:
    def __init__(self, nc, es):
        self.nc = nc
        self.es = es
        self.q = {e: [] for e in ENGS}
        self.esems = {e: [] for e in ENGS}
        self.cnt = {e: 0 for e in ENGS}
        self.waited = {e: {} for e in ENGS}
        self.nsem = 0
        self.nins = 0
        self.dsems = []
        self.alloc = es
        self.uid = 0
        self.sem_pool = []
        self.selfwait = True
        self.phase_dsems = []

    def take_sem(self):
        while self.sem_pool:
            ent = self.sem_pool.pop()
            if ent[1] + 16 * 64 <= DEPOCH:
                return ent
        return [self.new_sem(), 0]

    def new_sem(self):
        self.nsem += 1
        return self.es.enter_context(self.nc.semaphore('sm%d' % self.nsem))

    def sbuf(self, shape, dt, name, grp=None):
        self.uid += 1
        nm = '%s_%d' % (name, self.uid)
        t = self.alloc.enter_context(self.nc.sbuf_tensor(nm, list(shape), dt))
        return Buf(t, nm, grp)

    def psum(self, shape, dt, name):
        self.uid += 1
        nm = '%s_%d' % (name, self.uid)
        t = self.alloc.enter_context(self.nc.psum_tensor(nm, list(shape), dt))
        return Buf(t, nm)

    def dram(self, shape, dt, name, grp=None, kind="Internal"):
        t = self.nc.dram_tensor(name, list(shape), dt, kind=kind)
        return Buf(t, name, grp, persistent=True)

    def ring(self, n, shape, dt, name, psum=False):
        bufs = [(self.psum if psum else self.sbuf)(shape, dt, '%s%d' % (name, i)) for i in range(n)]
        st = {'i': 0}

        def nxt():
            b = bufs[st['i'] % n]
            st['i'] += 1
            return b
        return nxt

    def _esem(self, eng, epoch):
        lst = self.esems[eng]
        while len(lst) <= epoch:
            lst.append(self.new_sem())
        return lst[epoch]

    def _collect(self, reads, writes):
        toks = []
        for b in reads:
            if b.w is not None:
                toks.append(b.w)
        for b in writes:
            if b.w is not None:
                toks.append(b.w)
            toks.extend(b.r.values())
        return toks

    def _waits(self, eng, toks):
        need = {}
        for tk in toks:
            if tk[0] == 'e':
                _, pe, ep, seq = tk
                if pe == eng and (eng == 'tensor' or not self.selfwait):
                    continue
                key = ('e', pe, ep)
                if need.get(key, (0,))[0] < seq:
                    need[key] = (seq, self._esem(pe, ep))
            else:
                ds = tk[1]
                for i, (sem, c) in enumerate(ds.sems):
                    key = ('d', id(ds), i)
                    if need.get(key, (0,))[0] < c:
                        need[key] = (c, sem)
        wd = self.waited[eng]
        for key, (val, sem) in need.items():
            if wd.get(key, 0) >= val:
                continue
            wd[key] = val
            self.q[eng].append(lambda e, sem=sem, val=val: e.wait_ge(sem, val))

    def _tok(self, eng):
        c = self.cnt[eng]
        if c == 0:
            return None
        ep, seq = divmod(c - 1, EPOCH)
        return ('e', eng, ep, seq + 1)

    def op(self, eng, fn, reads=(), writes=()):
        self._waits(eng, self._collect(reads, writes))
        self.cnt[eng] += 1
        tok = self._tok(eng)
        sem = self._esem(eng, tok[2])
        self.q[eng].append(lambda e, fn=fn, sem=sem: fn(e).then_inc(sem, 1))
        for b in writes:
            b.w = tok
            b.r = {}
        for b in reads:
            if b.w is not tok:
                b.r[eng] = tok
        self.nins += 1
        return tok

    def dma(self, eng, out_ap, in_ap, dst, src, part=False, **kw):
        if dst.grp is None:
            dst.grp = DSem(self, dst.name, persistent=dst.persistent)
        ds = dst.grp
        if part and dst.w is not None and dst.w[0] == 'd' and dst.w[1] is ds:
            toks = self._collect([src], [])
            toks.extend(dst.r.values())
            self._waits(eng, toks)
        else:
            self._waits(eng, self._collect([src], [dst]))
        sem = ds.next_inc()
        self.q[eng].append(lambda e, sem=sem: e.dma_start(out=out_ap, in_=in_ap, **kw).then_inc(sem, 16))
        tok = ('d', ds)
        dst.w = tok
        dst.r = {}
        src.r[('d', id(ds))] = tok
        self.nins += 1
        return tok

    def wait_buf(self, eng, buf):
        self._waits(eng, self._collect([buf], []))

    def barrier(self):
        toks = [t for t in (self._tok(e) for e in ENGS) if t is not None]
        toks += [('d', ds) for ds in self.dsems]
        for e in ENGS:
            self._waits(e, toks)

    def flush(self):
        if not any(self.q.values()):
            return
        with self.nc.Block() as block:
            for en in ENGS:
                lst = self.q[en]
                if not lst:
                    continue

                def f(e, lst=lst):
                    for fn in lst:
                        fn(e)
                getattr(block, en)(f)
        self.q = {e: [] for e in ENGS}

    @contextmanager
    def phase(self):
        prev = self.alloc
        self.barrier()
        self.phase_dsems.append([])
        with ExitStack() as ph:
            self.alloc = ph
            yield
            self.barrier()
            self.flush()
        for ds in self.phase_dsems.pop():
            if ds.sems:
                self.sem_pool.append(ds.sems[-1])
        self.alloc = prev


def interleave(gens):
    gens = list(gens)
    while gens:
        for g in list(gens):
            try:
                next(g)
            except StopIteration:
                gens.remove(g)


W_NAMES = [('w_ada', [DEPTH, D, 6 * D]), ('b_ada', [DEPTH, 6 * D]), ('w_in', [DEPTH, D, DIN]),
           ('conv_w', [DEPTH, 31, D]), ('conv_b', [DEPTH, D]), ('conv_ln_g', [DEPTH, D]), ('conv_ln_b', [DEPTH, D]),
           ('w_conv_out', [DEPTH, D, D]), ('ssm_a_re', [DEPTH, 2, 64, 64]), ('ssm_a_im', [DEPTH, 2, 64, 64]),
           ('ssm_log_dt', [DEPTH, 2, 64]), ('ssm_b_re', [DEPTH, 2, 64, 64, 16]), ('ssm_b_im', [DEPTH, 2, 64, 64, 16]),
           ('ssm_c_re', [DEPTH, 2, 64, 16, 64]), ('ssm_c_im', [DEPTH, 2, 64, 16, 64]), ('ssm_d', [DEPTH, D]),
           ('w_ssm_glu', [DEPTH, D, D]), ('w_ssm_out', [DEPTH, D, D]), ('attn_lambda', [DEPTH, 4, 64]),
           ('attn_subln_g', [DEPTH, 128]), ('w_attn_out', [DEPTH, D, D]), ('w_o', [DEPTH, D, D]),
           ('ln1_g', [DEPTH, D]), ('ln1_b', [DEPTH, D]), ('w_router', [DEPTH, D, NEXP]),
           ('w_gate_up', [DEPTH, NEXP, D, 2 * DEXP]), ('w_down', [DEPTH, NEXP, DEXP, D]),
           ('ln2_g', [DEPTH, D]), ('ln2_b', [DEPTH, D])]


class Prog:
    def __init__(self, debug=(), stop=None, nlayers=DEPTH, skip_inputs=(), phases=None):
        self.debug = set(debug)
        self.phase_names = phases
        self.stop = stop
        self.nlayers = nlayers
        self.nc = bass.Bass("TRN2", target_bir_lowering=False)
        nc = self.nc
        self.I = {}
        for nm, shp in [('x', [NLAT, D]), ('c', [D]), ('ctx', [NCTX, D]), ('c_ctx', [D])] + W_NAMES:
            if nm in skip_inputs:
                continue
            self.I[nm] = Buf(nc.dram_tensor(nm, shp, F32, kind="ExternalInput"), nm)
        self.out = Buf(nc.dram_tensor("out", [NLAT, D], F32, kind="ExternalOutput"), 'out', persistent=True)
        self.scr = {}

    def scratch(self, name, shape, dt):
        if name not in self.scr:
            kind = "ExternalOutput" if name in self.debug else "Internal"
            self.scr[name] = self.S.dram(shape, dt, name, kind=kind)
        return self.scr[name]

    def build(self):
        nc = self.nc
        with ExitStack() as es:
            self.S = S = Sched(nc, es)
            self.consts()
            xres = self.scratch('xres', [NT, D], F32)
            S.dma('sync', xres[0:NCTX, :], self.I['ctx'][:, :], xres, self.I['ctx'])
            S.dma('sync', xres[NCTX:NT, :], self.I['x'][:, :], xres, self.I['x'])
            done = False
            for l in range(self.nlayers):
                plist = [self.ph_mod, self.ph_inproj, self.ph_convout, self.ph_attn, self.ph_ssm, self.ph_merge, self.ph_moe, self.ph_ln2]
                if self.phase_names is not None:
                    plist = [p for p in plist if p.__name__ in self.phase_names]
                for ph in plist:
                    ph(l)
                    if self.stop == (ph.__name__, l):
                        done = True
                        break
                if done:
                    break
            S.barrier()
            S.flush()
        return nc

    def consts(self):
        S = self.S
        self.ident_f = S.sbuf([128, 128], F32, 'ident_f')
        self.ident_b = S.sbuf([128, 128], BF16, 'ident_b')
        ones = S.sbuf([128, 128], F32, 'ones_f')
        self.ones_f = ones
        idf, idb = self.ident_f, self.ident_b
        S.op('gpsimd', lambda e: e.memset(ones[:, :], 1.0), writes=[ones])
        S.op('gpsimd', lambda e: e.affine_select(out=idf[:, :], in_=ones[:, :], pattern=[[1, 128]], compare_op=ALU.is_equal,
                                                 fill=0.0, base=0, channel_multiplier=-1), reads=[ones], writes=[idf])
        S.op('vector', lambda e: e.tensor_copy(out=idb[:, :], in_=idf[:, :]), reads=[idf], writes=[idb])

    def ph_mod(self, l):
        S = self.S
        I = self.I
        modscr = self.scratch('modscr%d' % l, [2, 128, 6 * D], F32)
        with S.phase():
            cT = S.sbuf([128, 2, 8], F32, 'cT')
            sil = S.sbuf([128, 2, 8], F32, 'sil')
            bc = S.sbuf([128, 2, 8, 128], F32, 'bc')
            bada = S.sbuf([128, 6 * D], F32, 'bada')
            S.dma('sync', cT[:, 0, :], I['c'].t.ap().rearrange("(k p) -> p k", p=128), cT, I['c'], allow_slow_non_contiguous=True)
            S.dma('sync', cT[:, 1, :], I['c_ctx'].t.ap().rearrange("(k p) -> p k", p=128), cT, I['c_ctx'], allow_slow_non_contiguous=True)
            S.dma('sync', bada[:, :], I['b_ada'][l:l + 1, :].partition_broadcast(128), bada, I['b_ada'])
            S.op('scalar', lambda e: e.activation(out=sil[:, :, :], in_=cT[:, :, :], func=AF.Silu), [cT], [sil])
            for w in range(2):
                S.op('vector', lambda e, w=w: e.tensor_copy(out=bc[:, w, :, :], in_=sil[:, w, :].unsqueeze(2).to_broadcast([128, 8, 128])), [sil], [bc])
            wring = S.ring(2, [128, 8, 512], F32, 'wada')
            pring = S.ring(4, [128, 512], F32, 'pmod', psum=True)
            mring = S.ring(4, [128, 512], F32, 'modt')
            wv = I['w_ada'].t.ap()[l].rearrange("(k p) n -> p k n", p=128)
            for nb in range(12):
                n0 = nb * 512
                wa = wring()
                S.dma('sync', wa[:, :, :], wv[:, :, n0:n0 + 512], wa, I['w_ada'])
                for w in range(2):
                    ps = pring()
                    for k in range(8):
                        S.op('tensor', lambda e, ps=ps, wa=wa, w=w, k=k: e.matmul(ps[:, :], lhsT=bc[:, w, k, :], rhs=wa[:, k, :], start=(k == 0), stop=(k == 7)), [bc, wa], [ps])
                    mt = mring()
                    S.op('vector', lambda e, ps=ps, mt=mt, n0=n0: e.tensor_tensor(out=mt[:, :], in0=ps[:, :], in1=bada[:, n0:n0 + 512], op=ALU.add), [ps, bada], [mt])
                    if nb in (2, 3, 8, 9):
                        S.op('vector', lambda e, mt=mt: e.tensor_scalar(out=mt[:, :], in0=mt[:, :], scalar1=1.0, scalar2=None, op0=ALU.add), [mt], [mt])
                    S.dma('gpsimd', modscr[w, :, n0:n0 + 512], mt[:, :], modscr, mt, part=True)

    def ph_inproj(self, l):
        S = self.S
        I = self.I
        modscr = self.scr['modscr%d' % l]
        xres = self.scr['xres']
        gates = self.scratch('gates', [NT, 3 * D], F32)
        qT = self.scratch('qT', [D, NT], BF16)
        kT = self.scratch('kT', [D, NT], BF16)
        uT = self.scratch('uT', [D, NT], F32)
        vtm = self.scratch('vtm', [NT, D], BF16)
        ycT = self.scratch('ycT', [D, NT], F32)
        with S.phase():
            hT = S.sbuf([128, 8, NT], BF16, 'hT')
            with S.phase():
                modv = S.sbuf([128, 2, 2, D], F32, 'modv')
                for w in range(2):
                    S.dma('sync', modv[:, w, 0, :], modscr[w, :, 0:D], modv, modscr)
                    S.dma('sync', modv[:, w, 1, :], modscr[w, :, D:2 * D], modv, modscr)
                xring = S.ring(3, [128, D], F32, 'xt')
                tring = S.ring(2, [128, D], F32, 'tt')
                hring = S.ring(2, [128, D], BF16, 'hb')
                sring = S.ring(2, [128, 16], F32, 'st')
                pring = S.ring(2, [128, 8, 128], BF16, 'pT', psum=True)
                for i in range(NTILE):
                    w = 1 if i < 2 else 0
                    xt = xring()
                    S.dma('sync', xt[:, :], xres[i * 128:(i + 1) * 128, :], xt, xres)
                    st = sring()
                    S.op('vector', lambda e, xt=xt, st=st: e.bn_stats(out=st[:, 0:6], in_=xt[:, 0:512]), [xt], [st])
                    S.op('vector', lambda e, xt=xt, st=st: e.bn_stats(out=st[:, 6:12], in_=xt[:, 512:1024]), [xt], [st])
                    S.op('vector', lambda e, st=st: e.bn_aggr(out=st[:, 12:14], in_=st[:, 0:12]), [st], [st])
                    self.rstd(st, 13, 14, LN_EPS)
                    tt = tring()
                    S.op('vector', lambda e, xt=xt, st=st, tt=tt: e.tensor_scalar(out=tt[:, :], in0=xt[:, :], scalar1=st[:, 12:13], scalar2=st[:, 14:15], op0=ALU.subtract, op1=ALU.mult), [xt, st], [tt])
                    S.op('vector', lambda e, tt=tt, w=w: e.tensor_tensor(out=tt[:, :], in0=tt[:, :], in1=modv[:, w, 1, :], op=ALU.mult), [tt, modv], [tt])
                    hb = hring()
                    S.op('vector', lambda e, tt=tt, hb=hb, w=w: e.tensor_tensor(out=hb[:, :], in0=tt[:, :], in1=modv[:, w, 0, :], op=ALU.add), [tt, modv], [hb])
                    pT = pring()
                    for k in range(8):
                        S.op('tensor', lambda e, pT=pT, hb=hb, k=k: e.transpose(out=pT[:, k, :], in_=hb[:, k * 128:(k + 1) * 128], identity=self.ident_b[:, :]), [hb, self.ident_b], [pT])
                    S.op('scalar', lambda e, pT=pT, i=i: e.copy(out=hT[:, :, i * 128:(i + 1) * 128], in_=pT[:, :, :]), [pT], [hT])
            if 'hT' in self.debug:
                hdbg = self.scratch('hT', [128, 8, NT], BF16)
                S.dma('sync', hdbg[:, :, :], hT[:, :, :], hdbg, hT)
            self.inproj_blocks(l, hT, gates, qT, kT, uT, vtm, ycT)

    def rstd(self, st, ci, co, eps):
        S = self.S
        S.op('vector', lambda e: e.tensor_scalar(out=st[:, co:co + 1], in0=st[:, ci:ci + 1], scalar1=eps, scalar2=None, op0=ALU.add), [st], [st])
        S.op('scalar', lambda e: e.activation(out=st[:, co:co + 1], in_=st[:, co:co + 1], func=AF.Sqrt), [st], [st])
        S.op('vector', lambda e: e.reciprocal(out=st[:, co:co + 1], in_=st[:, co:co + 1]), [st], [st])

    def load_w_bf16(self, dst, wap, src):
        self.S.dma('gpsimd', dst[:, :, :], wap, dst, src)

    def inproj_blocks(self, l, hT, gates, qT, kT, uT, vtm, ycT):
        S = self.S
        I = self.I
        wv = I['w_in'].t.ap()[l].rearrange("(k p) n -> p k n", p=128)
        TG = [(t0, min(t0 + 512, NT)) for t0 in range(0, NT, 512)]

        def mm_fm(ps, wb, cc, t0, t1):
            for k in range(8):
                S.op('tensor', lambda e, k=k: e.matmul(ps[:, 0:t1 - t0], lhsT=wb[:, k, cc * 128:(cc + 1) * 128], rhs=hT[:, k, t0:t1], start=(k == 0), stop=(k == 7)), [wb, hT], [ps])

        def mm_tm(ps, wb, i):
            for k in range(8):
                S.op('tensor', lambda e, k=k: e.matmul(ps[:, :], lhsT=hT[:, k, i * 128:(i + 1) * 128], rhs=wb[:, k, :], start=(k == 0), stop=(k == 7)), [wb, hT], [ps])

        with S.phase():
            wring = S.ring(2, [128, 8, 512], BF16, 'wb')
            pring = S.ring(4, [128, 512], F32, 'pz', psum=True)
            gring = S.ring(3, [128, 512], F32, 'gst')
            vring = S.ring(3, [128, 512], BF16, 'vst')
            for nb in list(range(4, 10)) + [16, 17]:
                wb = wring()
                self.load_w_bf16(wb, wv[:, :, nb * 512:(nb + 1) * 512], I['w_in'])
                for i in range(NTILE):
                    ps = pring()
                    mm_tm(ps, wb, i)
                    if nb < 10:
                        g = gring()
                        S.op('scalar', lambda e, ps=ps, g=g: e.activation(out=g[:, :], in_=ps[:, :], func=AF.Sigmoid), [ps], [g])
                        c0 = (nb - 4) * 512
                        S.dma('sync', gates[i * 128:(i + 1) * 128, c0:c0 + 512], g[:, :], gates, g, part=True)
                    else:
                        v = vring()
                        S.op('vector', lambda e, ps=ps, v=v: e.tensor_copy(out=v[:, :], in_=ps[:, :]), [ps], [v])
                        c0 = (nb - 16) * 512
                        S.dma('sync', vtm[i * 128:(i + 1) * 128, c0:c0 + 512], v[:, :], vtm, v, part=True)
        with S.phase():
            wring = S.ring(2, [128, 8, 512], BF16, 'wb')
            pring = S.ring(4, [128, 512], F32, 'pz', psum=True)
            uring = S.ring(3, [128, 512], F32, 'ust')
            for nb in (12, 13):
                wb = wring()
                self.load_w_bf16(wb, wv[:, :, nb * 512:(nb + 1) * 512], I['w_in'])
                for cc in range(4):
                    r0 = (nb - 12) * 512 + cc * 128
                    for (t0, t1) in TG:
                        ps = pring()
                        mm_fm(ps, wb, cc, t0, t1)
                        u = uring()
                        S.op('scalar', lambda e, ps=ps, u=u, n=t1 - t0: e.copy(out=u[:, 0:n], in_=ps[:, 0:n]), [ps], [u])
                        S.dma('sync', uT[r0:r0 + 128, t0:t1], u[:, 0:t1 - t0], uT, u, part=True)
        with S.phase():
            cosT, sinT = self.rope_tables()
            wring = S.ring(2, [128, 8, 512], BF16, 'wb')
            rring = S.ring(2, [128, 8, 512], BF16, 'wr')
            pring = S.ring(6, [128, 512], F32, 'pz', psum=True)
            t1ring = S.ring(2, [128, 512], F32, 'rt1')
            t2ring = S.ring(2, [128, 512], F32, 'rt2')
            oring = S.ring(3, [128, 512], BF16, 'rout')
            for nb in (10, 11, 14, 15):
                dst = qT if nb < 12 else kT
                base = (nb - 10) * 512 if nb < 12 else (nb - 14) * 512
                wb = wring()
                self.load_w_bf16(wb, wv[:, :, nb * 512:(nb + 1) * 512], I['w_in'])
                wr = rring()
                wb4 = wb.t.ap().rearrange("p k (g h f) -> p (k g) h f", h=2, f=16)
                wr4 = wr.t.ap().rearrange("p k (g h f) -> p (k g) h f", h=2, f=16)
                S.op('vector', lambda e, wb4=wb4, wr4=wr4: e.tensor_scalar(out=wr4[:, :, 0, :], in0=wb4[:, :, 1, :], scalar1=-1.0, scalar2=None, op0=ALU.mult), [wb], [wr])
                S.op('vector', lambda e, wb4=wb4, wr4=wr4: e.tensor_copy(out=wr4[:, :, 1, :], in_=wb4[:, :, 0, :]), [wb], [wr])
                for cc in range(4):
                    r0 = base + cc * 128
                    for (t0, t1) in TG:
                        n = t1 - t0
                        ps = pring()
                        mm_fm(ps, wb, cc, t0, t1)
                        pr = pring()
                        mm_fm(pr, wr, cc, t0, t1)
                        a = t1ring()
                        b = t2ring()
                        S.op('vector', lambda e, ps=ps, a=a, t0=t0, t1=t1, n=n: e.tensor_tensor(out=a[:, 0:n], in0=ps[:, 0:n], in1=cosT[:, t0:t1], op=ALU.mult), [ps, cosT], [a])
                        S.op('vector', lambda e, pr=pr, b=b, t0=t0, t1=t1, n=n: e.tensor_tensor(out=b[:, 0:n], in0=pr[:, 0:n], in1=sinT[:, t0:t1], op=ALU.mult), [pr, sinT], [b])
                        o = oring()
                        S.op('vector', lambda e, a=a, b=b, o=o, n=n: e.tensor_tensor(out=o[:, 0:n], in0=a[:, 0:n], in1=b[:, 0:n], op=ALU.add), [a, b], [o])
                        S.dma('sync', dst[r0:r0 + 128, t0:t1], o[:, 0:n], dst, o, part=True)
        with S.phase():
            wa_r = S.ring(2, [128, 8, 512], BF16, 'wa')
            wb_r = S.ring(2, [128, 8, 512], BF16, 'wbb')
            pring = S.ring(6, [128, 512], F32, 'pz', psum=True)
            sgring = S.ring(2, [128, 512], F32, 'sg')
            GW = 15 + NCTX + 15 + NLAT + 15
            OW = NCTX + 15 + NLAT
            gpad_r = S.ring(2, [128, GW], F32, 'gpad')
            acc_r = S.ring(2, [128, OW], F32, 'cacc')
            cw = S.sbuf([128, 8, 31], F32, 'cw')
            cb = S.sbuf([128, 8], F32, 'cb')
            for j in range(8):
                S.dma('sync', cw[:, j, :], I['conv_w'].t.ap()[l].rearrange("k (j p) -> j p k", p=128)[j], cw, I['conv_w'], allow_slow_non_contiguous=True)
            S.dma('sync', cb[:, :], I['conv_b'].t.ap()[l].rearrange("(j p) -> p j", p=128), cb, I['conv_b'], allow_slow_non_contiguous=True)
            for sb in range(2):
                wa = wa_r()
                wb = wb_r()
                self.load_w_bf16(wa, wv[:, :, sb * 512:(sb + 1) * 512], I['w_in'])
                self.load_w_bf16(wb, wv[:, :, (2 + sb) * 512:(3 + sb) * 512], I['w_in'])
                for cc in range(4):
                    j = sb * 4 + cc
                    gp = gpad_r()
                    S.op('gpsimd', lambda e, gp=gp: e.memset(gp[:, 0:15], 0.0), [], [gp])
                    S.op('gpsimd', lambda e, gp=gp: e.memset(gp[:, 15 + NCTX:30 + NCTX], 0.0), [], [gp])
                    S.op('gpsimd', lambda e, gp=gp: e.memset(gp[:, GW - 15:GW], 0.0), [], [gp])
                    for (t0, t1) in TG:
                        n = t1 - t0
                        pa = pring()
                        mm_fm(pa, wa, cc, t0, t1)
                        pb = pring()
                        mm_fm(pb, wb, cc, t0, t1)
                        sg = sgring()
                        S.op('scalar', lambda e, pb=pb, sg=sg, n=n: e.activation(out=sg[:, 0:n], in_=pb[:, 0:n], func=AF.Sigmoid), [pb], [sg])
                        segs = []
                        if t0 < NCTX:
                            segs.append((t0, min(t1, NCTX), 15))
                        if t1 > NCTX:
                            segs.append((max(t0, NCTX), t1, 30))
                        for (a0, a1, off) in segs:
                            S.op('vector', lambda e, pa=pa, sg=sg, gp=gp, a0=a0, a1=a1, off=off, t0=t0: e.tensor_tensor(out=gp[:, off + a0:off + a1], in0=pa[:, a0 - t0:a1 - t0], in1=sg[:, a0 - t0:a1 - t0], op=ALU.mult), [pa, sg], [gp])
                    acc = acc_r()
                    S.op('vector', lambda e, gp=gp, acc=acc, j=j: e.tensor_scalar(out=acc[:, :], in0=gp[:, 0:OW], scalar1=cw[:, j, 0:1], scalar2=cb[:, j:j + 1], op0=ALU.mult, op1=ALU.add), [gp, cw, cb], [acc])
                    for kk in range(1, 31):
                        S.op('vector', lambda e, gp=gp, acc=acc, j=j, kk=kk: e.scalar_tensor_tensor(out=acc[:, :], in0=gp[:, kk:kk + OW], scalar=cw[:, j, kk:kk + 1], in1=acc[:, :], op0=ALU.mult, op1=ALU.add), [gp, cw, acc], [acc])
                    S.dma('sync', ycT[j * 128:(j + 1) * 128, 0:NCTX], acc[:, 0:NCTX], ycT, acc, part=True)
                    S.dma('sync', ycT[j * 128:(j + 1) * 128, NCTX:NT], acc[:, NCTX + 15:OW], ycT, acc, part=True)


    def ph_convout(self, l):
        S = self.S
        I = self.I
        ycT = self.scratch('ycT', [D, NT], F32)
        gates = self.scratch('gates', [NT, 3 * D], F32)
        y0 = self.scratch('y0', [NT, D], F32)
        ycv = ycT.t.ap().rearrange("(j p) t -> p j t", p=128)
        TG = [(t0, min(t0 + 512, NT)) for t0 in range(0, NT, 512)]
        with S.phase():
            wc = S.sbuf([128, 8, D], BF16, 'wc')
            wcv = I['w_conv_out'].t.ap()[l].rearrange("(k p) n -> p k n", p=128)
            for hh in range(2):
                S.dma('gpsimd', wc[:, :, hh * 512:(hh + 1) * 512], wcv[:, :, hh * 512:(hh + 1) * 512], wc, I['w_conv_out'], part=True)
            gb = S.sbuf([128, 2, 8], F32, 'gb')
            S.dma('sync', gb[:, 0, :], I['conv_ln_g'].t.ap()[l].rearrange("(j p) -> p j", p=128), gb, I['conv_ln_g'], allow_slow_non_contiguous=True)
            S.dma('sync', gb[:, 1, :], I['conv_ln_b'].t.ap()[l].rearrange("(j p) -> p j", p=128), gb, I['conv_ln_b'], allow_slow_non_contiguous=True, part=True)
            ycr = S.ring(2, [128, 8, 512], F32, 'yc')
            sqr = S.ring(1, [128, 8, 512], F32, 'sq')
            ps1r = S.ring(2, [128, 512], F32, 'ps1', psum=True)
            ps2r = S.ring(2, [128, 512], F32, 'ps2', psum=True)
            pyr = S.ring(4, [128, 512], F32, 'py', psum=True)
            mr = S.ring(2, [128, 512], F32, 'mean')
            rr = S.ring(2, [128, 512], F32, 'rstd')
            tr = S.ring(2, [128, 512], F32, 'tn')
            ar = S.ring(2, [128, 8, 512], BF16, 'aT')
            gr = S.ring(2, [128, D], F32, 'g0')
            yr = S.ring(2, [128, D], F32, 'y0t')
            for (t0, t1) in TG:
                n = t1 - t0
                yc = ycr()
                S.dma('sync', yc[:, :, 0:n], ycv[:, :, t0:t1], yc, ycT)
                sq = sqr()
                S.op('scalar', lambda e, yc=yc, sq=sq, n=n: e.activation(out=sq[:, :, 0:n], in_=yc[:, :, 0:n], func=AF.Square), [yc], [sq])
                p1 = ps1r()
                p2 = ps2r()
                for j in range(8):
                    S.op('tensor', lambda e, p1=p1, yc=yc, j=j, n=n: e.matmul(p1[:, 0:n], lhsT=self.ones_f[:, :], rhs=yc[:, j, 0:n], start=(j == 0), stop=(j == 7)), [yc, self.ones_f], [p1])
                for j in range(8):
                    S.op('tensor', lambda e, p2=p2, sq=sq, j=j, n=n: e.matmul(p2[:, 0:n], lhsT=self.ones_f[:, :], rhs=sq[:, j, 0:n], start=(j == 0), stop=(j == 7)), [sq, self.ones_f], [p2])
                mean = mr()
                rstd = rr()
                S.op('scalar', lambda e, p1=p1, mean=mean, n=n: e.mul(out=mean[:, 0:n], in_=p1[:, 0:n], mul=1.0 / D), [p1], [mean])
                S.op('vector', lambda e, mean=mean, rstd=rstd, n=n: e.tensor_tensor(out=rstd[:, 0:n], in0=mean[:, 0:n], in1=mean[:, 0:n], op=ALU.mult), [mean], [rstd])
                S.op('vector', lambda e, p2=p2, rstd=rstd, n=n: e.scalar_tensor_tensor(out=rstd[:, 0:n], in0=p2[:, 0:n], scalar=1.0 / D, in1=rstd[:, 0:n], op0=ALU.mult, op1=ALU.subtract), [p2, rstd], [rstd])
                S.op('vector', lambda e, rstd=rstd, n=n: e.tensor_scalar(out=rstd[:, 0:n], in0=rstd[:, 0:n], scalar1=LN_EPS, scalar2=None, op0=ALU.add), [rstd], [rstd])
                S.op('scalar', lambda e, rstd=rstd, n=n: e.activation(out=rstd[:, 0:n], in_=rstd[:, 0:n], func=AF.Sqrt), [rstd], [rstd])
                S.op('vector', lambda e, rstd=rstd, n=n: e.reciprocal(out=rstd[:, 0:n], in_=rstd[:, 0:n]), [rstd], [rstd])
                aT = ar()
                for j in range(8):
                    tn = tr()
                    S.op('vector', lambda e, yc=yc, mean=mean, tn=tn, j=j, n=n: e.tensor_tensor(out=tn[:, 0:n], in0=yc[:, j, 0:n], in1=mean[:, 0:n], op=ALU.subtract), [yc, mean], [tn])
                    S.op('vector', lambda e, rstd=rstd, tn=tn, n=n: e.tensor_tensor(out=tn[:, 0:n], in0=tn[:, 0:n], in1=rstd[:, 0:n], op=ALU.mult), [tn, rstd], [tn])
                    S.op('scalar', lambda e, tn=tn, aT=aT, j=j, n=n: e.activation(out=aT[:, j, 0:n], in_=tn[:, 0:n], func=AF.Silu, scale=gb[:, 0, j:j + 1], bias=gb[:, 1, j:j + 1]), [tn, gb], [aT])
                for ii in range(n // 128):
                    i = t0 // 128 + ii
                    g0 = gr()
                    S.dma('sync', g0[:, :], gates[i * 128:(i + 1) * 128, 0:D], g0, gates)
                    yt = yr()
                    for nb in range(2):
                        py = pyr()
                        for j in range(8):
                            S.op('tensor', lambda e, py=py, aT=aT, j=j, ii=ii, nb=nb: e.matmul(py[:, :], lhsT=aT[:, j, ii * 128:(ii + 1) * 128], rhs=wc[:, j, nb * 512:(nb + 1) * 512], start=(j == 0), stop=(j == 7)), [aT, wc], [py])
                        S.op('vector', lambda e, py=py, yt=yt, g0=g0, nb=nb: e.tensor_tensor(out=yt[:, nb * 512:(nb + 1) * 512], in0=py[:, :], in1=g0[:, nb * 512:(nb + 1) * 512], op=ALU.mult), [py, g0], [yt])
                    S.dma('sync', y0[i * 128:(i + 1) * 128, :], yt[:, :], y0, yt, part=True)

    def ph_attn(self, l):
        S = self.S
        I = self.I
        qT = self.scratch('qT', [D, NT], BF16)
        kT = self.scratch('kT', [D, NT], BF16)
        vtm = self.scratch('vtm', [NT, D], BF16)
        oT = self.scratch('oT', [D, NT], BF16)
        gates = self.scratch('gates', [NT, 3 * D], F32)
        y2 = self.scratch('y2', [NT, D], F32)
        lam_init = 0.8 - 0.6 * math.exp(-0.3 * l)
        ctx_out = l < DEPTH - 1
        vv = vtm.t.ap().rearrange("(i p) c -> p i c", p=128)
        with S.phase():
            lamt = S.sbuf([128, 4, 64], F32, 'lamt')
            lw = S.sbuf([128, 8], F32, 'lw')
            gsub = S.sbuf([128, 128], F32, 'gsub')
            S.dma('sync', lamt[:, :, :], I['attn_lambda'][l:l + 1, :, :].partition_broadcast(128), lamt, I['attn_lambda'])
            S.dma('sync', gsub[:, :], I['attn_subln_g'][l:l + 1, :].partition_broadcast(128), gsub, I['attn_subln_g'])
            S.op('vector', lambda e: e.tensor_scalar(out=gsub[:, :], in0=gsub[:, :], scalar1=1.0 - lam_init, scalar2=None, op0=ALU.mult), [gsub], [gsub])
            prod = S.sbuf([128, 2, 64], F32, 'lprod')
            S.op('vector', lambda e: e.tensor_tensor(out=prod[:, 0, :], in0=lamt[:, 0, :], in1=lamt[:, 1, :], op=ALU.mult), [lamt], [prod])
            S.op('vector', lambda e: e.tensor_tensor(out=prod[:, 1, :], in0=lamt[:, 2, :], in1=lamt[:, 3, :], op=ALU.mult), [lamt], [prod])
            S.op('vector', lambda e: e.reduce_sum(out=lw[:, 0:2], in_=prod[:, :, :], axis=AX.X), [prod], [lw])
            S.op('scalar', lambda e: e.activation(out=lw[:, 2:4], in_=lw[:, 0:2], func=AF.Exp), [lw], [lw])
            S.op('vector', lambda e: e.tensor_tensor(out=lw[:, 4:5], in0=lw[:, 3:4], in1=lw[:, 2:3], op=ALU.subtract), [lw], [lw])
            S.op('vector', lambda e: e.tensor_scalar(out=lw[:, 5:6], in0=lw[:, 4:5], scalar1=-lam_init, scalar2=None, op0=ALU.add), [lw], [lw])
            neglam = lw
            qr = S.ring(2, [128, NT], BF16, 'qh')
            kzs = [[S.sbuf([128, NT], BF16, 'kz%d_%d' % (b_, s_)) for s_ in range(2)] for b_ in range(2)]
            for b_ in range(2):
                S.op('gpsimd', lambda e, t=kzs[b_][0]: e.memset(t[64:128, :], 0.0), [], [kzs[b_][0]])
                S.op('gpsimd', lambda e, t=kzs[b_][1]: e.memset(t[0:64, :], 0.0), [], [kzs[b_][1]])
            vr = S.ring(2, [128, NTILE, 129], BF16, 'vh')
            psr = S.ring(3, [128, 512], F32, 'psc', psum=True)
            por = S.ring(4, [128, 512], F32, 'pov', psum=True)
            ptr_ = S.ring(1, [128, 128], BF16, 'ptr', psum=True)
            pr = S.ring(4, [128, 512], BF16, 'pT')
            accr = S.ring(8, [128, 128], F32, 'oacc')
            smr = S.ring(8, [128, 8], F32, 'osm')
            onr = S.ring(3, [128, 128], BF16, 'onb')
            otr = S.ring(3, [128, 512], BF16, 'oTst')
            qgroups = []
            if ctx_out:
                qgroups.append((0, NCTX, 2))
            for g in range(8):
                qgroups.append((NCTX + g * 512, NCTX + (g + 1) * 512, NTILE))
            for h in range(8):
                qh = qr()
                kz = kzs[h % 2]
                vh = vr()
                S.dma('sync', qh[:, :], qT[h * 128:(h + 1) * 128, :], qh, qT)
                S.dma('sync', kz[0][0:64, :], kT[h * 128:h * 128 + 64, :], kz[0], kT)
                S.dma('sync', kz[1][64:128, :], kT[h * 128 + 64:(h + 1) * 128, :], kz[1], kT)
                S.dma('sync', vh[:, :, 0:128], vv[:, :, h * 128:(h + 1) * 128], vh, vtm)
                S.op('gpsimd', lambda e, vh=vh: e.memset(vh[:, :, 128:129], 1.0), [], [vh])
                for (q0, q1, nkt) in qgroups:
                    nq = q1 - q0
                    nqt = nq // 128
                    accs = [accr() for _ in range(nqt)]
                    for s_ in range(2):
                        pos = [por() for _ in range(nqt)]
                        def score(kt, s_=s_, q0=q0, q1=q1, nq=nq, kzt=kz[s_], qh=qh):
                            ps = psr()
                            S.op('tensor', lambda e, ps=ps: e.matmul(ps[:, 0:nq], lhsT=kzt[:, kt * 128:(kt + 1) * 128], rhs=qh[:, q0:q1], start=True, stop=True), [kzt, qh], [ps])
                            return ps
                        pend = [score(0)]
                        if nkt > 1:
                            pend.append(score(1))
                        for kt in range(nkt):
                            ps = pend.pop(0)
                            if kt + 2 < nkt:
                                pend.append(score(kt + 2))
                            pT = pr()
                            S.op('scalar', lambda e, ps=ps, pT=pT, nq=nq: e.activation(out=pT[:, 0:nq], in_=ps[:, 0:nq], func=AF.Exp, scale=0.125), [ps], [pT])
                            for qi in range(nqt):
                                S.op('tensor', lambda e, po=pos[qi], pT=pT, vh=vh, kt=kt, qi=qi, nkt=nkt: e.matmul(po[:, 0:129], lhsT=pT[:, qi * 128:(qi + 1) * 128], rhs=vh[:, kt, :], start=(kt == 0), stop=(kt == nkt - 1)), [pT, vh], [pos[qi]])
                        for qi in range(nqt):
                            po = pos[qi]
                            acc = accs[qi]
                            sm = smr()
                            S.op('vector', lambda e, po=po, sm=sm: e.reciprocal(out=sm[:, 0:1], in_=po[:, 128:129]), [po], [sm])
                            if s_ == 0:
                                S.op('vector', lambda e, po=po, sm=sm, acc=acc: e.tensor_scalar(out=acc[:, :], in0=po[:, 0:128], scalar1=sm[:, 0:1], scalar2=None, op0=ALU.mult), [po, sm], [acc])
                            else:
                                S.op('vector', lambda e, sm=sm: e.tensor_tensor(out=sm[:, 1:2], in0=sm[:, 0:1], in1=neglam[:, 5:6], op=ALU.mult), [sm, neglam], [sm])
                                S.op('vector', lambda e, po=po, sm=sm, acc=acc: e.scalar_tensor_tensor(out=acc[:, :], in0=po[:, 0:128], scalar=sm[:, 1:2], in1=acc[:, :], op0=ALU.mult, op1=ALU.add), [po, sm, acc], [acc])
                    ost = otr()
                    for qi in range(nqt):
                        acc = accs[qi]
                        sm = smr()
                        sqs = onr()
                        S.op('scalar', lambda e, acc=acc, sqs=sqs, sm=sm: e.activation(out=sqs[:, :], in_=acc[:, :], func=AF.Square, accum_out=sm[:, 0:1]), [acc], [sqs, sm])
                        S.op('vector', lambda e, sm=sm: e.tensor_scalar(out=sm[:, 1:2], in0=sm[:, 0:1], scalar1=1.0 / 128.0, scalar2=None, op0=ALU.mult), [sm], [sm])
                        self.rstd(sm, 1, 2, RMS_EPS)
                        S.op('vector', lambda e, acc=acc, sm=sm: e.tensor_scalar(out=acc[:, :], in0=acc[:, :], scalar1=sm[:, 2:3], scalar2=None, op0=ALU.mult), [acc, sm], [acc])
                        on = onr()
                        S.op('vector', lambda e, acc=acc, on=on: e.tensor_tensor(out=on[:, :], in0=acc[:, :], in1=gsub[:, :], op=ALU.mult), [acc, gsub], [on])
                        pt = ptr_()
                        S.op('tensor', lambda e, pt=pt, on=on: e.transpose(out=pt[:, :], in_=on[:, :], identity=self.ident_b[:, :]), [on, self.ident_b], [pt])
                        S.op('scalar', lambda e, pt=pt, ost=ost, qi=qi: e.copy(out=ost[:, qi * 128:(qi + 1) * 128], in_=pt[:, :]), [pt], [ost])
                    S.dma('sync', oT[h * 128:(h + 1) * 128, q0:q1], ost[:, 0:nq], oT, ost, part=True)
        with S.phase():
            wo = S.sbuf([128, 8, D], BF16, 'wao')
            wov = I['w_attn_out'].t.ap()[l].rearrange("(k p) n -> p k n", p=128)
            for hh in range(2):
                S.dma('gpsimd', wo[:, :, hh * 512:(hh + 1) * 512], wov[:, :, hh * 512:(hh + 1) * 512], wo, I['w_attn_out'], part=True)
            otr2 = S.ring(2, [128, 8, 512], BF16, 'oTl')
            gr = S.ring(2, [128, D], F32, 'g2')
            yr = S.ring(2, [128, D], F32, 'y2t')
            pyr = S.ring(4, [128, 512], F32, 'py', psum=True)
            otv = oT.t.ap().rearrange("(j p) t -> p j t", p=128)
            tstart = 0 if ctx_out else NCTX
            for t0 in range(tstart, NT, 512):
                t1 = min(t0 + 512, NT)
                n = t1 - t0
                ol = otr2()
                S.dma('sync', ol[:, :, 0:n], otv[:, :, t0:t1], ol, oT)
                for ii in range(n // 128):
                    i = t0 // 128 + ii
                    g2 = gr()
                    S.dma('sync', g2[:, :], gates[i * 128:(i + 1) * 128, 2 * D:3 * D], g2, gates)
                    yt = yr()
                    for nb in range(2):
                        py = pyr()
                        for j in range(8):
                            S.op('tensor', lambda e, py=py, ol=ol, j=j, ii=ii, nb=nb: e.matmul(py[:, :], lhsT=ol[:, j, ii * 128:(ii + 1) * 128], rhs=wo[:, j, nb * 512:(nb + 1) * 512], start=(j == 0), stop=(j == 7)), [ol, wo], [py])
                        S.op('vector', lambda e, py=py, yt=yt, g2=g2, nb=nb: e.tensor_tensor(out=yt[:, nb * 512:(nb + 1) * 512], in0=py[:, :], in1=g2[:, nb * 512:(nb + 1) * 512], op=ALU.mult), [py, g2], [yt])
                    S.dma('sync', y2[i * 128:(i + 1) * 128, :], yt[:, :], y2, yt, part=True)


    def rr_mixed(self, x, k):
        S = self.S
        S.op('gpsimd', lambda e: e.tensor_scalar(out=k[:, :], in0=x[:, :], scalar1=float(1.0 / TWO_PI), scalar2=MAGIC, op0=ALU.mult, op1=ALU.add), [x], [k])
        S.op('gpsimd', lambda e: e.tensor_scalar(out=k[:, :], in0=k[:, :], scalar1=-MAGIC, scalar2=None, op0=ALU.add), [k], [k])
        S.op('vector', lambda e: e.scalar_tensor_tensor(out=x[:, :], in0=k[:, :], scalar=-CW1, in1=x[:, :], op0=ALU.mult, op1=ALU.add), [x, k], [x])
        S.op('vector', lambda e: e.scalar_tensor_tensor(out=x[:, :], in0=k[:, :], scalar=-CW2, in1=x[:, :], op0=ALU.mult, op1=ALU.add), [x, k], [x])
        S.op('gpsimd', lambda e: e.tensor_scalar(out=x[:, :], in0=x[:, :], scalar1=PI_LO, scalar2=-PI_LO, op0=ALU.min, op1=ALU.max), [x], [x])

    def ph_ssm_old(self, l):
        S = self.S
        I = self.I
        uT = self.scratch('uT', [D, NT], F32)
        ygT = self.scratch('ygT', [D, NT], BF16)
        gates = self.scratch('gates', [NT, 3 * D], F32)
        y1 = self.scratch('y1', [NT, D], F32)
        TG = [(t0, min(t0 + 512, NT)) for t0 in range(0, NT, 512)]
        V = lambda fn, r, w: S.op('vector', fn, r, w)
        G = lambda fn, r, w: S.op('gpsimd', fn, r, w)
        A = lambda fn, r, w: S.op('scalar', fn, r, w)
        with S.phase():
            LT = {(v, par): S.sbuf([128, 16, 128], BF16, 'LT%d%d' % (v, par)) for v in (1, 2) for par in range(4)}
            Rf = {v: S.sbuf([128, 16, 128], BF16, 'Rf%d' % v) for v in (1, 2)}
            RHO = S.sbuf([128, 128], F32, 'RHO')
            THR = S.sbuf([128, 128], F32, 'THR')
            dvec = S.sbuf([128, 8], F32, 'dvec')
            S.dma('sync', dvec[:, :], I['ssm_d'].t.ap()[l].rearrange("(j p) -> p j", p=128), dvec, I['ssm_d'], allow_slow_non_contiguous=True)
            with S.phase():
                def T(name):
                    return S.sbuf([128, 128], F32, name)
                AR, AI, DT, TH, SN, CS, ABR, ABI, DEN, CR, CI, SCR, SCI, TMP, TMP2 = [T(n) for n in ['AR', 'AI', 'DT', 'TH', 'SN', 'CS', 'ABR', 'ABI', 'DEN', 'CR', 'CI', 'SCR', 'SCI', 'TMP', 'TMP2']]
                are = I['ssm_a_re'].t.ap()[l].rearrange("d g n -> n (d g)")
                aim = I['ssm_a_im'].t.ap()[l].rearrange("d g n -> n (d g)")
                for hf in range(2):
                    S.dma('sync', AR[hf * 64:(hf + 1) * 64, :], are, AR, I['ssm_a_re'], part=(hf == 1), allow_slow_non_contiguous=True)
                    S.dma('sync', AI[hf * 64:(hf + 1) * 64, :], aim, AI, I['ssm_a_im'], part=(hf == 1), allow_slow_non_contiguous=True)
                S.dma('sync', DT[:, :], I['ssm_log_dt'][l:l + 1, :, :].rearrange("o d g -> o (d g)").partition_broadcast(128), DT, I['ssm_log_dt'])
                A(lambda e: e.activation(out=DT[:, :], in_=DT[:, :], func=AF.Exp), [DT], [DT])
                V(lambda e: e.tensor_tensor(out=TMP[:, :], in0=DT[:, :], in1=AR[:, :], op=ALU.mult), [DT, AR], [TMP])
                V(lambda e: e.tensor_tensor(out=TH[:, :], in0=DT[:, :], in1=AI[:, :], op=ALU.mult), [DT, AI], [TH])
                A(lambda e: e.activation(out=RHO[:, :], in_=TMP[:, :], func=AF.Exp), [TMP], [RHO])
                V(lambda e: e.tensor_copy(out=THR[:, :], in_=TH[:, :]), [TH], [THR])
                self.rr_mixed(THR, TMP2)
                A(lambda e: e.activation(out=SN[:, :], in_=THR[:, :], func=AF.Sin), [THR], [SN])
                A(lambda e: e.activation(out=CS[:, :], in_=THR[:, :], func=AF.Sin, scale=0.5), [THR], [CS])
                V(lambda e: e.tensor_tensor(out=CS[:, :], in0=CS[:, :], in1=CS[:, :], op=ALU.mult), [CS], [CS])
                V(lambda e: e.tensor_scalar(out=CS[:, :], in0=CS[:, :], scalar1=-2.0, scalar2=1.0, op0=ALU.mult, op1=ALU.add), [CS], [CS])
                V(lambda e: e.tensor_tensor(out=ABR[:, :], in0=RHO[:, :], in1=CS[:, :], op=ALU.mult), [RHO, CS], [ABR])
                V(lambda e: e.tensor_scalar(out=ABR[:, :], in0=ABR[:, :], scalar1=-1.0, scalar2=None, op0=ALU.add), [ABR], [ABR])
                V(lambda e: e.tensor_tensor(out=ABI[:, :], in0=RHO[:, :], in1=SN[:, :], op=ALU.mult), [RHO, SN], [ABI])
                V(lambda e: e.tensor_tensor(out=DEN[:, :], in0=AR[:, :], in1=AR[:, :], op=ALU.mult), [AR], [DEN])
                V(lambda e: e.tensor_tensor(out=TMP[:, :], in0=AI[:, :], in1=AI[:, :], op=ALU.mult), [AI], [TMP])
                V(lambda e: e.tensor_tensor(out=DEN[:, :], in0=DEN[:, :], in1=TMP[:, :], op=ALU.add), [DEN, TMP], [DEN])
                V(lambda e: e.reciprocal(out=DEN[:, :], in_=DEN[:, :]), [DEN], [DEN])
                V(lambda e: e.tensor_tensor(out=CR[:, :], in0=ABR[:, :], in1=AR[:, :], op=ALU.mult), [ABR, AR], [CR])
                V(lambda e: e.tensor_tensor(out=TMP[:, :], in0=ABI[:, :], in1=AI[:, :], op=ALU.mult), [ABI, AI], [TMP])
                V(lambda e: e.tensor_tensor(out=CR[:, :], in0=CR[:, :], in1=TMP[:, :], op=ALU.add), [CR, TMP], [CR])
                V(lambda e: e.tensor_tensor(out=CR[:, :], in0=CR[:, :], in1=DEN[:, :], op=ALU.mult), [CR, DEN], [CR])
                V(lambda e: e.tensor_tensor(out=CI[:, :], in0=ABI[:, :], in1=AR[:, :], op=ALU.mult), [ABI, AR], [CI])
                V(lambda e: e.tensor_tensor(out=TMP[:, :], in0=ABR[:, :], in1=AI[:, :], op=ALU.mult), [ABR, AI], [TMP])
                V(lambda e: e.tensor_tensor(out=CI[:, :], in0=CI[:, :], in1=TMP[:, :], op=ALU.subtract), [CI, TMP], [CI])
                V(lambda e: e.tensor_tensor(out=CI[:, :], in0=CI[:, :], in1=DEN[:, :], op=ALU.mult), [CI, DEN], [CI])
                sg = S.sbuf([128, 8], F32, 'sg')
                pidx = S.sbuf([128, 2], I32, 'pidx')
                G(lambda e: e.memset(sg[:, 0:1], 1.0), [], [sg])
                G(lambda e: e.memset(sg[0:64, 0:1], -1.0), [sg], [sg])
                G(lambda e: e.iota(pidx[:, 0:1], pattern=[[0, 1]], base=0, channel_multiplier=1), [], [pidx])
                V(lambda e: e.tensor_scalar(out=pidx[:, 1:2], in0=pidx[:, 0:1], scalar1=4, scalar2=3, op0=ALU.logical_shift_right, op1=ALU.bitwise_and), [pidx], [pidx])
                V(lambda e: e.tensor_copy(out=sg[:, 1:2], in_=pidx[:, 1:2]), [pidx], [sg])
                for m_ in range(4):
                    V(lambda e, m_=m_: e.tensor_scalar(out=sg[:, 4 + m_:5 + m_], in0=sg[:, 1:2], scalar1=float(m_), scalar2=None, op0=ALU.is_equal), [sg], [sg])
                V(lambda e: e.tensor_scalar(out=SCR[:, :], in0=CR[:, :], scalar1=sg[:, 0:1], scalar2=None, op0=ALU.mult), [CR, sg], [SCR])
                V(lambda e: e.tensor_scalar(out=SCI[:, :], in0=CI[:, :], scalar1=sg[:, 0:1], scalar2=None, op0=ALU.mult), [CI, sg], [SCI])
                BA = S.sbuf([128, 128, 16], F32, 'BA')
                BB = S.sbuf([128, 128, 16], F32, 'BB')
                X1 = S.sbuf([128, 128, 16], F32, 'X1')
                X2 = S.sbuf([128, 128, 16], F32, 'X2')
                XT = S.sbuf([128, 128, 16], F32, 'XT')
                bre = I['ssm_b_re'].t.ap()[l].rearrange("d g n c -> n (d g) c")
                bim = I['ssm_b_im'].t.ap()[l].rearrange("d g n c -> n (d g) c")
                S.dma('sync', BA[0:64, :, :], bre, BA, I['ssm_b_re'])
                S.dma('sync', BA[64:128, :, :], bim, BA, I['ssm_b_im'], part=True)
                S.dma('sync', BB[0:64, :, :], bim, BB, I['ssm_b_im'])
                S.dma('sync', BB[64:128, :, :], bre, BB, I['ssm_b_re'], part=True)

                def bc(t):
                    return t[:, :].unsqueeze(2).to_broadcast([128, 128, 16])
                V(lambda e: e.tensor_tensor(out=X1[:, :, :], in0=BA[:, :, :], in1=bc(CR), op=ALU.mult), [BA, CR], [X1])
                V(lambda e: e.tensor_tensor(out=XT[:, :, :], in0=BB[:, :, :], in1=bc(SCI), op=ALU.mult), [BB, SCI], [XT])
                V(lambda e: e.tensor_tensor(out=X1[:, :, :], in0=X1[:, :, :], in1=XT[:, :, :], op=ALU.add), [X1, XT], [X1])
                V(lambda e: e.tensor_tensor(out=X2[:, :, :], in0=BA[:, :, :], in1=bc(CI), op=ALU.mult), [BA, CI], [X2])
                V(lambda e: e.tensor_tensor(out=XT[:, :, :], in0=BB[:, :, :], in1=bc(SCR), op=ALU.mult), [BB, SCR], [XT])
                V(lambda e: e.tensor_tensor(out=X2[:, :, :], in0=X2[:, :, :], in1=XT[:, :, :], op=ALU.subtract), [X2, XT], [X2])
                ptr_ = S.ring(3, [128, 128], F32, 'ptp', psum=True)
                for v, X in ((1, X1), (2, X2)):
                    for col in range(16):
                        pt = ptr_()
                        S.op('tensor', lambda e, pt=pt, X=X, col=col: e.transpose(out=pt[:, :], in_=X[:, col * 8:(col + 1) * 8, :], identity=self.ident_f[:, :]), [X, self.ident_f], [pt])
                        for m_ in range(4):
                            V(lambda e, pt=pt, v=v, col=col, m_=m_: e.tensor_scalar(out=LT[(v, m_)][:, col, :], in0=pt[:, :], scalar1=sg[:, 4 + m_:5 + m_], scalar2=None, op0=ALU.mult), [pt, sg], [LT[(v, m_)]])
                M1 = S.sbuf([128, 16, 128], F32, 'M1')
                M2 = S.sbuf([128, 16, 128], F32, 'M2')
                cre = I['ssm_c_re'].t.ap()[l].rearrange("d (b g) c n -> (g c) (d b) n", g=8)
                cim = I['ssm_c_im'].t.ap()[l].rearrange("d (b g) c n -> (g c) (d b) n", g=8)
                S.dma('sync', M1[:, :, 0:64], cre, M1, I['ssm_c_re'])
                S.dma('sync', M1[:, :, 64:128], cim, M1, I['ssm_c_im'], part=True)
                V(lambda e: e.tensor_scalar(out=M2[:, :, 64:128], in0=M1[:, :, 0:64], scalar1=-1.0, scalar2=None, op0=ALU.mult), [M1], [M2])
                V(lambda e: e.tensor_scalar(out=M2[:, :, 0:64], in0=M1[:, :, 64:128], scalar1=-1.0, scalar2=None, op0=ALU.mult), [M1], [M2])
                V(lambda e: e.tensor_scalar(out=M1[:, :, 64:128], in0=M1[:, :, 64:128], scalar1=-1.0, scalar2=None, op0=ALU.mult), [M1, M2], [M1])
                for v, M in ((1, M1), (2, M2)):
                    for col in range(16):
                        pt = ptr_()
                        S.op('tensor', lambda e, pt=pt, M=M, col=col: e.transpose(out=pt[:, :], in_=M[:, col, :], identity=self.ident_f[:, :]), [M, self.ident_f], [pt])
                        A(lambda e, pt=pt, v=v, col=col: e.copy(out=Rf[v][:, col, :], in_=pt[:, :]), [pt], [Rf[v]])
            with S.phase():
                pidxs = [S.sbuf([128, NT], F32, 'pidx%d' % d_) for d_ in range(2)]
                with S.phase():
                    tmpi = S.sbuf([128, NT], I32, 'tmpi')
                    for d_ in range(2):
                        pf = pidxs[d_]
                        if d_ == 0:
                            G(lambda e: e.iota(tmpi[:, :], pattern=[[1, NT]], base=0, channel_multiplier=0), [], [tmpi])
                        else:
                            G(lambda e: e.iota(tmpi[:, 0:NCTX], pattern=[[-1, NCTX]], base=NCTX - 1, channel_multiplier=0), [], [tmpi])
                            G(lambda e: e.iota(tmpi[:, NCTX:NT], pattern=[[-1, NLAT]], base=NT - 1, channel_multiplier=0), [tmpi], [tmpi])
                        V(lambda e, pf=pf: e.tensor_copy(out=pf[:, :], in_=tmpi[:, :]), [tmpi], [pf])
                bufA = tmpi_f = S.sbuf([128, NT], F32, 'bufA')
                bufB = S.sbuf([128, NT], F32, 'bufB')
                cosT = S.sbuf([128, NT], F32, 'cosS')
                sinT = S.sbuf([128, NT], F32, 'sinS')
                D1 = S.sbuf([128, NT], BF16, 'D1')
                D2 = S.sbuf([128, NT], BF16, 'D2')
                Yacc = S.sbuf([128, NT], F32, 'Yacc')
                ub = S.sbuf([128, NT], BF16, 'ub')
                t1r = S.ring(2, [128, 512], F32, 'st1')
                t2r = S.ring(2, [128, 512], F32, 'st2')
                rz1r = S.ring(2, [128, 128], BF16, 'rz1')
                rz2r = S.ring(2, [128, 128], BF16, 'rz2')
                p1r = S.ring(2, [128, 512], F32, 'pp1', psum=True)
                p2r = S.ring(2, [128, 512], F32, 'pp2', psum=True)
                pyr = S.ring(2, [128, 512], F32, 'ppy', psum=True)
                for blk in range(8):
                    S.dma('gpsimd', ub[:, :], uT[blk * 128:(blk + 1) * 128, :], ub, uT)
                    S.dma('sync', bufA[:, :], uT[blk * 128:(blk + 1) * 128, :], bufA, uT)
                    V(lambda e, blk=blk: e.tensor_scalar(out=Yacc[:, :], in0=bufA[:, :], scalar1=dvec[:, blk:blk + 1], scalar2=None, op0=ALU.mult), [bufA, dvec], [Yacc])
                    for d_ in range(2):
                        for g8 in range(8):
                            dg = d_ * 64 + blk * 8 + g8
                            col = d_ * 8 + blk
                            pr, par = g8 // 4, g8 % 4
                            pf = pidxs[d_]
                            G(lambda e, pf=pf, dg=dg: e.tensor_scalar(out=bufA[:, :], in0=pf[:, :], scalar1=THR[:, dg:dg + 1], scalar2=None, op0=ALU.mult), [pf, THR], [bufA])
                            self.rr_mixed(bufA, bufB)
                            A(lambda e: e.activation(out=sinT[:, :], in_=bufA[:, :], func=AF.Sin), [bufA], [sinT])
                            A(lambda e: e.activation(out=cosT[:, :], in_=bufA[:, :], func=AF.Sin, scale=0.5), [bufA], [cosT])
                            G(lambda e: e.tensor_tensor(out=cosT[:, :], in0=cosT[:, :], in1=cosT[:, :], op=ALU.mult), [cosT], [cosT])
                            G(lambda e: e.tensor_scalar(out=cosT[:, :], in0=cosT[:, :], scalar1=-2.0, scalar2=1.0, op0=ALU.mult, op1=ALU.add), [cosT], [cosT])
                            rz1, rz2 = rz1r(), rz2r()
                            for rz, v in ((rz1, 1), (rz2, 2)):
                                G(lambda e, rz=rz: e.memset(rz[:, :], 0.0), [], [rz])
                                G(lambda e, rz=rz, v=v, col=col, g8=g8: e.tensor_copy(out=rz[:, g8 * 16:(g8 + 1) * 16], in_=Rf[v][:, col, g8 * 16:(g8 + 1) * 16]), [Rf[v]], [rz])
                            l1, l2 = LT[(1, par)], LT[(2, par)]
                            for (t0, t1) in TG:
                                n = t1 - t0
                                p1, p2 = p1r(), p2r()
                                S.op('tensor', lambda e, p1=p1, l1=l1, pr=pr, col=col, t0=t0, t1=t1, n=n: e.matmul(p1[:, 0:n], lhsT=l1[64 * pr:64 * pr + 64, col, :], rhs=ub[64 * pr:64 * pr + 64, t0:t1], start=True, stop=True), [l1, ub], [p1])
                                S.op('tensor', lambda e, p2=p2, l2=l2, pr=pr, col=col, t0=t0, t1=t1, n=n: e.matmul(p2[:, 0:n], lhsT=l2[64 * pr:64 * pr + 64, col, :], rhs=ub[64 * pr:64 * pr + 64, t0:t1], start=True, stop=True), [l2, ub], [p2])
                                a, b = t1r(), t2r()
                                V(lambda e, p1=p1, a=a, t0=t0, t1=t1, n=n: e.tensor_tensor(out=a[:, 0:n], in0=p1[:, 0:n], in1=cosT[:, t0:t1], op=ALU.mult), [p1, cosT], [a])
                                V(lambda e, p2=p2, b=b, t0=t0, t1=t1, n=n: e.tensor_tensor(out=b[:, 0:n], in0=p2[:, 0:n], in1=sinT[:, t0:t1], op=ALU.mult), [p2, sinT], [b])
                                G(lambda e, a=a, b=b, t0=t0, t1=t1, n=n: e.tensor_tensor(out=bufB[:, t0:t1], in0=a[:, 0:n], in1=b[:, 0:n], op=ALU.add), [a, b], [bufB])
                            rho_b = RHO[:, dg:dg + 1]
                            if d_ == 0:
                                V(lambda e, rho_b=rho_b: e.tensor_tensor_scan(out=bufA[:, :], data0=rho_b.to_broadcast([128, NT]), data1=bufB[:, :], initial=0.0, op0=ALU.mult, op1=ALU.add), [bufB, RHO], [bufA])
                            else:
                                V(lambda e, rho_b=rho_b: e.tensor_tensor_scan(out=bufA[:, 0:NCTX][:, ::-1], data0=rho_b.to_broadcast([128, NCTX]), data1=bufB[:, 0:NCTX][:, ::-1], initial=0.0, op0=ALU.mult, op1=ALU.add), [bufB, RHO], [bufA])
                                V(lambda e, rho_b=rho_b: e.tensor_tensor_scan(out=bufA[:, NCTX:NT][:, ::-1], data0=rho_b.to_broadcast([128, NLAT]), data1=bufB[:, NCTX:NT][:, ::-1], initial=bufA[:, 0:1], op0=ALU.mult, op1=ALU.add), [bufB, RHO, bufA], [bufA])
                            V(lambda e: e.tensor_tensor(out=D1[:, :], in0=cosT[:, :], in1=bufA[:, :], op=ALU.mult), [cosT, bufA], [D1])
                            G(lambda e: e.tensor_tensor(out=D2[:, :], in0=sinT[:, :], in1=bufA[:, :], op=ALU.mult), [sinT, bufA], [D2])
                            for (t0, t1) in TG:
                                n = t1 - t0
                                py = pyr()
                                S.op('tensor', lambda e, py=py, rz1=rz1, t0=t0, t1=t1, n=n: e.matmul(py[:, 0:n], lhsT=rz1[:, :], rhs=D1[:, t0:t1], start=True, stop=False), [rz1, D1], [py])
                                S.op('tensor', lambda e, py=py, rz2=rz2, t0=t0, t1=t1, n=n: e.matmul(py[:, 0:n], lhsT=rz2[:, :], rhs=D2[:, t0:t1], start=False, stop=True), [rz2, D2], [py])
                                V(lambda e, py=py, t0=t0, t1=t1, n=n: e.tensor_tensor(out=Yacc[:, t0:t1], in0=py[:, 0:n], in1=Yacc[:, t0:t1], op=ALU.add), [py, Yacc], [Yacc])
                    if 'yscanT' in self.debug:
                        ysd = self.scratch('yscanT', [D, NT], F32)
                        S.dma('sync', bufA[:, :], uT[blk * 128:(blk + 1) * 128, :], bufA, uT)
                        V(lambda e, blk=blk: e.tensor_scalar(out=bufA[:, :], in0=bufA[:, :], scalar1=dvec[:, blk:blk + 1], scalar2=None, op0=ALU.mult), [bufA, dvec], [bufA])
                        V(lambda e: e.tensor_tensor(out=bufA[:, :], in0=Yacc[:, :], in1=bufA[:, :], op=ALU.subtract), [Yacc, bufA], [bufA])
                        S.dma('sync', ysd[blk * 128:(blk + 1) * 128, :], bufA[:, :], ysd, bufA, part=True)
                    G(lambda e: e.tensor_tensor(out=bufB[:, :], in0=Yacc[:, :], in1=Yacc[:, :], op=ALU.mult), [Yacc], [bufB])
                    G(lambda e: e.tensor_scalar(out=bufB[:, :], in0=bufB[:, :], scalar1=0.044715, scalar2=1.0, op0=ALU.mult, op1=ALU.add), [bufB], [bufB])
                    V(lambda e: e.tensor_tensor(out=bufB[:, :], in0=bufB[:, :], in1=Yacc[:, :], op=ALU.mult), [bufB, Yacc], [bufB])
                    A(lambda e: e.activation(out=bufB[:, :], in_=bufB[:, :], func=AF.Sigmoid, scale=2.0 * math.sqrt(2.0 / math.pi)), [bufB], [bufB])
                    V(lambda e: e.tensor_tensor(out=D1[:, :], in0=bufB[:, :], in1=Yacc[:, :], op=ALU.mult), [bufB, Yacc], [D1])
                    S.dma('sync', ygT[blk * 128:(blk + 1) * 128, :], D1[:, :], ygT, D1, part=True)
        self.ssm_glu_out(l)


    def rr_ap(self, x, k, xa, ka):
        S = self.S
        S.op('gpsimd', lambda e: e.tensor_scalar(out=ka, in0=xa, scalar1=float(1.0 / TWO_PI), scalar2=MAGIC, op0=ALU.mult, op1=ALU.add), [x], [k])
        S.op('gpsimd', lambda e: e.tensor_scalar(out=ka, in0=ka, scalar1=-MAGIC, scalar2=None, op0=ALU.add), [k], [k])
        S.op('vector', lambda e: e.scalar_tensor_tensor(out=xa, in0=ka, scalar=-CW1, in1=xa, op0=ALU.mult, op1=ALU.add), [x, k], [x])
        S.op('vector', lambda e: e.scalar_tensor_tensor(out=xa, in0=ka, scalar=-CW2, in1=xa, op0=ALU.mult, op1=ALU.add), [x, k], [x])
        S.op('gpsimd', lambda e: e.tensor_scalar(out=xa, in0=xa, scalar1=PI_LO, scalar2=-PI_LO, op0=ALU.min, op1=ALU.max), [x], [x])

    def rr_gen(self, x, k, xa, ka, eng='gpsimd'):
        S = self.S
        S.op(eng, lambda e: e.tensor_scalar(out=ka, in0=xa, scalar1=float(1.0 / TWO_PI), scalar2=MAGIC, op0=ALU.mult, op1=ALU.add), [x], [k])
        yield
        S.op(eng, lambda e: e.tensor_scalar(out=ka, in0=ka, scalar1=-MAGIC, scalar2=None, op0=ALU.add), [k], [k])
        yield
        S.op('vector', lambda e: e.scalar_tensor_tensor(out=xa, in0=ka, scalar=-CW1, in1=xa, op0=ALU.mult, op1=ALU.add), [x, k], [x])
        yield
        S.op('vector', lambda e: e.scalar_tensor_tensor(out=xa, in0=ka, scalar=-CW2, in1=xa, op0=ALU.mult, op1=ALU.add), [x, k], [x])
        yield
        S.op(eng, lambda e: e.tensor_scalar(out=xa, in0=xa, scalar1=PI_LO, scalar2=-PI_LO, op0=ALU.min, op1=ALU.max), [x], [x])
        yield

    def ph_ssm(self, l):
        S = self.S
        I = self.I
        uT = self.scratch('uT', [D, NT], F32)
        ygT = self.scratch('ygT', [D, NT], BF16)
        V = lambda fn, r, w: S.op('vector', fn, r, w)
        G = lambda fn, r, w: S.op('gpsimd', fn, r, w)
        A = lambda fn, r, w: S.op('scalar', fn, r, w)
        P = lambda fn, r, w: S.op('tensor', fn, r, w)
        NC = NT // 8
        NCC = NCTX // 8
        HC = [(0, NC // 2), (NC // 2, NC)]
        HW = NC // 2
        with S.phase():
            dvec = S.sbuf([128, 8], F32, 'dvec')
            S.dma('sync', dvec[:, :], I['ssm_d'].t.ap()[l].rearrange("(j p) -> p j", p=128), dvec, I['ssm_d'], allow_slow_non_contiguous=True)
            PWr = S.sbuf([128, 9, 128], F32, 'PWr')
            PWi = S.sbuf([128, 9, 128], F32, 'PWi')
            CRe, CIe = [S.sbuf([128, 8, 128], F32, n) for n in ('CRe', 'CIe')]
            T1 = S.sbuf([128, 16, 8, 16], F32, 'T1')
            T2 = S.sbuf([128, 16, 8, 16], F32, 'T2')
            sg = S.sbuf([128, 8], F32, 'sg')
            blkmask = S.sbuf([128, 128], F32, 'blkmask')
            TH8 = S.sbuf([128, 128], F32, 'TH8')
            RHO8 = S.sbuf([128, 128], F32, 'RHO8')
            with S.phase():
                def T(name, w=128):
                    return S.sbuf([128, w], F32, name)
                AR, AI, DT, ARD, TH, THR, SN, CS, ABR, ABI, DEN, CR, CI, TMP, TMP2, RHO = [T(n) for n in ['AR', 'AI', 'DT', 'ARD', 'TH', 'THR', 'SN', 'CS', 'ABR', 'ABI', 'DEN', 'CR', 'CI', 'TMP', 'TMP2', 'RHO']]
                are = I['ssm_a_re'].t.ap()[l].rearrange("d g n -> n (d g)")
                aim = I['ssm_a_im'].t.ap()[l].rearrange("d g n -> n (d g)")
                for hf in range(2):
                    S.dma('sync', AR[hf * 64:(hf + 1) * 64, :], are, AR, I['ssm_a_re'], part=(hf == 1), allow_slow_non_contiguous=True)
                    S.dma('sync', AI[hf * 64:(hf + 1) * 64, :], aim, AI, I['ssm_a_im'], part=(hf == 1), allow_slow_non_contiguous=True)
                S.dma('sync', DT[:, :], I['ssm_log_dt'][l:l + 1, :, :].rearrange("o d g -> o (d g)").partition_broadcast(128), DT, I['ssm_log_dt'])
                A(lambda e: e.activation(out=DT[:, :], in_=DT[:, :], func=AF.Exp), [DT], [DT])
                V(lambda e: e.tensor_tensor(out=ARD[:, :], in0=DT[:, :], in1=AR[:, :], op=ALU.mult), [DT, AR], [ARD])
                V(lambda e: e.tensor_tensor(out=TH[:, :], in0=DT[:, :], in1=AI[:, :], op=ALU.mult), [DT, AI], [TH])
                A(lambda e: e.activation(out=RHO[:, :], in_=ARD[:, :], func=AF.Exp), [ARD], [RHO])
                V(lambda e: e.tensor_copy(out=THR[:, :], in_=TH[:, :]), [TH], [THR])
                self.rr_ap(THR, TMP2, THR[:, :], TMP2[:, :])
                A(lambda e: e.activation(out=SN[:, :], in_=THR[:, :], func=AF.Sin), [THR], [SN])
                A(lambda e: e.activation(out=CS[:, :], in_=THR[:, :], func=AF.Sin, scale=0.5), [THR], [CS])
                V(lambda e: e.tensor_tensor(out=CS[:, :], in0=CS[:, :], in1=CS[:, :], op=ALU.mult), [CS], [CS])
                V(lambda e: e.tensor_scalar(out=CS[:, :], in0=CS[:, :], scalar1=-2.0, scalar2=1.0, op0=ALU.mult, op1=ALU.add), [CS], [CS])
                V(lambda e: e.tensor_tensor(out=ABR[:, :], in0=RHO[:, :], in1=CS[:, :], op=ALU.mult), [RHO, CS], [ABR])
                V(lambda e: e.tensor_scalar(out=ABR[:, :], in0=ABR[:, :], scalar1=-1.0, scalar2=None, op0=ALU.add), [ABR], [ABR])
                V(lambda e: e.tensor_tensor(out=ABI[:, :], in0=RHO[:, :], in1=SN[:, :], op=ALU.mult), [RHO, SN], [ABI])
                V(lambda e: e.tensor_tensor(out=DEN[:, :], in0=AR[:, :], in1=AR[:, :], op=ALU.mult), [AR], [DEN])
                V(lambda e: e.tensor_tensor(out=TMP[:, :], in0=AI[:, :], in1=AI[:, :], op=ALU.mult), [AI], [TMP])
                V(lambda e: e.tensor_tensor(out=DEN[:, :], in0=DEN[:, :], in1=TMP[:, :], op=ALU.add), [DEN, TMP], [DEN])
                V(lambda e: e.reciprocal(out=DEN[:, :], in_=DEN[:, :]), [DEN], [DEN])
                V(lambda e: e.tensor_tensor(out=CR[:, :], in0=ABR[:, :], in1=AR[:, :], op=ALU.mult), [ABR, AR], [CR])
                V(lambda e: e.tensor_tensor(out=TMP[:, :], in0=ABI[:, :], in1=AI[:, :], op=ALU.mult), [ABI, AI], [TMP])
                V(lambda e: e.tensor_tensor(out=CR[:, :], in0=CR[:, :], in1=TMP[:, :], op=ALU.add), [CR, TMP], [CR])
                V(lambda e: e.tensor_tensor(out=CR[:, :], in0=CR[:, :], in1=DEN[:, :], op=ALU.mult), [CR, DEN], [CR])
                V(lambda e: e.tensor_tensor(out=CI[:, :], in0=ABI[:, :], in1=AR[:, :], op=ALU.mult), [ABI, AR], [CI])
                V(lambda e: e.tensor_tensor(out=TMP[:, :], in0=ABR[:, :], in1=AI[:, :], op=ALU.mult), [ABR, AI], [TMP])
                V(lambda e: e.tensor_tensor(out=CI[:, :], in0=CI[:, :], in1=TMP[:, :], op=ALU.subtract), [CI, TMP], [CI])
                V(lambda e: e.tensor_tensor(out=CI[:, :], in0=CI[:, :], in1=DEN[:, :], op=ALU.mult), [CI, DEN], [CI])
                pidx = S.sbuf([128, 4], I32, 'pidx')
                G(lambda e: e.memset(sg[:, 0:1], 1.0), [], [sg])
                G(lambda e: e.memset(sg[0:64, 0:1], -1.0), [sg], [sg])
                V(lambda e: e.tensor_scalar(out=sg[:, 2:3], in0=sg[:, 0:1], scalar1=-1.0, scalar2=None, op0=ALU.mult), [sg], [sg])
                G(lambda e: e.iota(pidx[:, 0:1], pattern=[[0, 1]], base=0, channel_multiplier=1), [], [pidx])
                V(lambda e: e.tensor_scalar(out=pidx[:, 1:2], in0=pidx[:, 0:1], scalar1=4, scalar2=3, op0=ALU.logical_shift_right, op1=ALU.bitwise_and), [pidx], [pidx])
                V(lambda e: e.tensor_scalar(out=pidx[:, 2:3], in0=pidx[:, 0:1], scalar1=4, scalar2=None, op0=ALU.logical_shift_right), [pidx], [pidx])
                V(lambda e: e.tensor_copy(out=sg[:, 1:2], in_=pidx[:, 1:2]), [pidx], [sg])
                V(lambda e: e.tensor_copy(out=sg[:, 3:4], in_=pidx[:, 2:3]), [pidx], [sg])
                for m_ in range(4):
                    V(lambda e, m_=m_: e.tensor_scalar(out=sg[:, 4 + m_:5 + m_], in0=sg[:, 1:2], scalar1=float(m_), scalar2=None, op0=ALU.is_equal), [sg], [sg])
                iq = S.sbuf([128, 128], I32, 'iq')
                G(lambda e: e.iota(iq[:, :], pattern=[[1, 128]], base=0, channel_multiplier=0), [], [iq])
                V(lambda e: e.tensor_scalar(out=iq[:, :], in0=iq[:, :], scalar1=4, scalar2=None, op0=ALU.logical_shift_right), [iq], [iq])
                V(lambda e: e.tensor_copy(out=blkmask[:, :], in_=iq[:, :]), [iq], [blkmask])
                V(lambda e: e.tensor_scalar(out=blkmask[:, :], in0=blkmask[:, :], scalar1=sg[:, 3:4], scalar2=None, op0=ALU.is_equal), [blkmask, sg], [blkmask])
                ANG, KK, SNm, CSm, RHm = [T(n, 9 * 128) for n in ('ANG', 'KKp', 'SNm', 'CSm', 'RHm')]
                def pw_chain(m_):
                    V(lambda e, m_=m_: e.tensor_scalar(out=ANG[:, m_ * 128:(m_ + 1) * 128], in0=THR[:, :], scalar1=float(m_), scalar2=None, op0=ALU.mult), [THR], [ANG])
                    yield
                    V(lambda e, m_=m_: e.tensor_scalar(out=RHm[:, m_ * 128:(m_ + 1) * 128], in0=ARD[:, :], scalar1=float(m_), scalar2=None, op0=ALU.mult), [ARD], [RHm])
                    yield
                interleave([pw_chain(m_) for m_ in range(9)])
                self.rr_ap(ANG, KK, ANG[:, :], KK[:, :])
                A(lambda e: e.activation(out=RHm[:, :], in_=RHm[:, :], func=AF.Exp), [RHm], [RHm])
                A(lambda e: e.activation(out=SNm[:, :], in_=ANG[:, :], func=AF.Sin), [ANG], [SNm])
                A(lambda e: e.activation(out=CSm[:, :], in_=ANG[:, :], func=AF.Sin, scale=0.5), [ANG], [CSm])
                V(lambda e: e.tensor_tensor(out=CSm[:, :], in0=CSm[:, :], in1=CSm[:, :], op=ALU.mult), [CSm], [CSm])
                V(lambda e: e.tensor_scalar(out=CSm[:, :], in0=CSm[:, :], scalar1=-2.0, scalar2=1.0, op0=ALU.mult, op1=ALU.add), [CSm], [CSm])
                pwr2 = PWr.t.ap().rearrange("p m g -> p (m g)")
                pwi2 = PWi.t.ap().rearrange("p m g -> p (m g)")
                V(lambda e: e.tensor_tensor(out=pwr2, in0=RHm[:, :], in1=CSm[:, :], op=ALU.mult), [RHm, CSm], [PWr])
                V(lambda e: e.tensor_tensor(out=pwi2, in0=RHm[:, :], in1=SNm[:, :], op=ALU.mult), [RHm, SNm], [PWi])
                V(lambda e: e.tensor_copy(out=TH8[:, :], in_=ANG[:, 8 * 128:9 * 128]), [ANG], [TH8])
                V(lambda e: e.tensor_copy(out=RHO8[:, :], in_=RHm[:, 8 * 128:9 * 128]), [RHm], [RHO8])
                TM8 = S.sbuf([128, 8, 128], F32, 'TM8')

                def b8(t):
                    return t[:, :].unsqueeze(1).to_broadcast([128, 8, 128])
                V(lambda e: e.tensor_tensor(out=CRe[:, :, :], in0=PWr[:, 0:8, :], in1=b8(CR), op=ALU.mult), [PWr, CR], [CRe])
                V(lambda e: e.tensor_tensor(out=TM8[:, :, :], in0=PWi[:, 0:8, :], in1=b8(CI), op=ALU.mult), [PWi, CI], [TM8])
                V(lambda e: e.tensor_tensor(out=CRe[:, :, :], in0=CRe[:, :, :], in1=TM8[:, :, :], op=ALU.subtract), [CRe, TM8], [CRe])
                V(lambda e: e.tensor_tensor(out=CIe[:, :, :], in0=PWi[:, 0:8, :], in1=b8(CR), op=ALU.mult), [PWi, CR], [CIe])
                V(lambda e: e.tensor_tensor(out=TM8[:, :, :], in0=PWr[:, 0:8, :], in1=b8(CI), op=ALU.mult), [PWr, CI], [TM8])
                V(lambda e: e.tensor_tensor(out=CIe[:, :, :], in0=CIe[:, :, :], in1=TM8[:, :, :], op=ALU.add), [CIe, TM8], [CIe])
                M1 = S.sbuf([128, 16, 128], F32, 'M1')
                M2 = S.sbuf([128, 16, 128], F32, 'M2')
                cre = I['ssm_c_re'].t.ap()[l].rearrange("d (b g) c n -> (g c) (d b) n", g=8)
                cim = I['ssm_c_im'].t.ap()[l].rearrange("d (b g) c n -> (g c) (d b) n", g=8)
                S.dma('sync', M1[:, :, 0:64], cre, M1, I['ssm_c_re'])
                S.dma('sync', M1[:, :, 64:128], cim, M1, I['ssm_c_im'], part=True)
                S.dma('sync', M2[:, :, 0:64], cim, M2, I['ssm_c_im'])
                S.dma('sync', M2[:, :, 64:128], cre, M2, I['ssm_c_re'], part=True)
                ptr_ = S.ring(3, [128, 128], F32, 'ptp', psum=True)
                for M, Tt in ((M1, T1), (M2, T2)):
                    for col in range(16):
                        pt = ptr_()
                        P(lambda e, pt=pt, M=M, col=col: e.transpose(out=pt[:, :], in_=M[:, col, :], identity=self.ident_f[:, :]), [M, self.ident_f], [pt])
                        A(lambda e, pt=pt, Tt=Tt, col=col: e.copy(out=Tt[:, col, :, :], in_=pt[:, :].rearrange("p (g o) -> p g o", o=16)), [pt], [Tt])
            with S.phase():
                pidxc = [S.sbuf([128, NC], F32, 'pidxc%d' % d_) for d_ in range(2)]
                with S.phase():
                    tmpi = S.sbuf([128, NC], I32, 'tmpi')
                    G(lambda e: e.iota(tmpi[:, :], pattern=[[1, NC]], base=0, channel_multiplier=0), [], [tmpi])
                    V(lambda e: e.tensor_copy(out=pidxc[0][:, :], in_=tmpi[:, :]), [tmpi], [pidxc[0]])
                    G(lambda e: e.iota(tmpi[:, 0:NCC], pattern=[[-1, NCC]], base=NCC - 1, channel_multiplier=0), [pidxc[0]], [tmpi])
                    G(lambda e: e.iota(tmpi[:, NCC:NC], pattern=[[-1, NC - NCC]], base=NC - 1, channel_multiplier=0), [tmpi], [tmpi])
                    V(lambda e: e.tensor_copy(out=pidxc[1][:, :], in_=tmpi[:, :]), [tmpi], [pidxc[1]])
                bre = I['ssm_b_re'].t.ap()[l].rearrange("d g n c -> n (d g) c")
                bim = I['ssm_b_im'].t.ap()[l].rearrange("d g n c -> n (d g) c")
                bar = S.ring(1, [128, 8, 16], F32, 'BAb')
                bbr = S.ring(1, [128, 8, 16], F32, 'BBb')
                uf = S.sbuf([128, NT], F32, 'uf')
                Yacc = S.sbuf([128, NT], F32, 'Yacc')
                Ytp = [Buf(Yacc.t, 'Ytp%d' % tp_) for tp_ in range(8)]
                ud = S.sbuf([128, 8, NC], BF16, 'ud')
                udm = [S.sbuf([128, 8, NC], BF16, 'udm%d' % m_) for m_ in range(4)]
                LT = S.sbuf([128, 2, 8, 2, 128], BF16, 'LT')
                x1fr = S.ring(2, [128, 128], F32, 'X1f')
                K0 = S.sbuf([128, 128], F32, 'K0')
                R1f = S.sbuf([128, 9, 128], F32, 'R1f')
                Rb = S.sbuf([128, 2, 9, 2, 128], BF16, 'Rb')
                KT = S.sbuf([128, 15, 128], BF16, 'KT')
                xbig = [S.sbuf([128, 1152], F32, 'xbig%d' % k_) for k_ in range(3)]
                angr = S.ring(2, [128, NC], F32, 'angc')
                kkr = S.ring(2, [128, NC], F32, 'kkc')
                snr = S.ring(2, [128, NC], F32, 'snc')
                csr = S.ring(2, [128, NC], F32, 'csc')
                wmr = S.ring(2, [128, NC], F32, 'wmc')
                wsr = S.ring(2, [128, NC], F32, 'wsc')
                e1r = S.ring(4, [128, NC], BF16, 'e1c')
                e2r = S.ring(4, [128, NC], BF16, 'e2c')
                rzr = S.ring(2, [128, 16, 128], BF16, 'rzc')
                t1r = S.ring(2, [128, HW], F32, 'st1')
                t2r = S.ring(2, [128, HW], F32, 'st2')
                ptr_ = S.ring(2, [128, 128], F32, 'ptp', psum=True)
                p1r = S.ring(2, [128, HW], F32, 'pp1', psum=True)
                p2r = S.ring(2, [128, HW], F32, 'pp2', psum=True)
                pyr = S.ring(2, [128, HW], F32, 'ppy', psum=True)
                ud2 = ud.t.ap().rearrange("p j k -> p (j k)")
                ufv = uf.t.ap().rearrange("p (k j) -> p j k", j=8)
                Yv = Yacc.t.ap().rearrange("p (k j) -> p k j", j=8)
                for blk in range(8):
                    S.dma('sync', uf[:, :], uT[blk * 128:(blk + 1) * 128, :], uf, uT)
                    V(lambda e, blk=blk: e.tensor_scalar(out=Yacc[:, :], in0=uf[:, :], scalar1=dvec[:, blk:blk + 1], scalar2=None, op0=ALU.mult), [uf, dvec], Ytp)
                    V(lambda e: e.tensor_copy(out=ud[:, :, :], in_=ufv), [uf], [ud])
                    for m_ in range(4):
                        V(lambda e, m_=m_: e.tensor_scalar(out=udm[m_][:, :, :], in0=ud[:, :, :], scalar1=sg[:, 4 + m_:5 + m_], scalar2=None, op0=ALU.mult), [ud, sg], [udm[m_]])
                    for d_ in range(2):
                        g0 = d_ * 64 + blk * 8
                        col = d_ * 8 + blk
                        X1f = x1fr()
                        BA, BB = bar(), bbr()
                        S.dma('sync', BA[0:64, :, :], bre[:, g0:g0 + 8, :], BA, I['ssm_b_re'])
                        S.dma('sync', BA[64:128, :, :], bim[:, g0:g0 + 8, :], BA, I['ssm_b_im'], part=True)
                        S.dma('sync', BB[0:64, :, :], bim[:, g0:g0 + 8, :], BB, I['ssm_b_im'])
                        S.dma('sync', BB[64:128, :, :], bre[:, g0:g0 + 8, :], BB, I['ssm_b_re'], part=True)
                        BAb = BA[:, :, :]
                        BBb = BB[:, :, :]

                        XA, XB, XC = xbig[0], xbig[1], xbig[2]
                        xa4 = XA[:, 0:1024].rearrange("p (e g c) -> p e g c", e=8, g=8)
                        xb4 = XB[:, 0:1024].rearrange("p (e g c) -> p e g c", e=8, g=8)
                        xc4 = XC[:, 0:1024].rearrange("p (e g c) -> p e g c", e=8, g=8)
                        BAe = BAb.unsqueeze(1).to_broadcast([128, 8, 8, 16])
                        BBe = BBb.unsqueeze(1).to_broadcast([128, 8, 8, 16])
                        cre = CRe[:, :, g0:g0 + 8].unsqueeze(3).to_broadcast([128, 8, 8, 16])
                        cie = CIe[:, :, g0:g0 + 8].unsqueeze(3).to_broadcast([128, 8, 8, 16])
                        V(lambda e, xa4=xa4, BAe=BAe, cre=cre: e.tensor_tensor(out=xa4, in0=BAe, in1=cre, op=ALU.mult), [BA, CRe], [XA])
                        V(lambda e, xb4=xb4, BBe=BBe, cie=cie: e.tensor_tensor(out=xb4, in0=BBe, in1=cie, op=ALU.mult), [BB, CIe], [XB])
                        V(lambda e: e.scalar_tensor_tensor(out=XA[:, 0:1024], in0=XB[:, 0:1024], scalar=sg[:, 0:1], in1=XA[:, 0:1024], op0=ALU.mult, op1=ALU.add), [XA, XB, sg], [XA])
                        A(lambda e, X1f=X1f: e.copy(out=X1f[:, :], in_=XA[:, 0:128]), [XA], [X1f])
                        for e_ in range(8):
                            pt = ptr_()
                            P(lambda e, pt=pt, e_=e_: e.transpose(out=pt[:, :], in_=XA[:, e_ * 128:(e_ + 1) * 128], identity=self.ident_f[:, :]), [XA, self.ident_f], [pt])
                            A(lambda e, pt=pt, d_=d_, e_=e_: e.copy(out=LT[:, d_, e_, 0, :], in_=pt[:, :]), [pt], [LT])
                        V(lambda e, xb4=xb4, BAe=BAe, cie=cie: e.tensor_tensor(out=xb4, in0=BAe, in1=cie, op=ALU.mult), [BA, CIe], [XB])
                        V(lambda e, xc4=xc4, BBe=BBe, cre=cre: e.tensor_tensor(out=xc4, in0=BBe, in1=cre, op=ALU.mult), [BB, CRe], [XC])
                        V(lambda e: e.scalar_tensor_tensor(out=XB[:, 0:1024], in0=XC[:, 0:1024], scalar=sg[:, 2:3], in1=XB[:, 0:1024], op0=ALU.mult, op1=ALU.add), [XB, XC, sg], [XB])
                        for e_ in range(8):
                            pt = ptr_()
                            P(lambda e, pt=pt, e_=e_: e.transpose(out=pt[:, :], in_=XB[:, e_ * 128:(e_ + 1) * 128], identity=self.ident_f[:, :]), [XB, self.ident_f], [pt])
                            A(lambda e, pt=pt, d_=d_, e_=e_: e.copy(out=LT[:, d_, e_, 1, :], in_=pt[:, :]), [pt], [LT])
                        T1e = T1[:, col, :, :].unsqueeze(1).to_broadcast([128, 9, 8, 16])
                        T2e = T2[:, col, :, :].unsqueeze(1).to_broadcast([128, 9, 8, 16])
                        pwr = PWr[:, :, g0:g0 + 8].unsqueeze(3).to_broadcast([128, 9, 8, 16])
                        pwi = PWi[:, :, g0:g0 + 8].unsqueeze(3).to_broadcast([128, 9, 8, 16])
                        xa9 = XA[:, :].rearrange("p (m g o) -> p m g o", m=9, g=8)
                        xc9 = XC[:, :].rearrange("p (m g o) -> p m g o", m=9, g=8)
                        r1v = R1f.t.ap().rearrange("p m q -> p (m q)")
                        V(lambda e, xa9=xa9, T1e=T1e, pwr=pwr: e.tensor_tensor(out=xa9, in0=T1e, in1=pwr, op=ALU.mult), [T1, PWr], [XA])
                        V(lambda e, xc9=xc9, T2e=T2e, pwi=pwi: e.tensor_tensor(out=xc9, in0=T2e, in1=pwi, op=ALU.mult), [T2, PWi], [XC])
                        V(lambda e, r1v=r1v: e.scalar_tensor_tensor(out=r1v, in0=XA[:, :], scalar=sg[:, 2:3], in1=XC[:, :], op0=ALU.mult, op1=ALU.subtract), [XA, XC, sg], [R1f])
                        A(lambda e, d_=d_: e.copy(out=Rb[:, d_, :, 0, :], in_=R1f[:, :, :]), [R1f], [Rb])
                        V(lambda e, xa9=xa9, T1e=T1e, pwi=pwi: e.tensor_tensor(out=xa9, in0=T1e, in1=pwi, op=ALU.mult), [T1, PWi], [XA])
                        V(lambda e, xc9=xc9, T2e=T2e, pwr=pwr: e.tensor_tensor(out=xc9, in0=T2e, in1=pwr, op=ALU.mult), [T2, PWr], [XC])
                        V(lambda e, d_=d_: e.scalar_tensor_tensor(out=Rb[:, d_, :, 1, :], in0=XA[:, :].rearrange("p (m q) -> p m q", m=9), scalar=sg[:, 0:1], in1=XC[:, :].rearrange("p (m q) -> p m q", m=9), op0=ALU.mult, op1=ALU.subtract), [XA, XC, sg], [Rb])
                        for tau in range(8):
                            pt = ptr_()
                            P(lambda e, pt=pt, tau=tau, X1f=X1f: e.matmul(pt[:, :], lhsT=X1f[:, :], rhs=R1f[:, tau, :], start=True, stop=True), [X1f, R1f], [pt])
                            if tau == 0 and d_ == 0:
                                V(lambda e, pt=pt: e.tensor_tensor(out=K0[:, :], in0=pt[:, :], in1=blkmask[:, :], op=ALU.mult), [pt, blkmask], [K0])
                            elif tau == 0:
                                V(lambda e, pt=pt: e.tensor_tensor(out=K0[:, :], in0=pt[:, :], in1=K0[:, :], op=ALU.add), [pt, K0], [K0])
                                V(lambda e: e.tensor_tensor(out=KT[:, 7, :], in0=K0[:, :], in1=blkmask[:, :], op=ALU.mult), [K0, blkmask], [KT])
                            else:
                                ki = 7 + tau if d_ == 0 else 7 - tau
                                V(lambda e, pt=pt, ki=ki: e.tensor_tensor(out=KT[:, ki, :], in0=pt[:, :], in1=blkmask[:, :], op=ALU.mult), [pt, blkmask], [KT])
                    if 'dbgKT' in self.debug and blk == 0:
                        for nm, tt, shp in (('dbgKT', KT, [128, 15, 128]), ('dbgLT', LT, [128, 2, 8, 2, 128]), ('dbgRb', Rb, [128, 2, 9, 2, 128]), ('dbgud', ud, [128, 8, NC]), ('dbgudm1', udm[1], [128, 8, NC])):
                            dd = self.scratch(nm, shp, BF16)
                            S.dma('sync', dd.t.ap(), tt.t.ap(), dd, tt)
                        for nm, tt, shp in (('dbgPWr', PWr, [128, 9, 128]), ('dbgPWi', PWi, [128, 9, 128]), ('dbgCRe', CRe, [128, 8, 128]), ('dbgT1', T1, [128, 16, 8, 16]), ('dbgmask', blkmask, [128, 128]), ('dbgsg', sg, [128, 8])):
                            dd = self.scratch(nm, shp, F32)
                            S.dma('sync', dd.t.ap(), tt.t.ap(), dd, tt)
                    for tp in range(8):
                        for (h0, h1) in HC:
                            py = pyr()
                            for j in range(8):
                                P(lambda e, py=py, tp=tp, j=j, h0=h0, h1=h1: e.matmul(py[:, :], lhsT=KT[:, 7 + tp - j, :], rhs=ud[:, j, h0:h1], start=(j == 0), stop=(j == 7)), [KT, ud], [py])
                            V(lambda e, py=py, tp=tp, h0=h0, h1=h1: e.tensor_tensor(out=Yv[:, h0:h1, tp], in0=py[:, :], in1=Yv[:, h0:h1, tp], op=ALU.add), [py, Ytp[tp]], [Ytp[tp]])
                    for g8 in range(8):
                        pr, m4 = g8 // 4, g8 % 4
                        um = udm[m4]
                        per_dir = [None, None]

                        def dir_chain(d_, g8=g8, pr=pr, m4=m4, um=um):
                            dg = d_ * 64 + blk * 8 + g8
                            ang, kk = angr(), kkr()
                            V(lambda e, ang=ang, d_=d_, dg=dg: e.tensor_scalar(out=ang[:, :], in0=pidxc[d_][:, :], scalar1=TH8[:, dg:dg + 1], scalar2=None, op0=ALU.mult), [pidxc[d_], TH8], [ang])
                            yield
                            yield from self.rr_gen(ang, kk, ang[:, :], kk[:, :], eng='vector')
                            sn, cs = snr(), csr()
                            A(lambda e, ang=ang, sn=sn: e.activation(out=sn[:, :], in_=ang[:, :], func=AF.Sin), [ang], [sn])
                            yield
                            A(lambda e, ang=ang, cs=cs: e.activation(out=cs[:, :], in_=ang[:, :], func=AF.Sin, scale=0.5), [ang], [cs])
                            yield
                            A(lambda e, cs=cs: e.activation(out=cs[:, :], in_=cs[:, :], func=AF.Square), [cs], [cs])
                            yield
                            V(lambda e, cs=cs: e.tensor_scalar(out=cs[:, :], in0=cs[:, :], scalar1=-2.0, scalar2=1.0, op0=ALU.mult, op1=ALU.add), [cs], [cs])
                            yield
                            wm, ws = wmr(), wsr()
                            for (h0, h1) in HC:
                                p1, p2 = p1r(), p2r()
                                for v_, pp in ((0, p1), (1, p2)):
                                    for j in range(8):
                                        e_ = 7 - j if d_ == 0 else j
                                        P(lambda e, pp=pp, d_=d_, e_=e_, v_=v_, j=j, h0=h0, h1=h1, pr=pr, um=um: e.matmul(pp[:, :], lhsT=LT[64 * pr:64 * pr + 64, d_, e_, v_, :], rhs=um[64 * pr:64 * pr + 64, j, h0:h1], start=(j == 0), stop=(j == 7)), [LT, um], [pp])
                                a, b = t1r(), t2r()
                                V(lambda e, p1=p1, a=a, cs=cs, h0=h0, h1=h1: e.tensor_tensor(out=a[:, :], in0=p1[:, :], in1=cs[:, h0:h1], op=ALU.mult), [p1, cs], [a])
                                yield
                                V(lambda e, p2=p2, b=b, sn=sn, h0=h0, h1=h1: e.tensor_tensor(out=b[:, :], in0=p2[:, :], in1=sn[:, h0:h1], op=ALU.mult), [p2, sn], [b])
                                yield
                                V(lambda e, a=a, b=b, wm=wm, h0=h0, h1=h1: e.tensor_tensor(out=wm[:, h0:h1], in0=a[:, :], in1=b[:, :], op=ALU.add), [a, b], [wm])
                                yield
                            rho_b = RHO8[:, dg:dg + 1]
                            e1, e2 = e1r(), e2r()
                            if d_ == 0:
                                V(lambda e, rho_b=rho_b, wm=wm, ws=ws: e.tensor_tensor_scan(out=ws[:, :], data0=rho_b.to_broadcast([128, NC]), data1=wm[:, :], initial=0.0, op0=ALU.mult, op1=ALU.add), [wm, RHO8], [ws])
                                yield
                                V(lambda e, cs=cs, ws=ws, e1=e1: e.tensor_tensor(out=e1[:, 1:NC], in0=cs[:, 0:NC - 1], in1=ws[:, 0:NC - 1], op=ALU.mult), [cs, ws], [e1])
                                yield
                                G(lambda e, e1=e1: e.memset(e1[:, 0:1], 0.0), [e1], [e1])
                                yield
                                V(lambda e, sn=sn, ws=ws, e2=e2: e.tensor_tensor(out=e2[:, 1:NC], in0=sn[:, 0:NC - 1], in1=ws[:, 0:NC - 1], op=ALU.mult), [sn, ws], [e2])
                                yield
                                G(lambda e, e2=e2: e.memset(e2[:, 0:1], 0.0), [e2], [e2])
                                yield
                            else:
                                V(lambda e, rho_b=rho_b, wm=wm, ws=ws: e.tensor_tensor_scan(out=ws[:, 0:NCC][:, ::-1], data0=rho_b.to_broadcast([128, NCC]), data1=wm[:, 0:NCC][:, ::-1], initial=0.0, op0=ALU.mult, op1=ALU.add), [wm, RHO8], [ws])
                                yield
                                V(lambda e, rho_b=rho_b, wm=wm, ws=ws: e.tensor_tensor_scan(out=ws[:, NCC:NC][:, ::-1], data0=rho_b.to_broadcast([128, NC - NCC]), data1=wm[:, NCC:NC][:, ::-1], initial=ws[:, 0:1], op0=ALU.mult, op1=ALU.add), [wm, RHO8, ws], [ws])
                                yield
                                for (tb, ee, eng) in ((cs, e1, V), (sn, e2, V)):
                                    eng(lambda e, tb=tb, ee=ee, ws=ws: e.tensor_tensor(out=ee[:, 0:NC - 1], in0=tb[:, 1:NC], in1=ws[:, 1:NC], op=ALU.mult), [tb, ws], [ee])
                                    yield
                                    eng(lambda e, tb=tb, ee=ee, ws=ws: e.tensor_tensor(out=ee[:, NC - 1:NC], in0=tb[:, 0:1], in1=ws[:, 0:1], op=ALU.mult), [tb, ws, ee], [ee])
                                    yield
                                    G(lambda e, ee=ee: e.memset(ee[:, NCC - 1:NCC], 0.0), [ee], [ee])
                                    yield
                            rz = rzr()
                            G(lambda e, rz=rz: e.memset(rz[:, :, :], 0.0), [], [rz])
                            yield
                            G(lambda e, rz=rz, d_=d_, g8=g8: e.tensor_copy(out=rz[:, :, g8 * 16:(g8 + 1) * 16], in_=Rb[:, d_, 1:9, :, g8 * 16:(g8 + 1) * 16].rearrange("p m v o -> p (m v) o")), [Rb], [rz])
                            yield
                            per_dir[d_] = (e1, e2, rz)
                        interleave([dir_chain(0), dir_chain(1)])
                        (e1f, e2f, rzf), (e1b, e2b, rzb) = per_dir
                        for tp in range(8):
                            mf, mb = tp + 1, 8 - tp
                            for (h0, h1) in HC:
                                py = pyr()
                                P(lambda e, py=py, mf=mf, h0=h0, h1=h1, rzf=rzf, e1f=e1f: e.matmul(py[:, :], lhsT=rzf[:, (mf - 1) * 2, :], rhs=e1f[:, h0:h1], start=True, stop=False), [rzf, e1f], [py])
                                P(lambda e, py=py, mf=mf, h0=h0, h1=h1, rzf=rzf, e2f=e2f: e.matmul(py[:, :], lhsT=rzf[:, (mf - 1) * 2 + 1, :], rhs=e2f[:, h0:h1], start=False, stop=False), [rzf, e2f], [py])
                                P(lambda e, py=py, mb=mb, h0=h0, h1=h1, rzb=rzb, e1b=e1b: e.matmul(py[:, :], lhsT=rzb[:, (mb - 1) * 2, :], rhs=e1b[:, h0:h1], start=False, stop=False), [rzb, e1b], [py])
                                P(lambda e, py=py, mb=mb, h0=h0, h1=h1, rzb=rzb, e2b=e2b: e.matmul(py[:, :], lhsT=rzb[:, (mb - 1) * 2 + 1, :], rhs=e2b[:, h0:h1], start=False, stop=True), [rzb, e2b], [py])
                                V(lambda e, py=py, tp=tp, h0=h0, h1=h1: e.tensor_tensor(out=Yv[:, h0:h1, tp], in0=py[:, :], in1=Yv[:, h0:h1, tp], op=ALU.add), [py, Ytp[tp]], [Ytp[tp]])
                    if 'yscanT' in self.debug:
                        ysd = self.scratch('yscanT', [D, NT], F32)
                        V(lambda e, blk=blk: e.tensor_scalar(out=uf[:, :], in0=uf[:, :], scalar1=dvec[:, blk:blk + 1], scalar2=None, op0=ALU.mult), [uf, dvec], [uf])
                        V(lambda e: e.tensor_tensor(out=uf[:, :], in0=Yacc[:, :], in1=uf[:, :], op=ALU.subtract), Ytp + [uf], [uf])
                        S.dma('sync', ysd[blk * 128:(blk + 1) * 128, :], uf[:, :], ysd, uf, part=True)
                    V(lambda e: e.tensor_tensor(out=uf[:, :], in0=Yacc[:, :], in1=Yacc[:, :], op=ALU.mult), Ytp, [uf])
                    V(lambda e: e.tensor_scalar(out=uf[:, :], in0=uf[:, :], scalar1=0.044715, scalar2=1.0, op0=ALU.mult, op1=ALU.add), [uf], [uf])
                    V(lambda e: e.tensor_tensor(out=uf[:, :], in0=uf[:, :], in1=Yacc[:, :], op=ALU.mult), Ytp + [uf], [uf])
                    A(lambda e: e.activation(out=uf[:, :], in_=uf[:, :], func=AF.Sigmoid, scale=2.0 * math.sqrt(2.0 / math.pi)), [uf], [uf])
                    V(lambda e: e.tensor_tensor(out=ud2, in0=uf[:, :], in1=Yacc[:, :], op=ALU.mult), Ytp + [uf], [ud])
                    S.dma('sync', ygT[blk * 128:(blk + 1) * 128, :], ud2, ygT, ud, part=True)
        self.ssm_glu_out(l)

    def ssm_glu_out(self, l):
        S = self.S
        I = self.I
        ygT = self.scratch('ygT', [D, NT], BF16)
        gates = self.scratch('gates', [NT, 3 * D], F32)
        y1 = self.scratch('y1', [NT, D], F32)
        TG = [(t0, min(t0 + 512, NT)) for t0 in range(0, NT, 512)]
        V = lambda fn, r, w: S.op('vector', fn, r, w)
        A = lambda fn, r, w: S.op('scalar', fn, r, w)
        with S.phase():
            wg = S.sbuf([128, 8, D], BF16, 'wglu')
            wo = S.sbuf([128, 8, D], BF16, 'wsout')
            for (wt, nm) in ((wg, 'w_ssm_glu'), (wo, 'w_ssm_out')):
                wvv = I[nm].t.ap()[l].rearrange("(k p) n -> p k n", p=128)
                for hh in range(2):
                    S.dma('gpsimd', wt[:, :, hh * 512:(hh + 1) * 512], wvv[:, :, hh * 512:(hh + 1) * 512], wt, I[nm], part=(hh == 1))
            ygr = S.ring(2, [128, 8, 512], BF16, 'ygl')
            y2r = S.ring(2, [128, 8, 512], BF16, 'y2T')
            sgr = S.ring(2, [128, 512], F32, 'sgl')
            gr = S.ring(2, [128, D], F32, 'g1')
            yr = S.ring(2, [128, D], F32, 'y1t')
            pgr = S.ring(3, [128, 512], F32, 'pgl', psum=True)
            pyr = S.ring(4, [128, 512], F32, 'py', psum=True)
            ygv = ygT.t.ap().rearrange("(j p) t -> p j t", p=128)
            for (t0, t1) in TG:
                n = t1 - t0
                yg = ygr()
                S.dma('sync', yg[:, :, 0:n], ygv[:, :, t0:t1], yg, ygT)
                y2 = y2r()
                for m in range(8):
                    pg = pgr()
                    for k in range(8):
                        S.op('tensor', lambda e, pg=pg, yg=yg, m=m, k=k, n=n: e.matmul(pg[:, 0:n], lhsT=wg[:, k, m * 128:(m + 1) * 128], rhs=yg[:, k, 0:n], start=(k == 0), stop=(k == 7)), [wg, yg], [pg])
                    sgt = sgr()
                    A(lambda e, pg=pg, sgt=sgt, n=n: e.activation(out=sgt[:, 0:n], in_=pg[:, 0:n], func=AF.Sigmoid), [pg], [sgt])
                    V(lambda e, yg=yg, y2=y2, sgt=sgt, m=m, n=n: e.tensor_tensor(out=y2[:, m, 0:n], in0=yg[:, m, 0:n], in1=sgt[:, 0:n], op=ALU.mult), [yg, sgt], [y2])
                for ii in range(n // 128):
                    i = t0 // 128 + ii
                    g1 = gr()
                    S.dma('sync', g1[:, :], gates[i * 128:(i + 1) * 128, D:2 * D], g1, gates)
                    yt = yr()
                    for nb in range(2):
                        py = pyr()
                        for m in range(8):
                            S.op('tensor', lambda e, py=py, y2=y2, m=m, ii=ii, nb=nb: e.matmul(py[:, :], lhsT=y2[:, m, ii * 128:(ii + 1) * 128], rhs=wo[:, m, nb * 512:(nb + 1) * 512], start=(m == 0), stop=(m == 7)), [y2, wo], [py])
                        V(lambda e, py=py, yt=yt, g1=g1, nb=nb: e.tensor_tensor(out=yt[:, nb * 512:(nb + 1) * 512], in0=py[:, :], in1=g1[:, nb * 512:(nb + 1) * 512], op=ALU.mult), [py, g1], [yt])
                    S.dma('sync', y1[i * 128:(i + 1) * 128, :], yt[:, :], y1, yt, part=True)


    def ln_stats_gen(self, x, st):
        S = self.S
        S.op('vector', lambda e: e.bn_stats(out=st[:, 0:6], in_=x[:, 0:512]), [x], [st])
        yield
        S.op('vector', lambda e: e.bn_stats(out=st[:, 6:12], in_=x[:, 512:1024]), [x], [st])
        yield
        S.op('vector', lambda e: e.bn_aggr(out=st[:, 12:14], in_=st[:, 0:12]), [st], [st])
        yield
        S.op('vector', lambda e: e.tensor_scalar(out=st[:, 14:15], in0=st[:, 13:14], scalar1=LN_EPS, scalar2=None, op0=ALU.add), [st], [st])
        yield
        S.op('scalar', lambda e: e.activation(out=st[:, 14:15], in_=st[:, 14:15], func=AF.Sqrt), [st], [st])
        yield
        S.op('vector', lambda e: e.reciprocal(out=st[:, 14:15], in_=st[:, 14:15]), [st], [st])
        yield

    def ln_stats(self, x, st):
        S = self.S
        S.op('vector', lambda e: e.bn_stats(out=st[:, 0:6], in_=x[:, 0:512]), [x], [st])
        S.op('vector', lambda e: e.bn_stats(out=st[:, 6:12], in_=x[:, 512:1024]), [x], [st])
        S.op('vector', lambda e: e.bn_aggr(out=st[:, 12:14], in_=st[:, 0:12]), [st], [st])
        self.rstd(st, 13, 14, LN_EPS)

    def ph_merge(self, l):
        S = self.S
        I = self.I
        V = lambda fn, r, w: S.op('vector', fn, r, w)
        G = lambda fn, r, w: S.op('gpsimd', fn, r, w)
        A = lambda fn, r, w: S.op('scalar', fn, r, w)
        ctx_out = l < DEPTH - 1
        ys = [self.scratch('y%d' % b, [NT, D], F32) for b in range(3)]
        xres = self.scr['xres']
        modscr = self.scr['modscr%d' % l]
        h2tm = self.scratch('h2tm', [NT, D], BF16)
        aff = self.scratch('aff', [NT, NEXP], F32)
        affT = self.scratch('affT', [NEXP, NT], F32)
        with S.phase():
            wo = S.sbuf([128, 8, D], BF16, 'wo')
            wov = I['w_o'].t.ap()[l].rearrange("(k p) n -> p k n", p=128)
            for hh in range(2):
                S.dma('gpsimd', wo[:, :, hh * 512:(hh + 1) * 512], wov[:, :, hh * 512:(hh + 1) * 512], wo, I['w_o'], part=(hh == 1))
            wr = S.sbuf([128, 8, NEXP], F32, 'wr')
            S.dma('sync', wr[:, :, :], I['w_router'].t.ap()[l].rearrange("(k p) e -> p k e", p=128), wr, I['w_router'])
            modv = S.sbuf([128, 2, 3, D], F32, 'modv2')
            for w in range(2):
                for jj, c0 in enumerate((2 * D, 3 * D, 4 * D)):
                    S.dma('sync', modv[:, w, jj, :], modscr[w, :, c0:c0 + D], modv, modscr, part=not (w == 0 and jj == 0))
            lng = S.sbuf([128, 2, D], F32, 'ln1gb')
            S.dma('sync', lng[:, 0, :], I['ln1_g'][l:l + 1, :].partition_broadcast(128), lng, I['ln1_g'])
            S.dma('sync', lng[:, 1, :], I['ln1_b'][l:l + 1, :].partition_broadcast(128), lng, I['ln1_b'], part=True)
            yar, ybr, ycr = (S.ring(2, [128, D], F32, nm) for nm in ('ya', 'yb', 'yc'))
            mbr = S.ring(2, [128, D], BF16, 'mb')
            mtr = S.ring(2, [128, 8, 128], BF16, 'mT')
            xr = S.ring(2, [128, D], F32, 'xt')
            tr = S.ring(2, [128, D], F32, 'tt')
            hr = S.ring(2, [128, D], F32, 'h2f')
            hbr = S.ring(2, [128, D], BF16, 'h2b')
            htr = S.ring(2, [128, 8, 128], F32, 'h2T')
            str_ = S.ring(6, [128, 16], F32, 'st')
            afr = S.ring(2, [128, 2, NEXP], F32, 'afft')
            atr = S.ring(2, [NEXP, 128], F32, 'affTt')
            ptr_ = S.ring(1, [128, 8, 128], BF16, 'pTm', psum=True)
            pyr = S.ring(2, [128, 512], F32, 'pym', psum=True)
            phr = S.ring(1, [128, 8, 128], F32, 'pTh', psum=True)
            plr = S.ring(2, [128, NEXP], F32, 'plog', psum=True)
            par_ = S.ring(1, [NEXP, 128], F32, 'paT', psum=True)
            def tile_chain(i):
                w = 1 if i < 2 else 0
                rows = slice(i * 128, (i + 1) * 128)
                ya, yb, yc = yar(), ybr(), ycr()
                for yt_, ysrc in ((ya, ys[0]), (yb, ys[1]), (yc, ys[2])):
                    S.dma('sync', yt_[:, :], ysrc[rows, :], yt_, ysrc)
                V(lambda e, ya=ya, yb=yb: e.tensor_tensor(out=ya[:, :], in0=ya[:, :], in1=yb[:, :], op=ALU.add), [ya, yb], [ya])
                yield
                mb = mbr()
                V(lambda e, ya=ya, yc=yc, mb=mb: e.tensor_tensor(out=mb[:, :], in0=ya[:, :], in1=yc[:, :], op=ALU.add), [ya, yc], [mb])
                yield
                pT = ptr_()
                for k in range(8):
                    S.op('tensor', lambda e, pT=pT, mb=mb, k=k: e.transpose(out=pT[:, k, :], in_=mb[:, k * 128:(k + 1) * 128], identity=self.ident_b[:, :]), [mb, self.ident_b], [pT])
                mT = mtr()
                A(lambda e, pT=pT, mT=mT: e.copy(out=mT[:, :, :], in_=pT[:, :, :]), [pT], [mT])
                yield
                xt = xr()
                S.dma('sync', xt[:, :], xres[rows, :], xt, xres)
                tt = tr()
                for nb in range(2):
                    py = pyr()
                    for k in range(8):
                        S.op('tensor', lambda e, py=py, mT=mT, k=k, nb=nb: e.matmul(py[:, :], lhsT=mT[:, k, :], rhs=wo[:, k, nb * 512:(nb + 1) * 512], start=(k == 0), stop=(k == 7)), [mT, wo], [py])
                    V(lambda e, py=py, tt=tt, nb=nb, w=w: e.tensor_tensor(out=tt[:, nb * 512:(nb + 1) * 512], in0=py[:, :], in1=modv[:, w, 0, nb * 512:(nb + 1) * 512], op=ALU.mult), [py, modv], [tt])
                    yield
                V(lambda e, xt=xt, tt=tt: e.scalar_tensor_tensor(out=tt[:, :], in0=xt[:, :], scalar=ALPHA, in1=tt[:, :], op0=ALU.mult, op1=ALU.add), [xt, tt], [tt])
                yield
                st = str_()
                yield from self.ln_stats_gen(tt, st)
                V(lambda e, tt=tt, st=st: e.tensor_scalar(out=tt[:, :], in0=tt[:, :], scalar1=st[:, 12:13], scalar2=st[:, 14:15], op0=ALU.subtract, op1=ALU.mult), [tt, st], [tt])
                yield
                V(lambda e, tt=tt: e.tensor_tensor(out=tt[:, :], in0=tt[:, :], in1=lng[:, 0, :], op=ALU.mult), [tt, lng], [tt])
                yield
                V(lambda e, tt=tt, xt=xt: e.tensor_tensor(out=xt[:, :], in0=tt[:, :], in1=lng[:, 1, :], op=ALU.add), [tt, lng], [xt])
                yield
                S.dma('sync', xres[rows, :], xt[:, :], xres, xt, part=True)
                st2 = str_()
                yield from self.ln_stats_gen(xt, st2)
                hf = hr()
                V(lambda e, xt=xt, st2=st2, hf=hf: e.tensor_scalar(out=hf[:, :], in0=xt[:, :], scalar1=st2[:, 12:13], scalar2=st2[:, 14:15], op0=ALU.subtract, op1=ALU.mult), [xt, st2], [hf])
                yield
                V(lambda e, hf=hf, w=w: e.tensor_tensor(out=hf[:, :], in0=hf[:, :], in1=modv[:, w, 2, :], op=ALU.mult), [hf, modv], [hf])
                yield
                V(lambda e, hf=hf, w=w: e.tensor_tensor(out=hf[:, :], in0=hf[:, :], in1=modv[:, w, 1, :], op=ALU.add), [hf, modv], [hf])
                yield
                hb = hbr()
                A(lambda e, hf=hf, hb=hb: e.copy(out=hb[:, :], in_=hf[:, :]), [hf], [hb])
                yield
                S.dma('sync', h2tm[rows, :], hb[:, :], h2tm, hb, part=True)
                ph = phr()
                for k in range(8):
                    S.op('tensor', lambda e, ph=ph, hf=hf, k=k: e.transpose(out=ph[:, k, :], in_=hf[:, k * 128:(k + 1) * 128], identity=self.ident_f[:, :]), [hf, self.ident_f], [ph])
                hT = htr()
                A(lambda e, ph=ph, hT=hT: e.copy(out=hT[:, :, :], in_=ph[:, :, :]), [ph], [hT])
                yield
                pl = plr()
                for k in range(8):
                    S.op('tensor', lambda e, pl=pl, hT=hT, k=k: e.matmul(pl[:, :], lhsT=hT[:, k, :], rhs=wr[:, k, :], start=(k == 0), stop=(k == 7)), [hT, wr], [pl])
                st3 = str_()
                af = afr()
                V(lambda e, pl=pl, st3=st3: e.reduce_max(out=st3[:, 0:1], in_=pl[:, :], axis=AX.X), [pl], [st3])
                yield
                V(lambda e, st3=st3: e.tensor_scalar(out=st3[:, 1:2], in0=st3[:, 0:1], scalar1=-1.0, scalar2=None, op0=ALU.mult), [st3], [st3])
                yield
                A(lambda e, pl=pl, st3=st3, af=af: e.activation(out=af[:, 0, :], in_=pl[:, :], func=AF.Exp, bias=st3[:, 1:2], accum_out=st3[:, 2:3]), [pl, st3], [af, st3])
                yield
                V(lambda e, st3=st3: e.reciprocal(out=st3[:, 3:4], in_=st3[:, 2:3]), [st3], [st3])
                yield
                V(lambda e, st3=st3, af=af: e.tensor_scalar(out=af[:, 1, :], in0=af[:, 0, :], scalar1=st3[:, 3:4], scalar2=None, op0=ALU.mult), [af, st3], [af])
                yield
                S.dma('sync', aff[rows, :], af[:, 1, :], aff, af, part=True)
                pa = par_()
                S.op('tensor', lambda e, pa=pa, af=af: e.transpose(out=pa[:, :], in_=af[:, 1, :], identity=self.ident_f[:, :]), [af, self.ident_f], [pa])
                at = atr()
                A(lambda e, pa=pa, at=at: e.copy(out=at[:, :], in_=pa[:, :]), [pa], [at])
                yield
                S.dma('sync', affT[:, i * 128:(i + 1) * 128], at[:, :], affT, at, part=True)


            tiles = list(range(0 if ctx_out else 2, NTILE))
            for t0_ in range(0, len(tiles), 2):
                interleave([tile_chain(t_) for t_ in tiles[t0_:t0_ + 2]])
    def ph_moe(self, l):
        S = self.S
        I = self.I
        V = lambda fn, r, w: S.op('vector', fn, r, w)
        G = lambda fn, r, w: S.op('gpsimd', fn, r, w)
        A = lambda fn, r, w: S.op('scalar', fn, r, w)
        P = lambda fn, r, w: S.op('tensor', fn, r, w)
        ctx_out = l < DEPTH - 1
        h2tm = self.scr['h2tm']
        aff = self.scr['aff']
        affT = self.scr['affT']
        ymoe = self.scratch('ymoe', [NT, D], F32)
        sets = []
        if ctx_out:
            sets.append((0, 2, 2 * NCTX // NEXP, 1))
        sets.append((2, 32, 2 * NLAT // NEXP, 4))
        with S.phase():
            zt = S.sbuf([128, D], F32, 'zt')
            G(lambda e: e.memset(zt[:, :], 0.0), [], [zt])
            for i in range(NTILE):
                S.dma('sync', ymoe[i * 128:(i + 1) * 128, :], zt[:, :], ymoe, zt, part=(i > 0))
            ymoe_t = [Buf(ymoe.t, 'ymoe_t%d' % i) for i in range(NTILE)]
            for b in ymoe_t:
                b.w = ymoe.w
            acc_grp = [DSem(S, 'ymacc0'), DSem(S, 'ymacc1')]
            aff_all = S.sbuf([128, NTILE, NEXP], F32, 'aff_all')
            mask_all = S.sbuf([128, NTILE, NEXP], F32, 'mask_all')
            slot_all = S.sbuf([128, NTILE, NEXP], F32, 'slot_all')
            gate_all = S.sbuf([128, NTILE, NEXP], F32, 'gate_all')
            slotT = S.sbuf([NEXP, NT], F32, 'slotT')
            S.dma('sync', aff_all[:, :, :], aff.t.ap().rearrange("(i p) e -> p i e", p=128), aff_all, aff)
            esel = S.sbuf([NEXP, NEXP, 128], F32, 'esel')
            V(lambda e: e.tensor_copy(out=esel[:, :, :], in_=self.ident_f[0:NEXP, 0:NEXP].unsqueeze(2).to_broadcast([NEXP, NEXP, 128])), [self.ident_f], [esel])
            ustr = S.sbuf([128, 128], F32, 'ustr')
            G(lambda e: e.affine_select(out=ustr[:, :], in_=self.ones_f[:, :], pattern=[[1, 128]], compare_op=ALU.is_gt, fill=0.0, base=0, channel_multiplier=-1), [self.ones_f], [ustr])
            irow = S.sbuf([128, 512], F32, 'irow')
            pcol = S.sbuf([128, 4], F32, 'pcol')
            with S.phase():
                ti = S.sbuf([128, 512], I32, 'ti')
                G(lambda e: e.iota(ti[:, :], pattern=[[1, 512]], base=0, channel_multiplier=0), [], [ti])
                V(lambda e: e.tensor_copy(out=irow[:, :], in_=ti[:, :]), [ti], [irow])
                G(lambda e: e.iota(ti[:, 0:4], pattern=[[128, 4]], base=0, channel_multiplier=1), [irow], [ti])
                V(lambda e: e.tensor_copy(out=pcol[:, :], in_=ti[:, 0:4]), [ti], [pcol])
                affTs = S.sbuf([NEXP, NT], F32, 'affTs')
                junk = S.sbuf([NEXP, NLAT], F32, 'junk')
                S.dma('sync', affTs[:, :], affT[:, :], affTs, affT)
                pb = S.ring(2, [128, NEXP], F32, 'pthr', psum=True)
                pc = S.ring(2, [128, NEXP], F32, 'pcum', psum=True)
                pst = S.ring(2, [NEXP, 128], F32, 'pslT', psum=True)
                for (ti0, ntl, cap, nst) in sets:
                    c0, c1 = ti0 * 128, (ti0 + ntl) * 128
                    bs = S.sbuf([NEXP, 8], F32, 'bs')
                    V(lambda e, bs=bs: e.memset(bs[:, 0:1], 0.0), [], [bs])
                    V(lambda e, bs=bs: e.memset(bs[:, 1:2], 1.0), [bs], [bs])
                    for it in range(30):
                        V(lambda e, bs=bs: e.tensor_tensor(out=bs[:, 2:3], in0=bs[:, 0:1], in1=bs[:, 1:2], op=ALU.add), [bs], [bs])
                        V(lambda e, bs=bs: e.tensor_scalar(out=bs[:, 2:3], in0=bs[:, 2:3], scalar1=0.5, scalar2=None, op0=ALU.mult), [bs], [bs])
                        V(lambda e, bs=bs, c0=c0, c1=c1: e.tensor_scalar(out=junk[:, 0:c1 - c0], in0=affTs[:, c0:c1], scalar1=bs[:, 2:3], scalar2=0.0, op0=ALU.is_ge, op1=ALU.add, accum_out=bs[:, 3:4]), [affTs, bs], [junk, bs])
                        V(lambda e, bs=bs, cap=cap: e.tensor_scalar(out=bs[:, 4:5], in0=bs[:, 3:4], scalar1=float(cap), scalar2=None, op0=ALU.is_ge), [bs], [bs])
                        V(lambda e, bs=bs: e.tensor_scalar(out=bs[:, 5:6], in0=bs[:, 4:5], scalar1=-1.0, scalar2=1.0, op0=ALU.mult, op1=ALU.add), [bs], [bs])
                        V(lambda e, bs=bs: e.tensor_tensor(out=bs[:, 6:7], in0=bs[:, 2:3], in1=bs[:, 0:1], op=ALU.subtract), [bs], [bs])
                        V(lambda e, bs=bs: e.tensor_tensor(out=bs[:, 7:8], in0=bs[:, 2:3], in1=bs[:, 1:2], op=ALU.subtract), [bs], [bs])
                        V(lambda e, bs=bs: e.scalar_tensor_tensor(out=bs[:, 0:1], in0=bs[:, 6:7], scalar=bs[:, 4:5], in1=bs[:, 0:1], op0=ALU.mult, op1=ALU.add), [bs], [bs])
                        V(lambda e, bs=bs: e.scalar_tensor_tensor(out=bs[:, 1:2], in0=bs[:, 7:8], scalar=bs[:, 5:6], in1=bs[:, 1:2], op0=ALU.mult, op1=ALU.add), [bs], [bs])
                    dg = S.sbuf([NEXP, NEXP], F32, 'dg')
                    V(lambda e, bs=bs, dg=dg: e.tensor_scalar(out=dg[:, :], in0=self.ident_f[0:NEXP, 0:NEXP], scalar1=bs[:, 0:1], scalar2=None, op0=ALU.mult), [bs, self.ident_f], [dg])
                    pthr = pb()
                    P(lambda e, pthr=pthr, dg=dg: e.matmul(pthr[:, :], lhsT=self.ones_f[0:NEXP, :], rhs=dg[:, :], start=True, stop=True), [dg, self.ones_f], [pthr])
                    thr = S.sbuf([128, NEXP], F32, 'thr')
                    V(lambda e, pthr=pthr, thr=thr: e.tensor_copy(out=thr[:, :], in_=pthr[:, :]), [pthr], [thr])
                    for ii in range(ntl):
                        i = ti0 + ii
                        V(lambda e, i=i, thr=thr: e.tensor_tensor(out=mask_all[:, i, :], in0=aff_all[:, i, :], in1=thr[:, :], op=ALU.is_ge), [aff_all, thr], [mask_all])
                        V(lambda e, i=i: e.tensor_tensor(out=gate_all[:, i, :], in0=aff_all[:, i, :], in1=mask_all[:, i, :], op=ALU.mult), [aff_all, mask_all], [gate_all])
                        pcm = pc()
                        for jj in range(ii):
                            j = ti0 + jj
                            P(lambda e, pcm=pcm, j=j, jj=jj: e.matmul(pcm[:, :], lhsT=self.ones_f[:, :], rhs=mask_all[:, j, :], start=(jj == 0), stop=False), [mask_all, self.ones_f], [pcm])
                        P(lambda e, pcm=pcm, i=i, ii=ii: e.matmul(pcm[:, :], lhsT=ustr[:, :], rhs=mask_all[:, i, :], start=(ii == 0), stop=True), [mask_all, ustr], [pcm])
                        V(lambda e, pcm=pcm, i=i: e.tensor_tensor(out=slot_all[:, i, :], in0=pcm[:, :], in1=mask_all[:, i, :], op=ALU.mult), [pcm, mask_all], [slot_all])
                        V(lambda e, i=i: e.tensor_tensor(out=slot_all[:, i, :], in0=slot_all[:, i, :], in1=mask_all[:, i, :], op=ALU.add), [slot_all, mask_all], [slot_all])
                        V(lambda e, i=i: e.tensor_scalar(out=slot_all[:, i, :], in0=slot_all[:, i, :], scalar1=-1.0, scalar2=None, op0=ALU.add), [slot_all], [slot_all])
                        ps_ = pst()
                        P(lambda e, ps_=ps_, i=i: e.transpose(out=ps_[:, :], in_=slot_all[:, i, :], identity=self.ident_f[:, :]), [slot_all, self.ident_f], [ps_])
                        A(lambda e, ps_=ps_, i=i: e.copy(out=slotT[:, i * 128:(i + 1) * 128], in_=ps_[:, :]), [ps_], [slotT])
            if 'slotdbg' in self.debug:
                sd = self.scratch('slotdbg', [128, NTILE, NEXP], F32)
                S.dma('sync', sd[:, :, :], slot_all[:, :, :], sd, slot_all)
            with S.phase():
                selT = S.sbuf([128, 32, 512], BF16, 'selT')
                sel = S.sbuf([128, 4, NLAT], BF16, 'sel')
                xsT = S.sbuf([128, 8, 512], BF16, 'xsT')
                actT = S.sbuf([128, 22, 512], BF16, 'actT')
                ye = S.sbuf([128, 4, D], BF16, 'ye')
                h2r = S.ring(4, [128, D], BF16, 'h2l')
                wgr = S.ring(3, [128, 8, 256], BF16, 'wga')
                wur = S.ring(3, [128, 8, 256], BF16, 'wgu')
                wdr = S.ring(4, [128, 2, 512], BF16, 'wdn')
                sar = S.ring(2, [128, 512], F32, 'sact')
                ytr = S.ring(2, [128, D], F32, 'ysc')
                acc4 = S.ring(4, [128, 512], F32, 'pacc', psum=True)
                gen4 = S.ring(4, [128, 512], F32, 'pgen', psum=True)
                sbufs = []
                for (ti0, ntl, cap, nst) in sets:
                    if nst == 4:
                        sbufs.append(dict(selT=selT, sel=sel, xsT=xsT, actT=actT, ye=ye))
                    else:
                        sbufs.append(dict(selT=S.sbuf([128, ntl, 128 * nst], BF16, 'selTc'), sel=S.sbuf([128, nst, ntl * 128], BF16, 'selc'),
                                          xsT=S.sbuf([128, 8, 128 * nst], BF16, 'xsTc'), actT=S.sbuf([128, 22, 128 * nst], BF16, 'actTc'),
                                          ye=S.sbuf([128, nst, D], BF16, 'yec')))

                def make_item(ex, si):
                    ti0, ntl, cap, nst = sets[si]
                    bf = sbufs[si]
                    selT_, sel_, xsT_, ye_ = bf['selT'], bf['sel'], bf['xsT'], bf['ye']
                    ns = nst * 128
                    ntok = ntl * 128
                    tok0 = ti0 * 128

                    def selT_build():
                        for ii in range(ntl):
                            i = ti0 + ii
                            V(lambda e, ii=ii, i=i: e.tensor_scalar(out=selT_[:, ii, 0:ns], in0=irow[:, 0:ns], scalar1=slot_all[:, i, ex:ex + 1], scalar2=None, op0=ALU.is_equal), [irow, slot_all], [selT_])

                    def sel_build():
                        for c0 in range(0, ntok, 512):
                            cn = min(512, ntok - c0)
                            pbc = gen4()
                            P(lambda e, pbc=pbc, c0=c0, cn=cn: e.matmul(pbc[:, 0:cn], lhsT=esel[:, ex, :], rhs=slotT[:, tok0 + c0:tok0 + c0 + cn], start=True, stop=True), [esel, slotT], [pbc])
                            for st_ in range(nst):
                                V(lambda e, pbc=pbc, st_=st_, c0=c0, cn=cn: e.tensor_scalar(out=sel_[:, st_, c0:c0 + cn], in0=pbc[:, 0:cn], scalar1=pcol[:, st_:st_ + 1], scalar2=None, op0=ALU.is_equal), [pbc, pcol], [sel_])

                    def gather():
                        for kh in range(2):
                            pgs = [acc4() for _ in range(4)]
                            for ii in range(ntl):
                                i = ti0 + ii
                                ht = h2r()
                                S.dma('sync', ht[:, :], h2tm[i * 128:(i + 1) * 128, :], ht, h2tm)
                                for kk in range(4):
                                    k = kh * 4 + kk
                                    P(lambda e, pg=pgs[kk], ht=ht, k=k, ii=ii: e.matmul(pg[:, 0:ns], lhsT=ht[:, k * 128:(k + 1) * 128], rhs=selT_[:, ii, 0:ns], start=(ii == 0), stop=(ii == ntl - 1)), [ht, selT_], [pgs[kk]])
                            for kk in range(4):
                                k = kh * 4 + kk
                                A(lambda e, pg=pgs[kk], k=k: e.copy(out=xsT_[:, k, 0:ns], in_=pg[:, 0:ns]), [pgs[kk]], [xsT_])

                    def scatter():
                        for ii in range(ntl):
                            i = ti0 + ii
                            yt = ytr()
                            for nb in range(2):
                                psc = gen4()
                                for st_ in range(nst):
                                    P(lambda e, psc=psc, st_=st_, ii=ii, nb=nb: e.matmul(psc[:, :], lhsT=sel_[:, st_, ii * 128:(ii + 1) * 128], rhs=ye_[:, st_, nb * 512:(nb + 1) * 512], start=(st_ == 0), stop=(st_ == nst - 1)), [sel_, ye_], [psc])
                                if nb == 0:
                                    V(lambda e, psc=psc, yt=yt, nb=nb, i=i: e.tensor_scalar(out=yt[:, nb * 512:(nb + 1) * 512], in0=psc[:, :], scalar1=gate_all[:, i, ex:ex + 1], scalar2=None, op0=ALU.mult), [psc, gate_all], [yt])
                                else:
                                    A(lambda e, psc=psc, yt=yt, nb=nb, i=i: e.activation(out=yt[:, nb * 512:(nb + 1) * 512], in_=psc[:, :], func=AF.Identity, scale=gate_all[:, i, ex:ex + 1]), [psc, gate_all], [yt])
                            yb = ymoe_t[i]
                            yb.grp = acc_grp[ex % 2]
                            S.dma('gpsimd', ymoe[i * 128:(i + 1) * 128, :], yt[:, :], yb, yt, accum_op=ALU.add)
                    return (selT_build, sel_build, gather, scatter)

                def mlp_joint(ex):
                    wgv = I['w_gate_up'].t.ap()[l, ex].rearrange("(k p) n -> p k n", p=128)
                    wdv = I['w_down'].t.ap()[l, ex].rearrange("(f p) n -> p f n", p=128)
                    for b in range(11):
                        wa, wu = wgr(), wur()
                        S.dma('gpsimd', wa[:, :, :], wgv[:, :, b * 256:(b + 1) * 256], wa, I['w_gate_up'])
                        S.dma('gpsimd', wu[:, :, :], wgv[:, :, DEXP + b * 256:DEXP + (b + 1) * 256], wu, I['w_gate_up'])
                        for ff in range(2):
                            f = 2 * b + ff
                            for si in range(len(sets)):
                                ns = sets[si][3] * 128
                                xs_, act_ = sbufs[si]['xsT'], sbufs[si]['actT']
                                pa_, pu_ = gen4(), gen4()
                                for k in range(8):
                                    P(lambda e, pa_=pa_, wa=wa, k=k, ff=ff, ns=ns, xs_=xs_: e.matmul(pa_[:, 0:ns], lhsT=wa[:, k, ff * 128:(ff + 1) * 128], rhs=xs_[:, k, 0:ns], start=(k == 0), stop=(k == 7)), [wa, xs_], [pa_])
                                for k in range(8):
                                    P(lambda e, pu_=pu_, wu=wu, k=k, ff=ff, ns=ns, xs_=xs_: e.matmul(pu_[:, 0:ns], lhsT=wu[:, k, ff * 128:(ff + 1) * 128], rhs=xs_[:, k, 0:ns], start=(k == 0), stop=(k == 7)), [wu, xs_], [pu_])
                                sa = sar()
                                A(lambda e, pa_=pa_, sa=sa, ns=ns: e.activation(out=sa[:, 0:ns], in_=pa_[:, 0:ns], func=AF.Silu), [pa_], [sa])
                                V(lambda e, pu_=pu_, sa=sa, f=f, ns=ns, act_=act_: e.tensor_tensor(out=act_[:, f, 0:ns], in0=pu_[:, 0:ns], in1=sa[:, 0:ns], op=ALU.mult), [pu_, sa], [act_])
                    for nb in range(2):
                        pds = []
                        for si in range(len(sets)):
                            nst = sets[si][3]
                            pds.append([acc4() for _ in range(nst)] if nst == 4 else [gen4() for _ in range(nst)])
                        for f2 in range(11):
                            wd = wdr()
                            S.dma('gpsimd', wd[:, :, :], wdv[:, 2 * f2:2 * f2 + 2, nb * 512:(nb + 1) * 512], wd, I['w_down'])
                            for ff in range(2):
                                f = 2 * f2 + ff
                                for si in range(len(sets)):
                                    act_ = sbufs[si]['actT']
                                    for st_ in range(sets[si][3]):
                                        P(lambda e, pd=pds[si][st_], wd=wd, f=f, ff=ff, st_=st_, act_=act_: e.matmul(pd[:, :], lhsT=act_[:, f, st_ * 128:(st_ + 1) * 128], rhs=wd[:, ff, :], start=(f == 0), stop=(f == 21)), [act_, wd], [pds[si][st_]])
                        for si in range(len(sets)):
                            ye_ = sbufs[si]['ye']
                            for st_ in range(sets[si][3]):
                                A(lambda e, pd=pds[si][st_], st_=st_, nb=nb, ye_=ye_: e.copy(out=ye_[:, st_, nb * 512:(nb + 1) * 512], in_=pd[:, :]), [pds[si][st_]], [ye_])

                items = [[make_item(ex, si) for si in range(len(sets))] for ex in range(NEXP)]
                for it_ in items[0]:
                    it_[0]()
                for ex in range(NEXP):
                    for it_ in items[ex]:
                        it_[2]()
                    if ex + 1 < NEXP:
                        for it_ in items[ex + 1]:
                            it_[0]()
                    for it_ in items[ex]:
                        it_[1]()
                    mlp_joint(ex)
                    for it_ in items[ex]:
                        it_[3]()
            ymoe.grp = acc_grp[0]
            ymoe.w = ('d', acc_grp[0])
            ymoe.r = {('d', id(acc_grp[1])): ('d', acc_grp[1])}

    def ph_ln2(self, l):
        S = self.S
        I = self.I
        V = lambda fn, r, w: S.op('vector', fn, r, w)
        G = lambda fn, r, w: S.op('gpsimd', fn, r, w)
        ctx_out = l < DEPTH - 1
        last = (l == DEPTH - 1)
        xres = self.scr['xres']
        modscr = self.scr['modscr%d' % l]
        ymoe = self.scratch('ymoe', [NT, D], F32)
        with S.phase():
            g2 = S.sbuf([128, 2, D], F32, 'g2v')
            for w in range(2):
                S.dma('sync', g2[:, w, :], modscr[w, :, 5 * D:6 * D], g2, modscr, part=(w == 1))
            lng = S.sbuf([128, 2, D], F32, 'ln2gb')
            S.dma('sync', lng[:, 0, :], I['ln2_g'][l:l + 1, :].partition_broadcast(128), lng, I['ln2_g'])
            S.dma('sync', lng[:, 1, :], I['ln2_b'][l:l + 1, :].partition_broadcast(128), lng, I['ln2_b'], part=True)
            xr = S.ring(3, [128, D], F32, 'xt')
            yr = S.ring(3, [128, D], F32, 'ym')
            str_ = S.ring(3, [128, 16], F32, 'st')
            def tile_chain(i):
                w = 1 if i < 2 else 0
                rows = slice(i * 128, (i + 1) * 128)
                xt, ym = xr(), yr()
                S.dma('sync', xt[:, :], xres[rows, :], xt, xres)
                S.dma('sync', ym[:, :], ymoe[rows, :], ym, ymoe)
                V(lambda e, ym=ym, w=w: e.tensor_tensor(out=ym[:, :], in0=ym[:, :], in1=g2[:, w, :], op=ALU.mult), [ym, g2], [ym])
                yield
                V(lambda e, xt=xt, ym=ym: e.scalar_tensor_tensor(out=ym[:, :], in0=xt[:, :], scalar=ALPHA, in1=ym[:, :], op0=ALU.mult, op1=ALU.add), [xt, ym], [ym])
                yield
                st = str_()
                yield from self.ln_stats_gen(ym, st)
                V(lambda e, ym=ym, st=st: e.tensor_scalar(out=ym[:, :], in0=ym[:, :], scalar1=st[:, 12:13], scalar2=st[:, 14:15], op0=ALU.subtract, op1=ALU.mult), [ym, st], [ym])
                yield
                V(lambda e, ym=ym: e.tensor_tensor(out=ym[:, :], in0=ym[:, :], in1=lng[:, 0, :], op=ALU.mult), [ym, lng], [ym])
                yield
                V(lambda e, ym=ym, xt=xt: e.tensor_tensor(out=xt[:, :], in0=ym[:, :], in1=lng[:, 1, :], op=ALU.add), [ym, lng], [xt])
                yield
                if last:
                    S.dma('sync', self.out[(i - 2) * 128:(i - 1) * 128, :], xt[:, :], self.out, xt, part=True)
                else:
                    S.dma('sync', xres[rows, :], xt[:, :], xres, xt, part=True)

            tiles = list(range(0 if ctx_out else 2, NTILE))
            for t0_ in range(0, len(tiles), 2):
                interleave([tile_chain(t_) for t_ in tiles[t0_:t0_ + 2]])
    def range_reduce(self, eng, x, k):
        S = self.S
        S.op(eng, lambda e: e.tensor_scalar(out=k[:, :], in0=x[:, :], scalar1=float(1.0 / TWO_PI), scalar2=MAGIC, op0=ALU.mult, op1=ALU.add), [x], [k])
        S.op(eng, lambda e: e.tensor_scalar(out=k[:, :], in0=k[:, :], scalar1=-MAGIC, scalar2=None, op0=ALU.add), [k], [k])
        S.op('vector', lambda e: e.scalar_tensor_tensor(out=x[:, :], in0=k[:, :], scalar=-CW1, in1=x[:, :], op0=ALU.mult, op1=ALU.add), [x, k], [x])
        S.op('vector', lambda e: e.scalar_tensor_tensor(out=x[:, :], in0=k[:, :], scalar=-CW2, in1=x[:, :], op0=ALU.mult, op1=ALU.add), [x, k], [x])
        S.op(eng, lambda e: e.tensor_scalar(out=x[:, :], in0=x[:, :], scalar1=PI_LO, scalar2=-PI_LO, op0=ALU.min, op1=ALU.max), [x], [x])

    def rope_tables(self):
        S = self.S
        cosT = S.sbuf([128, NT], F32, 'cosT')
        sinT = S.sbuf([128, NT], F32, 'sinT')
        ang = S.sbuf([128, NLAT], F32, 'ang')
        kk = S.sbuf([128, NLAT], F32, 'kk')
        pidx = S.sbuf([128, 1], I32, 'pidx')
        pf = S.sbuf([128, 4], F32, 'pf')
        ti = S.sbuf([128, 2], I32, 'ti')
        rowi = S.sbuf([128, NLAT], I32, 'rowi')
        S.op('gpsimd', lambda e: e.iota(pidx[:, :], pattern=[[0, 1]], base=0, channel_multiplier=1), [], [pidx])
        S.op('vector', lambda e: e.tensor_scalar(out=ti[:, 0:1], in0=pidx[:, :], scalar1=15, scalar2=None, op0=ALU.bitwise_and), [pidx], [ti])
        S.op('vector', lambda e: e.tensor_scalar(out=ti[:, 1:2], in0=pidx[:, :], scalar1=5, scalar2=1, op0=ALU.logical_shift_right, op1=ALU.bitwise_and), [pidx], [ti])
        S.op('vector', lambda e: e.tensor_copy(out=pf[:, 0:2], in_=ti[:, 0:2]), [ti], [pf])
        S.op('scalar', lambda e: e.activation(out=pf[:, 2:3], in_=pf[:, 0:1], func=AF.Exp, scale=-math.log(10000.0) / 16.0), [pf], [pf])
        S.op('vector', lambda e: e.tensor_tensor(out=pf[:, 3:4], in0=pf[:, 1:2], in1=pf[:, 2:3], op=ALU.mult), [pf], [pf])
        S.op('vector', lambda e: e.tensor_tensor(out=pf[:, 0:1], in0=pf[:, 2:3], in1=pf[:, 3:4], op=ALU.subtract), [pf], [pf])
        S.op('gpsimd', lambda e: e.iota(rowi[:, :], pattern=[[1, 64], [0, 64]], base=0, channel_multiplier=0), [], [rowi])
        S.op('vector', lambda e: e.tensor_copy(out=ang[:, :], in_=rowi[:, :]), [rowi], [ang])
        S.op('vector', lambda e: e.tensor_scalar(out=ang[:, :], in0=ang[:, :], scalar1=pf[:, 0:1], scalar2=None, op0=ALU.mult), [ang, pf], [ang])
        S.op('gpsimd', lambda e: e.iota(rowi[:, :], pattern=[[0, 64], [1, 64]], base=0, channel_multiplier=0), [ang], [rowi])
        S.op('vector', lambda e: e.tensor_copy(out=kk[:, :], in_=rowi[:, :]), [rowi], [kk])
        S.op('vector', lambda e: e.scalar_tensor_tensor(out=ang[:, :], in0=kk[:, :], scalar=pf[:, 3:4], in1=ang[:, :], op0=ALU.mult, op1=ALU.add), [kk, pf, ang], [ang])
        self.range_reduce('vector', ang, kk)
        S.op('scalar', lambda e: e.activation(out=sinT[:, NCTX:NT], in_=ang[:, :], func=AF.Sin), [ang], [sinT])
        S.op('scalar', lambda e: e.activation(out=kk[:, :], in_=ang[:, :], func=AF.Sin, scale=0.5), [ang], [kk])
        S.op('vector', lambda e: e.tensor_tensor(out=kk[:, :], in0=kk[:, :], in1=kk[:, :], op=ALU.mult), [kk], [kk])
        S.op('vector', lambda e: e.tensor_scalar(out=cosT[:, NCTX:NT], in0=kk[:, :], scalar1=-2.0, scalar2=1.0, op0=ALU.mult, op1=ALU.add), [kk], [cosT])
        S.op('gpsimd', lambda e: e.memset(cosT[:, 0:NCTX], 1.0), [], [cosT])
        S.op('gpsimd', lambda e: e.memset(sinT[:, 0:NCTX], 0.0), [], [sinT])
        return cosT, sinT


def make_in_maps(inputs):
    maps = []
    for core in range(8):
        s = core % 4
        m = {'x': np.ascontiguousarray(inputs['x'][s]), 'c': np.ascontiguousarray(inputs['c'][s]),
             'ctx': np.ascontiguousarray(inputs['ctx'][s]), 'c_ctx': np.ascontiguousarray(inputs['c_ctx'])}
        for nm, _ in W_NAMES:
            m[nm] = np.ascontiguousarray(inputs[nm])
        maps.append(m)
    return maps


def kernel(**inputs):
    inputs = {k: np.asarray(v, dtype=np.float32) for k, v in inputs.items()}
    prog = Prog()
    nc = prog.build()
    res = run_bass_kernel_spmd(nc, make_in_maps(inputs), core_ids=list(range(8)))
    out = np.stack([np.asarray(res.results[s]['out'], dtype=np.float32) for s in range(4)], axis=0)
    return out
```

```python
import math
from contextlib import ExitStack, contextmanager
import numpy as np
import concourse.bass as bass
import concourse.mybir as mybir
from concourse.bass_utils import run_bass_kernel_spmd

F32 = mybir.dt.float32
BF16 = mybir.dt.bfloat16
I32 = mybir.dt.int32
AF = mybir.ActivationFunctionType
ALU = mybir.AluOpType
AX = mybir.AxisListType

ENGS = ['tensor', 'vector', 'scalar', 'gpsimd', 'sync']
EPOCH = 30000
DEPOCH = 28800

D = 1024
NCTX = 256
NLAT = 4096
NT = NCTX + NLAT
NTILE = NT // 128
DIN = 9216
DEPTH = 2
NEXP = 16
DEXP = 2816
LN_EPS = 1e-6
RMS_EPS = 1e-5
ALPHA = (2.0 * DEPTH) ** 0.25
TWO_PI = 2.0 * math.pi
CW1 = 6.28125
CW2 = float(np.float32(TWO_PI - CW1))
MAGIC = 12582912.0
PI_LO = 3.1415925


class DSem:
    def __init__(self, sched, name, persistent=False):
        self.sched = sched
        self.name = name
        self.sems = []
        self.persistent = persistent
        sched.dsems.append(self)
        if not persistent and sched.phase_dsems:
            sched.phase_dsems[-1].append(self)

    def next_inc(self):
        if not self.sems or self.sems[-1][1] + 16 > DEPOCH:
            self.sems.append(self.sched.take_sem())
        self.sems[-1][1] += 16
        return self.sems[-1][0]


class Buf:
    def __init__(self, t, name='', grp=None, persistent=False):
        self.persistent = persistent
        self.t = t
        self.name = name
        self.w = None
        self.r = {}
        self.grp = grp

    def __getitem__(self, idx):
        return self.t[idx]


class Sched:
    def __init__(self, nc, es):
        self.nc = nc
        self.es = es
        self.q = {e: [] for e in ENGS}
        self.esems = {e: [] for e in ENGS}
        self.cnt = {e: 0 for e in ENGS}
        self.waited = {e: {} for e in ENGS}
        self.nsem = 0
        self.nins = 0
        self.dsems = []
        self.alloc = es
        self.uid = 0
        self.sem_pool = []
        self.selfwait = True
        self.phase_dsems = []

    def take_sem(self):
        while self.sem_pool:
            ent = self.sem_pool.pop()
            if ent[1] + 16 * 64 <= DEPOCH:
                return ent
        return [self.new_sem(), 0]

    def new_sem(self):
        self.nsem += 1
        return self.es.enter_context(self.nc.semaphore('sm%d' % self.nsem))

    def sbuf(self, shape, dt, name, grp=None):
        self.uid += 1
        nm = '%s_%d' % (name, self.uid)
        t = self.alloc.enter_context(self.nc.sbuf_tensor(nm, list(shape), dt))
        return Buf(t, nm, grp)

    def psum(self, shape, dt, name):
        self.uid += 1
        nm = '%s_%d' % (name, self.uid)
        t = self.alloc.enter_context(self.nc.psum_tensor(nm, list(shape), dt))
        return Buf(t, nm)

    def dram(self, shape, dt, name, grp=None, kind="Internal"):
        t = self.nc.dram_tensor(name, list(shape), dt, kind=kind)
        return Buf(t, name, grp, persistent=True)

    def ring(self, n, shape, dt, name, psum=False):
        bufs = [(self.psum if psum else self.sbuf)(shape, dt, '%s%d' % (name, i)) for i in range(n)]
        st = {'i': 0}

        def nxt():
            b = bufs[st['i'] % n]
            st['i'] += 1
            return b
        return nxt

    def _esem(self, eng, epoch):
        lst = self.esems[eng]
        while len(lst) <= epoch:
            lst.append(self.new_sem())
        return lst[epoch]

    def _collect(self, reads, writes):
        toks = []
        for b in reads:
            if b.w is not None:
                toks.append(b.w)
        for b in writes:
            if b.w is not None:
                toks.append(b.w)
            toks.extend(b.r.values())
        return toks

    def _waits(self, eng, toks):
        need = {}
        for tk in toks:
            if tk[0] == 'e':
                _, pe, ep, seq = tk
                if pe == eng and (eng == 'tensor' or not self.selfwait):
                    continue
                key = ('e', pe, ep)
                if need.get(key, (0,))[0] < seq:
                    need[key] = (seq, self._esem(pe, ep))
            else:
                ds = tk[1]
                for i, (sem, c) in enumerate(ds.sems):
                    key = ('d', id(ds), i)
                    if need.get(key, (0,))[0] < c:
                        need[key] = (c, sem)
        wd = self.waited[eng]
        for key, (val, sem) in need.items():
            if wd.get(key, 0) >= val:
                continue
            wd[key] = val
            self.q[eng].append(lambda e, sem=sem, val=val: e.wait_ge(sem, val))

    def _tok(self, eng):
        c = self.cnt[eng]
        if c == 0:
            return None
        ep, seq = divmod(c - 1, EPOCH)
        return ('e', eng, ep, seq + 1)

    def op(self, eng, fn, reads=(), writes=()):
        self._waits(eng, self._collect(reads, writes))
        self.cnt[eng] += 1
        tok = self._tok(eng)
        sem = self._esem(eng, tok[2])
        self.q[eng].append(lambda e, fn=fn, sem=sem: fn(e).then_inc(sem, 1))
        for b in writes:
            b.w = tok
            b.r = {}
        for b in reads:
            if b.w is not tok:
                b.r[eng] = tok
        self.nins += 1
        return tok

    def dma(self, eng, out_ap, in_ap, dst, src, part=False, **kw):
        if dst.grp is None:
            dst.grp = DSem(self, dst.name, persistent=dst.persistent)
        ds = dst.grp
        if part and dst.w is not None and dst.w[0] == 'd' and dst.w[1] is ds:
            toks = self._collect([src], [])
            toks.extend(dst.r.values())
            self._waits(eng, toks)
        else:
            self._waits(eng, self._collect([src], [dst]))
        sem = ds.next_inc()
        self.q[eng].append(lambda e, sem=sem: e.dma_start(out=out_ap, in_=in_ap, **kw).then_inc(sem, 16))
        tok = ('d', ds)
        dst.w = tok
        dst.r = {}
        src.r[('d', id(ds))] = tok
        self.nins += 1
        return tok

    def wait_buf(self, eng, buf):
        self._waits(eng, self._collect([buf], []))

    def barrier(self):
        toks = [t for t in (self._tok(e) for e in ENGS) if t is not None]
        toks += [('d', ds) for ds in self.dsems]
        for e in ENGS:
            self._waits(e, toks)

    def flush(self):
        if not any(self.q.values()):
            return
        with self.nc.Block() as block:
            for en in ENGS:
                lst = self.q[en]
                if not lst:
                    continue

                def f(e, lst=lst):
                    for fn in lst:
                        fn(e)
                getattr(block, en)(f)
        self.q = {e: [] for e in ENGS}

    @contextmanager
    def phase(self):
        prev = self.alloc
        self.barrier()
        self.phase_dsems.append([])
        with ExitStack() as ph:
            self.alloc = ph
            yield
            self.barrier()
            self.flush()
        for ds in self.phase_dsems.pop():
            if ds.sems:
                self.sem_pool.append(ds.sems[-1])
        self.alloc = prev


def interleave(gens):
    gens = list(gens)
    while gens:
        for g in list(gens):
            try:
                next(g)
            except StopIteration:
                gens.remove(g)


W_NAMES = [('w_ada', [DEPTH, D, 6 * D]), ('b_ada', [DEPTH, 6 * D]), ('w_in', [DEPTH, D, DIN]),
           ('conv_w', [DEPTH, 31, D]), ('conv_b', [DEPTH, D]), ('conv_ln_g', [DEPTH, D]), ('conv_ln_b', [DEPTH, D]),
           ('w_conv_out', [DEPTH, D, D]), ('ssm_a_re', [DEPTH, 2, 64, 64]), ('ssm_a_im', [DEPTH, 2, 64, 64]),
           ('ssm_log_dt', [DEPTH, 2, 64]), ('ssm_b_re', [DEPTH, 2, 64, 64, 16]), ('ssm_b_im', [DEPTH, 2, 64, 64, 16]),
           ('ssm_c_re', [DEPTH, 2, 64, 16, 64]), ('ssm_c_im', [DEPTH, 2, 64, 16, 64]), ('ssm_d', [DEPTH, D]),
           ('w_ssm_glu', [DEPTH, D, D]), ('w_ssm_out', [DEPTH, D, D]), ('attn_lambda', [DEPTH, 4, 64]),
           ('attn_subln_g', [DEPTH, 128]), ('w_attn_out', [DEPTH, D, D]), ('w_o', [DEPTH, D, D]),
           ('ln1_g', [DEPTH, D]), ('ln1_b', [DEPTH, D]), ('w_router', [DEPTH, D, NEXP]),
           ('w_gate_up', [DEPTH, NEXP, D, 2 * DEXP]), ('w_down', [DEPTH, NEXP, DEXP, D]),
           ('ln2_g', [DEPTH, D]), ('ln2_b', [DEPTH, D])]


class Prog:
    def __init__(self, debug=(), stop=None, nlayers=DEPTH, skip_inputs=(), phases=None):
        self.debug = set(debug)
        self.phase_names = phases
        self.stop = stop
        self.nlayers = nlayers
        self.nc = bass.Bass("TRN2", target_bir_lowering=False)
        nc = self.nc
        self.I = {}
        for nm, shp in [('x', [NLAT, D]), ('c', [D]), ('ctx', [NCTX, D]), ('c_ctx', [D])] + W_NAMES:
            if nm in skip_inputs:
                continue
            self.I[nm] = Buf(nc.dram_tensor(nm, shp, F32, kind="ExternalInput"), nm)
        self.out = Buf(nc.dram_tensor("out", [NLAT, D], F32, kind="ExternalOutput"), 'out', persistent=True)
        self.scr = {}

    def scratch(self, name, shape, dt):
        if name not in self.scr:
            kind = "ExternalOutput" if name in self.debug else "Internal"
            self.scr[name] = self.S.dram(shape, dt, name, kind=kind)
        return self.scr[name]

    def build(self):
        nc = self.nc
        with ExitStack() as es:
            self.S = S = Sched(nc, es)
            self.consts()
            xres = self.scratch('xres', [NT, D], F32)
            S.dma('sync', xres[0:NCTX, :], self.I['ctx'][:, :], xres, self.I['ctx'])
            S.dma('sync', xres[NCTX:NT, :], self.I['x'][:, :], xres, self.I['x'])
            done = False
            for l in range(self.nlayers):
                plist = [self.ph_mod, self.ph_inproj, self.ph_convout, self.ph_attn, self.ph_ssm, self.ph_merge, self.ph_moe, self.ph_ln2]
                if self.phase_names is not None:
                    plist = [p for p in plist if p.__name__ in self.phase_names]
                for ph in plist:
                    ph(l)
                    if self.stop == (ph.__name__, l):
                        done = True
                        break
                if done:
                    break
            S.barrier()
            S.flush()
        return nc

    def consts(self):
        S = self.S
        self.ident_f = S.sbuf([128, 128], F32, 'ident_f')
        self.ident_b = S.sbuf([128, 128], BF16, 'ident_b')
        ones = S.sbuf([128, 128], F32, 'ones_f')
        self.ones_f = ones
        idf, idb = self.ident_f, self.ident_b
        S.op('gpsimd', lambda e: e.memset(ones[:, :], 1.0), writes=[ones])
        S.op('gpsimd', lambda e: e.affine_select(out=idf[:, :], in_=ones[:, :], pattern=[[1, 128]], compare_op=ALU.is_equal,
                                                 fill=0.0, base=0, channel_multiplier=-1), reads=[ones], writes=[idf])
        S.op('vector', lambda e: e.tensor_copy(out=idb[:, :], in_=idf[:, :]), reads=[idf], writes=[idb])

    def ph_mod(self, l):
        S = self.S
        I = self.I
        modscr = self.scratch('modscr%d' % l, [2, 128, 6 * D], F32)
        with S.phase():
            cT = S.sbuf([128, 2, 8], F32, 'cT')
            sil = S.sbuf([128, 2, 8], F32, 'sil')
            bc = S.sbuf([128, 2, 8, 128], F32, 'bc')
            bada = S.sbuf([128, 6 * D], F32, 'bada')
            S.dma('sync', cT[:, 0, :], I['c'].t.ap().rearrange("(k p) -> p k", p=128), cT, I['c'], allow_slow_non_contiguous=True)
            S.dma('sync', cT[:, 1, :], I['c_ctx'].t.ap().rearrange("(k p) -> p k", p=128), cT, I['c_ctx'], allow_slow_non_contiguous=True)
            S.dma('sync', bada[:, :], I['b_ada'][l:l + 1, :].partition_broadcast(128), bada, I['b_ada'])
            S.op('scalar', lambda e: e.activation(out=sil[:, :, :], in_=cT[:, :, :], func=AF.Silu), [cT], [sil])
            for w in range(2):
                S.op('vector', lambda e, w=w: e.tensor_copy(out=bc[:, w, :, :], in_=sil[:, w, :].unsqueeze(2).to_broadcast([128, 8, 128])), [sil], [bc])
            wring = S.ring(2, [128, 8, 512], F32, 'wada')
            pring = S.ring(4, [128, 512], F32, 'pmod', psum=True)
            mring = S.ring(4, [128, 512], F32, 'modt')
            wv = I['w_ada'].t.ap()[l].rearrange("(k p) n -> p k n", p=128)
            for nb in range(12):
                n0 = nb * 512
                wa = wring()
                S.dma('sync', wa[:, :, :], wv[:, :, n0:n0 + 512], wa, I['w_ada'])
                for w in range(2):
                    ps = pring()
                    for k in range(8):
                        S.op('tensor', lambda e, ps=ps, wa=wa, w=w, k=k: e.matmul(ps[:, :], lhsT=bc[:, w, k, :], rhs=wa[:, k, :], start=(k == 0), stop=(k == 7)), [bc, wa], [ps])
                    mt = mring()
                    S.op('vector', lambda e, ps=ps, mt=mt, n0=n0: e.tensor_tensor(out=mt[:, :], in0=ps[:, :], in1=bada[:, n0:n0 + 512], op=ALU.add), [ps, bada], [mt])
                    if nb in (2, 3, 8, 9):
                        S.op('vector', lambda e, mt=mt: e.tensor_scalar(out=mt[:, :], in0=mt[:, :], scalar1=1.0, scalar2=None, op0=ALU.add), [mt], [mt])
                    S.dma('gpsimd', modscr[w, :, n0:n0 + 512], mt[:, :], modscr, mt, part=True)

    def ph_inproj(self, l):
        S = self.S
        I = self.I
        modscr = self.scr['modscr%d' % l]
        xres = self.scr['xres']
        gates = self.scratch('gates', [NT, 3 * D], F32)
        qT = self.scratch('qT', [D, NT], BF16)
        kT = self.scratch('kT', [D, NT], BF16)
        uT = self.scratch('uT', [D, NT], F32)
        vtm = self.scratch('vtm', [NT, D], BF16)
        ycT = self.scratch('ycT', [D, NT], F32)
        with S.phase():
            hT = S.sbuf([128, 8, NT], BF16, 'hT')
            with S.phase():
                modv = S.sbuf([128, 2, 2, D], F32, 'modv')
                for w in range(2):
                    S.dma('sync', modv[:, w, 0, :], modscr[w, :, 0:D], modv, modscr)
                    S.dma('sync', modv[:, w, 1, :], modscr[w, :, D:2 * D], modv, modscr)
                xring = S.ring(3, [128, D], F32, 'xt')
                tring = S.ring(2, [128, D], F32, 'tt')
                hring = S.ring(2, [128, D], BF16, 'hb')
                sring = S.ring(2, [128, 16], F32, 'st')
                pring = S.ring(2, [128, 8, 128], BF16, 'pT', psum=True)
                def ln_chain(i):
                    w = 1 if i < 2 else 0
                    xt = xring()
                    S.dma('sync', xt[:, :], xres[i * 128:(i + 1) * 128, :], xt, xres)
                    st = sring()
                    S.op('vector', lambda e, xt=xt, st=st: e.bn_stats(out=st[:, 0:6], in_=xt[:, 0:512]), [xt], [st])
                    yield
                    S.op('vector', lambda e, xt=xt, st=st: e.bn_stats(out=st[:, 6:12], in_=xt[:, 512:1024]), [xt], [st])
                    yield
                    S.op('vector', lambda e, st=st: e.bn_aggr(out=st[:, 12:14], in_=st[:, 0:12]), [st], [st])
                    yield
                    self.rstd(st, 13, 14, LN_EPS)
                    yield
                    tt = tring()
                    S.op('vector', lambda e, xt=xt, st=st, tt=tt: e.tensor_scalar(out=tt[:, :], in0=xt[:, :], scalar1=st[:, 12:13], scalar2=st[:, 14:15], op0=ALU.subtract, op1=ALU.mult), [xt, st], [tt])
                    yield
                    S.op('vector', lambda e, tt=tt, w=w: e.tensor_tensor(out=tt[:, :], in0=tt[:, :], in1=modv[:, w, 1, :], op=ALU.mult), [tt, modv], [tt])
                    yield
                    hb = hring()
                    S.op('vector', lambda e, tt=tt, hb=hb, w=w: e.tensor_tensor(out=hb[:, :], in0=tt[:, :], in1=modv[:, w, 0, :], op=ALU.add), [tt, modv], [hb])
                    yield
                    pT = pring()
                    for k in range(8):
                        S.op('tensor', lambda e, pT=pT, hb=hb, k=k: e.transpose(out=pT[:, k, :], in_=hb[:, k * 128:(k + 1) * 128], identity=self.ident_b[:, :]), [hb, self.ident_b], [pT])
                    S.op('scalar', lambda e, pT=pT, i=i: e.copy(out=hT[:, :, i * 128:(i + 1) * 128], in_=pT[:, :, :]), [pT], [hT])
                    yield
                for t0_ in range(0, NTILE, 2):
                    interleave([ln_chain(t_) for t_ in range(t0_, min(t0_ + 2, NTILE))])
            if 'hT' in self.debug:
                hdbg = self.scratch('hT', [128, 8, NT], BF16)
                S.dma('sync', hdbg[:, :, :], hT[:, :, :], hdbg, hT)
            self.inproj_blocks(l, hT, gates, qT, kT, uT, vtm, ycT)

    def rstd(self, st, ci, co, eps):
        S = self.S
        S.op('vector', lambda e: e.tensor_scalar(out=st[:, co:co + 1], in0=st[:, ci:ci + 1], scalar1=eps, scalar2=None, op0=ALU.add), [st], [st])
        S.op('scalar', lambda e: e.activation(out=st[:, co:co + 1], in_=st[:, co:co + 1], func=AF.Sqrt), [st], [st])
        S.op('vector', lambda e: e.reciprocal(out=st[:, co:co + 1], in_=st[:, co:co + 1]), [st], [st])

    def load_w_bf16(self, dst, wap, src):
        self.S.dma('gpsimd', dst[:, :, :], wap, dst, src)

    def inproj_blocks(self, l, hT, gates, qT, kT, uT, vtm, ycT):
        S = self.S
        I = self.I
        wv = I['w_in'].t.ap()[l].rearrange("(k p) n -> p k n", p=128)
        TG = [(t0, min(t0 + 512, NT)) for t0 in range(0, NT, 512)]

        def mm_fm(ps, wb, cc, t0, t1):
            for k in range(8):
                S.op('tensor', lambda e, k=k: e.matmul(ps[:, 0:t1 - t0], lhsT=wb[:, k, cc * 128:(cc + 1) * 128], rhs=hT[:, k, t0:t1], start=(k == 0), stop=(k == 7)), [wb, hT], [ps])

        def mm_tm(ps, wb, i):
            for k in range(8):
                S.op('tensor', lambda e, k=k: e.matmul(ps[:, :], lhsT=hT[:, k, i * 128:(i + 1) * 128], rhs=wb[:, k, :], start=(k == 0), stop=(k == 7)), [wb, hT], [ps])

        with S.phase():
            wring = S.ring(2, [128, 8, 512], BF16, 'wb')
            pring = S.ring(4, [128, 512], F32, 'pz', psum=True)
            gring = S.ring(3, [128, 512], F32, 'gst')
            vring = S.ring(3, [128, 512], BF16, 'vst')
            for nb in list(range(4, 10)) + [16, 17]:
                wb = wring()
                self.load_w_bf16(wb, wv[:, :, nb * 512:(nb + 1) * 512], I['w_in'])
                for i in range(NTILE):
                    ps = pring()
                    mm_tm(ps, wb, i)
                    if nb < 10:
                        g = gring()
                        S.op('scalar', lambda e, ps=ps, g=g: e.activation(out=g[:, :], in_=ps[:, :], func=AF.Sigmoid), [ps], [g])
                        c0 = (nb - 4) * 512
                        S.dma('sync', gates[i * 128:(i + 1) * 128, c0:c0 + 512], g[:, :], gates, g, part=True)
                    else:
                        v = vring()
                        S.op('vector', lambda e, ps=ps, v=v: e.tensor_copy(out=v[:, :], in_=ps[:, :]), [ps], [v])
                        c0 = (nb - 16) * 512
                        S.dma('sync', vtm[i * 128:(i + 1) * 128, c0:c0 + 512], v[:, :], vtm, v, part=True)
        with S.phase():
            wring = S.ring(2, [128, 8, 512], BF16, 'wb')
            pring = S.ring(4, [128, 512], F32, 'pz', psum=True)
            uring = S.ring(3, [128, 512], F32, 'ust')
            for nb in (12, 13):
                wb = wring()
                self.load_w_bf16(wb, wv[:, :, nb * 512:(nb + 1) * 512], I['w_in'])
                for cc in range(4):
                    r0 = (nb - 12) * 512 + cc * 128
                    for (t0, t1) in TG:
                        ps = pring()
                        mm_fm(ps, wb, cc, t0, t1)
                        u = uring()
                        S.op('scalar', lambda e, ps=ps, u=u, n=t1 - t0: e.copy(out=u[:, 0:n], in_=ps[:, 0:n]), [ps], [u])
                        S.dma('sync', uT[r0:r0 + 128, t0:t1], u[:, 0:t1 - t0], uT, u, part=True)
        with S.phase():
            cosT, sinT = self.rope_tables()
            wring = S.ring(2, [128, 8, 512], BF16, 'wb')
            rring = S.ring(2, [128, 8, 512], BF16, 'wr')
            pring = S.ring(6, [128, 512], F32, 'pz', psum=True)
            t1ring = S.ring(2, [128, 512], F32, 'rt1')
            t2ring = S.ring(2, [128, 512], F32, 'rt2')
            oring = S.ring(3, [128, 512], BF16, 'rout')
            for nb in (10, 11, 14, 15):
                dst = qT if nb < 12 else kT
                base = (nb - 10) * 512 if nb < 12 else (nb - 14) * 512
                wb = wring()
                self.load_w_bf16(wb, wv[:, :, nb * 512:(nb + 1) * 512], I['w_in'])
                wr = rring()
                wb4 = wb.t.ap().rearrange("p k (g h f) -> p (k g) h f", h=2, f=16)
                wr4 = wr.t.ap().rearrange("p k (g h f) -> p (k g) h f", h=2, f=16)
                S.op('vector', lambda e, wb4=wb4, wr4=wr4: e.tensor_scalar(out=wr4[:, :, 0, :], in0=wb4[:, :, 1, :], scalar1=-1.0, scalar2=None, op0=ALU.mult), [wb], [wr])
                S.op('vector', lambda e, wb4=wb4, wr4=wr4: e.tensor_copy(out=wr4[:, :, 1, :], in_=wb4[:, :, 0, :]), [wb], [wr])
                for cc in range(4):
                    r0 = base + cc * 128
                    for (t0, t1) in TG:
                        n = t1 - t0
                        ps = pring()
                        mm_fm(ps, wb, cc, t0, t1)
                        pr = pring()
                        mm_fm(pr, wr, cc, t0, t1)
                        a = t1ring()
                        b = t2ring()
                        S.op('vector', lambda e, ps=ps, a=a, t0=t0, t1=t1, n=n: e.tensor_tensor(out=a[:, 0:n], in0=ps[:, 0:n], in1=cosT[:, t0:t1], op=ALU.mult), [ps, cosT], [a])
                        S.op('vector', lambda e, pr=pr, b=b, t0=t0, t1=t1, n=n: e.tensor_tensor(out=b[:, 0:n], in0=pr[:, 0:n], in1=sinT[:, t0:t1], op=ALU.mult), [pr, sinT], [b])
                        o = oring()
                        S.op('vector', lambda e, a=a, b=b, o=o, n=n: e.tensor_tensor(out=o[:, 0:n], in0=a[:, 0:n], in1=b[:, 0:n], op=ALU.add), [a, b], [o])
                        S.dma('sync', dst[r0:r0 + 128, t0:t1], o[:, 0:n], dst, o, part=True)
        with S.phase():
            wa_r = S.ring(2, [128, 8, 512], BF16, 'wa')
            wb_r = S.ring(2, [128, 8, 512], BF16, 'wbb')
            pring = S.ring(6, [128, 512], F32, 'pz', psum=True)
            sgring = S.ring(2, [128, 512], F32, 'sg')
            GW = 15 + NCTX + 15 + NLAT + 15
            OW = NCTX + 15 + NLAT
            gpad_r = S.ring(2, [128, GW], F32, 'gpad')
            acc_r = S.ring(2, [128, OW], F32, 'cacc')
            cw = S.sbuf([128, 8, 31], F32, 'cw')
            cb = S.sbuf([128, 8], F32, 'cb')
            for j in range(8):
                S.dma('sync', cw[:, j, :], I['conv_w'].t.ap()[l].rearrange("k (j p) -> j p k", p=128)[j], cw, I['conv_w'], allow_slow_non_contiguous=True)
            S.dma('sync', cb[:, :], I['conv_b'].t.ap()[l].rearrange("(j p) -> p j", p=128), cb, I['conv_b'], allow_slow_non_contiguous=True)
            for sb in range(2):
                wa = wa_r()
                wb = wb_r()
                self.load_w_bf16(wa, wv[:, :, sb * 512:(sb + 1) * 512], I['w_in'])
                self.load_w_bf16(wb, wv[:, :, (2 + sb) * 512:(3 + sb) * 512], I['w_in'])
                for cc in range(4):
                    j = sb * 4 + cc
                    gp = gpad_r()
                    S.op('gpsimd', lambda e, gp=gp: e.memset(gp[:, 0:15], 0.0), [], [gp])
                    S.op('gpsimd', lambda e, gp=gp: e.memset(gp[:, 15 + NCTX:30 + NCTX], 0.0), [], [gp])
                    S.op('gpsimd', lambda e, gp=gp: e.memset(gp[:, GW - 15:GW], 0.0), [], [gp])
                    for (t0, t1) in TG:
                        n = t1 - t0
                        pa = pring()
                        mm_fm(pa, wa, cc, t0, t1)
                        pb = pring()
                        mm_fm(pb, wb, cc, t0, t1)
                        sg = sgring()
                        S.op('scalar', lambda e, pb=pb, sg=sg, n=n: e.activation(out=sg[:, 0:n], in_=pb[:, 0:n], func=AF.Sigmoid), [pb], [sg])
                        segs = []
                        if t0 < NCTX:
                            segs.append((t0, min(t1, NCTX), 15))
                        if t1 > NCTX:
                            segs.append((max(t0, NCTX), t1, 30))
                        for (a0, a1, off) in segs:
                            S.op('vector', lambda e, pa=pa, sg=sg, gp=gp, a0=a0, a1=a1, off=off, t0=t0: e.tensor_tensor(out=gp[:, off + a0:off + a1], in0=pa[:, a0 - t0:a1 - t0], in1=sg[:, a0 - t0:a1 - t0], op=ALU.mult), [pa, sg], [gp])
                    acc = acc_r()
                    S.op('vector', lambda e, gp=gp, acc=acc, j=j: e.tensor_scalar(out=acc[:, :], in0=gp[:, 0:OW], scalar1=cw[:, j, 0:1], scalar2=cb[:, j:j + 1], op0=ALU.mult, op1=ALU.add), [gp, cw, cb], [acc])
                    for kk in range(1, 31):
                        S.op('vector', lambda e, gp=gp, acc=acc, j=j, kk=kk: e.scalar_tensor_tensor(out=acc[:, :], in0=gp[:, kk:kk + OW], scalar=cw[:, j, kk:kk + 1], in1=acc[:, :], op0=ALU.mult, op1=ALU.add), [gp, cw, acc], [acc])
                    S.dma('sync', ycT[j * 128:(j + 1) * 128, 0:NCTX], acc[:, 0:NCTX], ycT, acc, part=True)
                    S.dma('sync', ycT[j * 128:(j + 1) * 128, NCTX:NT], acc[:, NCTX + 15:OW], ycT, acc, part=True)


    def ph_convout(self, l):
        S = self.S
        I = self.I
        ycT = self.scratch('ycT', [D, NT], F32)
        gates = self.scratch('gates', [NT, 3 * D], F32)
        y0 = self.scratch('y0', [NT, D], F32)
        ycv = ycT.t.ap().rearrange("(j p) t -> p j t", p=128)
        TG = [(t0, min(t0 + 512, NT)) for t0 in range(0, NT, 512)]
        with S.phase():
            wc = S.sbuf([128, 8, D], BF16, 'wc')
            wcv = I['w_conv_out'].t.ap()[l].rearrange("(k p) n -> p k n", p=128)
            for hh in range(2):
                S.dma('gpsimd', wc[:, :, hh * 512:(hh + 1) * 512], wcv[:, :, hh * 512:(hh + 1) * 512], wc, I['w_conv_out'], part=True)
            gb = S.sbuf([128, 2, 8], F32, 'gb')
            S.dma('sync', gb[:, 0, :], I['conv_ln_g'].t.ap()[l].rearrange("(j p) -> p j", p=128), gb, I['conv_ln_g'], allow_slow_non_contiguous=True)
            S.dma('sync', gb[:, 1, :], I['conv_ln_b'].t.ap()[l].rearrange("(j p) -> p j", p=128), gb, I['conv_ln_b'], allow_slow_non_contiguous=True, part=True)
            ycr = S.ring(2, [128, 8, 512], F32, 'yc')
            sqr = S.ring(1, [128, 8, 512], F32, 'sq')
            ps1r = S.ring(2, [128, 512], F32, 'ps1', psum=True)
            ps2r = S.ring(2, [128, 512], F32, 'ps2', psum=True)
            pyr = S.ring(4, [128, 512], F32, 'py', psum=True)
            mr = S.ring(2, [128, 512], F32, 'mean')
            rr = S.ring(2, [128, 512], F32, 'rstd')
            tr = S.ring(2, [128, 512], F32, 'tn')
            ar = S.ring(2, [128, 8, 512], BF16, 'aT')
            gr = S.ring(2, [128, D], F32, 'g0')
            yr = S.ring(2, [128, D], F32, 'y0t')
            for (t0, t1) in TG:
                n = t1 - t0
                yc = ycr()
                S.dma('sync', yc[:, :, 0:n], ycv[:, :, t0:t1], yc, ycT)
                sq = sqr()
                S.op('scalar', lambda e, yc=yc, sq=sq, n=n: e.activation(out=sq[:, :, 0:n], in_=yc[:, :, 0:n], func=AF.Square), [yc], [sq])
                p1 = ps1r()
                p2 = ps2r()
                for j in range(8):
                    S.op('tensor', lambda e, p1=p1, yc=yc, j=j, n=n: e.matmul(p1[:, 0:n], lhsT=self.ones_f[:, :], rhs=yc[:, j, 0:n], start=(j == 0), stop=(j == 7)), [yc, self.ones_f], [p1])
                for j in range(8):
                    S.op('tensor', lambda e, p2=p2, sq=sq, j=j, n=n: e.matmul(p2[:, 0:n], lhsT=self.ones_f[:, :], rhs=sq[:, j, 0:n], start=(j == 0), stop=(j == 7)), [sq, self.ones_f], [p2])
                mean = mr()
                rstd = rr()
                S.op('scalar', lambda e, p1=p1, mean=mean, n=n: e.mul(out=mean[:, 0:n], in_=p1[:, 0:n], mul=1.0 / D), [p1], [mean])
                S.op('vector', lambda e, mean=mean, rstd=rstd, n=n: e.tensor_tensor(out=rstd[:, 0:n], in0=mean[:, 0:n], in1=mean[:, 0:n], op=ALU.mult), [mean], [rstd])
                S.op('vector', lambda e, p2=p2, rstd=rstd, n=n: e.scalar_tensor_tensor(out=rstd[:, 0:n], in0=p2[:, 0:n], scalar=1.0 / D, in1=rstd[:, 0:n], op0=ALU.mult, op1=ALU.subtract), [p2, rstd], [rstd])
                S.op('vector', lambda e, rstd=rstd, n=n: e.tensor_scalar(out=rstd[:, 0:n], in0=rstd[:, 0:n], scalar1=LN_EPS, scalar2=None, op0=ALU.add), [rstd], [rstd])
                S.op('scalar', lambda e, rstd=rstd, n=n: e.activation(out=rstd[:, 0:n], in_=rstd[:, 0:n], func=AF.Sqrt), [rstd], [rstd])
                S.op('vector', lambda e, rstd=rstd, n=n: e.reciprocal(out=rstd[:, 0:n], in_=rstd[:, 0:n]), [rstd], [rstd])
                aT = ar()
                for j in range(8):
                    tn = tr()
                    S.op('vector', lambda e, yc=yc, mean=mean, tn=tn, j=j, n=n: e.tensor_tensor(out=tn[:, 0:n], in0=yc[:, j, 0:n], in1=mean[:, 0:n], op=ALU.subtract), [yc, mean], [tn])
                    S.op('vector', lambda e, rstd=rstd, tn=tn, n=n: e.tensor_tensor(out=tn[:, 0:n], in0=tn[:, 0:n], in1=rstd[:, 0:n], op=ALU.mult), [tn, rstd], [tn])
                    S.op('scalar', lambda e, tn=tn, aT=aT, j=j, n=n: e.activation(out=aT[:, j, 0:n], in_=tn[:, 0:n], func=AF.Silu, scale=gb[:, 0, j:j + 1], bias=gb[:, 1, j:j + 1]), [tn, gb], [aT])
                for ii in range(n // 128):
                    i = t0 // 128 + ii
                    g0 = gr()
                    S.dma('sync', g0[:, :], gates[i * 128:(i + 1) * 128, 0:D], g0, gates)
                    yt = yr()
                    for nb in range(2):
                        py = pyr()
                        for j in range(8):
                            S.op('tensor', lambda e, py=py, aT=aT, j=j, ii=ii, nb=nb: e.matmul(py[:, :], lhsT=aT[:, j, ii * 128:(ii + 1) * 128], rhs=wc[:, j, nb * 512:(nb + 1) * 512], start=(j == 0), stop=(j == 7)), [aT, wc], [py])
                        S.op('vector', lambda e, py=py, yt=yt, g0=g0, nb=nb: e.tensor_tensor(out=yt[:, nb * 512:(nb + 1) * 512], in0=py[:, :], in1=g0[:, nb * 512:(nb + 1) * 512], op=ALU.mult), [py, g0], [yt])
                    S.dma('sync', y0[i * 128:(i + 1) * 128, :], yt[:, :], y0, yt, part=True)

    def ph_attn(self, l):
        S = self.S
        I = self.I
        qT = self.scratch('qT', [D, NT], BF16)
        kT = self.scratch('kT', [D, NT], BF16)
        vtm = self.scratch('vtm', [NT, D], BF16)
        oT = self.scratch('oT', [D, NT], BF16)
        gates = self.scratch('gates', [NT, 3 * D], F32)
        y2 = self.scratch('y2', [NT, D], F32)
        lam_init = 0.8 - 0.6 * math.exp(-0.3 * l)
        ctx_out = l < DEPTH - 1
        vv = vtm.t.ap().rearrange("(i p) c -> p i c", p=128)
        with S.phase():
            lamt = S.sbuf([128, 4, 64], F32, 'lamt')
            lw = S.sbuf([128, 8], F32, 'lw')
            gsub = S.sbuf([128, 128], F32, 'gsub')
            S.dma('sync', lamt[:, :, :], I['attn_lambda'][l:l + 1, :, :].partition_broadcast(128), lamt, I['attn_lambda'])
            S.dma('sync', gsub[:, :], I['attn_subln_g'][l:l + 1, :].partition_broadcast(128), gsub, I['attn_subln_g'])
            S.op('vector', lambda e: e.tensor_scalar(out=gsub[:, :], in0=gsub[:, :], scalar1=1.0 - lam_init, scalar2=None, op0=ALU.mult), [gsub], [gsub])
            prod = S.sbuf([128, 2, 64], F32, 'lprod')
            S.op('vector', lambda e: e.tensor_tensor(out=prod[:, 0, :], in0=lamt[:, 0, :], in1=lamt[:, 1, :], op=ALU.mult), [lamt], [prod])
            S.op('vector', lambda e: e.tensor_tensor(out=prod[:, 1, :], in0=lamt[:, 2, :], in1=lamt[:, 3, :], op=ALU.mult), [lamt], [prod])
            S.op('vector', lambda e: e.reduce_sum(out=lw[:, 0:2], in_=prod[:, :, :], axis=AX.X), [prod], [lw])
            S.op('scalar', lambda e: e.activation(out=lw[:, 2:4], in_=lw[:, 0:2], func=AF.Exp), [lw], [lw])
            S.op('vector', lambda e: e.tensor_tensor(out=lw[:, 4:5], in0=lw[:, 3:4], in1=lw[:, 2:3], op=ALU.subtract), [lw], [lw])
            S.op('vector', lambda e: e.tensor_scalar(out=lw[:, 5:6], in0=lw[:, 4:5], scalar1=-lam_init, scalar2=None, op0=ALU.add), [lw], [lw])
            neglam = lw
            qr = S.ring(2, [128, NT], BF16, 'qh')
            kzs = [[S.sbuf([128, NT], BF16, 'kz%d_%d' % (b_, s_)) for s_ in range(2)] for b_ in range(2)]
            for b_ in range(2):
                S.op('gpsimd', lambda e, t=kzs[b_][0]: e.memset(t[64:128, :], 0.0), [], [kzs[b_][0]])
                S.op('gpsimd', lambda e, t=kzs[b_][1]: e.memset(t[0:64, :], 0.0), [], [kzs[b_][1]])
            vr = S.ring(2, [128, NTILE, 129], BF16, 'vh')
            psr = S.ring(3, [128, 512], F32, 'psc', psum=True)
            por = S.ring(4, [128, 512], F32, 'pov', psum=True)
            ptr_ = S.ring(1, [128, 128], BF16, 'ptr', psum=True)
            pr = S.ring(4, [128, 512], BF16, 'pT')
            accr = S.ring(8, [128, 128], F32, 'oacc')
            smr = S.ring(8, [128, 8], F32, 'osm')
            onr = S.ring(8, [128, 128], BF16, 'onb')
            otr = S.ring(3, [128, 512], BF16, 'oTst')
            qgroups = []
            if ctx_out:
                qgroups.append((0, NCTX, 2))
            for g in range(8):
                qgroups.append((NCTX + g * 512, NCTX + (g + 1) * 512, NTILE))
            for h in range(8):
                qh = qr()
                kz = kzs[h % 2]
                vh = vr()
                S.dma('sync', qh[:, :], qT[h * 128:(h + 1) * 128, :], qh, qT)
                S.dma('sync', kz[0][0:64, :], kT[h * 128:h * 128 + 64, :], kz[0], kT)
                S.dma('sync', kz[1][64:128, :], kT[h * 128 + 64:(h + 1) * 128, :], kz[1], kT)
                S.dma('sync', vh[:, :, 0:128], vv[:, :, h * 128:(h + 1) * 128], vh, vtm)
                S.op('gpsimd', lambda e, vh=vh: e.memset(vh[:, :, 128:129], 1.0), [], [vh])
                for (q0, q1, nkt) in qgroups:
                    nq = q1 - q0
                    nqt = nq // 128
                    accs = [accr() for _ in range(nqt)]
                    for s_ in range(2):
                        pos = [por() for _ in range(nqt)]
                        def score(kt, s_=s_, q0=q0, q1=q1, nq=nq, kzt=kz[s_], qh=qh):
                            ps = psr()
                            S.op('tensor', lambda e, ps=ps: e.matmul(ps[:, 0:nq], lhsT=kzt[:, kt * 128:(kt + 1) * 128], rhs=qh[:, q0:q1], start=True, stop=True), [kzt, qh], [ps])
                            return ps
                        pend = [score(0)]
                        if nkt > 1:
                            pend.append(score(1))
                        for kt in range(nkt):
                            ps = pend.pop(0)
                            if kt + 2 < nkt:
                                pend.append(score(kt + 2))
                            pT = pr()
                            S.op('scalar', lambda e, ps=ps, pT=pT, nq=nq: e.activation(out=pT[:, 0:nq], in_=ps[:, 0:nq], func=AF.Exp, scale=0.125), [ps], [pT])
                            for qi in range(nqt):
                                S.op('tensor', lambda e, po=pos[qi], pT=pT, vh=vh, kt=kt, qi=qi, nkt=nkt: e.matmul(po[:, 0:129], lhsT=pT[:, qi * 128:(qi + 1) * 128], rhs=vh[:, kt, :], start=(kt == 0), stop=(kt == nkt - 1)), [pT, vh], [pos[qi]])
                        def ep1(qi, pos=pos, accs=accs, s_=s_):
                            po = pos[qi]
                            acc = accs[qi]
                            sm = smr()
                            S.op('vector', lambda e, po=po, sm=sm: e.reciprocal(out=sm[:, 0:1], in_=po[:, 128:129]), [po], [sm])
                            yield
                            if s_ == 0:
                                S.op('vector', lambda e, po=po, sm=sm, acc=acc: e.tensor_scalar(out=acc[:, :], in0=po[:, 0:128], scalar1=sm[:, 0:1], scalar2=None, op0=ALU.mult), [po, sm], [acc])
                                yield
                            else:
                                S.op('vector', lambda e, sm=sm: e.tensor_tensor(out=sm[:, 1:2], in0=sm[:, 0:1], in1=neglam[:, 5:6], op=ALU.mult), [sm, neglam], [sm])
                                yield
                                S.op('vector', lambda e, po=po, sm=sm, acc=acc: e.scalar_tensor_tensor(out=acc[:, :], in0=po[:, 0:128], scalar=sm[:, 1:2], in1=acc[:, :], op0=ALU.mult, op1=ALU.add), [po, sm, acc], [acc])
                                yield
                        interleave([ep1(qi_) for qi_ in range(nqt)])
                    ost = otr()
                    def ep2(qi, accs=accs, ost=ost):
                        acc = accs[qi]
                        sm = smr()
                        sqs = onr()
                        S.op('scalar', lambda e, acc=acc, sqs=sqs, sm=sm: e.activation(out=sqs[:, :], in_=acc[:, :], func=AF.Square, accum_out=sm[:, 0:1]), [acc], [sqs, sm])
                        yield
                        S.op('vector', lambda e, sm=sm: e.tensor_scalar(out=sm[:, 1:2], in0=sm[:, 0:1], scalar1=1.0 / 128.0, scalar2=None, op0=ALU.mult), [sm], [sm])
                        yield
                        self.rstd(sm, 1, 2, RMS_EPS)
                        yield
                        S.op('vector', lambda e, acc=acc, sm=sm: e.tensor_scalar(out=acc[:, :], in0=acc[:, :], scalar1=sm[:, 2:3], scalar2=None, op0=ALU.mult), [acc, sm], [acc])
                        yield
                        on = onr()
                        S.op('vector', lambda e, acc=acc, on=on: e.tensor_tensor(out=on[:, :], in0=acc[:, :], in1=gsub[:, :], op=ALU.mult), [acc, gsub], [on])
                        yield
                        pt = ptr_()
                        S.op('tensor', lambda e, pt=pt, on=on: e.transpose(out=pt[:, :], in_=on[:, :], identity=self.ident_b[:, :]), [on, self.ident_b], [pt])
                        S.op('scalar', lambda e, pt=pt, ost=ost, qi=qi: e.copy(out=ost[:, qi * 128:(qi + 1) * 128], in_=pt[:, :]), [pt], [ost])
                        yield
                    interleave([ep2(qi_) for qi_ in range(nqt)])
                    S.dma('sync', oT[h * 128:(h + 1) * 128, q0:q1], ost[:, 0:nq], oT, ost, part=True)
        with S.phase():
            wo = S.sbuf([128, 8, D], BF16, 'wao')
            wov = I['w_attn_out'].t.ap()[l].rearrange("(k p) n -> p k n", p=128)
            for hh in range(2):
                S.dma('gpsimd', wo[:, :, hh * 512:(hh + 1) * 512], wov[:, :, hh * 512:(hh + 1) * 512], wo, I['w_attn_out'], part=True)
            otr2 = S.ring(2, [128, 8, 512], BF16, 'oTl')
            gr = S.ring(2, [128, D], F32, 'g2')
            yr = S.ring(2, [128, D], F32, 'y2t')
            pyr = S.ring(4, [128, 512], F32, 'py', psum=True)
            otv = oT.t.ap().rearrange("(j p) t -> p j t", p=128)
            tstart = 0 if ctx_out else NCTX
            for t0 in range(tstart, NT, 512):
                t1 = min(t0 + 512, NT)
                n = t1 - t0
                ol = otr2()
                S.dma('sync', ol[:, :, 0:n], otv[:, :, t0:t1], ol, oT)
                for ii in range(n // 128):
                    i = t0 // 128 + ii
                    g2 = gr()
                    S.dma('sync', g2[:, :], gates[i * 128:(i + 1) * 128, 2 * D:3 * D], g2, gates)
                    yt = yr()
                    for nb in range(2):
                        py = pyr()
                        for j in range(8):
                            S.op('tensor', lambda e, py=py, ol=ol, j=j, ii=ii, nb=nb: e.matmul(py[:, :], lhsT=ol[:, j, ii * 128:(ii + 1) * 128], rhs=wo[:, j, nb * 512:(nb + 1) * 512], start=(j == 0), stop=(j == 7)), [ol, wo], [py])
                        S.op('vector', lambda e, py=py, yt=yt, g2=g2, nb=nb: e.tensor_tensor(out=yt[:, nb * 512:(nb + 1) * 512], in0=py[:, :], in1=g2[:, nb * 512:(nb + 1) * 512], op=ALU.mult), [py, g2], [yt])
                    S.dma('sync', y2[i * 128:(i + 1) * 128, :], yt[:, :], y2, yt, part=True)


    def rr_mixed(self, x, k):
        S = self.S
        S.op('gpsimd', lambda e: e.tensor_scalar(out=k[:, :], in0=x[:, :], scalar1=float(1.0 / TWO_PI), scalar2=MAGIC, op0=ALU.mult, op1=ALU.add), [x], [k])
        S.op('gpsimd', lambda e: e.tensor_scalar(out=k[:, :], in0=k[:, :], scalar1=-MAGIC, scalar2=None, op0=ALU.add), [k], [k])
        S.op('vector', lambda e: e.scalar_tensor_tensor(out=x[:, :], in0=k[:, :], scalar=-CW1, in1=x[:, :], op0=ALU.mult, op1=ALU.add), [x, k], [x])
        S.op('vector', lambda e: e.scalar_tensor_tensor(out=x[:, :], in0=k[:, :], scalar=-CW2, in1=x[:, :], op0=ALU.mult, op1=ALU.add), [x, k], [x])
        S.op('gpsimd', lambda e: e.tensor_scalar(out=x[:, :], in0=x[:, :], scalar1=PI_LO, scalar2=-PI_LO, op0=ALU.min, op1=ALU.max), [x], [x])

    def ph_ssm_old(self, l):
        S = self.S
        I = self.I
        uT = self.scratch('uT', [D, NT], F32)
        ygT = self.scratch('ygT', [D, NT], BF16)
        gates = self.scratch('gates', [NT, 3 * D], F32)
        y1 = self.scratch('y1', [NT, D], F32)
        TG = [(t0, min(t0 + 512, NT)) for t0 in range(0, NT, 512)]
        V = lambda fn, r, w: S.op('vector', fn, r, w)
        G = lambda fn, r, w: S.op('gpsimd', fn, r, w)
        A = lambda fn, r, w: S.op('scalar', fn, r, w)
        with S.phase():
            LT = {(v, par): S.sbuf([128, 16, 128], BF16, 'LT%d%d' % (v, par)) for v in (1, 2) for par in range(4)}
            Rf = {v: S.sbuf([128, 16, 128], BF16, 'Rf%d' % v) for v in (1, 2)}
            RHO = S.sbuf([128, 128], F32, 'RHO')
            THR = S.sbuf([128, 128], F32, 'THR')
            dvec = S.sbuf([128, 8], F32, 'dvec')
            S.dma('sync', dvec[:, :], I['ssm_d'].t.ap()[l].rearrange("(j p) -> p j", p=128), dvec, I['ssm_d'], allow_slow_non_contiguous=True)
            with S.phase():
                def T(name):
                    return S.sbuf([128, 128], F32, name)
                AR, AI, DT, TH, SN, CS, ABR, ABI, DEN, CR, CI, SCR, SCI, TMP, TMP2 = [T(n) for n in ['AR', 'AI', 'DT', 'TH', 'SN', 'CS', 'ABR', 'ABI', 'DEN', 'CR', 'CI', 'SCR', 'SCI', 'TMP', 'TMP2']]
                are = I['ssm_a_re'].t.ap()[l].rearrange("d g n -> n (d g)")
                aim = I['ssm_a_im'].t.ap()[l].rearrange("d g n -> n (d g)")
                for hf in range(2):
                    S.dma('sync', AR[hf * 64:(hf + 1) * 64, :], are, AR, I['ssm_a_re'], part=(hf == 1), allow_slow_non_contiguous=True)
                    S.dma('sync', AI[hf * 64:(hf + 1) * 64, :], aim, AI, I['ssm_a_im'], part=(hf == 1), allow_slow_non_contiguous=True)
                S.dma('sync', DT[:, :], I['ssm_log_dt'][l:l + 1, :, :].rearrange("o d g -> o (d g)").partition_broadcast(128), DT, I['ssm_log_dt'])
                A(lambda e: e.activation(out=DT[:, :], in_=DT[:, :], func=AF.Exp), [DT], [DT])
                V(lambda e: e.tensor_tensor(out=TMP[:, :], in0=DT[:, :], in1=AR[:, :], op=ALU.mult), [DT, AR], [TMP])
                V(lambda e: e.tensor_tensor(out=TH[:, :], in0=DT[:, :], in1=AI[:, :], op=ALU.mult), [DT, AI], [TH])
                A(lambda e: e.activation(out=RHO[:, :], in_=TMP[:, :], func=AF.Exp), [TMP], [RHO])
                V(lambda e: e.tensor_copy(out=THR[:, :], in_=TH[:, :]), [TH], [THR])
                self.rr_mixed(THR, TMP2)
                A(lambda e: e.activation(out=SN[:, :], in_=THR[:, :], func=AF.Sin), [THR], [SN])
                A(lambda e: e.activation(out=CS[:, :], in_=THR[:, :], func=AF.Sin, scale=0.5), [THR], [CS])
                V(lambda e: e.tensor_tensor(out=CS[:, :], in0=CS[:, :], in1=CS[:, :], op=ALU.mult), [CS], [CS])
                V(lambda e: e.tensor_scalar(out=CS[:, :], in0=CS[:, :], scalar1=-2.0, scalar2=1.0, op0=ALU.mult, op1=ALU.add), [CS], [CS])
                V(lambda e: e.tensor_tensor(out=ABR[:, :], in0=RHO[:, :], in1=CS[:, :], op=ALU.mult), [RHO, CS], [ABR])
                V(lambda e: e.tensor_scalar(out=ABR[:, :], in0=ABR[:, :], scalar1=-1.0, scalar2=None, op0=ALU.add), [ABR], [ABR])
                V(lambda e: e.tensor_tensor(out=ABI[:, :], in0=RHO[:, :], in1=SN[:, :], op=ALU.mult), [RHO, SN], [ABI])
                V(lambda e: e.tensor_tensor(out=DEN[:, :], in0=AR[:, :], in1=AR[:, :], op=ALU.mult), [AR], [DEN])
                V(lambda e: e.tensor_tensor(out=TMP[:, :], in0=AI[:, :], in1=AI[:, :], op=ALU.mult), [AI], [TMP])
                V(lambda e: e.tensor_tensor(out=DEN[:, :], in0=DEN[:, :], in1=TMP[:, :], op=ALU.add), [DEN, TMP], [DEN])
                V(lambda e: e.reciprocal(out=DEN[:, :], in_=DEN[:, :]), [DEN], [DEN])
                V(lambda e: e.tensor_tensor(out=CR[:, :], in0=ABR[:, :], in1=AR[:, :], op=ALU.mult), [ABR, AR], [CR])
                V(lambda e: e.tensor_tensor(out=TMP[:, :], in0=ABI[:, :], in1=AI[:, :], op=ALU.mult), [ABI, AI], [TMP])
                V(lambda e: e.tensor_tensor(out=CR[:, :], in0=CR[:, :], in1=TMP[:, :], op=ALU.add), [CR, TMP], [CR])
                V(lambda e: e.tensor_tensor(out=CR[:, :], in0=CR[:, :], in1=DEN[:, :], op=ALU.mult), [CR, DEN], [CR])
                V(lambda e: e.tensor_tensor(out=CI[:, :], in0=ABI[:, :], in1=AR[:, :], op=ALU.mult), [ABI, AR], [CI])
                V(lambda e: e.tensor_tensor(out=TMP[:, :], in0=ABR[:, :], in1=AI[:, :], op=ALU.mult), [ABR, AI], [TMP])
                V(lambda e: e.tensor_tensor(out=CI[:, :], in0=CI[:, :], in1=TMP[:, :], op=ALU.subtract), [CI, TMP], [CI])
                V(lambda e: e.tensor_tensor(out=CI[:, :], in0=CI[:, :], in1=DEN[:, :], op=ALU.mult), [CI, DEN], [CI])
                sg = S.sbuf([128, 8], F32, 'sg')
                pidx = S.sbuf([128, 2], I32, 'pidx')
                G(lambda e: e.memset(sg[:, 0:1], 1.0), [], [sg])
                G(lambda e: e.memset(sg[0:64, 0:1], -1.0), [sg], [sg])
                G(lambda e: e.iota(pidx[:, 0:1], pattern=[[0, 1]], base=0, channel_multiplier=1), [], [pidx])
                V(lambda e: e.tensor_scalar(out=pidx[:, 1:2], in0=pidx[:, 0:1], scalar1=4, scalar2=3, op0=ALU.logical_shift_right, op1=ALU.bitwise_and), [pidx], [pidx])
                V(lambda e: e.tensor_copy(out=sg[:, 1:2], in_=pidx[:, 1:2]), [pidx], [sg])
                for m_ in range(4):
                    V(lambda e, m_=m_: e.tensor_scalar(out=sg[:, 4 + m_:5 + m_], in0=sg[:, 1:2], scalar1=float(m_), scalar2=None, op0=ALU.is_equal), [sg], [sg])
                V(lambda e: e.tensor_scalar(out=SCR[:, :], in0=CR[:, :], scalar1=sg[:, 0:1], scalar2=None, op0=ALU.mult), [CR, sg], [SCR])
                V(lambda e: e.tensor_scalar(out=SCI[:, :], in0=CI[:, :], scalar1=sg[:, 0:1], scalar2=None, op0=ALU.mult), [CI, sg], [SCI])
                BA = S.sbuf([128, 128, 16], F32, 'BA')
                BB = S.sbuf([128, 128, 16], F32, 'BB')
                X1 = S.sbuf([128, 128, 16], F32, 'X1')
                X2 = S.sbuf([128, 128, 16], F32, 'X2')
                XT = S.sbuf([128, 128, 16], F32, 'XT')
                bre = I['ssm_b_re'].t.ap()[l].rearrange("d g n c -> n (d g) c")
                bim = I['ssm_b_im'].t.ap()[l].rearrange("d g n c -> n (d g) c")
                S.dma('sync', BA[0:64, :, :], bre, BA, I['ssm_b_re'])
                S.dma('sync', BA[64:128, :, :], bim, BA, I['ssm_b_im'], part=True)
                S.dma('sync', BB[0:64, :, :], bim, BB, I['ssm_b_im'])
                S.dma('sync', BB[64:128, :, :], bre, BB, I['ssm_b_re'], part=True)

                def bc(t):
                    return t[:, :].unsqueeze(2).to_broadcast([128, 128, 16])
                V(lambda e: e.tensor_tensor(out=X1[:, :, :], in0=BA[:, :, :], in1=bc(CR), op=ALU.mult), [BA, CR], [X1])
                V(lambda e: e.tensor_tensor(out=XT[:, :, :], in0=BB[:, :, :], in1=bc(SCI), op=ALU.mult), [BB, SCI], [XT])
                V(lambda e: e.tensor_tensor(out=X1[:, :, :], in0=X1[:, :, :], in1=XT[:, :, :], op=ALU.add), [X1, XT], [X1])
                V(lambda e: e.tensor_tensor(out=X2[:, :, :], in0=BA[:, :, :], in1=bc(CI), op=ALU.mult), [BA, CI], [X2])
                V(lambda e: e.tensor_tensor(out=XT[:, :, :], in0=BB[:, :, :], in1=bc(SCR), op=ALU.mult), [BB, SCR], [XT])
                V(lambda e: e.tensor_tensor(out=X2[:, :, :], in0=X2[:, :, :], in1=XT[:, :, :], op=ALU.subtract), [X2, XT], [X2])
                ptr_ = S.ring(3, [128, 128], F32, 'ptp', psum=True)
                for v, X in ((1, X1), (2, X2)):
                    for col in range(16):
                        pt = ptr_()
                        S.op('tensor', lambda e, pt=pt, X=X, col=col: e.transpose(out=pt[:, :], in_=X[:, col * 8:(col + 1) * 8, :], identity=self.ident_f[:, :]), [X, self.ident_f], [pt])
                        for m_ in range(4):
                            V(lambda e, pt=pt, v=v, col=col, m_=m_: e.tensor_scalar(out=LT[(v, m_)][:, col, :], in0=pt[:, :], scalar1=sg[:, 4 + m_:5 + m_], scalar2=None, op0=ALU.mult), [pt, sg], [LT[(v, m_)]])
                M1 = S.sbuf([128, 16, 128], F32, 'M1')
                M2 = S.sbuf([128, 16, 128], F32, 'M2')
                cre = I['ssm_c_re'].t.ap()[l].rearrange("d (b g) c n -> (g c) (d b) n", g=8)
                cim = I['ssm_c_im'].t.ap()[l].rearrange("d (b g) c n -> (g c) (d b) n", g=8)
                S.dma('sync', M1[:, :, 0:64], cre, M1, I['ssm_c_re'])
                S.dma('sync', M1[:, :, 64:128], cim, M1, I['ssm_c_im'], part=True)
                V(lambda e: e.tensor_scalar(out=M2[:, :, 64:128], in0=M1[:, :, 0:64], scalar1=-1.0, scalar2=None, op0=ALU.mult), [M1], [M2])
                V(lambda e: e.tensor_scalar(out=M2[:, :, 0:64], in0=M1[:, :, 64:128], scalar1=-1.0, scalar2=None, op0=ALU.mult), [M1], [M2])
                V(lambda e: e.tensor_scalar(out=M1[:, :, 64:128], in0=M1[:, :, 64:128], scalar1=-1.0, scalar2=None, op0=ALU.mult), [M1, M2], [M1])
                for v, M in ((1, M1), (2, M2)):
                    for col in range(16):
                        pt = ptr_()
                        S.op('tensor', lambda e, pt=pt, M=M, col=col: e.transpose(out=pt[:, :], in_=M[:, col, :], identity=self.ident_f[:, :]), [M, self.ident_f], [pt])
                        A(lambda e, pt=pt, v=v, col=col: e.copy(out=Rf[v][:, col, :], in_=pt[:, :]), [pt], [Rf[v]])
            with S.phase():
                pidxs = [S.sbuf([128, NT], F32, 'pidx%d' % d_) for d_ in range(2)]
                with S.phase():
                    tmpi = S.sbuf([128, NT], I32, 'tmpi')
                    for d_ in range(2):
                        pf = pidxs[d_]
                        if d_ == 0:
                            G(lambda e: e.iota(tmpi[:, :], pattern=[[1, NT]], base=0, channel_multiplier=0), [], [tmpi])
                        else:
                            G(lambda e: e.iota(tmpi[:, 0:NCTX], pattern=[[-1, NCTX]], base=NCTX - 1, channel_multiplier=0), [], [tmpi])
                            G(lambda e: e.iota(tmpi[:, NCTX:NT], pattern=[[-1, NLAT]], base=NT - 1, channel_multiplier=0), [tmpi], [tmpi])
                        V(lambda e, pf=pf: e.tensor_copy(out=pf[:, :], in_=tmpi[:, :]), [tmpi], [pf])
                bufA = tmpi_f = S.sbuf([128, NT], F32, 'bufA')
                bufB = S.sbuf([128, NT], F32, 'bufB')
                cosT = S.sbuf([128, NT], F32, 'cosS')
                sinT = S.sbuf([128, NT], F32, 'sinS')
                D1 = S.sbuf([128, NT], BF16, 'D1')
                D2 = S.sbuf([128, NT], BF16, 'D2')
                Yacc = S.sbuf([128, NT], F32, 'Yacc')
                ub = S.sbuf([128, NT], BF16, 'ub')
                t1r = S.ring(2, [128, 512], F32, 'st1')
                t2r = S.ring(2, [128, 512], F32, 'st2')
                rz1r = S.ring(2, [128, 128], BF16, 'rz1')
                rz2r = S.ring(2, [128, 128], BF16, 'rz2')
                p1r = S.ring(2, [128, 512], F32, 'pp1', psum=True)
                p2r = S.ring(2, [128, 512], F32, 'pp2', psum=True)
                pyr = S.ring(2, [128, 512], F32, 'ppy', psum=True)
                for blk in range(8):
                    S.dma('gpsimd', ub[:, :], uT[blk * 128:(blk + 1) * 128, :], ub, uT)
                    S.dma('sync', bufA[:, :], uT[blk * 128:(blk + 1) * 128, :], bufA, uT)
                    V(lambda e, blk=blk: e.tensor_scalar(out=Yacc[:, :], in0=bufA[:, :], scalar1=dvec[:, blk:blk + 1], scalar2=None, op0=ALU.mult), [bufA, dvec], [Yacc])
                    for d_ in range(2):
                        for g8 in range(8):
                            dg = d_ * 64 + blk * 8 + g8
                            col = d_ * 8 + blk
                            pr, par = g8 // 4, g8 % 4
                            pf = pidxs[d_]
                            G(lambda e, pf=pf, dg=dg: e.tensor_scalar(out=bufA[:, :], in0=pf[:, :], scalar1=THR[:, dg:dg + 1], scalar2=None, op0=ALU.mult), [pf, THR], [bufA])
                            self.rr_mixed(bufA, bufB)
                            A(lambda e: e.activation(out=sinT[:, :], in_=bufA[:, :], func=AF.Sin), [bufA], [sinT])
                            A(lambda e: e.activation(out=cosT[:, :], in_=bufA[:, :], func=AF.Sin, scale=0.5), [bufA], [cosT])
                            G(lambda e: e.tensor_tensor(out=cosT[:, :], in0=cosT[:, :], in1=cosT[:, :], op=ALU.mult), [cosT], [cosT])
                            G(lambda e: e.tensor_scalar(out=cosT[:, :], in0=cosT[:, :], scalar1=-2.0, scalar2=1.0, op0=ALU.mult, op1=ALU.add), [cosT], [cosT])
                            rz1, rz2 = rz1r(), rz2r()
                            for rz, v in ((rz1, 1), (rz2, 2)):
                                G(lambda e, rz=rz: e.memset(rz[:, :], 0.0), [], [rz])
                                G(lambda e, rz=rz, v=v, col=col, g8=g8: e.tensor_copy(out=rz[:, g8 * 16:(g8 + 1) * 16], in_=Rf[v][:, col, g8 * 16:(g8 + 1) * 16]), [Rf[v]], [rz])
                            l1, l2 = LT[(1, par)], LT[(2, par)]
                            for (t0, t1) in TG:
                                n = t1 - t0
                                p1, p2 = p1r(), p2r()
                                S.op('tensor', lambda e, p1=p1, l1=l1, pr=pr, col=col, t0=t0, t1=t1, n=n: e.matmul(p1[:, 0:n], lhsT=l1[64 * pr:64 * pr + 64, col, :], rhs=ub[64 * pr:64 * pr + 64, t0:t1], start=True, stop=True), [l1, ub], [p1])
                                S.op('tensor', lambda e, p2=p2, l2=l2, pr=pr, col=col, t0=t0, t1=t1, n=n: e.matmul(p2[:, 0:n], lhsT=l2[64 * pr:64 * pr + 64, col, :], rhs=ub[64 * pr:64 * pr + 64, t0:t1], start=True, stop=True), [l2, ub], [p2])
                                a, b = t1r(), t2r()
                                V(lambda e, p1=p1, a=a, t0=t0, t1=t1, n=n: e.tensor_tensor(out=a[:, 0:n], in0=p1[:, 0:n], in1=cosT[:, t0:t1], op=ALU.mult), [p1, cosT], [a])
                                V(lambda e, p2=p2, b=b, t0=t0, t1=t1, n=n: e.tensor_tensor(out=b[:, 0:n], in0=p2[:, 0:n], in1=sinT[:, t0:t1], op=ALU.mult), [p2, sinT], [b])
                                G(lambda e, a=a, b=b, t0=t0, t1=t1, n=n: e.tensor_tensor(out=bufB[:, t0:t1], in0=a[:, 0:n], in1=b[:, 0:n], op=ALU.add), [a, b], [bufB])
                            rho_b = RHO[:, dg:dg + 1]
                            if d_ == 0:
                                V(lambda e, rho_b=rho_b: e.tensor_tensor_scan(out=bufA[:, :], data0=rho_b.to_broadcast([128, NT]), data1=bufB[:, :], initial=0.0, op0=ALU.mult, op1=ALU.add), [bufB, RHO], [bufA])
                            else:
                                V(lambda e, rho_b=rho_b: e.tensor_tensor_scan(out=bufA[:, 0:NCTX][:, ::-1], data0=rho_b.to_broadcast([128, NCTX]), data1=bufB[:, 0:NCTX][:, ::-1], initial=0.0, op0=ALU.mult, op1=ALU.add), [bufB, RHO], [bufA])
                                V(lambda e, rho_b=rho_b: e.tensor_tensor_scan(out=bufA[:, NCTX:NT][:, ::-1], data0=rho_b.to_broadcast([128, NLAT]), data1=bufB[:, NCTX:NT][:, ::-1], initial=bufA[:, 0:1], op0=ALU.mult, op1=ALU.add), [bufB, RHO, bufA], [bufA])
                            V(lambda e: e.tensor_tensor(out=D1[:, :], in0=cosT[:, :], in1=bufA[:, :], op=ALU.mult), [cosT, bufA], [D1])
                            G(lambda e: e.tensor_tensor(out=D2[:, :], in0=sinT[:, :], in1=bufA[:, :], op=ALU.mult), [sinT, bufA], [D2])
                            for (t0, t1) in TG:
                                n = t1 - t0
                                py = pyr()
                                S.op('tensor', lambda e, py=py, rz1=rz1, t0=t0, t1=t1, n=n: e.matmul(py[:, 0:n], lhsT=rz1[:, :], rhs=D1[:, t0:t1], start=True, stop=False), [rz1, D1], [py])
                                S.op('tensor', lambda e, py=py, rz2=rz2, t0=t0, t1=t1, n=n: e.matmul(py[:, 0:n], lhsT=rz2[:, :], rhs=D2[:, t0:t1], start=False, stop=True), [rz2, D2], [py])
                                V(lambda e, py=py, t0=t0, t1=t1, n=n: e.tensor_tensor(out=Yacc[:, t0:t1], in0=py[:, 0:n], in1=Yacc[:, t0:t1], op=ALU.add), [py, Yacc], [Yacc])
                    if 'yscanT' in self.debug:
                        ysd = self.scratch('yscanT', [D, NT], F32)
                        S.dma('sync', bufA[:, :], uT[blk * 128:(blk + 1) * 128, :], bufA, uT)
                        V(lambda e, blk=blk: e.tensor_scalar(out=bufA[:, :], in0=bufA[:, :], scalar1=dvec[:, blk:blk + 1], scalar2=None, op0=ALU.mult), [bufA, dvec], [bufA])
                        V(lambda e: e.tensor_tensor(out=bufA[:, :], in0=Yacc[:, :], in1=bufA[:, :], op=ALU.subtract), [Yacc, bufA], [bufA])
                        S.dma('sync', ysd[blk * 128:(blk + 1) * 128, :], bufA[:, :], ysd, bufA, part=True)
                    G(lambda e: e.tensor_tensor(out=bufB[:, :], in0=Yacc[:, :], in1=Yacc[:, :], op=ALU.mult), [Yacc], [bufB])
                    G(lambda e: e.tensor_scalar(out=bufB[:, :], in0=bufB[:, :], scalar1=0.044715, scalar2=1.0, op0=ALU.mult, op1=ALU.add), [bufB], [bufB])
                    V(lambda e: e.tensor_tensor(out=bufB[:, :], in0=bufB[:, :], in1=Yacc[:, :], op=ALU.mult), [bufB, Yacc], [bufB])
                    A(lambda e: e.activation(out=bufB[:, :], in_=bufB[:, :], func=AF.Sigmoid, scale=2.0 * math.sqrt(2.0 / math.pi)), [bufB], [bufB])
                    V(lambda e: e.tensor_tensor(out=D1[:, :], in0=bufB[:, :], in1=Yacc[:, :], op=ALU.mult), [bufB, Yacc], [D1])
                    S.dma('sync', ygT[blk * 128:(blk + 1) * 128, :], D1[:, :], ygT, D1, part=True)
        self.ssm_glu_out(l)


    def rr_ap(self, x, k, xa, ka):
        S = self.S
        S.op('gpsimd', lambda e: e.tensor_scalar(out=ka, in0=xa, scalar1=float(1.0 / TWO_PI), scalar2=MAGIC, op0=ALU.mult, op1=ALU.add), [x], [k])
        S.op('gpsimd', lambda e: e.tensor_scalar(out=ka, in0=ka, scalar1=-MAGIC, scalar2=None, op0=ALU.add), [k], [k])
        S.op('vector', lambda e: e.scalar_tensor_tensor(out=xa, in0=ka, scalar=-CW1, in1=xa, op0=ALU.mult, op1=ALU.add), [x, k], [x])
        S.op('vector', lambda e: e.scalar_tensor_tensor(out=xa, in0=ka, scalar=-CW2, in1=xa, op0=ALU.mult, op1=ALU.add), [x, k], [x])
        S.op('gpsimd', lambda e: e.tensor_scalar(out=xa, in0=xa, scalar1=PI_LO, scalar2=-PI_LO, op0=ALU.min, op1=ALU.max), [x], [x])

    def rr_gen(self, x, k, xa, ka, eng='gpsimd'):
        S = self.S
        S.op(eng, lambda e: e.tensor_scalar(out=ka, in0=xa, scalar1=float(1.0 / TWO_PI), scalar2=MAGIC, op0=ALU.mult, op1=ALU.add), [x], [k])
        yield
        S.op(eng, lambda e: e.tensor_scalar(out=ka, in0=ka, scalar1=-MAGIC, scalar2=None, op0=ALU.add), [k], [k])
        yield
        S.op('vector', lambda e: e.scalar_tensor_tensor(out=xa, in0=ka, scalar=-CW1, in1=xa, op0=ALU.mult, op1=ALU.add), [x, k], [x])
        yield
        S.op('vector', lambda e: e.scalar_tensor_tensor(out=xa, in0=ka, scalar=-CW2, in1=xa, op0=ALU.mult, op1=ALU.add), [x, k], [x])
        yield
        S.op(eng, lambda e: e.tensor_scalar(out=xa, in0=xa, scalar1=PI_LO, scalar2=-PI_LO, op0=ALU.min, op1=ALU.max), [x], [x])
        yield

    def ph_ssm(self, l):
        S = self.S
        I = self.I
        uT = self.scratch('uT', [D, NT], F32)
        ygT = self.scratch('ygT', [D, NT], BF16)
        V = lambda fn, r, w: S.op('vector', fn, r, w)
        G = lambda fn, r, w: S.op('gpsimd', fn, r, w)
        A = lambda fn, r, w: S.op('scalar', fn, r, w)
        P = lambda fn, r, w: S.op('tensor', fn, r, w)
        NC = NT // 8
        NCC = NCTX // 8
        HC = [(0, NC // 2), (NC // 2, NC)]
        HW = NC // 2
        with S.phase():
            dvec = S.sbuf([128, 8], F32, 'dvec')
            S.dma('sync', dvec[:, :], I['ssm_d'].t.ap()[l].rearrange("(j p) -> p j", p=128), dvec, I['ssm_d'], allow_slow_non_contiguous=True)
            PWr = S.sbuf([128, 9, 128], F32, 'PWr')
            PWi = S.sbuf([128, 9, 128], F32, 'PWi')
            CRe, CIe = [S.sbuf([128, 8, 128], F32, n) for n in ('CRe', 'CIe')]
            T1 = S.sbuf([128, 16, 8, 16], F32, 'T1')
            T2 = S.sbuf([128, 16, 8, 16], F32, 'T2')
            sg = S.sbuf([128, 8], F32, 'sg')
            blkmask = S.sbuf([128, 128], F32, 'blkmask')
            TH8 = S.sbuf([128, 128], F32, 'TH8')
            RHO8 = S.sbuf([128, 128], F32, 'RHO8')
            with S.phase():
                def T(name, w=128):
                    return S.sbuf([128, w], F32, name)
                AR, AI, DT, ARD, TH, THR, SN, CS, ABR, ABI, DEN, CR, CI, TMP, TMP2, RHO = [T(n) for n in ['AR', 'AI', 'DT', 'ARD', 'TH', 'THR', 'SN', 'CS', 'ABR', 'ABI', 'DEN', 'CR', 'CI', 'TMP', 'TMP2', 'RHO']]
                are = I['ssm_a_re'].t.ap()[l].rearrange("d g n -> n (d g)")
                aim = I['ssm_a_im'].t.ap()[l].rearrange("d g n -> n (d g)")
                for hf in range(2):
                    S.dma('sync', AR[hf * 64:(hf + 1) * 64, :], are, AR, I['ssm_a_re'], part=(hf == 1), allow_slow_non_contiguous=True)
                    S.dma('sync', AI[hf * 64:(hf + 1) * 64, :], aim, AI, I['ssm_a_im'], part=(hf == 1), allow_slow_non_contiguous=True)
                S.dma('sync', DT[:, :], I['ssm_log_dt'][l:l + 1, :, :].rearrange("o d g -> o (d g)").partition_broadcast(128), DT, I['ssm_log_dt'])
                A(lambda e: e.activation(out=DT[:, :], in_=DT[:, :], func=AF.Exp), [DT], [DT])
                V(lambda e: e.tensor_tensor(out=ARD[:, :], in0=DT[:, :], in1=AR[:, :], op=ALU.mult), [DT, AR], [ARD])
                V(lambda e: e.tensor_tensor(out=TH[:, :], in0=DT[:, :], in1=AI[:, :], op=ALU.mult), [DT, AI], [TH])
                A(lambda e: e.activation(out=RHO[:, :], in_=ARD[:, :], func=AF.Exp), [ARD], [RHO])
                V(lambda e: e.tensor_copy(out=THR[:, :], in_=TH[:, :]), [TH], [THR])
                self.rr_ap(THR, TMP2, THR[:, :], TMP2[:, :])
                A(lambda e: e.activation(out=SN[:, :], in_=THR[:, :], func=AF.Sin), [THR], [SN])
                A(lambda e: e.activation(out=CS[:, :], in_=THR[:, :], func=AF.Sin, scale=0.5), [THR], [CS])
                V(lambda e: e.tensor_tensor(out=CS[:, :], in0=CS[:, :], in1=CS[:, :], op=ALU.mult), [CS], [CS])
                V(lambda e: e.tensor_scalar(out=CS[:, :], in0=CS[:, :], scalar1=-2.0, scalar2=1.0, op0=ALU.mult, op1=ALU.add), [CS], [CS])
                V(lambda e: e.tensor_tensor(out=ABR[:, :], in0=RHO[:, :], in1=CS[:, :], op=ALU.mult), [RHO, CS], [ABR])
                V(lambda e: e.tensor_scalar(out=ABR[:, :], in0=ABR[:, :], scalar1=-1.0, scalar2=None, op0=ALU.add), [ABR], [ABR])
                V(lambda e: e.tensor_tensor(out=ABI[:, :], in0=RHO[:, :], in1=SN[:, :], op=ALU.mult), [RHO, SN], [ABI])
                V(lambda e: e.tensor_tensor(out=DEN[:, :], in0=AR[:, :], in1=AR[:, :], op=ALU.mult), [AR], [DEN])
                V(lambda e: e.tensor_tensor(out=TMP[:, :], in0=AI[:, :], in1=AI[:, :], op=ALU.mult), [AI], [TMP])
                V(lambda e: e.tensor_tensor(out=DEN[:, :], in0=DEN[:, :], in1=TMP[:, :], op=ALU.add), [DEN, TMP], [DEN])
                V(lambda e: e.reciprocal(out=DEN[:, :], in_=DEN[:, :]), [DEN], [DEN])
                V(lambda e: e.tensor_tensor(out=CR[:, :], in0=ABR[:, :], in1=AR[:, :], op=ALU.mult), [ABR, AR], [CR])
                V(lambda e: e.tensor_tensor(out=TMP[:, :], in0=ABI[:, :], in1=AI[:, :], op=ALU.mult), [ABI, AI], [TMP])
                V(lambda e: e.tensor_tensor(out=CR[:, :], in0=CR[:, :], in1=TMP[:, :], op=ALU.add), [CR, TMP], [CR])
                V(lambda e: e.tensor_tensor(out=CR[:, :], in0=CR[:, :], in1=DEN[:, :], op=ALU.mult), [CR, DEN], [CR])
                V(lambda e: e.tensor_tensor(out=CI[:, :], in0=ABI[:, :], in1=AR[:, :], op=ALU.mult), [ABI, AR], [CI])
                V(lambda e: e.tensor_tensor(out=TMP[:, :], in0=ABR[:, :], in1=AI[:, :], op=ALU.mult), [ABR, AI], [TMP])
                V(lambda e: e.tensor_tensor(out=CI[:, :], in0=CI[:, :], in1=TMP[:, :], op=ALU.subtract), [CI, TMP], [CI])
                V(lambda e: e.tensor_tensor(out=CI[:, :], in0=CI[:, :], in1=DEN[:, :], op=ALU.mult), [CI, DEN], [CI])
                pidx = S.sbuf([128, 4], I32, 'pidx')
                G(lambda e: e.memset(sg[:, 0:1], 1.0), [], [sg])
                G(lambda e: e.memset(sg[0:64, 0:1], -1.0), [sg], [sg])
                V(lambda e: e.tensor_scalar(out=sg[:, 2:3], in0=sg[:, 0:1], scalar1=-1.0, scalar2=None, op0=ALU.mult), [sg], [sg])
                G(lambda e: e.iota(pidx[:, 0:1], pattern=[[0, 1]], base=0, channel_multiplier=1), [], [pidx])
                V(lambda e: e.tensor_scalar(out=pidx[:, 1:2], in0=pidx[:, 0:1], scalar1=4, scalar2=3, op0=ALU.logical_shift_right, op1=ALU.bitwise_and), [pidx], [pidx])
                V(lambda e: e.tensor_scalar(out=pidx[:, 2:3], in0=pidx[:, 0:1], scalar1=4, scalar2=None, op0=ALU.logical_shift_right), [pidx], [pidx])
                V(lambda e: e.tensor_copy(out=sg[:, 1:2], in_=pidx[:, 1:2]), [pidx], [sg])
                V(lambda e: e.tensor_copy(out=sg[:, 3:4], in_=pidx[:, 2:3]), [pidx], [sg])
                for m_ in range(4):
                    V(lambda e, m_=m_: e.tensor_scalar(out=sg[:, 4 + m_:5 + m_], in0=sg[:, 1:2], scalar1=float(m_), scalar2=None, op0=ALU.is_equal), [sg], [sg])
                iq = S.sbuf([128, 128], I32, 'iq')
                G(lambda e: e.iota(iq[:, :], pattern=[[1, 128]], base=0, channel_multiplier=0), [], [iq])
                V(lambda e: e.tensor_scalar(out=iq[:, :], in0=iq[:, :], scalar1=4, scalar2=None, op0=ALU.logical_shift_right), [iq], [iq])
                V(lambda e: e.tensor_copy(out=blkmask[:, :], in_=iq[:, :]), [iq], [blkmask])
                V(lambda e: e.tensor_scalar(out=blkmask[:, :], in0=blkmask[:, :], scalar1=sg[:, 3:4], scalar2=None, op0=ALU.is_equal), [blkmask, sg], [blkmask])
                ANG, KK, SNm, CSm, RHm = [T(n, 9 * 128) for n in ('ANG', 'KKp', 'SNm', 'CSm', 'RHm')]
                def pw_chain(m_):
                    V(lambda e, m_=m_: e.tensor_scalar(out=ANG[:, m_ * 128:(m_ + 1) * 128], in0=THR[:, :], scalar1=float(m_), scalar2=None, op0=ALU.mult), [THR], [ANG])
                    yield
                    V(lambda e, m_=m_: e.tensor_scalar(out=RHm[:, m_ * 128:(m_ + 1) * 128], in0=ARD[:, :], scalar1=float(m_), scalar2=None, op0=ALU.mult), [ARD], [RHm])
                    yield
                interleave([pw_chain(m_) for m_ in range(9)])
                self.rr_ap(ANG, KK, ANG[:, :], KK[:, :])
                A(lambda e: e.activation(out=RHm[:, :], in_=RHm[:, :], func=AF.Exp), [RHm], [RHm])
                A(lambda e: e.activation(out=SNm[:, :], in_=ANG[:, :], func=AF.Sin), [ANG], [SNm])
                A(lambda e: e.activation(out=CSm[:, :], in_=ANG[:, :], func=AF.Sin, scale=0.5), [ANG], [CSm])
                V(lambda e: e.tensor_tensor(out=CSm[:, :], in0=CSm[:, :], in1=CSm[:, :], op=ALU.mult), [CSm], [CSm])
                V(lambda e: e.tensor_scalar(out=CSm[:, :], in0=CSm[:, :], scalar1=-2.0, scalar2=1.0, op0=ALU.mult, op1=ALU.add), [CSm], [CSm])
                pwr2 = PWr.t.ap().rearrange("p m g -> p (m g)")
                pwi2 = PWi.t.ap().rearrange("p m g -> p (m g)")
                V(lambda e: e.tensor_tensor(out=pwr2, in0=RHm[:, :], in1=CSm[:, :], op=ALU.mult), [RHm, CSm], [PWr])
                V(lambda e: e.tensor_tensor(out=pwi2, in0=RHm[:, :], in1=SNm[:, :], op=ALU.mult), [RHm, SNm], [PWi])
                V(lambda e: e.tensor_copy(out=TH8[:, :], in_=ANG[:, 8 * 128:9 * 128]), [ANG], [TH8])
                V(lambda e: e.tensor_copy(out=RHO8[:, :], in_=RHm[:, 8 * 128:9 * 128]), [RHm], [RHO8])
                TM8 = S.sbuf([128, 8, 128], F32, 'TM8')

                def b8(t):
                    return t[:, :].unsqueeze(1).to_broadcast([128, 8, 128])
                V(lambda e: e.tensor_tensor(out=CRe[:, :, :], in0=PWr[:, 0:8, :], in1=b8(CR), op=ALU.mult), [PWr, CR], [CRe])
                V(lambda e: e.tensor_tensor(out=TM8[:, :, :], in0=PWi[:, 0:8, :], in1=b8(CI), op=ALU.mult), [PWi, CI], [TM8])
                V(lambda e: e.tensor_tensor(out=CRe[:, :, :], in0=CRe[:, :, :], in1=TM8[:, :, :], op=ALU.subtract), [CRe, TM8], [CRe])
                V(lambda e: e.tensor_tensor(out=CIe[:, :, :], in0=PWi[:, 0:8, :], in1=b8(CR), op=ALU.mult), [PWi, CR], [CIe])
                V(lambda e: e.tensor_tensor(out=TM8[:, :, :], in0=PWr[:, 0:8, :], in1=b8(CI), op=ALU.mult), [PWr, CI], [TM8])
                V(lambda e: e.tensor_tensor(out=CIe[:, :, :], in0=CIe[:, :, :], in1=TM8[:, :, :], op=ALU.add), [CIe, TM8], [CIe])
                M1 = S.sbuf([128, 16, 128], F32, 'M1')
                M2 = S.sbuf([128, 16, 128], F32, 'M2')
                cre = I['ssm_c_re'].t.ap()[l].rearrange("d (b g) c n -> (g c) (d b) n", g=8)
                cim = I['ssm_c_im'].t.ap()[l].rearrange("d (b g) c n -> (g c) (d b) n", g=8)
                S.dma('sync', M1[:, :, 0:64], cre, M1, I['ssm_c_re'])
                S.dma('sync', M1[:, :, 64:128], cim, M1, I['ssm_c_im'], part=True)
                S.dma('sync', M2[:, :, 0:64], cim, M2, I['ssm_c_im'])
                S.dma('sync', M2[:, :, 64:128], cre, M2, I['ssm_c_re'], part=True)
                ptr_ = S.ring(3, [128, 128], F32, 'ptp', psum=True)
                for M, Tt in ((M1, T1), (M2, T2)):
                    for col in range(16):
                        pt = ptr_()
                        P(lambda e, pt=pt, M=M, col=col: e.transpose(out=pt[:, :], in_=M[:, col, :], identity=self.ident_f[:, :]), [M, self.ident_f], [pt])
                        A(lambda e, pt=pt, Tt=Tt, col=col: e.copy(out=Tt[:, col, :, :], in_=pt[:, :].rearrange("p (g o) -> p g o", o=16)), [pt], [Tt])
            with S.phase():
                pidxc = [S.sbuf([128, NC], F32, 'pidxc%d' % d_) for d_ in range(2)]
                with S.phase():
                    tmpi = S.sbuf([128, NC], I32, 'tmpi')
                    G(lambda e: e.iota(tmpi[:, :], pattern=[[1, NC]], base=0, channel_multiplier=0), [], [tmpi])
                    V(lambda e: e.tensor_copy(out=pidxc[0][:, :], in_=tmpi[:, :]), [tmpi], [pidxc[0]])
                    G(lambda e: e.iota(tmpi[:, 0:NCC], pattern=[[-1, NCC]], base=NCC - 1, channel_multiplier=0), [pidxc[0]], [tmpi])
                    G(lambda e: e.iota(tmpi[:, NCC:NC], pattern=[[-1, NC - NCC]], base=NC - 1, channel_multiplier=0), [tmpi], [tmpi])
                    V(lambda e: e.tensor_copy(out=pidxc[1][:, :], in_=tmpi[:, :]), [tmpi], [pidxc[1]])
                bre = I['ssm_b_re'].t.ap()[l].rearrange("d g n c -> n (d g) c")
                bim = I['ssm_b_im'].t.ap()[l].rearrange("d g n c -> n (d g) c")
                bar = S.ring(1, [128, 8, 16], F32, 'BAb')
                bbr = S.ring(1, [128, 8, 16], F32, 'BBb')
                uf = S.sbuf([128, NT], F32, 'uf')
                Yacc = S.sbuf([128, NT], F32, 'Yacc')
                Ytp = [Buf(Yacc.t, 'Ytp%d' % tp_) for tp_ in range(8)]
                ud = S.sbuf([128, 8, NC], BF16, 'ud')
                udm = [S.sbuf([128, 8, NC], BF16, 'udm%d' % m_) for m_ in range(4)]
                LT = S.sbuf([128, 2, 8, 2, 128], BF16, 'LT')
                x1fr = S.ring(2, [128, 128], F32, 'X1f')
                K0 = S.sbuf([128, 128], F32, 'K0')
                R1f = S.sbuf([128, 9, 128], F32, 'R1f')
                Rb = S.sbuf([128, 2, 9, 2, 128], BF16, 'Rb')
                KT = S.sbuf([128, 15, 128], BF16, 'KT')
                xbig = [S.sbuf([128, 1152], F32, 'xbig%d' % k_) for k_ in range(3)]
                angr = S.ring(2, [128, NC], F32, 'angc')
                kkr = S.ring(2, [128, NC], F32, 'kkc')
                snr = S.ring(2, [128, NC], F32, 'snc')
                csr = S.ring(2, [128, NC], F32, 'csc')
                wmr = S.ring(2, [128, NC], F32, 'wmc')
                wsr = S.ring(2, [128, NC], F32, 'wsc')
                e1r = S.ring(4, [128, NC], BF16, 'e1c')
                e2r = S.ring(4, [128, NC], BF16, 'e2c')
                rzr = S.ring(2, [128, 16, 128], BF16, 'rzc')
                t1r = S.ring(2, [128, HW], F32, 'st1')
                t2r = S.ring(2, [128, HW], F32, 'st2')
                ptr_ = S.ring(2, [128, 128], F32, 'ptp', psum=True)
                p1r = S.ring(2, [128, HW], F32, 'pp1', psum=True)
                p2r = S.ring(2, [128, HW], F32, 'pp2', psum=True)
                pyr = S.ring(2, [128, HW], F32, 'ppy', psum=True)
                ud2 = ud.t.ap().rearrange("p j k -> p (j k)")
                ufv = uf.t.ap().rearrange("p (k j) -> p j k", j=8)
                Yv = Yacc.t.ap().rearrange("p (k j) -> p k j", j=8)
                for blk in range(8):
                    S.dma('sync', uf[:, :], uT[blk * 128:(blk + 1) * 128, :], uf, uT)
                    V(lambda e, blk=blk: e.tensor_scalar(out=Yacc[:, :], in0=uf[:, :], scalar1=dvec[:, blk:blk + 1], scalar2=None, op0=ALU.mult), [uf, dvec], Ytp)
                    V(lambda e: e.tensor_copy(out=ud[:, :, :], in_=ufv), [uf], [ud])
                    for m_ in range(4):
                        V(lambda e, m_=m_: e.tensor_scalar(out=udm[m_][:, :, :], in0=ud[:, :, :], scalar1=sg[:, 4 + m_:5 + m_], scalar2=None, op0=ALU.mult), [ud, sg], [udm[m_]])
                    for d_ in range(2):
                        g0 = d_ * 64 + blk * 8
                        col = d_ * 8 + blk
                        X1f = x1fr()
                        BA, BB = bar(), bbr()
                        S.dma('sync', BA[0:64, :, :], bre[:, g0:g0 + 8, :], BA, I['ssm_b_re'])
                        S.dma('sync', BA[64:128, :, :], bim[:, g0:g0 + 8, :], BA, I['ssm_b_im'], part=True)
                        S.dma('sync', BB[0:64, :, :], bim[:, g0:g0 + 8, :], BB, I['ssm_b_im'])
                        S.dma('sync', BB[64:128, :, :], bre[:, g0:g0 + 8, :], BB, I['ssm_b_re'], part=True)
                        BAb = BA[:, :, :]
                        BBb = BB[:, :, :]

                        XA, XB, XC = xbig[0], xbig[1], xbig[2]
                        xa4 = XA[:, 0:1024].rearrange("p (e g c) -> p e g c", e=8, g=8)
                        xb4 = XB[:, 0:1024].rearrange("p (e g c) -> p e g c", e=8, g=8)
                        xc4 = XC[:, 0:1024].rearrange("p (e g c) -> p e g c", e=8, g=8)
                        BAe = BAb.unsqueeze(1).to_broadcast([128, 8, 8, 16])
                        BBe = BBb.unsqueeze(1).to_broadcast([128, 8, 8, 16])
                        cre = CRe[:, :, g0:g0 + 8].unsqueeze(3).to_broadcast([128, 8, 8, 16])
                        cie = CIe[:, :, g0:g0 + 8].unsqueeze(3).to_broadcast([128, 8, 8, 16])
                        V(lambda e, xa4=xa4, BAe=BAe, cre=cre: e.tensor_tensor(out=xa4, in0=BAe, in1=cre, op=ALU.mult), [BA, CRe], [XA])
                        V(lambda e, xb4=xb4, BBe=BBe, cie=cie: e.tensor_tensor(out=xb4, in0=BBe, in1=cie, op=ALU.mult), [BB, CIe], [XB])
                        V(lambda e: e.scalar_tensor_tensor(out=XA[:, 0:1024], in0=XB[:, 0:1024], scalar=sg[:, 0:1], in1=XA[:, 0:1024], op0=ALU.mult, op1=ALU.add), [XA, XB, sg], [XA])
                        A(lambda e, X1f=X1f: e.copy(out=X1f[:, :], in_=XA[:, 0:128]), [XA], [X1f])
                        for e_ in range(8):
                            pt = ptr_()
                            P(lambda e, pt=pt, e_=e_: e.transpose(out=pt[:, :], in_=XA[:, e_ * 128:(e_ + 1) * 128], identity=self.ident_f[:, :]), [XA, self.ident_f], [pt])
                            A(lambda e, pt=pt, d_=d_, e_=e_: e.copy(out=LT[:, d_, e_, 0, :], in_=pt[:, :]), [pt], [LT])
                        V(lambda e, xb4=xb4, BAe=BAe, cie=cie: e.tensor_tensor(out=xb4, in0=BAe, in1=cie, op=ALU.mult), [BA, CIe], [XB])
                        V(lambda e, xc4=xc4, BBe=BBe, cre=cre: e.tensor_tensor(out=xc4, in0=BBe, in1=cre, op=ALU.mult), [BB, CRe], [XC])
                        V(lambda e: e.scalar_tensor_tensor(out=XB[:, 0:1024], in0=XC[:, 0:1024], scalar=sg[:, 2:3], in1=XB[:, 0:1024], op0=ALU.mult, op1=ALU.add), [XB, XC, sg], [XB])
                        for e_ in range(8):
                            pt = ptr_()
                            P(lambda e, pt=pt, e_=e_: e.transpose(out=pt[:, :], in_=XB[:, e_ * 128:(e_ + 1) * 128], identity=self.ident_f[:, :]), [XB, self.ident_f], [pt])
                            A(lambda e, pt=pt, d_=d_, e_=e_: e.copy(out=LT[:, d_, e_, 1, :], in_=pt[:, :]), [pt], [LT])
                        T1e = T1[:, col, :, :].unsqueeze(1).to_broadcast([128, 9, 8, 16])
                        T2e = T2[:, col, :, :].unsqueeze(1).to_broadcast([128, 9, 8, 16])
                        pwr = PWr[:, :, g0:g0 + 8].unsqueeze(3).to_broadcast([128, 9, 8, 16])
                        pwi = PWi[:, :, g0:g0 + 8].unsqueeze(3).to_broadcast([128, 9, 8, 16])
                        xa9 = XA[:, :].rearrange("p (m g o) -> p m g o", m=9, g=8)
                        xc9 = XC[:, :].rearrange("p (m g o) -> p m g o", m=9, g=8)
                        r1v = R1f.t.ap().rearrange("p m q -> p (m q)")
                        V(lambda e, xa9=xa9, T1e=T1e, pwr=pwr: e.tensor_tensor(out=xa9, in0=T1e, in1=pwr, op=ALU.mult), [T1, PWr], [XA])
                        V(lambda e, xc9=xc9, T2e=T2e, pwi=pwi: e.tensor_tensor(out=xc9, in0=T2e, in1=pwi, op=ALU.mult), [T2, PWi], [XC])
                        V(lambda e, r1v=r1v: e.scalar_tensor_tensor(out=r1v, in0=XA[:, :], scalar=sg[:, 2:3], in1=XC[:, :], op0=ALU.mult, op1=ALU.subtract), [XA, XC, sg], [R1f])
                        A(lambda e, d_=d_: e.copy(out=Rb[:, d_, :, 0, :], in_=R1f[:, :, :]), [R1f], [Rb])
                        V(lambda e, xa9=xa9, T1e=T1e, pwi=pwi: e.tensor_tensor(out=xa9, in0=T1e, in1=pwi, op=ALU.mult), [T1, PWi], [XA])
                        V(lambda e, xc9=xc9, T2e=T2e, pwr=pwr: e.tensor_tensor(out=xc9, in0=T2e, in1=pwr, op=ALU.mult), [T2, PWr], [XC])
                        V(lambda e, d_=d_: e.scalar_tensor_tensor(out=Rb[:, d_, :, 1, :], in0=XA[:, :].rearrange("p (m q) -> p m q", m=9), scalar=sg[:, 0:1], in1=XC[:, :].rearrange("p (m q) -> p m q", m=9), op0=ALU.mult, op1=ALU.subtract), [XA, XC, sg], [Rb])
                        for tau in range(8):
                            pt = ptr_()
                            P(lambda e, pt=pt, tau=tau, X1f=X1f: e.matmul(pt[:, :], lhsT=X1f[:, :], rhs=R1f[:, tau, :], start=True, stop=True), [X1f, R1f], [pt])
                            if tau == 0 and d_ == 0:
                                V(lambda e, pt=pt: e.tensor_tensor(out=K0[:, :], in0=pt[:, :], in1=blkmask[:, :], op=ALU.mult), [pt, blkmask], [K0])
                            elif tau == 0:
                                V(lambda e, pt=pt: e.tensor_tensor(out=K0[:, :], in0=pt[:, :], in1=K0[:, :], op=ALU.add), [pt, K0], [K0])
                                V(lambda e: e.tensor_tensor(out=KT[:, 7, :], in0=K0[:, :], in1=blkmask[:, :], op=ALU.mult), [K0, blkmask], [KT])
                            else:
                                ki = 7 + tau if d_ == 0 else 7 - tau
                                V(lambda e, pt=pt, ki=ki: e.tensor_tensor(out=KT[:, ki, :], in0=pt[:, :], in1=blkmask[:, :], op=ALU.mult), [pt, blkmask], [KT])
                    if 'dbgKT' in self.debug and blk == 0:
                        for nm, tt, shp in (('dbgKT', KT, [128, 15, 128]), ('dbgLT', LT, [128, 2, 8, 2, 128]), ('dbgRb', Rb, [128, 2, 9, 2, 128]), ('dbgud', ud, [128, 8, NC]), ('dbgudm1', udm[1], [128, 8, NC])):
                            dd = self.scratch(nm, shp, BF16)
                            S.dma('sync', dd.t.ap(), tt.t.ap(), dd, tt)
                        for nm, tt, shp in (('dbgPWr', PWr, [128, 9, 128]), ('dbgPWi', PWi, [128, 9, 128]), ('dbgCRe', CRe, [128, 8, 128]), ('dbgT1', T1, [128, 16, 8, 16]), ('dbgmask', blkmask, [128, 128]), ('dbgsg', sg, [128, 8])):
                            dd = self.scratch(nm, shp, F32)
                            S.dma('sync', dd.t.ap(), tt.t.ap(), dd, tt)
                    for tp in range(8):
                        for (h0, h1) in HC:
                            py = pyr()
                            for j in range(8):
                                P(lambda e, py=py, tp=tp, j=j, h0=h0, h1=h1: e.matmul(py[:, :], lhsT=KT[:, 7 + tp - j, :], rhs=ud[:, j, h0:h1], start=(j == 0), stop=(j == 7)), [KT, ud], [py])
                            V(lambda e, py=py, tp=tp, h0=h0, h1=h1: e.tensor_tensor(out=Yv[:, h0:h1, tp], in0=py[:, :], in1=Yv[:, h0:h1, tp], op=ALU.add), [py, Ytp[tp]], [Ytp[tp]])
                    for g8 in range(8):
                        pr, m4 = g8 // 4, g8 % 4
                        um = udm[m4]
                        per_dir = [None, None]

                        def dir_chain(d_, g8=g8, pr=pr, m4=m4, um=um):
                            dg = d_ * 64 + blk * 8 + g8
                            ang, kk = angr(), kkr()
                            V(lambda e, ang=ang, d_=d_, dg=dg: e.tensor_scalar(out=ang[:, :], in0=pidxc[d_][:, :], scalar1=TH8[:, dg:dg + 1], scalar2=None, op0=ALU.mult), [pidxc[d_], TH8], [ang])
                            yield
                            yield from self.rr_gen(ang, kk, ang[:, :], kk[:, :], eng='vector')
                            sn, cs = snr(), csr()
                            A(lambda e, ang=ang, sn=sn: e.activation(out=sn[:, :], in_=ang[:, :], func=AF.Sin), [ang], [sn])
                            yield
                            A(lambda e, ang=ang, cs=cs: e.activation(out=cs[:, :], in_=ang[:, :], func=AF.Sin, scale=0.5), [ang], [cs])
                            yield
                            A(lambda e, cs=cs: e.activation(out=cs[:, :], in_=cs[:, :], func=AF.Square), [cs], [cs])
                            yield
                            V(lambda e, cs=cs: e.tensor_scalar(out=cs[:, :], in0=cs[:, :], scalar1=-2.0, scalar2=1.0, op0=ALU.mult, op1=ALU.add), [cs], [cs])
                            yield
                            wm, ws = wmr(), wsr()
                            for (h0, h1) in HC:
                                p1, p2 = p1r(), p2r()
                                for v_, pp in ((0, p1), (1, p2)):
                                    for j in range(8):
                                        e_ = 7 - j if d_ == 0 else j
                                        P(lambda e, pp=pp, d_=d_, e_=e_, v_=v_, j=j, h0=h0, h1=h1, pr=pr, um=um: e.matmul(pp[:, :], lhsT=LT[64 * pr:64 * pr + 64, d_, e_, v_, :], rhs=um[64 * pr:64 * pr + 64, j, h0:h1], start=(j == 0), stop=(j == 7)), [LT, um], [pp])
                                a, b = t1r(), t2r()
                                V(lambda e, p1=p1, a=a, cs=cs, h0=h0, h1=h1: e.tensor_tensor(out=a[:, :], in0=p1[:, :], in1=cs[:, h0:h1], op=ALU.mult), [p1, cs], [a])
                                yield
                                V(lambda e, p2=p2, b=b, sn=sn, h0=h0, h1=h1: e.tensor_tensor(out=b[:, :], in0=p2[:, :], in1=sn[:, h0:h1], op=ALU.mult), [p2, sn], [b])
                                yield
                                V(lambda e, a=a, b=b, wm=wm, h0=h0, h1=h1: e.tensor_tensor(out=wm[:, h0:h1], in0=a[:, :], in1=b[:, :], op=ALU.add), [a, b], [wm])
                                yield
                            rho_b = RHO8[:, dg:dg + 1]
                            e1, e2 = e1r(), e2r()
                            if d_ == 0:
                                V(lambda e, rho_b=rho_b, wm=wm, ws=ws: e.tensor_tensor_scan(out=ws[:, :], data0=rho_b.to_broadcast([128, NC]), data1=wm[:, :], initial=0.0, op0=ALU.mult, op1=ALU.add), [wm, RHO8], [ws])
                                yield
                                V(lambda e, cs=cs, ws=ws, e1=e1: e.tensor_tensor(out=e1[:, 1:NC], in0=cs[:, 0:NC - 1], in1=ws[:, 0:NC - 1], op=ALU.mult), [cs, ws], [e1])
                                yield
                                G(lambda e, e1=e1: e.memset(e1[:, 0:1], 0.0), [e1], [e1])
                                yield
                                V(lambda e, sn=sn, ws=ws, e2=e2: e.tensor_tensor(out=e2[:, 1:NC], in0=sn[:, 0:NC - 1], in1=ws[:, 0:NC - 1], op=ALU.mult), [sn, ws], [e2])
                                yield
                                G(lambda e, e2=e2: e.memset(e2[:, 0:1], 0.0), [e2], [e2])
                                yield
                            else:
                                V(lambda e, rho_b=rho_b, wm=wm, ws=ws: e.tensor_tensor_scan(out=ws[:, 0:NCC][:, ::-1], data0=rho_b.to_broadcast([128, NCC]), data1=wm[:, 0:NCC][:, ::-1], initial=0.0, op0=ALU.mult, op1=ALU.add), [wm, RHO8], [ws])
                                yield
                                V(lambda e, rho_b=rho_b, wm=wm, ws=ws: e.tensor_tensor_scan(out=ws[:, NCC:NC][:, ::-1], data0=rho_b.to_broadcast([128, NC - NCC]), data1=wm[:, NCC:NC][:, ::-1], initial=ws[:, 0:1], op0=ALU.mult, op1=ALU.add), [wm, RHO8, ws], [ws])
                                yield
                                for (tb, ee, eng) in ((cs, e1, V), (sn, e2, V)):
                                    eng(lambda e, tb=tb, ee=ee, ws=ws: e.tensor_tensor(out=ee[:, 0:NC - 1], in0=tb[:, 1:NC], in1=ws[:, 1:NC], op=ALU.mult), [tb, ws], [ee])
                                    yield
                                    eng(lambda e, tb=tb, ee=ee, ws=ws: e.tensor_tensor(out=ee[:, NC - 1:NC], in0=tb[:, 0:1], in1=ws[:, 0:1], op=ALU.mult), [tb, ws, ee], [ee])
                                    yield
                                    G(lambda e, ee=ee: e.memset(ee[:, NCC - 1:NCC], 0.0), [ee], [ee])
                                    yield
                            rz = rzr()
                            G(lambda e, rz=rz: e.memset(rz[:, :, :], 0.0), [], [rz])
                            yield
                            G(lambda e, rz=rz, d_=d_, g8=g8: e.tensor_copy(out=rz[:, :, g8 * 16:(g8 + 1) * 16], in_=Rb[:, d_, 1:9, :, g8 * 16:(g8 + 1) * 16].rearrange("p m v o -> p (m v) o")), [Rb], [rz])
                            yield
                            per_dir[d_] = (e1, e2, rz)
                        interleave([dir_chain(0), dir_chain(1)])
                        (e1f, e2f, rzf), (e1b, e2b, rzb) = per_dir
                        for tp in range(8):
                            mf, mb = tp + 1, 8 - tp
                            for (h0, h1) in HC:
                                py = pyr()
                                P(lambda e, py=py, mf=mf, h0=h0, h1=h1, rzf=rzf, e1f=e1f: e.matmul(py[:, :], lhsT=rzf[:, (mf - 1) * 2, :], rhs=e1f[:, h0:h1], start=True, stop=False), [rzf, e1f], [py])
                                P(lambda e, py=py, mf=mf, h0=h0, h1=h1, rzf=rzf, e2f=e2f: e.matmul(py[:, :], lhsT=rzf[:, (mf - 1) * 2 + 1, :], rhs=e2f[:, h0:h1], start=False, stop=False), [rzf, e2f], [py])
                                P(lambda e, py=py, mb=mb, h0=h0, h1=h1, rzb=rzb, e1b=e1b: e.matmul(py[:, :], lhsT=rzb[:, (mb - 1) * 2, :], rhs=e1b[:, h0:h1], start=False, stop=False), [rzb, e1b], [py])
                                P(lambda e, py=py, mb=mb, h0=h0, h1=h1, rzb=rzb, e2b=e2b: e.matmul(py[:, :], lhsT=rzb[:, (mb - 1) * 2 + 1, :], rhs=e2b[:, h0:h1], start=False, stop=True), [rzb, e2b], [py])
                                V(lambda e, py=py, tp=tp, h0=h0, h1=h1: e.tensor_tensor(out=Yv[:, h0:h1, tp], in0=py[:, :], in1=Yv[:, h0:h1, tp], op=ALU.add), [py, Ytp[tp]], [Ytp[tp]])
                    if 'yscanT' in self.debug:
                        ysd = self.scratch('yscanT', [D, NT], F32)
                        V(lambda e, blk=blk: e.tensor_scalar(out=uf[:, :], in0=uf[:, :], scalar1=dvec[:, blk:blk + 1], scalar2=None, op0=ALU.mult), [uf, dvec], [uf])
                        V(lambda e: e.tensor_tensor(out=uf[:, :], in0=Yacc[:, :], in1=uf[:, :], op=ALU.subtract), Ytp + [uf], [uf])
                        S.dma('sync', ysd[blk * 128:(blk + 1) * 128, :], uf[:, :], ysd, uf, part=True)
                    V(lambda e: e.tensor_tensor(out=uf[:, :], in0=Yacc[:, :], in1=Yacc[:, :], op=ALU.mult), Ytp, [uf])
                    V(lambda e: e.tensor_scalar(out=uf[:, :], in0=uf[:, :], scalar1=0.044715, scalar2=1.0, op0=ALU.mult, op1=ALU.add), [uf], [uf])
                    V(lambda e: e.tensor_tensor(out=uf[:, :], in0=uf[:, :], in1=Yacc[:, :], op=ALU.mult), Ytp + [uf], [uf])
                    A(lambda e: e.activation(out=uf[:, :], in_=uf[:, :], func=AF.Sigmoid, scale=2.0 * math.sqrt(2.0 / math.pi)), [uf], [uf])
                    V(lambda e: e.tensor_tensor(out=ud2, in0=uf[:, :], in1=Yacc[:, :], op=ALU.mult), Ytp + [uf], [ud])
                    S.dma('sync', ygT[blk * 128:(blk + 1) * 128, :], ud2, ygT, ud, part=True)
        self.ssm_glu_out(l)

    def ssm_glu_out(self, l):
        S = self.S
        I = self.I
        ygT = self.scratch('ygT', [D, NT], BF16)
        gates = self.scratch('gates', [NT, 3 * D], F32)
        y1 = self.scratch('y1', [NT, D], F32)
        TG = [(t0, min(t0 + 512, NT)) for t0 in range(0, NT, 512)]
        V = lambda fn, r, w: S.op('vector', fn, r, w)
        A = lambda fn, r, w: S.op('scalar', fn, r, w)
        with S.phase():
            wg = S.sbuf([128, 8, D], BF16, 'wglu')
            wo = S.sbuf([128, 8, D], BF16, 'wsout')
            for (wt, nm) in ((wg, 'w_ssm_glu'), (wo, 'w_ssm_out')):
                wvv = I[nm].t.ap()[l].rearrange("(k p) n -> p k n", p=128)
                for hh in range(2):
                    S.dma('gpsimd', wt[:, :, hh * 512:(hh + 1) * 512], wvv[:, :, hh * 512:(hh + 1) * 512], wt, I[nm], part=(hh == 1))
            ygr = S.ring(2, [128, 8, 512], BF16, 'ygl')
            y2r = S.ring(2, [128, 8, 512], BF16, 'y2T')
            sgr = S.ring(2, [128, 512], F32, 'sgl')
            gr = S.ring(2, [128, D], F32, 'g1')
            yr = S.ring(2, [128, D], F32, 'y1t')
            pgr = S.ring(3, [128, 512], F32, 'pgl', psum=True)
            pyr = S.ring(4, [128, 512], F32, 'py', psum=True)
            ygv = ygT.t.ap().rearrange("(j p) t -> p j t", p=128)
            for (t0, t1) in TG:
                n = t1 - t0
                yg = ygr()
                S.dma('sync', yg[:, :, 0:n], ygv[:, :, t0:t1], yg, ygT)
                y2 = y2r()
                for m in range(8):
                    pg = pgr()
                    for k in range(8):
                        S.op('tensor', lambda e, pg=pg, yg=yg, m=m, k=k, n=n: e.matmul(pg[:, 0:n], lhsT=wg[:, k, m * 128:(m + 1) * 128], rhs=yg[:, k, 0:n], start=(k == 0), stop=(k == 7)), [wg, yg], [pg])
                    sgt = sgr()
                    A(lambda e, pg=pg, sgt=sgt, n=n: e.activation(out=sgt[:, 0:n], in_=pg[:, 0:n], func=AF.Sigmoid), [pg], [sgt])
                    V(lambda e, yg=yg, y2=y2, sgt=sgt, m=m, n=n: e.tensor_tensor(out=y2[:, m, 0:n], in0=yg[:, m, 0:n], in1=sgt[:, 0:n], op=ALU.mult), [yg, sgt], [y2])
                for ii in range(n // 128):
                    i = t0 // 128 + ii
                    g1 = gr()
                    S.dma('sync', g1[:, :], gates[i * 128:(i + 1) * 128, D:2 * D], g1, gates)
                    yt = yr()
                    for nb in range(2):
                        py = pyr()
                        for m in range(8):
                            S.op('tensor', lambda e, py=py, y2=y2, m=m, ii=ii, nb=nb: e.matmul(py[:, :], lhsT=y2[:, m, ii * 128:(ii + 1) * 128], rhs=wo[:, m, nb * 512:(nb + 1) * 512], start=(m == 0), stop=(m == 7)), [y2, wo], [py])
                        V(lambda e, py=py, yt=yt, g1=g1, nb=nb: e.tensor_tensor(out=yt[:, nb * 512:(nb + 1) * 512], in0=py[:, :], in1=g1[:, nb * 512:(nb + 1) * 512], op=ALU.mult), [py, g1], [yt])
                    S.dma('sync', y1[i * 128:(i + 1) * 128, :], yt[:, :], y1, yt, part=True)


    def ln_stats_gen(self, x, st):
        S = self.S
        S.op('vector', lambda e: e.bn_stats(out=st[:, 0:6], in_=x[:, 0:512]), [x], [st])
        yield
        S.op('vector', lambda e: e.bn_stats(out=st[:, 6:12], in_=x[:, 512:1024]), [x], [st])
        yield
        S.op('vector', lambda e: e.bn_aggr(out=st[:, 12:14], in_=st[:, 0:12]), [st], [st])
        yield
        S.op('vector', lambda e: e.tensor_scalar(out=st[:, 14:15], in0=st[:, 13:14], scalar1=LN_EPS, scalar2=None, op0=ALU.add), [st], [st])
        yield
        S.op('scalar', lambda e: e.activation(out=st[:, 14:15], in_=st[:, 14:15], func=AF.Sqrt), [st], [st])
        yield
        S.op('vector', lambda e: e.reciprocal(out=st[:, 14:15], in_=st[:, 14:15]), [st], [st])
        yield

    def ln_stats(self, x, st):
        S = self.S
        S.op('vector', lambda e: e.bn_stats(out=st[:, 0:6], in_=x[:, 0:512]), [x], [st])
        S.op('vector', lambda e: e.bn_stats(out=st[:, 6:12], in_=x[:, 512:1024]), [x], [st])
        S.op('vector', lambda e: e.bn_aggr(out=st[:, 12:14], in_=st[:, 0:12]), [st], [st])
        self.rstd(st, 13, 14, LN_EPS)

    def ph_merge(self, l):
        S = self.S
        I = self.I
        V = lambda fn, r, w: S.op('vector', fn, r, w)
        G = lambda fn, r, w: S.op('gpsimd', fn, r, w)
        A = lambda fn, r, w: S.op('scalar', fn, r, w)
        ctx_out = l < DEPTH - 1
        ys = [self.scratch('y%d' % b, [NT, D], F32) for b in range(3)]
        xres = self.scr['xres']
        modscr = self.scr['modscr%d' % l]
        h2tm = self.scratch('h2tm', [NT, D], BF16)
        aff = self.scratch('aff', [NT, NEXP], F32)
        affT = self.scratch('affT', [NEXP, NT], F32)
        with S.phase():
            wo = S.sbuf([128, 8, D], BF16, 'wo')
            wov = I['w_o'].t.ap()[l].rearrange("(k p) n -> p k n", p=128)
            for hh in range(2):
                S.dma('gpsimd', wo[:, :, hh * 512:(hh + 1) * 512], wov[:, :, hh * 512:(hh + 1) * 512], wo, I['w_o'], part=(hh == 1))
            wr = S.sbuf([128, 8, NEXP], F32, 'wr')
            S.dma('sync', wr[:, :, :], I['w_router'].t.ap()[l].rearrange("(k p) e -> p k e", p=128), wr, I['w_router'])
            modv = S.sbuf([128, 2, 3, D], F32, 'modv2')
            for w in range(2):
                for jj, c0 in enumerate((2 * D, 3 * D, 4 * D)):
                    S.dma('sync', modv[:, w, jj, :], modscr[w, :, c0:c0 + D], modv, modscr, part=not (w == 0 and jj == 0))
            lng = S.sbuf([128, 2, D], F32, 'ln1gb')
            S.dma('sync', lng[:, 0, :], I['ln1_g'][l:l + 1, :].partition_broadcast(128), lng, I['ln1_g'])
            S.dma('sync', lng[:, 1, :], I['ln1_b'][l:l + 1, :].partition_broadcast(128), lng, I['ln1_b'], part=True)
            yar, ybr, ycr = (S.ring(2, [128, D], F32, nm) for nm in ('ya', 'yb', 'yc'))
            mbr = S.ring(2, [128, D], BF16, 'mb')
            mtr = S.ring(2, [128, 8, 128], BF16, 'mT')
            xr = S.ring(2, [128, D], F32, 'xt')
            tr = S.ring(2, [128, D], F32, 'tt')
            hr = S.ring(2, [128, D], F32, 'h2f')
            hbr = S.ring(2, [128, D], BF16, 'h2b')
            htr = S.ring(2, [128, 8, 128], F32, 'h2T')
            str_ = S.ring(6, [128, 16], F32, 'st')
            afr = S.ring(2, [128, 2, NEXP], F32, 'afft')
            atr = S.ring(2, [NEXP, 128], F32, 'affTt')
            ptr_ = S.ring(1, [128, 8, 128], BF16, 'pTm', psum=True)
            pyr = S.ring(2, [128, 512], F32, 'pym', psum=True)
            phr = S.ring(1, [128, 8, 128], F32, 'pTh', psum=True)
            plr = S.ring(2, [128, NEXP], F32, 'plog', psum=True)
            par_ = S.ring(1, [NEXP, 128], F32, 'paT', psum=True)
            def tile_chain(i):
                w = 1 if i < 2 else 0
                rows = slice(i * 128, (i + 1) * 128)
                ya, yb, yc = yar(), ybr(), ycr()
                for yt_, ysrc in ((ya, ys[0]), (yb, ys[1]), (yc, ys[2])):
                    S.dma('sync', yt_[:, :], ysrc[rows, :], yt_, ysrc)
                V(lambda e, ya=ya, yb=yb: e.tensor_tensor(out=ya[:, :], in0=ya[:, :], in1=yb[:, :], op=ALU.add), [ya, yb], [ya])
                yield
                mb = mbr()
                V(lambda e, ya=ya, yc=yc, mb=mb: e.tensor_tensor(out=mb[:, :], in0=ya[:, :], in1=yc[:, :], op=ALU.add), [ya, yc], [mb])
                yield
                pT = ptr_()
                for k in range(8):
                    S.op('tensor', lambda e, pT=pT, mb=mb, k=k: e.transpose(out=pT[:, k, :], in_=mb[:, k * 128:(k + 1) * 128], identity=self.ident_b[:, :]), [mb, self.ident_b], [pT])
                mT = mtr()
                A(lambda e, pT=pT, mT=mT: e.copy(out=mT[:, :, :], in_=pT[:, :, :]), [pT], [mT])
                yield
                xt = xr()
                S.dma('sync', xt[:, :], xres[rows, :], xt, xres)
                tt = tr()
                for nb in range(2):
                    py = pyr()
                    for k in range(8):
                        S.op('tensor', lambda e, py=py, mT=mT, k=k, nb=nb: e.matmul(py[:, :], lhsT=mT[:, k, :], rhs=wo[:, k, nb * 512:(nb + 1) * 512], start=(k == 0), stop=(k == 7)), [mT, wo], [py])
                    V(lambda e, py=py, tt=tt, nb=nb, w=w: e.tensor_tensor(out=tt[:, nb * 512:(nb + 1) * 512], in0=py[:, :], in1=modv[:, w, 0, nb * 512:(nb + 1) * 512], op=ALU.mult), [py, modv], [tt])
                    yield
                V(lambda e, xt=xt, tt=tt: e.scalar_tensor_tensor(out=tt[:, :], in0=xt[:, :], scalar=ALPHA, in1=tt[:, :], op0=ALU.mult, op1=ALU.add), [xt, tt], [tt])
                yield
                st = str_()
                yield from self.ln_stats_gen(tt, st)
                V(lambda e, tt=tt, st=st: e.tensor_scalar(out=tt[:, :], in0=tt[:, :], scalar1=st[:, 12:13], scalar2=st[:, 14:15], op0=ALU.subtract, op1=ALU.mult), [tt, st], [tt])
                yield
                V(lambda e, tt=tt: e.tensor_tensor(out=tt[:, :], in0=tt[:, :], in1=lng[:, 0, :], op=ALU.mult), [tt, lng], [tt])
                yield
                V(lambda e, tt=tt, xt=xt: e.tensor_tensor(out=xt[:, :], in0=tt[:, :], in1=lng[:, 1, :], op=ALU.add), [tt, lng], [xt])
                yield
                S.dma('sync', xres[rows, :], xt[:, :], xres, xt, part=True)
                st2 = str_()
                yield from self.ln_stats_gen(xt, st2)
                hf = hr()
                V(lambda e, xt=xt, st2=st2, hf=hf: e.tensor_scalar(out=hf[:, :], in0=xt[:, :], scalar1=st2[:, 12:13], scalar2=st2[:, 14:15], op0=ALU.subtract, op1=ALU.mult), [xt, st2], [hf])
                yield
                V(lambda e, hf=hf, w=w: e.tensor_tensor(out=hf[:, :], in0=hf[:, :], in1=modv[:, w, 2, :], op=ALU.mult), [hf, modv], [hf])
                yield
                V(lambda e, hf=hf, w=w: e.tensor_tensor(out=hf[:, :], in0=hf[:, :], in1=modv[:, w, 1, :], op=ALU.add), [hf, modv], [hf])
                yield
                hb = hbr()
                A(lambda e, hf=hf, hb=hb: e.copy(out=hb[:, :], in_=hf[:, :]), [hf], [hb])
                yield
                S.dma('sync', h2tm[rows, :], hb[:, :], h2tm, hb, part=True)
                ph = phr()
                for k in range(8):
                    S.op('tensor', lambda e, ph=ph, hf=hf, k=k: e.transpose(out=ph[:, k, :], in_=hf[:, k * 128:(k + 1) * 128], identity=self.ident_f[:, :]), [hf, self.ident_f], [ph])
                hT = htr()
                A(lambda e, ph=ph, hT=hT: e.copy(out=hT[:, :, :], in_=ph[:, :, :]), [ph], [hT])
                yield
                pl = plr()
                for k in range(8):
                    S.op('tensor', lambda e, pl=pl, hT=hT, k=k: e.matmul(pl[:, :], lhsT=hT[:, k, :], rhs=wr[:, k, :], start=(k == 0), stop=(k == 7)), [hT, wr], [pl])
                st3 = str_()
                af = afr()
                V(lambda e, pl=pl, st3=st3: e.reduce_max(out=st3[:, 0:1], in_=pl[:, :], axis=AX.X), [pl], [st3])
                yield
                V(lambda e, st3=st3: e.tensor_scalar(out=st3[:, 1:2], in0=st3[:, 0:1], scalar1=-1.0, scalar2=None, op0=ALU.mult), [st3], [st3])
                yield
                A(lambda e, pl=pl, st3=st3, af=af: e.activation(out=af[:, 0, :], in_=pl[:, :], func=AF.Exp, bias=st3[:, 1:2], accum_out=st3[:, 2:3]), [pl, st3], [af, st3])
                yield
                V(lambda e, st3=st3: e.reciprocal(out=st3[:, 3:4], in_=st3[:, 2:3]), [st3], [st3])
                yield
                V(lambda e, st3=st3, af=af: e.tensor_scalar(out=af[:, 1, :], in0=af[:, 0, :], scalar1=st3[:, 3:4], scalar2=None, op0=ALU.mult), [af, st3], [af])
                yield
                S.dma('sync', aff[rows, :], af[:, 1, :], aff, af, part=True)
                pa = par_()
                S.op('tensor', lambda e, pa=pa, af=af: e.transpose(out=pa[:, :], in_=af[:, 1, :], identity=self.ident_f[:, :]), [af, self.ident_f], [pa])
                at = atr()
                A(lambda e, pa=pa, at=at: e.copy(out=at[:, :], in_=pa[:, :]), [pa], [at])
                yield
                S.dma('sync', affT[:, i * 128:(i + 1) * 128], at[:, :], affT, at, part=True)


            tiles = list(range(0 if ctx_out else 2, NTILE))
            for t0_ in range(0, len(tiles), 2):
                interleave([tile_chain(t_) for t_ in tiles[t0_:t0_ + 2]])
    def ph_moe(self, l):
        S = self.S
        I = self.I
        V = lambda fn, r, w: S.op('vector', fn, r, w)
        G = lambda fn, r, w: S.op('gpsimd', fn, r, w)
        A = lambda fn, r, w: S.op('scalar', fn, r, w)
        P = lambda fn, r, w: S.op('tensor', fn, r, w)
        ctx_out = l < DEPTH - 1
        h2tm = self.scr['h2tm']
        aff = self.scr['aff']
        affT = self.scr['affT']
        ymoe = self.scratch('ymoe', [NT, D], F32)
        sets = []
        if ctx_out:
            sets.append((0, 2, 2 * NCTX // NEXP, 1))
        sets.append((2, 32, 2 * NLAT // NEXP, 4))
        with S.phase():
            zt = S.sbuf([128, D], F32, 'zt')
            G(lambda e: e.memset(zt[:, :], 0.0), [], [zt])
            for i in range(NTILE):
                S.dma('sync', ymoe[i * 128:(i + 1) * 128, :], zt[:, :], ymoe, zt, part=(i > 0))
            ymoe_t = [Buf(ymoe.t, 'ymoe_t%d' % i) for i in range(NTILE)]
            for b in ymoe_t:
                b.w = ymoe.w
            acc_grp = [DSem(S, 'ymacc0'), DSem(S, 'ymacc1')]
            aff_all = S.sbuf([128, NTILE, NEXP], F32, 'aff_all')
            mask_all = S.sbuf([128, NTILE, NEXP], F32, 'mask_all')
            slot_all = S.sbuf([128, NTILE, NEXP], F32, 'slot_all')
            gate_all = S.sbuf([128, NTILE, NEXP], F32, 'gate_all')
            slotT = S.sbuf([NEXP, NT], F32, 'slotT')
            S.dma('sync', aff_all[:, :, :], aff.t.ap().rearrange("(i p) e -> p i e", p=128), aff_all, aff)
            esel = S.sbuf([NEXP, NEXP, 128], F32, 'esel')
            V(lambda e: e.tensor_copy(out=esel[:, :, :], in_=self.ident_f[0:NEXP, 0:NEXP].unsqueeze(2).to_broadcast([NEXP, NEXP, 128])), [self.ident_f], [esel])
            ustr = S.sbuf([128, 128], F32, 'ustr')
            G(lambda e: e.affine_select(out=ustr[:, :], in_=self.ones_f[:, :], pattern=[[1, 128]], compare_op=ALU.is_gt, fill=0.0, base=0, channel_multiplier=-1), [self.ones_f], [ustr])
            irow = S.sbuf([128, 512], F32, 'irow')
            pcol = S.sbuf([128, 4], F32, 'pcol')
            with S.phase():
                ti = S.sbuf([128, 512], I32, 'ti')
                G(lambda e: e.iota(ti[:, :], pattern=[[1, 512]], base=0, channel_multiplier=0), [], [ti])
                V(lambda e: e.tensor_copy(out=irow[:, :], in_=ti[:, :]), [ti], [irow])
                G(lambda e: e.iota(ti[:, 0:4], pattern=[[128, 4]], base=0, channel_multiplier=1), [irow], [ti])
                V(lambda e: e.tensor_copy(out=pcol[:, :], in_=ti[:, 0:4]), [ti], [pcol])
                affTs = S.sbuf([NEXP, NT], F32, 'affTs')
                junk = S.sbuf([NEXP, NLAT], F32, 'junk')
                S.dma('sync', affTs[:, :], affT[:, :], affTs, affT)
                pb = S.ring(2, [128, NEXP], F32, 'pthr', psum=True)
                pc = S.ring(2, [128, NEXP], F32, 'pcum', psum=True)
                pst = S.ring(2, [NEXP, 128], F32, 'pslT', psum=True)
                for (ti0, ntl, cap, nst) in sets:
                    c0, c1 = ti0 * 128, (ti0 + ntl) * 128
                    bs = S.sbuf([NEXP, 8], F32, 'bs')
                    V(lambda e, bs=bs: e.memset(bs[:, 0:1], 0.0), [], [bs])
                    V(lambda e, bs=bs: e.memset(bs[:, 1:2], 1.0), [bs], [bs])
                    for it in range(30):
                        V(lambda e, bs=bs: e.tensor_tensor(out=bs[:, 2:3], in0=bs[:, 0:1], in1=bs[:, 1:2], op=ALU.add), [bs], [bs])
                        V(lambda e, bs=bs: e.tensor_scalar(out=bs[:, 2:3], in0=bs[:, 2:3], scalar1=0.5, scalar2=None, op0=ALU.mult), [bs], [bs])
                        V(lambda e, bs=bs, c0=c0, c1=c1: e.tensor_scalar(out=junk[:, 0:c1 - c0], in0=affTs[:, c0:c1], scalar1=bs[:, 2:3], scalar2=0.0, op0=ALU.is_ge, op1=ALU.add, accum_out=bs[:, 3:4]), [affTs, bs], [junk, bs])
                        V(lambda e, bs=bs, cap=cap: e.tensor_scalar(out=bs[:, 4:5], in0=bs[:, 3:4], scalar1=float(cap), scalar2=None, op0=ALU.is_ge), [bs], [bs])
                        V(lambda e, bs=bs: e.tensor_scalar(out=bs[:, 5:6], in0=bs[:, 4:5], scalar1=-1.0, scalar2=1.0, op0=ALU.mult, op1=ALU.add), [bs], [bs])
                        V(lambda e, bs=bs: e.tensor_tensor(out=bs[:, 6:7], in0=bs[:, 2:3], in1=bs[:, 0:1], op=ALU.subtract), [bs], [bs])
                        V(lambda e, bs=bs: e.tensor_tensor(out=bs[:, 7:8], in0=bs[:, 2:3], in1=bs[:, 1:2], op=ALU.subtract), [bs], [bs])
                        V(lambda e, bs=bs: e.scalar_tensor_tensor(out=bs[:, 0:1], in0=bs[:, 6:7], scalar=bs[:, 4:5], in1=bs[:, 0:1], op0=ALU.mult, op1=ALU.add), [bs], [bs])
                        V(lambda e, bs=bs: e.scalar_tensor_tensor(out=bs[:, 1:2], in0=bs[:, 7:8], scalar=bs[:, 5:6], in1=bs[:, 1:2], op0=ALU.mult, op1=ALU.add), [bs], [bs])
                    dg = S.sbuf([NEXP, NEXP], F32, 'dg')
                    V(lambda e, bs=bs, dg=dg: e.tensor_scalar(out=dg[:, :], in0=self.ident_f[0:NEXP, 0:NEXP], scalar1=bs[:, 0:1], scalar2=None, op0=ALU.mult), [bs, self.ident_f], [dg])
                    pthr = pb()
                    P(lambda e, pthr=pthr, dg=dg: e.matmul(pthr[:, :], lhsT=self.ones_f[0:NEXP, :], rhs=dg[:, :], start=True, stop=True), [dg, self.ones_f], [pthr])
                    thr = S.sbuf([128, NEXP], F32, 'thr')
                    V(lambda e, pthr=pthr, thr=thr: e.tensor_copy(out=thr[:, :], in_=pthr[:, :]), [pthr], [thr])
                    for ii in range(ntl):
                        i = ti0 + ii
                        V(lambda e, i=i, thr=thr: e.tensor_tensor(out=mask_all[:, i, :], in0=aff_all[:, i, :], in1=thr[:, :], op=ALU.is_ge), [aff_all, thr], [mask_all])
                        V(lambda e, i=i: e.tensor_tensor(out=gate_all[:, i, :], in0=aff_all[:, i, :], in1=mask_all[:, i, :], op=ALU.mult), [aff_all, mask_all], [gate_all])
                        pcm = pc()
                        for jj in range(ii):
                            j = ti0 + jj
                            P(lambda e, pcm=pcm, j=j, jj=jj: e.matmul(pcm[:, :], lhsT=self.ones_f[:, :], rhs=mask_all[:, j, :], start=(jj == 0), stop=False), [mask_all, self.ones_f], [pcm])
                        P(lambda e, pcm=pcm, i=i, ii=ii: e.matmul(pcm[:, :], lhsT=ustr[:, :], rhs=mask_all[:, i, :], start=(ii == 0), stop=True), [mask_all, ustr], [pcm])
                        V(lambda e, pcm=pcm, i=i: e.tensor_tensor(out=slot_all[:, i, :], in0=pcm[:, :], in1=mask_all[:, i, :], op=ALU.mult), [pcm, mask_all], [slot_all])
                        V(lambda e, i=i: e.tensor_tensor(out=slot_all[:, i, :], in0=slot_all[:, i, :], in1=mask_all[:, i, :], op=ALU.add), [slot_all, mask_all], [slot_all])
                        V(lambda e, i=i: e.tensor_scalar(out=slot_all[:, i, :], in0=slot_all[:, i, :], scalar1=-1.0, scalar2=None, op0=ALU.add), [slot_all], [slot_all])
                        ps_ = pst()
                        P(lambda e, ps_=ps_, i=i: e.transpose(out=ps_[:, :], in_=slot_all[:, i, :], identity=self.ident_f[:, :]), [slot_all, self.ident_f], [ps_])
                        A(lambda e, ps_=ps_, i=i: e.copy(out=slotT[:, i * 128:(i + 1) * 128], in_=ps_[:, :]), [ps_], [slotT])
            if 'slotdbg' in self.debug:
                sd = self.scratch('slotdbg', [128, NTILE, NEXP], F32)
                S.dma('sync', sd[:, :, :], slot_all[:, :, :], sd, slot_all)
            with S.phase():
                selT = S.sbuf([128, 32, 512], BF16, 'selT')
                sel = S.sbuf([128, 4, NLAT], BF16, 'sel')
                xsT = S.sbuf([128, 8, 512], BF16, 'xsT')
                actT = S.sbuf([128, 22, 512], BF16, 'actT')
                ye = S.sbuf([128, 4, D], BF16, 'ye')
                h2r = S.ring(4, [128, D], BF16, 'h2l')
                wgr = S.ring(3, [128, 8, 256], BF16, 'wga')
                wur = S.ring(3, [128, 8, 256], BF16, 'wgu')
                wdr = S.ring(4, [128, 2, 512], BF16, 'wdn')
                sar = S.ring(2, [128, 512], F32, 'sact')
                ytr = S.ring(2, [128, D], F32, 'ysc')
                acc4 = S.ring(4, [128, 512], F32, 'pacc', psum=True)
                gen4 = S.ring(4, [128, 512], F32, 'pgen', psum=True)
                sbufs = []
                for (ti0, ntl, cap, nst) in sets:
                    if nst == 4:
                        sbufs.append(dict(selT=selT, sel=sel, xsT=xsT, actT=actT, ye=ye))
                    else:
                        sbufs.append(dict(selT=S.sbuf([128, ntl, 128 * nst], BF16, 'selTc'), sel=S.sbuf([128, nst, ntl * 128], BF16, 'selc'),
                                          xsT=S.sbuf([128, 8, 128 * nst], BF16, 'xsTc'), actT=S.sbuf([128, 22, 128 * nst], BF16, 'actTc'),
                                          ye=S.sbuf([128, nst, D], BF16, 'yec')))

                def make_item(ex, si):
                    ti0, ntl, cap, nst = sets[si]
                    bf = sbufs[si]
                    selT_, sel_, xsT_, ye_ = bf['selT'], bf['sel'], bf['xsT'], bf['ye']
                    ns = nst * 128
                    ntok = ntl * 128
                    tok0 = ti0 * 128

                    def selT_build():
                        for ii in range(ntl):
                            i = ti0 + ii
                            V(lambda e, ii=ii, i=i: e.tensor_scalar(out=selT_[:, ii, 0:ns], in0=irow[:, 0:ns], scalar1=slot_all[:, i, ex:ex + 1], scalar2=None, op0=ALU.is_equal), [irow, slot_all], [selT_])

                    def sel_build():
                        for c0 in range(0, ntok, 512):
                            cn = min(512, ntok - c0)
                            pbc = gen4()
                            P(lambda e, pbc=pbc, c0=c0, cn=cn: e.matmul(pbc[:, 0:cn], lhsT=esel[:, ex, :], rhs=slotT[:, tok0 + c0:tok0 + c0 + cn], start=True, stop=True), [esel, slotT], [pbc])
                            for st_ in range(nst):
                                V(lambda e, pbc=pbc, st_=st_, c0=c0, cn=cn: e.tensor_scalar(out=sel_[:, st_, c0:c0 + cn], in0=pbc[:, 0:cn], scalar1=pcol[:, st_:st_ + 1], scalar2=None, op0=ALU.is_equal), [pbc, pcol], [sel_])

                    def gather():
                        for kh in range(2):
                            pgs = [acc4() for _ in range(4)]
                            for ii in range(ntl):
                                i = ti0 + ii
                                ht = h2r()
                                S.dma('sync', ht[:, :], h2tm[i * 128:(i + 1) * 128, :], ht, h2tm)
                                for kk in range(4):
                                    k = kh * 4 + kk
                                    P(lambda e, pg=pgs[kk], ht=ht, k=k, ii=ii: e.matmul(pg[:, 0:ns], lhsT=ht[:, k * 128:(k + 1) * 128], rhs=selT_[:, ii, 0:ns], start=(ii == 0), stop=(ii == ntl - 1)), [ht, selT_], [pgs[kk]])
                            for kk in range(4):
                                k = kh * 4 + kk
                                A(lambda e, pg=pgs[kk], k=k: e.copy(out=xsT_[:, k, 0:ns], in_=pg[:, 0:ns]), [pgs[kk]], [xsT_])

                    def scatter():
                        for ii in range(ntl):
                            i = ti0 + ii
                            yt = ytr()
                            for nb in range(2):
                                psc = gen4()
                                for st_ in range(nst):
                                    P(lambda e, psc=psc, st_=st_, ii=ii, nb=nb: e.matmul(psc[:, :], lhsT=sel_[:, st_, ii * 128:(ii + 1) * 128], rhs=ye_[:, st_, nb * 512:(nb + 1) * 512], start=(st_ == 0), stop=(st_ == nst - 1)), [sel_, ye_], [psc])
                                if nb == 0:
                                    V(lambda e, psc=psc, yt=yt, nb=nb, i=i: e.tensor_scalar(out=yt[:, nb * 512:(nb + 1) * 512], in0=psc[:, :], scalar1=gate_all[:, i, ex:ex + 1], scalar2=None, op0=ALU.mult), [psc, gate_all], [yt])
                                else:
                                    A(lambda e, psc=psc, yt=yt, nb=nb, i=i: e.activation(out=yt[:, nb * 512:(nb + 1) * 512], in_=psc[:, :], func=AF.Identity, scale=gate_all[:, i, ex:ex + 1]), [psc, gate_all], [yt])
                            yb = ymoe_t[i]
                            yb.grp = acc_grp[ex % 2]
                            S.dma('gpsimd', ymoe[i * 128:(i + 1) * 128, :], yt[:, :], yb, yt, accum_op=ALU.add)
                    return (selT_build, sel_build, gather, scatter)

                def mlp_joint(ex):
                    wgv = I['w_gate_up'].t.ap()[l, ex].rearrange("(k p) n -> p k n", p=128)
                    wdv = I['w_down'].t.ap()[l, ex].rearrange("(f p) n -> p f n", p=128)
                    for b in range(11):
                        wa, wu = wgr(), wur()
                        S.dma('gpsimd', wa[:, :, :], wgv[:, :, b * 256:(b + 1) * 256], wa, I['w_gate_up'])
                        S.dma('gpsimd', wu[:, :, :], wgv[:, :, DEXP + b * 256:DEXP + (b + 1) * 256], wu, I['w_gate_up'])
                        for ff in range(2):
                            f = 2 * b + ff
                            for si in range(len(sets)):
                                ns = sets[si][3] * 128
                                xs_, act_ = sbufs[si]['xsT'], sbufs[si]['actT']
                                pa_, pu_ = gen4(), gen4()
                                for k in range(8):
                                    P(lambda e, pa_=pa_, wa=wa, k=k, ff=ff, ns=ns, xs_=xs_: e.matmul(pa_[:, 0:ns], lhsT=wa[:, k, ff * 128:(ff + 1) * 128], rhs=xs_[:, k, 0:ns], start=(k == 0), stop=(k == 7)), [wa, xs_], [pa_])
                                for k in range(8):
                                    P(lambda e, pu_=pu_, wu=wu, k=k, ff=ff, ns=ns, xs_=xs_: e.matmul(pu_[:, 0:ns], lhsT=wu[:, k, ff * 128:(ff + 1) * 128], rhs=xs_[:, k, 0:ns], start=(k == 0), stop=(k == 7)), [wu, xs_], [pu_])
                                sa = sar()
                                A(lambda e, pa_=pa_, sa=sa, ns=ns: e.activation(out=sa[:, 0:ns], in_=pa_[:, 0:ns], func=AF.Silu), [pa_], [sa])
                                V(lambda e, pu_=pu_, sa=sa, f=f, ns=ns, act_=act_: e.tensor_tensor(out=act_[:, f, 0:ns], in0=pu_[:, 0:ns], in1=sa[:, 0:ns], op=ALU.mult), [pu_, sa], [act_])
                    for nb in range(2):
                        pds = []
                        for si in range(len(sets)):
                            nst = sets[si][3]
                            pds.append([acc4() for _ in range(nst)] if nst == 4 else [gen4() for _ in range(nst)])
                        for f2 in range(11):
                            wd = wdr()
                            S.dma('gpsimd', wd[:, :, :], wdv[:, 2 * f2:2 * f2 + 2, nb * 512:(nb + 1) * 512], wd, I['w_down'])
                            for ff in range(2):
                                f = 2 * f2 + ff
                                for si in range(len(sets)):
                                    act_ = sbufs[si]['actT']
                                    for st_ in range(sets[si][3]):
                                        P(lambda e, pd=pds[si][st_], wd=wd, f=f, ff=ff, st_=st_, act_=act_: e.matmul(pd[:, :], lhsT=act_[:, f, st_ * 128:(st_ + 1) * 128], rhs=wd[:, ff, :], start=(f == 0), stop=(f == 21)), [act_, wd], [pds[si][st_]])
                        for si in range(len(sets)):
                            ye_ = sbufs[si]['ye']
                            for st_ in range(sets[si][3]):
                                A(lambda e, pd=pds[si][st_], st_=st_, nb=nb, ye_=ye_: e.copy(out=ye_[:, st_, nb * 512:(nb + 1) * 512], in_=pd[:, :]), [pds[si][st_]], [ye_])

                items = [[make_item(ex, si) for si in range(len(sets))] for ex in range(NEXP)]
                for it_ in items[0]:
                    it_[0]()
                for ex in range(NEXP):
                    for it_ in items[ex]:
                        it_[2]()
                    if ex + 1 < NEXP:
                        for it_ in items[ex + 1]:
                            it_[0]()
                    for it_ in items[ex]:
                        it_[1]()
                    mlp_joint(ex)
                    for it_ in items[ex]:
                        it_[3]()
            ymoe.grp = acc_grp[0]
            ymoe.w = ('d', acc_grp[0])
            ymoe.r = {('d', id(acc_grp[1])): ('d', acc_grp[1])}

    def ph_ln2(self, l):
        S = self.S
        I = self.I
        V = lambda fn, r, w: S.op('vector', fn, r, w)
        G = lambda fn, r, w: S.op('gpsimd', fn, r, w)
        ctx_out = l < DEPTH - 1
        last = (l == DEPTH - 1)
        xres = self.scr['xres']
        modscr = self.scr['modscr%d' % l]
        ymoe = self.scratch('ymoe', [NT, D], F32)
        with S.phase():
            g2 = S.sbuf([128, 2, D], F32, 'g2v')
            for w in range(2):
                S.dma('sync', g2[:, w, :], modscr[w, :, 5 * D:6 * D], g2, modscr, part=(w == 1))
            lng = S.sbuf([128, 2, D], F32, 'ln2gb')
            S.dma('sync', lng[:, 0, :], I['ln2_g'][l:l + 1, :].partition_broadcast(128), lng, I['ln2_g'])
            S.dma('sync', lng[:, 1, :], I['ln2_b'][l:l + 1, :].partition_broadcast(128), lng, I['ln2_b'], part=True)
            xr = S.ring(3, [128, D], F32, 'xt')
            yr = S.ring(3, [128, D], F32, 'ym')
            str_ = S.ring(3, [128, 16], F32, 'st')
            def tile_chain(i):
                w = 1 if i < 2 else 0
                rows = slice(i * 128, (i + 1) * 128)
                xt, ym = xr(), yr()
                S.dma('sync', xt[:, :], xres[rows, :], xt, xres)
                S.dma('sync', ym[:, :], ymoe[rows, :], ym, ymoe)
                V(lambda e, ym=ym, w=w: e.tensor_tensor(out=ym[:, :], in0=ym[:, :], in1=g2[:, w, :], op=ALU.mult), [ym, g2], [ym])
                yield
                V(lambda e, xt=xt, ym=ym: e.scalar_tensor_tensor(out=ym[:, :], in0=xt[:, :], scalar=ALPHA, in1=ym[:, :], op0=ALU.mult, op1=ALU.add), [xt, ym], [ym])
                yield
                st = str_()
                yield from self.ln_stats_gen(ym, st)
                V(lambda e, ym=ym, st=st: e.tensor_scalar(out=ym[:, :], in0=ym[:, :], scalar1=st[:, 12:13], scalar2=st[:, 14:15], op0=ALU.subtract, op1=ALU.mult), [ym, st], [ym])
                yield
                V(lambda e, ym=ym: e.tensor_tensor(out=ym[:, :], in0=ym[:, :], in1=lng[:, 0, :], op=ALU.mult), [ym, lng], [ym])
                yield
                V(lambda e, ym=ym, xt=xt: e.tensor_tensor(out=xt[:, :], in0=ym[:, :], in1=lng[:, 1, :], op=ALU.add), [ym, lng], [xt])
                yield
                if last:
                    S.dma('sync', self.out[(i - 2) * 128:(i - 1) * 128, :], xt[:, :], self.out, xt, part=True)
                else:
                    S.dma('sync', xres[rows, :], xt[:, :], xres, xt, part=True)

            tiles = list(range(0 if ctx_out else 2, NTILE))
            for t0_ in range(0, len(tiles), 2):
                interleave([tile_chain(t_) for t_ in tiles[t0_:t0_ + 2]])
    def range_reduce(self, eng, x, k):
        S = self.S
        S.op(eng, lambda e: e.tensor_scalar(out=k[:, :], in0=x[:, :], scalar1=float(1.0 / TWO_PI), scalar2=MAGIC, op0=ALU.mult, op1=ALU.add), [x], [k])
        S.op(eng, lambda e: e.tensor_scalar(out=k[:, :], in0=k[:, :], scalar1=-MAGIC, scalar2=None, op0=ALU.add), [k], [k])
        S.op('vector', lambda e: e.scalar_tensor_tensor(out=x[:, :], in0=k[:, :], scalar=-CW1, in1=x[:, :], op0=ALU.mult, op1=ALU.add), [x, k], [x])
        S.op('vector', lambda e: e.scalar_tensor_tensor(out=x[:, :], in0=k[:, :], scalar=-CW2, in1=x[:, :], op0=ALU.mult, op1=ALU.add), [x, k], [x])
        S.op(eng, lambda e: e.tensor_scalar(out=x[:, :], in0=x[:, :], scalar1=PI_LO, scalar2=-PI_LO, op0=ALU.min, op1=ALU.max), [x], [x])

    def rope_tables(self):
        S = self.S
        cosT = S.sbuf([128, NT], F32, 'cosT')
        sinT = S.sbuf([128, NT], F32, 'sinT')
        ang = S.sbuf([128, NLAT], F32, 'ang')
        kk = S.sbuf([128, NLAT], F32, 'kk')
        pidx = S.sbuf([128, 1], I32, 'pidx')
        pf = S.sbuf([128, 4], F32, 'pf')
        ti = S.sbuf([128, 2], I32, 'ti')
        rowi = S.sbuf([128, NLAT], I32, 'rowi')
        S.op('gpsimd', lambda e: e.iota(pidx[:, :], pattern=[[0, 1]], base=0, channel_multiplier=1), [], [pidx])
        S.op('vector', lambda e: e.tensor_scalar(out=ti[:, 0:1], in0=pidx[:, :], scalar1=15, scalar2=None, op0=ALU.bitwise_and), [pidx], [ti])
        S.op('vector', lambda e: e.tensor_scalar(out=ti[:, 1:2], in0=pidx[:, :], scalar1=5, scalar2=1, op0=ALU.logical_shift_right, op1=ALU.bitwise_and), [pidx], [ti])
        S.op('vector', lambda e: e.tensor_copy(out=pf[:, 0:2], in_=ti[:, 0:2]), [ti], [pf])
        S.op('scalar', lambda e: e.activation(out=pf[:, 2:3], in_=pf[:, 0:1], func=AF.Exp, scale=-math.log(10000.0) / 16.0), [pf], [pf])
        S.op('vector', lambda e: e.tensor_tensor(out=pf[:, 3:4], in0=pf[:, 1:2], in1=pf[:, 2:3], op=ALU.mult), [pf], [pf])
        S.op('vector', lambda e: e.tensor_tensor(out=pf[:, 0:1], in0=pf[:, 2:3], in1=pf[:, 3:4], op=ALU.subtract), [pf], [pf])
        S.op('gpsimd', lambda e: e.iota(rowi[:, :], pattern=[[1, 64], [0, 64]], base=0, channel_multiplier=0), [], [rowi])
        S.op('vector', lambda e: e.tensor_copy(out=ang[:, :], in_=rowi[:, :]), [rowi], [ang])
        S.op('vector', lambda e: e.tensor_scalar(out=ang[:, :], in0=ang[:, :], scalar1=pf[:, 0:1], scalar2=None, op0=ALU.mult), [ang, pf], [ang])
        S.op('gpsimd', lambda e: e.iota(rowi[:, :], pattern=[[0, 64], [1, 64]], base=0, channel_multiplier=0), [ang], [rowi])
        S.op('vector', lambda e: e.tensor_copy(out=kk[:, :], in_=rowi[:, :]), [rowi], [kk])
        S.op('vector', lambda e: e.scalar_tensor_tensor(out=ang[:, :], in0=kk[:, :], scalar=pf[:, 3:4], in1=ang[:, :], op0=ALU.mult, op1=ALU.add), [kk, pf, ang], [ang])
        self.range_reduce('vector', ang, kk)
        S.op('scalar', lambda e: e.activation(out=sinT[:, NCTX:NT], in_=ang[:, :], func=AF.Sin), [ang], [sinT])
        S.op('scalar', lambda e: e.activation(out=kk[:, :], in_=ang[:, :], func=AF.Sin, scale=0.5), [ang], [kk])
        S.op('vector', lambda e: e.tensor_tensor(out=kk[:, :], in0=kk[:, :], in1=kk[:, :], op=ALU.mult), [kk], [kk])
        S.op('vector', lambda e: e.tensor_scalar(out=cosT[:, NCTX:NT], in0=kk[:, :], scalar1=-2.0, scalar2=1.0, op0=ALU.mult, op1=ALU.add), [kk], [cosT])
        S.op('gpsimd', lambda e: e.memset(cosT[:, 0:NCTX], 1.0), [], [cosT])
        S.op('gpsimd', lambda e: e.memset(sinT[:, 0:NCTX], 0.0), [], [sinT])
        return cosT, sinT


def make_in_maps(inputs):
    maps = []
    for core in range(8):
        s = core % 4
        m = {'x': np.ascontiguousarray(inputs['x'][s]), 'c': np.ascontiguousarray(inputs['c'][s]),
             'ctx': np.ascontiguousarray(inputs['ctx'][s]), 'c_ctx': np.ascontiguousarray(inputs['c_ctx'])}
        for nm, _ in W_NAMES:
            m[nm] = np.ascontiguousarray(inputs[nm])
        maps.append(m)
    return maps


def kernel(**inputs):
    inputs = {k: np.asarray(v, dtype=np.float32) for k, v in inputs.items()}
    prog = Prog()
    nc = prog.build()
    res = run_bass_kernel_spmd(nc, make_in_maps(inputs), core_ids=list(range(8)))
    out = np.stack([np.asarray(res.results[s]['out'], dtype=np.float32) for s in range(4)], axis=0)
    return out
```
